# Optimizing a Trainium2 kernel written in Bass

```python
import math, functools
import jax, jax.numpy as jnp
from jax import lax
import numpy as np

D_MODEL = 1024
BATCH = 16
SEQ = 256
DEPTH = 4
DEC_BATCH = 4
DEC_SEQ = 2048
PAST_LEN = 256

GRID_W = 64
N_MIXERS = 3
N_GDN = (DEPTH + 2) // 3
N_SSD = (DEPTH + 1) // 3
N_LRU = DEPTH // 3
DN_ALPHA = (2 * DEPTH) ** 0.25
DN_BETA = (8 * DEPTH) ** -0.25
LN_EPS = 1e-5
CHUNK = 64
GDN_HEADS = 8
GDN_DK = 128
GDN_DV = 128
GDN_CONV = 4
GDN_PROJ = 2 * GDN_HEADS * GDN_DK + 2 * GDN_HEADS * GDN_DV + 4 * GDN_HEADS
SSD_EXPAND = 2
SSD_INNER = SSD_EXPAND * D_MODEL
SSD_HEADDIM = 64
SSD_HEADS = SSD_INNER // SSD_HEADDIM
SSD_GROUPS = 4
SSD_STATE = 128
SSD_CONV = 4
SSD_CONV_DIM = SSD_INNER + 2 * SSD_GROUPS * SSD_STATE
SSD_PROJ = SSD_INNER + SSD_CONV_DIM + 2 * SSD_HEADS
LRU_WIDTH = 1280
LRU_BLOCKS = 10
LRU_BW = LRU_WIDTH // LRU_BLOCKS
LRU_CONV = 4
LRU_C = 8.0
FFN_HIDDEN = 2816
FFN_CONV = 3

kernel_name = 'hybrid_bidir_deltanet_ssd_rglru_diffusion_step'

F32 = jnp.float32


def layer_norm(x, g, b):
    xf = x.astype(F32)
    mu = jnp.mean(xf, -1, keepdims=True)
    var = jnp.mean(jnp.square(xf - mu), -1, keepdims=True)
    return ((xf - mu) * lax.rsqrt(var + LN_EPS)).astype(x.dtype) * g + b


def rms_norm(x, g, groups=1):
    shp = x.shape
    xf = x.astype(F32).reshape(*shp[:-1], groups, shp[-1] // groups)
    xf = xf * lax.rsqrt(jnp.mean(jnp.square(xf), -1, keepdims=True) + LN_EPS)
    return xf.reshape(shp).astype(x.dtype) * g


def l2norm(x):
    xf = x.astype(F32)
    return xf * lax.rsqrt(jnp.sum(jnp.square(xf), -1, keepdims=True) + 1e-6)


def dwconv1d(x, w):
    k, ch = w.shape
    return lax.conv_general_dilated(x, w[:, None, :], (1,), [((k - 1) // 2, k // 2)],
                                    dimension_numbers=('NWC', 'WIO', 'NWC'), feature_group_count=ch)


def bidirectional(core, seq_f, seq_b, s0):
    y_f, s_f = core(*seq_f, s0[:, 0])
    y_b, s_b = core(*[jnp.flip(a, 1) for a in seq_b], s0[:, 1])
    return y_f + jnp.flip(y_b, 1), jnp.stack([s_f, s_b], axis=1)


def gdn_chunked(q, k, v, g, beta, s0):
    b, t, h, dk = q.shape
    dv = v.shape[-1]
    n = t // CHUNK

    def chunked(a):
        a = a.astype(F32).reshape(b, n, CHUNK, h, *a.shape[3:])
        return jnp.moveaxis(a, 3, 2)

    qc = chunked(q) * (dk ** -0.5)
    kc, vc, bc = chunked(k), chunked(v), chunked(beta)
    gc = jnp.cumsum(chunked(g), axis=-1)
    incl = jnp.tril(jnp.ones((CHUNK, CHUNK), bool))
    strict = jnp.tril(jnp.ones((CHUNK, CHUNK), bool), -1)
    decay = jnp.where(incl, jnp.exp(jnp.where(incl, gc[..., :, None] - gc[..., None, :], 0.0)), 0.0)
    kk = jnp.einsum('bnhld,bnhsd->bnhls', kc, kc)
    lower = jnp.where(strict, bc[..., :, None] * kk * decay, 0.0)
    eye = jnp.eye(CHUNK, dtype=F32)
    tmat = lax.linalg.triangular_solve(eye + lower, jnp.broadcast_to(eye, lower.shape),
                                       left_side=True, lower=True)
    u = tmat @ (vc * bc[..., None])
    w = tmat @ (kc * (bc * jnp.exp(gc))[..., None])
    attn = jnp.where(incl, jnp.einsum('bnhld,bnhsd->bnhls', qc, kc) * decay, 0.0)
    q_in = qc * jnp.exp(gc)[..., None]
    g_last = gc[..., -1]
    k_out = kc * jnp.exp(g_last[..., None] - gc)[..., None]

    def step(s, inp):
        u_i, w_i, q_i, k_i, a_i, gl_i = inp
        v_new = u_i - w_i @ s
        o_i = q_i @ s + a_i @ v_new
        s = s * jnp.exp(gl_i)[..., None, None] + jnp.swapaxes(k_i, -1, -2) @ v_new
        return s, o_i

    xs = tuple(jnp.moveaxis(a, 1, 0) for a in (u, w, q_in, k_out, attn, g_last))
    s_fin, o = lax.scan(step, s0.astype(F32), xs)
    o = jnp.moveaxis(jnp.moveaxis(o, 0, 1), 2, 3).reshape(b, t, h, dv)
    return o, s_fin


def ssd_chunked(x, dt, da, bmat, cmat, s0):
    b, t, nh, p = x.shape
    ng, ns = bmat.shape[2:]
    nj = nh // ng
    n = t // CHUNK
    xd = (x.astype(F32) * dt.astype(F32)[..., None]).reshape(b, n, CHUNK, ng, nj, p)
    acum = jnp.moveaxis(jnp.cumsum(da.astype(F32).reshape(b, n, CHUNK, ng, nj), axis=2), 2, -1)
    bc = bmat.astype(F32).reshape(b, n, CHUNK, ng, ns)
    cc = cmat.astype(F32).reshape(b, n, CHUNK, ng, ns)
    incl = jnp.tril(jnp.ones((CHUNK, CHUNK), bool))
    decay = jnp.where(incl, jnp.exp(jnp.where(incl, acum[..., :, None] - acum[..., None, :], 0.0)), 0.0)
    cb = jnp.einsum('bnlgk,bnsgk->bngls', cc, bc)
    y_diag = jnp.einsum('bngls,bngjls,bnsgjp->bnlgjp', cb, decay, xd)
    to_end = jnp.exp(acum[..., -1:] - acum)
    chunk_states = jnp.einsum('bnlgk,bngjl,bnlgjp->bngjpk', bc, to_end, xd)
    chunk_decay = jnp.exp(acum[..., -1])

    def step(s, inp):
        st_i, dec_i = inp
        return s * dec_i[..., None, None] + st_i, s

    s_fin, s_enter = lax.scan(step, s0.astype(F32).reshape(b, ng, nj, p, ns),
                              (jnp.moveaxis(chunk_states, 1, 0), jnp.moveaxis(chunk_decay, 1, 0)))
    s_enter = jnp.moveaxis(s_enter, 0, 1)
    y_off = jnp.einsum('bnlgk,bngjpk,bngjl->bnlgjp', cc, s_enter, jnp.exp(acum))
    return (y_diag + y_off).reshape(b, t, nh, p), s_fin.reshape(b, nh, p, ns)


def linear_scan(a, bx, h0):
    def combine(l, r):
        return l[0] * r[0], r[0] * l[1] + r[1]
    a_cum, b_cum = lax.associative_scan(combine, (a, bx), axis=1)
    hs = a_cum * h0.astype(F32)[:, None] + b_cum
    return hs, hs[:, -1]


def gdn_mixer(h, s0, w_in, w_conv, a_log, dt_bias, norm_g, w_out):
    b, t, _ = h.shape
    hk, hv = GDN_HEADS * GDN_DK, GDN_HEADS * GDN_DV
    qkv, z, beta_raw, a_raw = jnp.split(h @ w_in, [2 * hk + hv, 2 * hk + 2 * hv, 2 * hk + 2 * hv + 2 * GDN_HEADS], axis=-1)
    qkv = jax.nn.silu(dwconv1d(qkv, w_conv))
    q, k, v = jnp.split(qkv, [hk, 2 * hk], axis=-1)
    q = l2norm(q.reshape(b, t, GDN_HEADS, GDN_DK))
    k = l2norm(k.reshape(b, t, GDN_HEADS, GDN_DK))
    v = v.reshape(b, t, GDN_HEADS, GDN_DV)
    beta = jax.nn.sigmoid(beta_raw.reshape(b, t, 2, GDN_HEADS).astype(F32))
    g = -jnp.exp(a_log.astype(F32)) * jax.nn.softplus(a_raw.reshape(b, t, 2, GDN_HEADS).astype(F32) + dt_bias)
    o, s_fin = bidirectional(gdn_chunked, (q, k, v, g[:, :, 0], beta[:, :, 0]),
                             (q, k, v, g[:, :, 1], beta[:, :, 1]), s0)
    o = rms_norm(o.astype(h.dtype), norm_g) * jax.nn.silu(z.reshape(b, t, GDN_HEADS, GDN_DV))
    return o.reshape(b, t, hv) @ w_out, s_fin


def ssd_mixer(h, s0, w_in, w_conv, conv_b, dt_bias, a_log, d_skip, norm_g, w_out):
    b, t, _ = h.shape
    gn = SSD_GROUPS * SSD_STATE
    z, xbc, dt_raw = jnp.split(h @ w_in, [SSD_INNER, SSD_INNER + SSD_CONV_DIM], axis=-1)
    xbc = jax.nn.silu(dwconv1d(xbc, w_conv) + conv_b)
    xs, bm, cm = jnp.split(xbc, [SSD_INNER, SSD_INNER + gn], axis=-1)
    xs = xs.reshape(b, t, SSD_HEADS, SSD_HEADDIM)
    bm = bm.reshape(b, t, SSD_GROUPS, SSD_STATE)
    cm = cm.reshape(b, t, SSD_GROUPS, SSD_STATE)
    dt = jax.nn.softplus(dt_raw.reshape(b, t, 2, SSD_HEADS).astype(F32) + dt_bias)
    da = dt * -jnp.exp(a_log.astype(F32))
    y, s_fin = bidirectional(ssd_chunked, (xs, dt[:, :, 0], da[:, :, 0], bm, cm),
                             (xs, dt[:, :, 1], da[:, :, 1], bm, cm), s0)
    y = y.astype(h.dtype) + d_skip[:, None] * xs
    y = rms_norm(y.reshape(b, t, SSD_INNER) * jax.nn.silu(z), norm_g, SSD_GROUPS)
    return y @ w_out, s_fin


def lru_mixer(h, s0, w_in, w_conv, conv_b, gate_w, gate_b, lam, w_out):
    b, t, _ = h.shape
    gate_branch, xr = jnp.split(h @ w_in, 2, axis=-1)
    xr = dwconv1d(xr, w_conv) + conv_b
    gates = jnp.einsum('btnk,dgnkm->btdgnm', xr.reshape(b, t, LRU_BLOCKS, LRU_BW), gate_w)
    gates = jax.nn.sigmoid(gates.reshape(b, t, 2, 2, LRU_WIDTH).astype(F32) + gate_b)
    r, i = gates[:, :, :, 0], gates[:, :, :, 1]
    log_a = -LRU_C * r * jax.nn.softplus(-lam.astype(F32))
    bx = jnp.sqrt(-jnp.expm1(2.0 * log_a)) * i * xr.astype(F32)[:, :, None]
    a = jnp.exp(log_a)
    y, s_fin = bidirectional(linear_scan, (a[:, :, 0], bx[:, :, 0]), (a[:, :, 1], bx[:, :, 1]), s0)
    y = y.astype(h.dtype) * jax.nn.gelu(gate_branch)
    return y @ w_out, s_fin


def conv_ffn(h, w_in, w_conv, w_out, grid_rows):
    b, t, _ = h.shape
    u = (h @ w_in).reshape(b, grid_rows, t // grid_rows, 2 * FFN_HIDDEN)
    u = lax.conv_general_dilated(u, w_conv[:, :, None, :], (1, 1), [(1, 1), (1, 1)],
                                 dimension_numbers=('NHWC', 'HWIO', 'NHWC'), feature_group_count=2 * FFN_HIDDEN)
    gate, val = jnp.split(u.reshape(b, t, 2 * FFN_HIDDEN), 2, axis=-1)
    return (jax.nn.silu(gate) * val) @ w_out


def trunk_layer(x, mod, mixer, mixer_params, s0, ln_g, ln_b, ffn_w_in, ffn_conv, ffn_w_out, grid_rows):
    sh1, sc1, g1, sh2, sc2, g2 = jnp.split(mod[:, None, :], 6, axis=-1)
    m, s_fin = mixer(x * (1 + sc1) + sh1, s0, *mixer_params)
    x = layer_norm(DN_ALPHA * x + g1 * m, ln_g[0], ln_b[0])
    f = conv_ffn(x * (1 + sc2) + sh2, ffn_w_in, ffn_conv, ffn_w_out, grid_rows)
    x = layer_norm(DN_ALPHA * x + g2 * f, ln_g[1], ln_b[1])
    return x, s_fin


def setup_inputs(seed: int = 0) -> dict:
    key = jax.random.key(seed)
    ks = iter(jax.random.split(key, 48))
    D = D_MODEL

    def nrm(shape, scale):
        return jax.random.normal(next(ks), shape, F32) * scale

    def uni(shape, lo, hi):
        return jax.random.uniform(next(ks), shape, F32, lo, hi)

    def dt_bias(shape):
        dt = jnp.exp(uni(shape, math.log(1e-3), math.log(1e-1)))
        return dt + jnp.log(-jnp.expm1(-dt))

    lam_u = uni((N_LRU, 2, LRU_WIDTH), 0.9, 0.999) ** (1.0 / LRU_C)
    lru_lambda = jnp.log(lam_u) - jnp.log1p(-lam_u)
    return {
        'x_prompt': nrm((BATCH, SEQ, D), 1.0),
        'x_sample': nrm((DEC_BATCH, DEC_SEQ, D), 1.0),
        'state_gdn': nrm((DEC_BATCH, N_GDN, 2, GDN_HEADS, GDN_DK, GDN_DV), 0.1),
        'state_ssd': nrm((DEC_BATCH, N_SSD, 2, SSD_HEADS, SSD_HEADDIM, SSD_STATE), 0.1),
        'state_lru': nrm((DEC_BATCH, N_LRU, 2, LRU_WIDTH), 0.5),
        'c': nrm((DEC_BATCH, D), 1.0),
        'c_ctx': nrm((D,), 1.0),
        'ada_w': nrm((DEPTH, D, 6 * D), 0.5 * D ** -0.5),
        'ada_b': nrm((DEPTH, 6 * D), 0.02),
        'ln_g': 1.0 + nrm((DEPTH, 2, D), 0.02),
        'ln_b': nrm((DEPTH, 2, D), 0.02),
        'ffn_w_in': nrm((DEPTH, D, 2 * FFN_HIDDEN), D ** -0.5),
        'ffn_conv': nrm((DEPTH, FFN_CONV, FFN_CONV, 2 * FFN_HIDDEN), 1.0 / FFN_CONV),
        'ffn_w_out': nrm((DEPTH, FFN_HIDDEN, D), DN_BETA * FFN_HIDDEN ** -0.5),
        'gdn_w_in': nrm((N_GDN, D, GDN_PROJ), D ** -0.5),
        'gdn_conv': nrm((N_GDN, GDN_CONV, 2 * GDN_HEADS * GDN_DK + GDN_HEADS * GDN_DV), GDN_CONV ** -0.5),
        'gdn_a_log': jnp.log(uni((N_GDN, 2, GDN_HEADS), 1.0, 16.0)),
        'gdn_dt_bias': dt_bias((N_GDN, 2, GDN_HEADS)),
        'gdn_norm': 1.0 + nrm((N_GDN, GDN_DV), 0.02),
        'gdn_w_out': nrm((N_GDN, GDN_HEADS * GDN_DV, D), DN_BETA * (GDN_HEADS * GDN_DV) ** -0.5),
        'ssd_w_in': nrm((N_SSD, D, SSD_PROJ), D ** -0.5),
        'ssd_conv': nrm((N_SSD, SSD_CONV, SSD_CONV_DIM), SSD_CONV ** -0.5),
        'ssd_conv_b': nrm((N_SSD, SSD_CONV_DIM), 0.02),
        'ssd_dt_bias': dt_bias((N_SSD, 2, SSD_HEADS)),
        'ssd_a_log': jnp.log(uni((N_SSD, 2, SSD_HEADS), 1.0, 16.0)),
        'ssd_d': 1.0 + nrm((N_SSD, SSD_HEADS), 0.1),
        'ssd_norm': 1.0 + nrm((N_SSD, SSD_INNER), 0.02),
        'ssd_w_out': nrm((N_SSD, SSD_INNER, D), DN_BETA * SSD_INNER ** -0.5),
        'lru_w_in': nrm((N_LRU, D, 2 * LRU_WIDTH), D ** -0.5),
        'lru_conv': nrm((N_LRU, LRU_CONV, LRU_WIDTH), LRU_CONV ** -0.5),
        'lru_conv_b': nrm((N_LRU, LRU_WIDTH), 0.02),
        'lru_gate_w': nrm((N_LRU, 2, 2, LRU_BLOCKS, LRU_BW, LRU_BW), LRU_BW ** -0.5),
        'lru_gate_b': nrm((N_LRU, 2, 2, LRU_WIDTH), 0.02),
        'lru_lambda': lru_lambda,
        'lru_w_out': nrm((N_LRU, LRU_WIDTH, D), DN_BETA * LRU_WIDTH ** -0.5),
    }


def reference(x_prompt, x_sample, state_gdn, state_ssd, state_lru, c, c_ctx, ada_w, ada_b, ln_g, ln_b,
              ffn_w_in, ffn_conv, ffn_w_out, gdn_w_in, gdn_conv, gdn_a_log, gdn_dt_bias, gdn_norm, gdn_w_out,
              ssd_w_in, ssd_conv, ssd_conv_b, ssd_dt_bias, ssd_a_log, ssd_d, ssd_norm, ssd_w_out,
              lru_w_in, lru_conv, lru_conv_b, lru_gate_w, lru_gate_b, lru_lambda, lru_w_out):
    n_ctx_req = x_prompt.shape[0]
    rows = x_sample.shape[1] // GRID_W
    xp, xs = x_prompt, x_sample
    new_states = ([], [], [])
    for i in range(DEPTH):
        kind, j = i % N_MIXERS, i // N_MIXERS
        if kind == 0:
            mixer, cache = gdn_mixer, state_gdn
            params = (gdn_w_in[j], gdn_conv[j], gdn_a_log[j], gdn_dt_bias[j], gdn_norm[j], gdn_w_out[j])
        elif kind == 1:
            mixer, cache = ssd_mixer, state_ssd
            params = (ssd_w_in[j], ssd_conv[j], ssd_conv_b[j], ssd_dt_bias[j], ssd_a_log[j], ssd_d[j],
                      ssd_norm[j], ssd_w_out[j])
        else:
            mixer, cache = lru_mixer, state_lru
            params = (lru_w_in[j], lru_conv[j], lru_conv_b[j], lru_gate_w[j], lru_gate_b[j], lru_lambda[j],
                      lru_w_out[j])
        layer = functools.partial(trunk_layer, mixer=mixer, mixer_params=params, ln_g=ln_g[i], ln_b=ln_b[i],
                                  ffn_w_in=ffn_w_in[i], ffn_conv=ffn_conv[i], ffn_w_out=ffn_w_out[i])
        mod_ctx = (jax.nn.silu(c_ctx) @ ada_w[i] + ada_b[i])[None, :]
        mod_lat = jax.nn.silu(c) @ ada_w[i] + ada_b[i]
        s_ctx0 = jnp.zeros((n_ctx_req,) + cache.shape[2:], cache.dtype)
        xp, s_ctx = layer(xp, mod_ctx, s0=s_ctx0, grid_rows=1)
        xs, _ = layer(xs, mod_lat, s0=cache[:, j], grid_rows=rows)
        new_states[kind].append(s_ctx.astype(cache.dtype))
    new_gdn = jnp.stack(new_states[0], axis=1)
    new_ssd = jnp.stack(new_states[1], axis=1)
    new_lru = jnp.stack(new_states[2], axis=1)
    return (xp, xs, new_gdn, new_ssd, new_lru)
```

```python
import numpy as np
import concourse.bass as bass
import concourse.mybir as mybir
from concourse.bass_utils import run_bass_kernel_spmd
from contextlib import ExitStack

F32 = mybir.dt.float32
F32R = mybir.dt.float32r
BF16 = mybir.dt.bfloat16
ALU = mybir.AluOpType
AF = mybir.ActivationFunctionType
AX = mybir.AxisListType

ENGS = ["pe", "act", "dve", "pool", "sp"]
NDS = 24
MAXEMB = 1
ARENA_WORDS = 53000


class R:
    __slots__ = ("ap", "keys")

    def __init__(self, ap, keys):
        self.ap = ap
        self.keys = keys


class Buf:
    def __init__(self, prog, name, shape, dtype, nsub=1, psum=False):
        self.name = name
        self.nsub = nsub
        if psum:
            self.t = prog.st.enter_context(prog.nc.psum_tensor(name, shape, dtype))
        else:
            n = 1
            for d in shape[1:]:
                n *= d
            esz = 2 if dtype == BF16 else 4
            words = (n * esz + 3) // 4
            off = prog.aoff
            prog.aoff += words
            assert prog.aoff <= ARENA_WORDS, (name, prog.aoff)
            prog.apeak = max(prog.apeak, prog.aoff)
            ap = prog.arena[:, off:off + words]
            if dtype == BF16:
                ap = ap.bitcast(BF16)
            ap = ap[:, 0:n]
            if len(shape) > 2:
                names = " ".join("d%d" % i for i in range(len(shape) - 1))
                kw = {"d%d" % i: shape[i + 1] for i in range(len(shape) - 1)}
                ap = ap.rearrange("p (%s) -> p %s" % (names, names), **kw)
            self.t = ap

    def r(self, ap=None, sub=None):
        if ap is None:
            ap = self.t[:] if not hasattr(self.t, "rearrange") else self.t
        if sub is None:
            keys = [(self.name, i) for i in range(self.nsub)]
        elif isinstance(sub, (list, tuple, range)):
            keys = [(self.name, i) for i in sub]
        else:
            keys = [(self.name, sub)]
        return R(ap, keys)


class Prog:
    def __init__(self, nc, st):
        self.nc = nc
        self.st = st
        self.q = {e: [] for e in ENGS}
        self.cnt = {e: 0 for e in ENGS}
        self.sem = {e: st.enter_context(nc.semaphore("s_" + e)) for e in ENGS}
        self.dsem = [st.enter_context(nc.semaphore("d%d" % i)) for i in range(NDS)]
        self.dcnt = [0] * NDS
        self.dnext = 0
        self.known = {e: {} for e in ENGS}
        self.last_w = {}
        self.readers = {}
        self.nops = 0
        self.nwaits = 0
        self.bar = {e: None for e in ENGS}
        self._bank = 0
        self.arena_t = st.enter_context(nc.sbuf_tensor("arena", [128, ARENA_WORDS], F32))
        self.arena = self.arena_t[:]
        self.aoff = 0
        self.apeak = 0

    def mark(self):
        return self.aoff

    def release(self, m):
        self.aoff = m

    def bank(self):
        b = self._bank
        self._bank = (self._bank + 1) % 8
        return b

    def barrier(self):
        snap = {e: self.cnt[e] for e in ENGS if self.cnt[e] > 0}
        for i in range(NDS):
            if self.dcnt[i] > 0:
                snap["d%d" % i] = self.dcnt[i]
        for e in ENGS:
            self.bar[e] = dict(snap)

    def buf(self, name, shape, dtype, nsub=1, psum=False):
        return Buf(self, name, shape, dtype, nsub, psum)

    def _collect(self, eng, reads, writes):
        need = {}

        def add(tok):
            if tok is None:
                return
            semid, val, snap = tok
            if eng == "pe" and semid == "pe":
                return
            if need.get(semid, (0, None))[0] < val:
                need[semid] = (val, snap)

        for k in reads:
            add(self.last_w.get(k))
        for k in writes:
            add(self.last_w.get(k))
            rd = self.readers.get(k)
            if rd:
                for tok in rd.values():
                    add(tok)
        kn = self.known[eng]
        if self.bar[eng] is not None:
            for semid, val in self.bar[eng].items():
                if eng == "pe" and semid == "pe":
                    continue
                if semid == eng and val >= self.cnt[eng] + 1:
                    continue
                if need.get(semid, (0, None))[0] < val:
                    need[semid] = (val, None)
            self.bar[eng] = None
        waits = []
        for semid, (val, snap) in need.items():
            if kn.get(semid, 0) >= val:
                continue
            waits.append((semid, val))
        for semid, (val, snap) in need.items():
            if kn.get(semid, 0) < val:
                kn[semid] = val
            if snap:
                for s2, v2 in snap.items():
                    if kn.get(s2, 0) < v2:
                        kn[s2] = v2
        return waits

    def _commit(self, tok, reads, writes):
        for k in writes:
            self.last_w[k] = tok
            self.readers[k] = {}
        for k in reads:
            if k in writes:
                continue
            self.readers.setdefault(k, {})[tok[0]] = tok

    def op(self, eng, fn, reads=(), writes=()):
        rk = [k for r in reads if r is not None for k in r.keys]
        wk = [k for r in writes if r is not None for k in r.keys]
        if eng != "pe":
            for k_ in rk:
                if k_[0] == "ps" and k_ not in wk:
                    wk.append(k_)
        waits = self._collect(eng, rk, wk)
        self.cnt[eng] += 1
        tok = (eng, self.cnt[eng], dict(self.known[eng]))
        self.q[eng].append((waits, fn, None))
        self._commit(tok, rk, wk)
        self.nops += 1
        self.nwaits += len(waits)
        return tok

    def dma(self, eng, out, in_, **kw):
        rk = list(in_.keys)
        wk = list(out.keys)
        i = self.dnext
        self.dnext = (self.dnext + 1) % NDS
        semid = "d%d" % i
        waits = self._collect(eng, rk, wk)
        kn = self.known[eng]
        if kn.get(semid, 0) < self.dcnt[i]:
            waits.append((semid, self.dcnt[i]))
            kn[semid] = self.dcnt[i]
        self.dcnt[i] += 16
        tok = (semid, self.dcnt[i], dict(kn))
        oap, iap = out.ap, in_.ap

        def fn(e):
            return e.dma_start(out=oap, in_=iap, **kw)

        self.q[eng].append((waits, fn, i))
        self._commit(tok, rk, wk)
        self.nops += 1
        self.nwaits += len(waits)
        return tok

    def _semh(self, semid):
        if semid in self.sem:
            return self.sem[semid]
        return self.dsem[int(semid[1:])]

    def emit(self, final_wait_eng="sp"):
        nc = self.nc
        finals = []
        for e in ENGS:
            if self.cnt[e] > 0:
                finals.append((e, self.cnt[e]))
        for i in range(NDS):
            if self.dcnt[i] > 0:
                finals.append(("d%d" % i, self.dcnt[i]))
        eng_objs = {}
        with nc.Block() as block:
            def mk(ename):
                def body(e):
                    for waits, fn, dsi in self.q[ename]:
                        if dsi is not None or len(waits) > MAXEMB:
                            for semid, val in waits:
                                e.wait_ge(self._semh(semid), val)
                            ins = fn(e)
                        else:
                            ins = fn(e)
                            for semid, val in waits:
                                ins._wait_ge(self._semh(semid), val)
                        if dsi is None:
                            ins.then_inc(self.sem[ename], 1)
                        else:
                            ins.then_inc(self.dsem[dsi], 16)
                    if ename == final_wait_eng:
                        for semid, val in finals:
                            e.wait_ge(self._semh(semid), val)
                return body
            block.tensor(mk("pe"))
            block.scalar(mk("act"))
            block.vector(mk("dve"))
            block.gpsimd(mk("pool"))
            block.sync(mk("sp"))

    def mm(self, out, lhsT, rhs, start=True, stop=True, extra_reads=()):
        o, l, r = out.ap, lhsT.ap, rhs.ap
        return self.op("pe", lambda e: e.matmul(o, l, r, start=start, stop=stop),
                       reads=[lhsT, rhs] + list(extra_reads) + ([] if start else [out]), writes=[out])

    def tr(self, out, in_, ident):
        o, i, d = out.ap, in_.ap, ident.ap
        return self.op("pe", lambda e: e.transpose(o, i, d), reads=[in_, ident], writes=[out])

    def act(self, out, in_, func, bias=None, scale=None, eng="act"):
        o, i = out.ap, in_.ap
        kw = {}
        rd = [in_]
        if bias is not None:
            if isinstance(bias, R):
                kw["bias"] = bias.ap
                rd.append(bias)
            else:
                kw["bias"] = float(bias)
        if scale is not None:
            if isinstance(scale, R):
                kw["scale"] = scale.ap
                rd.append(scale)
            else:
                kw["scale"] = float(scale)
        return self.op("act", lambda e: e.activation(out=o, in_=i, func=func, **kw), reads=rd, writes=[out])

    def tt(self, eng, out, in0, in1, op):
        o, a, b = out.ap, in0.ap, in1.ap
        return self.op(eng, lambda e: e.tensor_tensor(out=o, in0=a, in1=b, op=op), reads=[in0, in1], writes=[out])

    def ts(self, eng, out, in0, s1, op0, s2=None, op1=None):
        o, a = out.ap, in0.ap
        rd = [in0]
        v1 = s1
        if isinstance(s1, R):
            rd.append(s1)
            v1 = s1.ap
        v2 = s2
        if isinstance(s2, R):
            rd.append(s2)
            v2 = s2.ap
        if op1 is None:
            return self.op(eng, lambda e: e.tensor_scalar(out=o, in0=a, scalar1=v1, scalar2=None, op0=op0), reads=rd, writes=[out])
        return self.op(eng, lambda e: e.tensor_scalar(out=o, in0=a, scalar1=v1, scalar2=v2, op0=op0, op1=op1), reads=rd, writes=[out])

    def stt(self, out, in0, scalar, in1, op0, op1):
        o, a, b = out.ap, in0.ap, in1.ap
        rd = [in0, in1]
        sv = scalar
        if isinstance(scalar, R):
            rd.append(scalar)
            sv = scalar.ap
        return self.op("dve", lambda e: e.scalar_tensor_tensor(out=o, in0=a, scalar=sv, in1=b, op0=op0, op1=op1), reads=rd, writes=[out])

    def copy(self, eng, out, in_):
        o, i = out.ap, in_.ap
        if eng == "act":
            return self.op("act", lambda e: e.copy(out=o, in_=i), reads=[in_], writes=[out])
        return self.op(eng, lambda e: e.tensor_copy(out=o, in_=i), reads=[in_], writes=[out])

    def memset(self, eng, out, val):
        o = out.ap
        return self.op(eng, lambda e: e.memset(o, val), reads=[], writes=[out])


D = 1024
NT = 2560
NBLK = 5
BLK = 512
DEPTH = 4
FH = 2816
NPAIR = 22
UPW = 2 * 258 + 34 * 66
ALPHA = (2 * DEPTH) ** 0.25
LN_EPS = 1e-5
NW = 5
WSL = 1024


def bc(b):
    return 0 if b == 0 else 1


class K:
    pass


def build(cfg):
    nc = bass.Bass("TRN2", target_bir_lowering=False)
    k = K()
    k.nc = nc
    k.cfg = cfg

    shapes = {
        "xin": [D, NT], "cond": [128, 8, 2], "ident": [128, 128],
        "ada_b": [DEPTH, 128, 48], "lng": [DEPTH, 2, 128, 8], "lnb": [DEPTH, 2, 128, 8],
        "fcw": [DEPTH, 128, 44, 9],
        "lru_w_in": [1, D, 2560], "lru_w_out": [1, 1280, D], "lru_gate_w": [1, 2, 2, 10, 128, 128],
        "lru_sm": [128, 10, 12], "lru_s0": [128, 10, 2],
        "cmask": [4, 128, 128],
        "ssd_w_in": [1, D, 5184], "ssd_w_out": [1, 2048, D], "ssd_cw": [128, 24, 5], "ssd_rows": [128, 160],
        "ssd_nrm": [128, 2048], "ssd_s0": [2, 8, 128, 256],
    }
    for jj in range(2):
        shapes["gdn_w_in%d" % jj] = [1, D, 4128]
        shapes["gdn_w_out%d" % jj] = [1, D, D]
    shapes.update({"gdn_cw": [2, 128, 24, 4], "gdn_rows": [2, 128, 160], "gdn_s0": [2, 2, 8, 128, 128], "glvl": [2, 7, 2, 128, 128]})
    for L in range(DEPTH):
        shapes["ada_w%d" % L] = [1, D, 6 * D]
        shapes["ffn_w_in%d" % L] = [1, D, 2 * FH]
        shapes["ffn_w_out%d" % L] = [1, FH, D]
    oshapes = {"yout": [D, NT], "lru_out": [128, 10, 2, 2], "ssd_out": [2, 2, 2048, 128], "gdn_out": [2, 2, 2, 8, 128, 128]}
    k.shapes = shapes
    k.oshapes = oshapes
    k.decl = {}

    def dram(name):
        if name in k.decl:
            return k.decl[name]
        if name in shapes:
            t = nc.dram_tensor(name, list(shapes[name]), F32, kind="ExternalInput").ap()
        else:
            t = nc.dram_tensor(name, list(oshapes[name]), F32, kind="ExternalOutput").ap()
        k.decl[name] = R(t, [("dram_" + name, 0)])
        return k.decl[name]
    k.dram = dram

    with ExitStack() as st:
        P = Prog(nc, st)
        k.P = P
        k.x = P.buf("x", [128, 8, NT], F32, nsub=40)
        k.h = P.buf("h", [128, 8, NT], BF16, nsub=40)
        k.wr = P.buf("wr", [128, NW, WSL], BF16, nsub=NW)
        k.wnext = 0
        k.ps = P.buf("ps", [128, 8, 512], F32, nsub=8, psum=True)
        k.identf = P.buf("identf", [128, 128], F32)
        k.identb = P.buf("identb", [128, 128], BF16)
        k.onesb = P.buf("onesb", [128, 128], BF16)
        k.csil = P.buf("csil", [128, 8, 2], BF16)
        k.mod = P.buf("mod", [128, 48, 2], F32)
        k.msc = P.buf("msc", [128, 2, 8, 2], F32)
        k.lngb = P.buf("lngb", [128, DEPTH, 2, 8], F32)
        k.lnbb = P.buf("lnbb", [128, DEPTH, 2, 8], F32)
        k.adab = P.buf("adab", [128, DEPTH, 48], F32)

        prologue(k)
        for L in cfg.get("layers", list(range(DEPTH))):
            layer(k, L)
        for j in range(8):
            P.dma("sp", R(k.dram("yout").ap[j * 128:(j + 1) * 128, :], k.dram("yout").keys), k.x.r(k.x.t[:, j, :], range(j * 5, j * 5 + 5)))
        P.emit()
        print("ops", P.nops, "waits", P.nwaits, {e: P.cnt[e] for e in ENGS})
    return nc, set(n for n in k.decl if n in k.shapes)


def xs(k, j, b):
    return k.x.r(k.x.t[:, j, b * BLK:(b + 1) * BLK], j * 5 + b)


def hs(k, j, b):
    return k.h.r(k.h.t[:, j, b * BLK:(b + 1) * BLK], j * 5 + b)


def psb(k, b, n=512):
    return k.ps.r(k.ps.t[:, b, 0:n], b)


def prologue(k):
    P = k.P
    for j in range(8):
        P.dma("sp", k.x.r(k.x.t[:, j, :], range(j * 5, j * 5 + 5)), R(k.dram("xin").ap[j * 128:(j + 1) * 128, :], k.dram("xin").keys))
    P.dma("sp", k.identf.r(), k.dram("ident"))
    P.copy("dve", k.identb.r(), k.identf.r())
    P.memset("dve", k.onesb.r(), 1.0)
    ctmp = P.buf("ctmp", [128, 8, 2], F32)
    P.dma("sp", ctmp.r(), k.dram("cond"))
    P.act(k.csil.r(), ctmp.r(), AF.Silu)
    for L in range(DEPTH):
        P.dma("sp", k.lngb.r(k.lngb.t[:, L]), R(k.dram("lng").ap[L].rearrange("s p j -> p s j"), k.dram("lng").keys))
        P.dma("sp", k.lnbb.r(k.lnbb.t[:, L]), R(k.dram("lnb").ap[L].rearrange("s p j -> p s j"), k.dram("lnb").keys))
        P.dma("sp", k.adab.r(k.adab.t[:, L]), R(k.dram("ada_b").ap[L], k.dram("ada_b").keys))


def wslot(k):
    s = k.wnext
    k.wnext = (k.wnext + 1) % NW
    return s


def load_in_w(k, W, L, c0, ncols=128):
    P = k.P
    s = wslot(k)
    src = W.ap[L][:, c0:c0 + ncols].rearrange("(kk p) c -> p kk c", p=128)
    dst = k.wr.t[:, s, 0:8 * ncols].rearrange("p (kk c) -> p kk c", c=ncols)
    P.dma("pool", k.wr.r(dst, s), R(src, W.keys))
    return s, dst


def load_out_w(k, W, L, r0):
    P = k.P
    s = wslot(k)
    src = W.ap[L][r0:r0 + 128, :]
    dst = k.wr.t[:, s, 0:1024]
    P.dma("pool", k.wr.r(dst, s), R(src, W.keys))
    return s, dst


def ada(k, L):
    P = k.P
    b = P.bank()
    for q in range(48):
        s, w = load_in_w(k, k.dram("ada_w%d" % L), 0, q * 128)
        for kk in range(8):
            P.mm(k.ps.r(k.ps.t[:, b, q * 2:q * 2 + 2], b), k.wr.r(w[:, kk, :], s),
                 k.csil.r(k.csil.t[:, kk, :]), start=(kk == 0), stop=(kk == 7))
    src = k.ps.t[:, b, 0:96].rearrange("p (q c) -> p q c", c=2)
    bias = k.adab.t[:, L, :].unsqueeze(2).to_broadcast([128, 48, 2])
    P.tt("dve", k.mod.r(), k.ps.r(src, b), k.adab.r(bias), ALU.add)
    for sl in range(2):
        q0 = (1 + 3 * sl) * 8
        P.ts("dve", k.msc.r(k.msc.t[:, sl]), k.mod.r(k.mod.t[:, q0:q0 + 8, :]), 1.0, ALU.add)


def modulate(k, sl):
    P = k.P
    q_sh = (0 + 3 * sl) * 8
    for j in range(8):
        for b in range(NBLK):
            c = bc(b)
            P.act(hs(k, j, b), xs(k, j, b), AF.Identity,
                  bias=k.mod.r(k.mod.t[:, q_sh + j, c:c + 1]), scale=k.msc.r(k.msc.t[:, sl, j, c:c + 1]))
    for j in range(8):
        for b in range(NBLK):
            P.ts("dve", xs(k, j, b), xs(k, j, b), ALPHA, ALU.mult)


def layernorm(k, L, sl, lb):
    P = k.P
    xb, sq, mean, m2, var, rstd, nmr, t1 = lb["xb"], lb["sq"], lb["mean"], lb["m2"], lb["var"], lb["rstd"], lb["nmr"], lb["t1"]
    for b in range(NBLK):
        pb = b % 2
        for j in range(8):
            P.act(xb.r(xb.t[:, pb, j, :], pb * 8 + j), xs(k, j, b), AF.Identity)
            P.act(sq.r(sq.t[:, pb, j, :], pb * 8 + j), xs(k, j, b), AF.Square)
        b1 = P.bank()
        b2 = P.bank()
        for j in range(8):
            P.mm(psb(k, b1), k.onesb.r(), xb.r(xb.t[:, pb, j, :], pb * 8 + j), start=(j == 0), stop=(j == 7))
        for j in range(8):
            P.mm(psb(k, b2), k.onesb.r(), sq.r(sq.t[:, pb, j, :], pb * 8 + j), start=(j == 0), stop=(j == 7))
        P.act(mean.r(mean.t[:, pb], pb), psb(k, b1), AF.Identity, scale=1.0 / D)
        P.tt("dve", m2.r(m2.t[:, pb], pb), mean.r(mean.t[:, pb], pb), mean.r(mean.t[:, pb], pb), ALU.mult)
        P.stt(var.r(var.t[:, pb], pb), psb(k, b2), 1.0 / D, m2.r(m2.t[:, pb], pb), ALU.mult, ALU.subtract)
        P.act(var.r(var.t[:, pb], pb), var.r(var.t[:, pb], pb), AF.Sqrt, bias=lb["eps"].r())
        o, i = rstd.t[:, pb], var.t[:, pb]
        P.op("dve", lambda e, o=o, i=i: e.reciprocal(out=o, in_=i), reads=[var.r(var.t[:, pb], pb)], writes=[rstd.r(rstd.t[:, pb], pb)])
        P.stt(nmr.r(nmr.t[:, pb], pb), mean.r(mean.t[:, pb], pb), -1.0, rstd.r(rstd.t[:, pb], pb), ALU.mult, ALU.mult)
        for j in range(8):
            tb = j % 2
            P.tt("dve", t1.r(t1.t[:, tb], tb), xs(k, j, b), rstd.r(rstd.t[:, pb], pb), ALU.mult)
            P.tt("dve", t1.r(t1.t[:, tb], tb), t1.r(t1.t[:, tb], tb), nmr.r(nmr.t[:, pb], pb), ALU.add)
            P.act(xs(k, j, b), t1.r(t1.t[:, tb], tb), AF.Identity,
                  bias=k.lnbb.r(k.lnbb.t[:, L, sl, j:j + 1]), scale=k.lngb.r(k.lngb.t[:, L, sl, j:j + 1]))


def ln_scope(k, L, sl):
    P = k.P
    P.barrier()
    mk = P.mark()
    if True:
        lb = {
            "xb": P.buf("ln_xb", [128, 2, 8, BLK], BF16, nsub=16),
            "sq": P.buf("ln_sq", [128, 2, 8, BLK], BF16, nsub=16),
            "mean": P.buf("ln_mean", [128, 2, BLK], F32, nsub=2),
            "m2": P.buf("ln_m2", [128, 2, BLK], F32, nsub=2),
            "var": P.buf("ln_var", [128, 2, BLK], F32, nsub=2),
            "rstd": P.buf("ln_rstd", [128, 2, BLK], F32, nsub=2),
            "nmr": P.buf("ln_nmr", [128, 2, BLK], F32, nsub=2),
            "t1": P.buf("ln_t1", [128, 2, BLK], F32, nsub=2),
            "eps": P.buf("ln_eps", [128, 1], F32),
        }
        P.memset("dve", lb["eps"].r(), LN_EPS)
        layernorm(k, L, sl, lb)
        P.barrier()
        P.release(mk)


def ffn(k, L):
    P = k.P
    G = 2
    P.barrier()
    mk = P.mark()
    if True:
        up = P.buf("f_up", [128, 2, 2, UPW], BF16, nsub=4)
        ab = P.buf("f_ab", [128, G, NT], BF16, nsub=G * 5)
        dg = P.buf("f_dg", [128, 2, 2, 9, 128], BF16, nsub=4)
        sg = P.buf("f_sg", [128, 2, BLK], F32, nsub=2)
        fcw = P.buf("f_fcw", [128, 44, 9], F32)
        P.dma("sp", fcw.r(), R(k.dram("fcw").ap[L], k.dram("fcw").keys))
        P.memset("pool", up.r(), 0.0)
        q_g = 5 * 8
        nsg = 0
        for g0 in range(0, NPAIR, G):
            pairs = list(range(g0, min(g0 + G, NPAIR)))
            for jj, j in enumerate(pairs):
                db = j % 2
                sg_, wg = load_in_w(k, k.dram("ffn_w_in%d" % L), 0, j * 128)
                sv_, wv = load_in_w(k, k.dram("ffn_w_in%d" % L), 0, FH + j * 128)
                ws = [wg, wv]
                wss = [sg_, sv_]
                for gv in range(2):
                    tile_idx = j + gv * NPAIR
                    i0 = k.identf.t[:].unsqueeze(1).to_broadcast([128, 9, 128])
                    i1 = fcw.t[:, tile_idx, :].unsqueeze(2).to_broadcast([128, 9, 128])
                    P.tt("pool", dg.r(dg.t[:, db, gv], db * 2 + gv), k.identf.r(i0), fcw.r(i1), ALU.mult)
                for b in range(NBLK):
                    for gv in range(2):
                        bk = P.bank()
                        for kk in range(8):
                            P.mm(psb(k, bk), k.wr.r(ws[gv][:, kk, :], wss[gv]), hs(k, kk, b), start=(kk == 0), stop=(kk == 7))
                        upt = up.t[:, db, gv]
                        if b == 0:
                            dst = upt[:, 0:516].rearrange("p (s w) -> p s w", w=258)[:, :, 1:257]
                            src = k.ps.t[:, bk, :].rearrange("p (s w) -> p s w", w=256)
                        else:
                            r0 = 8 * (b - 1)
                            dst = upt[:, 516:].rearrange("p (r w) -> p r w", w=66)[:, 1 + r0:9 + r0, 1:65]
                            src = k.ps.t[:, bk, :].rearrange("p (r w) -> p r w", w=64)
                        P.copy("act" if gv == 0 else "dve", up.r(dst, db * 2 + gv), k.ps.r(src, bk))
                for b in range(NBLK):
                    bks = []
                    for gv in range(2):
                        bk = P.bank()
                        bks.append(bk)
                        upt = up.t[:, db, gv]
                        if b == 0:
                            taps = [(1, kw) for kw in range(3)]
                            outv = k.ps.t[:, bk, :].rearrange("p (s w) -> p s w", w=256)
                        else:
                            taps = [(kh, kw) for kh in range(3) for kw in range(3)]
                            outv = k.ps.t[:, bk, :].rearrange("p (r w) -> p r w", w=64)
                        for ti, (kh, kw) in enumerate(taps):
                            if b == 0:
                                rhs = upt[:, 0:516].rearrange("p (s w) -> p s w", w=258)[:, :, kw:kw + 256]
                            else:
                                r0 = 8 * (b - 1)
                                rhs = upt[:, 516:].rearrange("p (r w) -> p r w", w=66)[:, r0 + kh:r0 + kh + 8, kw:kw + 64]
                            P.mm(k.ps.r(outv, bk), dg.r(dg.t[:, db, gv, kh * 3 + kw, :], db * 2 + gv), up.r(rhs, db * 2 + gv),
                                 start=(ti == 0), stop=(ti == len(taps) - 1))
                    sb_ = nsg % 2
                    nsg += 1
                    P.act(sg.r(sg.t[:, sb_], sb_), psb(k, bks[0]), AF.Silu)
                    P.tt("dve", ab.r(ab.t[:, jj, b * BLK:(b + 1) * BLK], jj * 5 + b), psb(k, bks[1]), sg.r(sg.t[:, sb_], sb_), ALU.mult)
            wos = [load_out_w(k, k.dram("ffn_w_out%d" % L), 0, (g0 + jj) * 128) for jj in range(len(pairs))]
            for m in range(8):
                for b in range(NBLK):
                    bk = P.bank()
                    for jj in range(len(pairs)):
                        P.mm(psb(k, bk), k.wr.r(wos[jj][1][:, m * 128:(m + 1) * 128], wos[jj][0]), ab.r(ab.t[:, jj, b * BLK:(b + 1) * BLK], jj * 5 + b),
                             start=(jj == 0), stop=(jj == len(pairs) - 1))
                    c = bc(b)
                    P.stt(xs(k, m, b), psb(k, bk), k.mod.r(k.mod.t[:, q_g + m, c:c + 1]), xs(k, m, b), ALU.mult, ALU.add)
        P.barrier()
        P.release(mk)


def layer(k, L):
    cfg = k.cfg
    ada(k, L)
    modulate(k, 0)
    if cfg.get("mixers", True):
        mixer(k, L)
    ln_scope(k, L, 0)
    modulate(k, 1)
    if cfg.get("ffn", True):
        ffn(k, L)
    ln_scope(k, L, 1)


LW = 1280
SEGS = [(0, 256), (256, 256), (512, 2048)]
Q_G1 = 2 * 8


def mixer(k, L):
    kind = L % 3
    if kind == 2:
        lru(k, L // 3)
    elif kind == 1:
        ssd(k, L // 3)
    else:
        gdn(k, L // 3)


def outproj_acc(k, W, Lw, r0, ntile, o_regions):
    P = k.P
    slots = [load_out_w(k, W, Lw, r0 + i * 128) for i in range(ntile)]
    for m in range(8):
        for b in range(NBLK):
            bk = P.bank()
            for jj in range(ntile):
                s, wo = slots[jj]
                P.mm(psb(k, bk), k.wr.r(wo[:, m * 128:(m + 1) * 128], s), o_regions(jj, b), start=(jj == 0), stop=(jj == ntile - 1))
            c = bc(b)
            P.stt(xs(k, m, b), psb(k, bk), k.mod.r(k.mod.t[:, Q_G1 + m, c:c + 1]), xs(k, m, b), ALU.mult, ALU.add)


def conv1d_pad_layout():
    offs = []
    o = 0
    for (t0, n) in SEGS:
        offs.append(o)
        o += n + 3
    return offs, o


CPO, CPW = conv1d_pad_layout()


def conv_dst(xpad_t, b):
    if b == 0:
        return xpad_t[:, 0:518].rearrange("p (s w) -> p s w", w=259)[:, :, 1:257], "p (s w) -> p s w", 256
    o = CPO[2] + 1 + (b - 1) * BLK
    return xpad_t[:, o:o + BLK], None, None


def conv_rhs(xpad_t, b, kk):
    if b == 0:
        return xpad_t[:, 0:518].rearrange("p (s w) -> p s w", w=259)[:, :, kk:kk + 256]
    o = CPO[2] + (b - 1) * BLK + kk
    return xpad_t[:, o:o + BLK]


def psview(k, bk, b):
    if b == 0:
        return k.ps.t[:, bk, :].rearrange("p (s w) -> p s w", w=256)
    return k.ps.t[:, bk, :]


def lru(k, j):
    P = k.P
    G = 2
    P.barrier()
    mk = P.mark()
    xpad = P.buf("l_xpad", [128, CPW + 1], BF16)
    xrb = P.buf("l_xrb", [128, NT], BF16)
    gg = P.buf("l_gg", [128, NT], BF16)
    Ib = P.buf("l_i", [128, NT], BF16)
    A = P.buf("l_a", [128, NT], F32)
    T = P.buf("l_t", [128, NT], F32)
    H = [P.buf("l_h0", [128, NT], BF16), P.buf("l_h1", [128, NT], BF16)]
    ob = P.buf("l_o", [128, G, NT], BF16, nsub=G)
    dgl = P.buf("l_dg", [128, 4, 128], BF16)
    sm = P.buf("l_sm", [128, 10, 12], F32)
    s0 = P.buf("l_s0", [128, 10, 2], F32)
    sp = P.buf("l_sp", [128, 10, 2], F32)
    ep = P.buf("l_ep", [128, 10, 2], F32)
    one = P.buf("l_one", [128, 1], F32)
    sto = P.buf("l_sto", [128, 10, 2, 2], F32)
    P.dma("sp", sm.r(), k.dram("lru_sm"))
    P.dma("sp", s0.r(), k.dram("lru_s0"))
    P.memset("dve", one.r(), 1.0)
    P.memset("pool", xpad.r(), 0.0)
    P.act(ep.r(), sm.r(sm.t[:, :, 9:11]), AF.Exp, scale=-1.0)
    P.ts("dve", sp.r(), ep.r(), -0.2, ALU.mult, 0.25, ALU.add)
    for cst in (1.0 / 3, 0.5, 1.0):
        P.tt("dve", sp.r(), sp.r(), ep.r(), ALU.mult)
        P.ts("dve", sp.r(), sp.r(), -1.0, ALU.mult, cst, ALU.add)
    P.tt("dve", sp.r(), sp.r(), ep.r(), ALU.mult)
    P.ts("dve", sp.r(), sp.r(), -8.0, ALU.mult)

    for g0 in range(0, 10, G):
        tiles = list(range(g0, min(g0 + G, 10)))
        for jj, n in enumerate(tiles):
            s, wgb = load_in_w(k, k.dram("lru_w_in"), j, n * 128)
            sx, wxr = load_in_w(k, k.dram("lru_w_in"), j, LW + n * 128)
            s2 = wslot(k)
            gsrc = k.dram("lru_gate_w").ap[j][:, :, n].rearrange("d g kk m -> kk (d g) m")
            gw = k.wr.t[:, s2, 0:512].rearrange("p (a m) -> p a m", m=128)
            P.dma("pool", k.wr.r(gw, s2), R(gsrc, k.dram("lru_gate_w").keys))
            i0 = k.identf.t[:].unsqueeze(1).to_broadcast([128, 4, 128])
            i1 = sm.t[:, n, 0:4].unsqueeze(2).to_broadcast([128, 4, 128])
            P.tt("pool", dgl.r(), k.identf.r(i0), sm.r(i1), ALU.mult)
            for b in range(NBLK):
                bk = P.bank()
                for kk in range(8):
                    P.mm(psb(k, bk), k.wr.r(wgb[:, kk, :], s), hs(k, kk, b), start=(kk == 0), stop=(kk == 7))
                P.act(gg.r(gg.t[:, b * BLK:(b + 1) * BLK]), psb(k, bk), AF.Gelu_apprx_tanh)
            for b in range(NBLK):
                bk = P.bank()
                for kk in range(8):
                    P.mm(psb(k, bk), k.wr.r(wxr[:, kk, :], sx), hs(k, kk, b), start=(kk == 0), stop=(kk == 7))
                dst, _, _ = conv_dst(xpad.t, b)
                P.copy("dve", xpad.r(dst), k.ps.r(psview(k, bk, b), bk))
            for b in range(NBLK):
                bk = P.bank()
                for kk in range(4):
                    P.mm(k.ps.r(psview(k, bk, b), bk), dgl.r(dgl.t[:, kk, :]), xpad.r(conv_rhs(xpad.t, b, kk)), start=(kk == 0), stop=(kk == 3))
                P.act(xrb.r(xrb.t[:, b * BLK:(b + 1) * BLK]), psb(k, bk), AF.Identity, bias=sm.r(sm.t[:, n, 4:5]))
            for d in range(2):
                for b in range(NBLK):
                    for g in range(2):
                        bk = P.bank()
                        P.mm(psb(k, bk), k.wr.r(gw[:, d * 2 + g, :], s2), xrb.r(xrb.t[:, b * BLK:(b + 1) * BLK]))
                        dstb = A if g == 0 else Ib
                        P.act(dstb.r(dstb.t[:, b * BLK:(b + 1) * BLK]), psb(k, bk), AF.Sigmoid, bias=sm.r(sm.t[:, n, 5 + d * 2 + g:6 + d * 2 + g]))
                P.act(A.r(), A.r(), AF.Exp, scale=sp.r(sp.t[:, n, d:d + 1]))
                P.tt("dve", T.r(), A.r(), A.r(), ALU.mult)
                P.act(T.r(), T.r(), AF.Sqrt, bias=one.r(), scale=-1.0)
                P.tt("dve", T.r(), T.r(), Ib.r(), ALU.mult)
                P.tt("dve", T.r(), T.r(), xrb.r(), ALU.mult)
                for si, (t0, n_t) in enumerate(SEGS):
                    if d == 0:
                        o_, a_, b_ = H[0].t[:, t0:t0 + n_t], A.t[:, t0:t0 + n_t], T.t[:, t0:t0 + n_t]
                    else:
                        lo = t0 - 1 if t0 > 0 else None
                        o_, a_, b_ = H[1].t[:, t0 + n_t - 1:lo:-1], A.t[:, t0 + n_t - 1:lo:-1], T.t[:, t0 + n_t - 1:lo:-1]
                    if si == 2:
                        init = s0.t[:, n, d:d + 1]
                        rd = [A.r(), T.r(), s0.r()]
                    else:
                        init = 0.0
                        rd = [A.r(), T.r()]
                    P.op("dve", lambda e, o_=o_, a_=a_, b_=b_, init=init: e.tensor_tensor_scan(out=o_, data0=a_, data1=b_, initial=init, op0=ALU.mult, op1=ALU.add),
                         reads=rd, writes=[H[d].r()])
                    if si < 2:
                        tl = t0 + n_t - 1 if d == 0 else t0
                        P.copy("act", sto.r(sto.t[:, n, si, d:d + 1]), H[d].r(H[d].t[:, tl:tl + 1]))
            P.tt("dve", H[0].r(), H[0].r(), H[1].r(), ALU.add)
            P.tt("dve", ob.r(ob.t[:, jj, :], jj), H[0].r(), gg.r(), ALU.mult)
        outproj_acc(k, k.dram("lru_w_out"), j, g0 * 128, len(tiles), lambda jj, b: ob.r(ob.t[:, jj, b * BLK:(b + 1) * BLK], jj))
    P.dma("sp", k.dram("lru_out"), sto.r())
    P.barrier()
    P.release(mk)


NCH = 20
SEG_CH = [(0, 2), (2, 2), (4, 16)]


def seg_of_chunk(c):
    return 0 if c < 2 else (1 if c < 4 else 2)


def hch(k, kk, c):
    b = c // 4
    return k.h.r(k.h.t[:, kk, c * 128:(c + 1) * 128], kk * 5 + b)


def ssd(k, j):
    P = k.P
    P.barrier()
    mk = P.mark()
    y_tm = P.buf("s_ytm", [128, NCH, 512], BF16, nsub=NCH)
    xs_tm = P.buf("s_xstm", [128, NCH, 256], BF16, nsub=NCH)
    Bfm = P.buf("s_bfm", [128, NT], BF16)
    Cfm = P.buf("s_cfm", [128, NT], BF16)
    xpad = P.buf("s_xpad", [128, CPW + 1], BF16)
    wz = P.buf("s_wz", [128, 8, 256], BF16)
    wdt = P.buf("s_wdt", [128, 8, 64], BF16)
    dgl = P.buf("s_dg", [128, 4, 128], BF16)
    cw = P.buf("s_cw", [128, 24, 5], F32)
    rows = P.buf("s_rows", [128, 160], F32)
    nrm = P.buf("s_nrm", [128, 512], BF16)
    masks = P.buf("s_masks", [128, 2, 128], F32)
    negb = P.buf("s_negb", [128, 2, 128], BF16)
    onesf = P.buf("s_onesf", [128, 128], F32)
    one1 = P.buf("s_one1", [128, 1], F32)
    eps1 = P.buf("s_eps1", [128, 1], F32)
    sc = {nm: P.buf("s_" + nm, [128, NCH, 8], F32) for nm in ["dt", "da", "nacum", "eac", "cdec", "ce"]}
    sc["tmp"] = sc["ce"]
    fm_tmp = Cfm
    ssq = P.buf("s_ssq", [128, NCH, 2], F32)
    rstd = P.buf("s_rstd", [128, NCH], F32)
    cbT = P.buf("s_cbT", [128, 2, 2, 128], BF16, nsub=2)
    Btm = P.buf("s_btm", [128, 2, 128], BF16, nsub=2)
    dec = P.buf("s_dec", [128, 2, 4, 128], BF16, nsub=2)
    Mb = P.buf("s_M", [128, 2, 4, 128], BF16, nsub=2)
    xd = P.buf("s_xd", [128, 2, 256], BF16, nsub=2)
    xdt = P.buf("s_xdt", [128, 2, 256], BF16, nsub=2)
    ST = P.buf("s_ST", [128, 2, 256], F32, nsub=2)
    STb = P.buf("s_STb", [128, 2, 256], BF16, nsub=2)
    t1 = P.buf("s_t1", [128, 1, 256], F32, nsub=1)
    t2 = P.buf("s_t2", [128, 1, 256], F32, nsub=1)
    sz = t2
    junk = P.buf("s_junk", [128, 256], BF16)
    sto = P.buf("s_sto", [128, 2, 128], F32, nsub=2)

    P.dma("sp", cw.r(), k.dram("ssd_cw"))
    P.dma("sp", rows.r(), k.dram("ssd_rows"))
    P.dma("sp", masks.r(), R(k.dram("cmask").ap[0:2].rearrange("a p m -> p a m"), k.dram("cmask").keys))
    P.dma("pool", negb.r(), R(k.dram("cmask").ap[2:4].rearrange("a p m -> p a m"), k.dram("cmask").keys))
    P.memset("dve", onesf.r(), 1.0)
    P.memset("dve", one1.r(), 1.0)
    P.memset("dve", eps1.r(), 1e-5)
    P.memset("pool", xpad.r(), 0.0)
    P.act(rows.r(rows.t[:, 64:128]), rows.r(rows.t[:, 64:128]), AF.Exp)
    P.ts("dve", rows.r(rows.t[:, 64:128]), rows.r(rows.t[:, 64:128]), -1.0, ALU.mult)
    src = k.dram("ssd_w_in").ap[j][:, 5120:5184].rearrange("(kk p) c -> p kk c", p=128)
    P.dma("pool", wdt.r(), R(src, k.dram("ssd_w_in").keys))

    def conv_tile(col0, cidx, dst_writer):
        s, w = load_in_w(k, k.dram("ssd_w_in"), j, col0)
        i0 = k.identf.t[:].unsqueeze(1).to_broadcast([128, 4, 128])
        i1 = cw.t[:, cidx, 0:4].unsqueeze(2).to_broadcast([128, 4, 128])
        P.tt("pool", dgl.r(), k.identf.r(i0), cw.r(i1), ALU.mult)
        for b in range(NBLK):
            bk = P.bank()
            for kk in range(8):
                P.mm(psb(k, bk), k.wr.r(w[:, kk, :], s), hs(k, kk, b), start=(kk == 0), stop=(kk == 7))
            dst, _, _ = conv_dst(xpad.t, b)
            P.copy("dve", xpad.r(dst), k.ps.r(psview(k, bk, b), bk))
        for b in range(NBLK):
            bk = P.bank()
            for kk in range(4):
                P.mm(k.ps.r(psview(k, bk, b), bk), dgl.r(dgl.t[:, kk, :]), xpad.r(conv_rhs(xpad.t, b, kk)), start=(kk == 0), stop=(kk == 3))
            P.act(dst_writer(b), psb(k, bk), AF.Silu, bias=cw.r(cw.t[:, cidx, 4:5]))

    stop = k.cfg.get("ssd_stop", 99)
    for hg in range(k.cfg.get("ssd_nhg", 8)):
        g, half = hg // 2, hg % 2
        src = k.dram("ssd_w_in").ap[j][:, hg * 256:(hg + 1) * 256].rearrange("(kk p) c -> p kk c", p=128)
        P.dma("pool", wz.r(), R(src, k.dram("ssd_w_in").keys))
        if half == 0:
            P.dma("pool", nrm.r(), R(k.dram("ssd_nrm").ap[:, g * 512:(g + 1) * 512], k.dram("ssd_nrm").keys))
        if stop <= 0.5:
            continue
        bk = P.bank()
        for c in range(NCH):
            for kk in range(8):
                rhs = wdt.t[:, kk, :].rearrange("p (d h) -> p d h", d=2)[:, :, hg * 4:hg * 4 + 4]
                out = k.ps.t[:, bk, c * 8:(c + 1) * 8].rearrange("p (d h) -> p d h", d=2)
                P.mm(k.ps.r(out, bk), hch(k, kk, c), wdt.r(rhs), start=(kk == 0), stop=(kk == 7))
        psv = k.ps.t[:, bk, 0:NCH * 8].rearrange("p (c d h) -> p c d h", d=2, h=4)
        if stop <= 0.7:
            continue

        def rowbc(off):
            return rows.t[:, off:off + 64].rearrange("p (d h) -> p d h", d=2)[:, :, hg * 4:hg * 4 + 4].unsqueeze(1).to_broadcast([128, NCH, 2, 4])

        def v4(bf):
            return bf.t.rearrange("p c (d h) -> p c d h", d=2)
        tmp, dt, da = sc["tmp"], sc["dt"], sc["da"]
        P.tt("dve", tmp.r(v4(tmp)), k.ps.r(psv, bk), rows.r(rowbc(0)), ALU.add)
        P.ts("dve", dt.r(), tmp.r(), -1.0, ALU.mult)
        P.tt("dve", dt.r(), dt.r(), tmp.r(), ALU.max)
        P.act(dt.r(), dt.r(), AF.Exp, scale=-1.0)
        P.act(dt.r(), dt.r(), AF.Ln, bias=one1.r())
        P.ts("dve", tmp.r(), tmp.r(), 0.0, ALU.max)
        P.tt("dve", dt.r(), dt.r(), tmp.r(), ALU.add)
        P.tt("dve", da.r(v4(da)), dt.r(v4(dt)), rows.r(rowbc(64)), ALU.mult)
        if stop <= 0.8:
            continue
        bk2 = P.bank()
        bk3 = P.bank()
        for c in range(NCH):
            for d in range(2):
                P.mm(k.ps.r(k.ps.t[:, bk2, c * 8 + d * 4:c * 8 + d * 4 + 4], bk2), masks.r(masks.t[:, d, :]), da.r(da.t[:, c, d * 4:d * 4 + 4]))
            P.mm(k.ps.r(k.ps.t[:, bk3, c * 8:(c + 1) * 8], bk3), onesf.r(), da.r(da.t[:, c, :]))
        nacum, eac, cdec, ce = sc["nacum"], sc["eac"], sc["cdec"], sc["ce"]
        ps2 = k.ps.t[:, bk2, 0:NCH * 8].rearrange("p (c e) -> p c e", e=8)
        ps3 = k.ps.t[:, bk3, 0:NCH * 8].rearrange("p (c e) -> p c e", e=8)
        if stop <= 0.9:
            continue
        P.ts("dve", nacum.r(), k.ps.r(ps2, bk2), -1.0, ALU.mult)
        P.act(eac.r(), nacum.r(), AF.Exp, scale=-1.0)
        if stop <= 0.95:
            continue
        P.copy("dve", cdec.r(), k.ps.r(ps3, bk3))
        P.tt("dve", ce.r(), cdec.r(), nacum.r(), ALU.add)
        if stop <= 0.96:
            continue
        P.act(cdec.r(), cdec.r(), AF.Exp)
        if stop <= 0.97:
            continue
        P.act(ce.r(), ce.r(), AF.Exp)
        P.tt("dve", ce.r(), ce.r(), dt.r(), ALU.mult)
        if stop <= 1:
            continue
        for ti in range(2):
            conv_tile(2048 + hg * 256 + ti * 128, hg * 2 + ti, lambda b: fm_tmp.r(fm_tmp.t[:, b * BLK:(b + 1) * BLK]))
            for c0 in range(0, NCH, 8):
                bk = P.bank()
                pb = k.ps.t[:, bk, :].bitcast(BF16)
                n = min(8, NCH - c0)
                for ci in range(n):
                    c = c0 + ci
                    P.tr(k.ps.r(pb[:, ci * 128:(ci + 1) * 128], bk), fm_tmp.r(fm_tmp.t[:, c * 128:(c + 1) * 128]), k.identb.r())
                dst = xs_tm.t[:, c0:c0 + n, ti * 128:(ti + 1) * 128]
                srcv = pb[:, 0:n * 128].rearrange("p (c m) -> p c m", m=128)
                P.copy("act", xs_tm.r(dst, range(c0, c0 + n)), k.ps.r(srcv, bk))
        conv_tile(2048 + 2048 + g * 128, 16 + g, lambda b: Bfm.r(Bfm.t[:, b * BLK:(b + 1) * BLK]))
        conv_tile(2048 + 2560 + g * 128, 20 + g, lambda b: Cfm.r(Cfm.t[:, b * BLK:(b + 1) * BLK]))
        if stop <= 2:
            continue
        for step in range(k.cfg.get("ssd_steps", NCH)):
            for d in range(2):
                c = step if d == 0 else NCH - 1 - step
                first_visit = (c <= 9) if d == 0 else (c >= 10)
                seg = seg_of_chunk(c)
                c_first, c_n = SEG_CH[seg]
                seg_start = (c == c_first) if d == 0 else (c == c_first + c_n - 1)
                seg_end = (c == c_first + c_n - 1) if d == 0 else (c == c_first)
                pb = step % 2
                tok = slice(c * 128, (c + 1) * 128)
                if seg_start:
                    if seg == 2:
                        P.dma("sp", ST.r(ST.t[:, d], d), R(k.dram("ssd_s0").ap[d, hg], k.dram("ssd_s0").keys))
                    else:
                        P.memset("dve", ST.r(ST.t[:, d], d), 0.0)
                    P.copy("act", STb.r(STb.t[:, d], d), ST.r(ST.t[:, d], d))
                if d == 0:
                    pass
                sl = d
                bk = P.bank()
                P.mm(k.ps.r(k.ps.t[:, bk, 0:128], bk), Bfm.r(Bfm.t[:, tok]), Cfm.r(Cfm.t[:, tok]))
                P.tt("dve", cbT.r(cbT.t[:, sl, d], sl), k.ps.r(k.ps.t[:, bk, 0:128], bk), masks.r(masks.t[:, d, :]), ALU.mult)
                bkt = P.bank()
                pbb = k.ps.t[:, bkt, :].bitcast(BF16)
                P.tr(k.ps.r(pbb[:, 0:128], bkt), Bfm.r(Bfm.t[:, tok]), k.identb.r())
                P.copy("act", Btm.r(Btm.t[:, sl], sl), k.ps.r(pbb[:, 0:128], bkt))
                xsv = xs_tm.t[:, c, :].rearrange("p (q e) -> p q e", e=64)
                dtb = dt.t[:, c, d * 4:d * 4 + 4].unsqueeze(2).to_broadcast([128, 4, 64])
                ceb = ce.t[:, c, d * 4:d * 4 + 4].unsqueeze(2).to_broadcast([128, 4, 64])
                P.tt("dve", xd.r(xd.t[:, sl].rearrange("p (q e) -> p q e", e=64), sl), xs_tm.r(xsv, c), dt.r(dtb), ALU.mult)
                P.tt("dve", xdt.r(xdt.t[:, sl].rearrange("p (q e) -> p q e", e=64), sl), xs_tm.r(xsv, c), ce.r(ceb), ALU.mult)
                bk = P.bank()
                for q in range(4):
                    col = d * 4 + q
                    lhs = da.t[:, c, col:col + 1].to_broadcast([128, 128])
                    o_ = k.ps.r(k.ps.t[:, bk, q * 128:(q + 1) * 128], bk)
                    P.mm(o_, da.r(lhs), masks.r(masks.t[:, d, :]), start=True, stop=False)
                    P.mm(o_, k.identb.r(), negb.r(negb.t[:, d, :]), start=False, stop=True)
                    P.act(dec.r(dec.t[:, sl, q, :], sl), o_, AF.Exp, bias=nacum.r(nacum.t[:, c, col:col + 1]))
                cb_b = cbT.t[:, sl, d].unsqueeze(1).to_broadcast([128, 4, 128])
                P.tt("dve", Mb.r(Mb.t[:, sl], sl), dec.r(dec.t[:, sl], sl), cbT.r(cb_b, sl), ALU.mult)
                bkY = P.bank()
                for q in range(4):
                    P.mm(k.ps.r(k.ps.t[:, bkY, q * 64:(q + 1) * 64], bkY), Mb.r(Mb.t[:, sl, q, :], sl), xd.r(xd.t[:, sl, q * 64:(q + 1) * 64], sl))
                P.mm(k.ps.r(k.ps.t[:, bkY, 256:512], bkY), Cfm.r(Cfm.t[:, tok]), STb.r(STb.t[:, d], d))
                eab = eac.t[:, c, d * 4:d * 4 + 4].unsqueeze(2).to_broadcast([128, 4, 64])
                t1v = t1.t[:, 0].rearrange("p (q e) -> p q e", e=64)
                P.tt("dve", t1.r(t1v, 0), k.ps.r(k.ps.t[:, bkY, 256:512].rearrange("p (q e) -> p q e", e=64), bkY), eac.r(eab), ALU.mult)
                P.tt("dve", t1.r(t1.t[:, 0], 0), t1.r(t1.t[:, 0], 0), k.ps.r(k.ps.t[:, bkY, 0:256], bkY), ALU.add)
                ysl = y_tm.r(y_tm.t[:, c, half * 256:(half + 1) * 256], c)
                bkS = P.bank()
                P.mm(k.ps.r(k.ps.t[:, bkS, 0:256], bkS), Btm.r(Btm.t[:, sl], sl), xdt.r(xdt.t[:, sl], sl))
                cdb = cdec.t[:, c, d * 4:d * 4 + 4].unsqueeze(2).to_broadcast([128, 4, 64])
                STv = ST.t[:, d].rearrange("p (q e) -> p q e", e=64)
                P.tt("dve", ST.r(STv, d), ST.r(STv, d), cdec.r(cdb), ALU.mult)
                P.tt("dve", ST.r(ST.t[:, d], d), ST.r(ST.t[:, d], d), k.ps.r(k.ps.t[:, bkS, 0:256], bkS), ALU.add)
                P.copy("act", STb.r(STb.t[:, d], d), ST.r(ST.t[:, d], d))
                if first_visit:
                    P.copy("act", ysl, t1.r(t1.t[:, 0], 0))
                else:
                    P.tt("dve", t1.r(t1.t[:, 0], 0), t1.r(t1.t[:, 0], 0), ysl, ALU.add)
                    dsb = rows.t[:, 128 + hg * 4:128 + hg * 4 + 4].unsqueeze(2).to_broadcast([128, 4, 64])
                    P.tt("pool", t2.r(t2.t[:, 0].rearrange("p (q e) -> p q e", e=64), 0), xs_tm.r(xsv, c), rows.r(dsb), ALU.mult)
                    P.tt("dve", t1.r(t1.t[:, 0], 0), t1.r(t1.t[:, 0], 0), t2.r(t2.t[:, 0], 0), ALU.add)
                    bkZ = P.bank()
                    for kk in range(8):
                        P.mm(k.ps.r(k.ps.t[:, bkZ, 0:256], bkZ), hch(k, kk, c), wz.r(wz.t[:, kk, :]), start=(kk == 0), stop=(kk == 7))
                    P.act(t2.r(t2.t[:, 0], 0), k.ps.r(k.ps.t[:, bkZ, 0:256], bkZ), AF.Silu)
                    P.tt("dve", t1.r(t1.t[:, 0], 0), t1.r(t1.t[:, 0], 0), t2.r(t2.t[:, 0], 0), ALU.mult)
                    P.copy("dve", ysl, t1.r(t1.t[:, 0], 0))
                    o_, i_, a_ = junk.t, t1.t[:, 0], ssq.t[:, c, half:half + 1]
                    P.op("act", lambda e, o_=o_, i_=i_, a_=a_: e.activation(out=o_, in_=i_, func=AF.Square, accum_out=a_),
                         reads=[t1.r(t1.t[:, 0], 0)], writes=[junk.r(), ssq.r()])
                if seg_end and seg < 2:
                    for pr in range(2):
                        bk = P.bank()
                        P.tr(k.ps.r(k.ps.t[:, bk, 0:128], bk), ST.r(ST.t[:, d, pr * 128:(pr + 1) * 128], d), k.identf.r())
                        P.copy("dve", sto.r(sto.t[:, pr], pr), k.ps.r(k.ps.t[:, bk, 0:128], bk))
                        r0 = (hg * 4 + pr * 2) * 64
                        P.dma("sp", R(k.dram("ssd_out").ap[seg, d, r0:r0 + 128, :], k.dram("ssd_out").keys), sto.r(sto.t[:, pr], pr))
        if half == 1 and stop > 3:
            P.tt("dve", rstd.r(), ssq.r(ssq.t[:, :, 0]), ssq.r(ssq.t[:, :, 1]), ALU.add)
            P.act(rstd.r(), rstd.r(), AF.Sqrt, bias=eps1.r(), scale=1.0 / 512)
            o_, i_ = rstd.t, rstd.t
            P.op("dve", lambda e, o_=o_, i_=i_: e.reciprocal(out=o_, in_=i_), reads=[rstd.r()], writes=[rstd.r()])
            slots = [load_out_w(k, k.dram("ssd_w_out"), j, g * 512 + ti * 128) for ti in range(4)]
            ofm_t = xs_tm.t[:, 0:8, :].rearrange("p c e -> p (c e)").rearrange("p (t m) -> p t m", m=512)
            yn_t = xs_tm.t[:, 8:12, :].rearrange("p c e -> p (c e)").rearrange("p (t m) -> p t m", m=512)
            OK_ = list(range(0, 8))
            YK_ = list(range(8, 12))
            for b in range(NBLK):
                for ci in range(4):
                    c = b * 4 + ci
                    yb = ci % 2
                    P.stt(xs_tm.r(yn_t[:, yb], YK_), y_tm.r(y_tm.t[:, c, :], c), rstd.r(rstd.t[:, c:c + 1]), nrm.r(), ALU.mult, ALU.mult)
                    bk = P.bank()
                    pbb = k.ps.t[:, bk, :].bitcast(BF16)
                    for ti in range(4):
                        P.tr(k.ps.r(pbb[:, ti * 128:(ti + 1) * 128], bk), xs_tm.r(yn_t[:, yb, ti * 128:(ti + 1) * 128], YK_), k.identb.r())
                    srcv = pbb[:, 0:512].rearrange("p (t m) -> p t m", m=128)
                    P.copy("act", xs_tm.r(ofm_t[:, :, ci * 128:(ci + 1) * 128], OK_), k.ps.r(srcv, bk))
                for m in range(8):
                    bk = P.bank()
                    for ti in range(4):
                        s_, wo = slots[ti]
                        P.mm(psb(k, bk), k.wr.r(wo[:, m * 128:(m + 1) * 128], s_), xs_tm.r(ofm_t[:, ti, :], OK_), start=(ti == 0), stop=(ti == 3))
                    cc = bc(b)
                    P.stt(xs(k, m, b), psb(k, bk), k.mod.r(k.mod.t[:, Q_G1 + m, cc:cc + 1]), xs(k, m, b), ALU.mult, ALU.add)
    P.barrier()
    P.release(mk)


def gdn(k, j):
    P = k.P
    P.barrier()
    mk = P.mark()
    W_in = k.dram("gdn_w_in%d" % j)
    W_out = k.dram("gdn_w_out%d" % j)
    kq = P.buf("g_kq", [128, NCH, 2, 128], BF16, nsub=NCH)
    v_fm = P.buf("g_vfm", [128, NT], BF16, nsub=NCH)
    v_tm = P.buf("g_vtm", [128, NCH, 128], BF16, nsub=NCH)
    k_tm = P.buf("g_ktm", [128, NCH, 128], BF16, nsub=NCH)
    o_tm = P.buf("g_otm", [128, NCH, 128], BF16, nsub=NCH)
    xpad = P.buf("g_xpad", [128, CPW + 1], BF16)
    dgl = P.buf("g_dg", [128, 4, 128], BF16)
    wz = P.buf("g_wz", [128, 8, 128], BF16)
    wsm = P.buf("g_wsm", [128, 8, 32], BF16)
    cw = P.buf("g_cw", [128, 24, 4], F32)
    rows = P.buf("g_rows", [128, 160], F32)
    masks = P.buf("g_masks", [128, 2, 128], F32)
    negb = P.buf("g_negb", [128, 2, 128], BF16)
    lvl = P.buf("g_lvl", [128, 2, 7, 2, 128], BF16)
    I2 = P.buf("g_I2", [128, 2, 128], BF16)
    onesf = P.buf("g_onesf", [128, 128], F32)
    c_one = P.buf("g_c1", [128, 1], F32)
    c_eps6 = P.buf("g_c2", [128, 1], F32)
    c_eps6q = P.buf("g_c3", [128, 1], F32)
    c_eps5 = P.buf("g_c4", [128, 1], F32)
    sc = {nm: P.buf("g_" + nm, [128, NCH, 2], F32) for nm in ["beta", "g", "ngc", "negegc", "eout", "egl", "tmp"]}
    qf = P.buf("g_qf", [128, BLK], F32)
    sqb = P.buf("g_sq", [128, BLK], BF16)
    rn = P.buf("g_rn", [128, BLK], F32)
    egcr = P.buf("g_egcr", [128, 2, 128], F32, nsub=2)
    decB = P.buf("g_dec", [128, 2, 128], F32, nsub=2)
    NBb = P.buf("g_NB", [128, 2, 128], BF16, nsub=2)
    NAb = P.buf("g_NA", [128, 2, 128], BF16, nsub=2)
    attnB = P.buf("g_attn", [128, 2, 128], BF16, nsub=2)
    qin = P.buf("g_qin", [128, 2, 128], BF16, nsub=2)
    kout = P.buf("g_kout", [128, 2, 128], BF16, nsub=2)
    Tm = P.buf("g_T", [128, 2, 2, 128], BF16, nsub=2)
    Yn = P.buf("g_Y", [128, 2, 2, 128], BF16, nsub=2)
    Rp = P.buf("g_Rp", [128, 2, 128], BF16, nsub=2)
    vn = P.buf("g_vn", [128, 2, 128], BF16, nsub=2)
    S = P.buf("g_S", [128, 2, 128], F32, nsub=2)
    Sb = P.buf("g_Sb", [128, 2, 128], BF16, nsub=2)
    ot = P.buf("g_ot", [128, 128], F32)
    sz = P.buf("g_sz", [128, 128], F32)
    ogb = P.buf("g_og", [128, 128], BF16)
    junk = P.buf("g_junk", [128, 128], BF16)
    ssq = P.buf("g_ssq", [128, 2], F32)

    P.dma("sp", cw.r(), R(k.dram("gdn_cw").ap[j], k.dram("gdn_cw").keys))
    P.dma("sp", rows.r(), R(k.dram("gdn_rows").ap[j], k.dram("gdn_rows").keys))
    P.dma("sp", masks.r(), R(k.dram("cmask").ap[0:2].rearrange("a p m -> p a m"), k.dram("cmask").keys))
    P.dma("pool", negb.r(), R(k.dram("cmask").ap[2:4].rearrange("a p m -> p a m"), k.dram("cmask").keys))
    for d in range(2):
        P.dma("pool", lvl.r(lvl.t[:, d]), R(k.dram("glvl").ap[d].rearrange("l a p m -> p l a m"), k.dram("glvl").keys))
    for a in range(2):
        P.copy("dve", I2.r(I2.t[:, a, :]), k.identf.r())
    P.memset("dve", onesf.r(), 1.0)
    P.memset("dve", c_one.r(), 1.0)
    P.memset("dve", c_eps6.r(), 1e-6)
    P.memset("dve", c_eps6q.r(), 128e-6)
    P.memset("dve", c_eps5.r(), 1e-5)
    P.memset("pool", xpad.r(), 0.0)
    P.act(rows.r(rows.t[:, 16:32]), rows.r(rows.t[:, 16:32]), AF.Exp)
    P.ts("dve", rows.r(rows.t[:, 16:32]), rows.r(rows.t[:, 16:32]), -1.0, ALU.mult)
    src = W_in.ap[0][:, 4096:4128].rearrange("(kk p) c -> p kk c", p=128)
    P.dma("pool", wsm.r(), R(src, W_in.keys))

    def conv_tile(col0, cidx, post):
        s, w = load_in_w(k, W_in, 0, col0)
        i0 = k.identf.t[:].unsqueeze(1).to_broadcast([128, 4, 128])
        i1 = cw.t[:, cidx, 0:4].unsqueeze(2).to_broadcast([128, 4, 128])
        P.tt("pool", dgl.r(), k.identf.r(i0), cw.r(i1), ALU.mult)
        for b in range(NBLK):
            bk = P.bank()
            for kk in range(8):
                P.mm(psb(k, bk), k.wr.r(w[:, kk, :], s), hs(k, kk, b), start=(kk == 0), stop=(kk == 7))
            dst, _, _ = conv_dst(xpad.t, b)
            P.copy("dve", xpad.r(dst), k.ps.r(psview(k, bk, b), bk))
        for b in range(NBLK):
            bk = P.bank()
            for kk in range(4):
                P.mm(k.ps.r(psview(k, bk, b), bk), dgl.r(dgl.t[:, kk, :]), xpad.r(conv_rhs(xpad.t, b, kk)), start=(kk == 0), stop=(kk == 3))
            post(b, bk)

    def norm_post(which, scale, epsb):
        def post(b, bk):
            P.act(qf.r(), psb(k, bk), AF.Silu)
            P.act(sqb.r(), qf.r(), AF.Square)
            b2 = P.bank()
            P.mm(psb(k, b2), k.onesb.r(), sqb.r())
            P.act(rn.r(), psb(k, b2), AF.Sqrt, bias=epsb.r(), scale=scale)
            o_, i_ = rn.t, rn.t
            P.op("dve", lambda e, o_=o_, i_=i_: e.reciprocal(out=o_, in_=i_), reads=[rn.r()], writes=[rn.r()])
            dst = kq.t[:, 4 * b:4 * b + 4, which, :]
            P.tt("dve", kq.r(dst, range(4 * b, 4 * b + 4)), qf.r(qf.t.rearrange("p (c m) -> p c m", m=128)), rn.r(rn.t.rearrange("p (c m) -> p c m", m=128)), ALU.mult)
        return post

    def v_post(b, bk):
        P.act(v_fm.r(v_fm.t[:, b * BLK:(b + 1) * BLK], range(4 * b, 4 * b + 4)), psb(k, bk), AF.Silu)

    nheads = k.cfg.get("gdn_heads", 8)
    for hh in range(nheads):
        src = W_in.ap[0][:, 3072 + hh * 128:3072 + (hh + 1) * 128].rearrange("(kk p) c -> p kk c", p=128)
        P.dma("pool", wz.r(), R(src, W_in.keys))
        bk = P.bank()
        for c in range(NCH):
            for kk in range(8):
                rhs = wsm.t[:, kk, :].rearrange("p (t d h) -> p t d h", t=2, d=2)[:, :, :, hh]
                out = k.ps.t[:, bk, c * 4:(c + 1) * 4].rearrange("p (t d) -> p t d", t=2)
                P.mm(k.ps.r(out, bk), hch(k, kk, c), wsm.r(rhs), start=(kk == 0), stop=(kk == 7))
        psv = k.ps.t[:, bk, 0:NCH * 4].rearrange("p (c t d) -> p c t d", t=2, d=2)
        beta, g, ngc, negegc, eout, egl, tmp = sc["beta"], sc["g"], sc["ngc"], sc["negegc"], sc["eout"], sc["egl"], sc["tmp"]
        P.copy("dve", tmp.r(), k.ps.r(psv[:, :, 0, :], bk))
        P.act(beta.r(), tmp.r(), AF.Sigmoid)

        def rowbc(off):
            return rows.t[:, off:off + 16].rearrange("p (d h) -> p d h", d=2)[:, :, hh].unsqueeze(1).to_broadcast([128, NCH, 2])
        P.tt("dve", tmp.r(), k.ps.r(psv[:, :, 1, :], bk), rows.r(rowbc(0)), ALU.add)
        P.ts("dve", g.r(), tmp.r(), -1.0, ALU.mult)
        P.tt("dve", g.r(), g.r(), tmp.r(), ALU.max)
        P.act(g.r(), g.r(), AF.Exp, scale=-1.0)
        P.act(g.r(), g.r(), AF.Ln, bias=c_one.r())
        P.ts("dve", tmp.r(), tmp.r(), 0.0, ALU.max)
        P.tt("dve", g.r(), g.r(), tmp.r(), ALU.add)
        P.tt("dve", g.r(), g.r(), rows.r(rowbc(16)), ALU.mult)
        bk2 = P.bank()
        bk3 = P.bank()
        for c in range(NCH):
            for d in range(2):
                P.mm(k.ps.r(k.ps.t[:, bk2, c * 2 + d:c * 2 + d + 1], bk2), masks.r(masks.t[:, d, :]), g.r(g.t[:, c, d:d + 1]))
            P.mm(k.ps.r(k.ps.t[:, bk3, c * 2:c * 2 + 2], bk3), onesf.r(), g.r(g.t[:, c, :]))
        ps2 = k.ps.t[:, bk2, 0:NCH * 2].rearrange("p (c d) -> p c d", d=2)
        ps3 = k.ps.t[:, bk3, 0:NCH * 2].rearrange("p (c d) -> p c d", d=2)
        P.ts("dve", ngc.r(), k.ps.r(ps2, bk2), -1.0, ALU.mult)
        P.act(negegc.r(), ngc.r(), AF.Exp, scale=-1.0)
        P.ts("dve", negegc.r(), negegc.r(), -1.0, ALU.mult)
        P.copy("dve", egl.r(), k.ps.r(ps3, bk3))
        P.tt("dve", eout.r(), egl.r(), ngc.r(), ALU.add)
        P.act(eout.r(), eout.r(), AF.Exp)
        P.act(egl.r(), egl.r(), AF.Exp)
        conv_tile(hh * 128, hh, norm_post(1, 128.0, c_eps6q))
        conv_tile(1024 + hh * 128, 8 + hh, norm_post(0, 1.0, c_eps6))
        conv_tile(2048 + hh * 128, 16 + hh, v_post)
        for (srcfn, dstb) in ((lambda c: v_fm.r(v_fm.t[:, c * 128:(c + 1) * 128], c), v_tm), (lambda c: kq.r(kq.t[:, c, 0, :], c), k_tm)):
            for c0 in range(0, NCH, 8):
                bk = P.bank()
                pb = k.ps.t[:, bk, :].bitcast(BF16)
                n = min(8, NCH - c0)
                for ci in range(n):
                    P.tr(k.ps.r(pb[:, ci * 128:(ci + 1) * 128], bk), srcfn(c0 + ci), k.identb.r())
                srcv = pb[:, 0:n * 128].rearrange("p (c m) -> p c m", m=128)
                P.copy("act", dstb.r(dstb.t[:, c0:c0 + n, :], range(c0, c0 + n)), k.ps.r(srcv, bk))
        nsteps = k.cfg.get("gdn_steps", NCH)
        for step in range(nsteps):
            for d in range(2):
                c = step if d == 0 else NCH - 1 - step
                first_visit = (c <= 9) if d == 0 else (c >= 10)
                if nsteps < NCH:
                    first_visit = True
                seg = seg_of_chunk(c)
                c_first, c_n = SEG_CH[seg]
                seg_start = (c == c_first) if d == 0 else (c == c_first + c_n - 1)
                seg_end = (c == c_first + c_n - 1) if d == 0 else (c == c_first)
                if seg_start:
                    if seg == 2:
                        P.dma("sp", S.r(S.t[:, d], d), R(k.dram("gdn_s0").ap[j, d, hh], k.dram("gdn_s0").keys))
                    else:
                        P.memset("dve", S.r(S.t[:, d], d), 0.0)
                    P.copy("act", Sb.r(Sb.t[:, d], d), S.r(S.t[:, d], d))
                kc = kq.r(kq.t[:, c, 0, :], c)
                bkR = P.bank()
                gbc = g.t[:, c, d:d + 1].to_broadcast([128, 128])
                r0 = k.ps.r(k.ps.t[:, bkR, 0:128], bkR)
                r1 = k.ps.r(k.ps.t[:, bkR, 128:256], bkR)
                P.mm(r0, g.r(gbc), masks.r(masks.t[:, d, :]))
                P.mm(r1, g.r(gbc), masks.r(masks.t[:, d, :]), start=True, stop=False)
                P.mm(r1, k.identb.r(), negb.r(negb.t[:, d, :]), start=False, stop=True)
                P.act(egcr.r(egcr.t[:, d], d), r0, AF.Exp)
                P.act(decB.r(decB.t[:, d], d), r1, AF.Exp, bias=ngc.r(ngc.t[:, c, d:d + 1]))
                bkK = P.bank()
                P.mm(k.ps.r(k.ps.t[:, bkK, 0:256], bkK), kc, kq.r(kq.t[:, c, :, :].rearrange("p a m -> p (a m)"), c))
                P.stt(NBb.r(NBb.t[:, d], d), k.ps.r(k.ps.t[:, bkK, 0:128], bkK), beta.r(beta.t[:, c, d:d + 1]), decB.r(decB.t[:, d], d), ALU.mult, ALU.mult)
                P.tt("dve", attnB.r(attnB.t[:, d], d), k.ps.r(k.ps.t[:, bkK, 128:256], bkK), decB.r(decB.t[:, d], d), ALU.mult)
                bkT = P.bank()
                pbT = k.ps.t[:, bkT, :].bitcast(BF16)
                P.tr(k.ps.r(pbT[:, 0:128], bkT), NBb.r(NBb.t[:, d], d), k.identb.r())
                P.copy("act", NAb.r(NAb.t[:, d], d), k.ps.r(pbT[:, 0:128], bkT))
                P.tt("dve", qin.r(qin.t[:, d], d), kq.r(kq.t[:, c, 1, :], c), egcr.r(egcr.t[:, d], d), ALU.mult)
                P.act(kout.r(kout.t[:, d], d), k_tm.r(k_tm.t[:, c, :], c), AF.Identity, scale=eout.r(eout.t[:, c, d:d + 1]))
                P.tt("dve", Yn.r(Yn.t[:, d, 0], d), NAb.r(NAb.t[:, d], d), lvl.r(lvl.t[:, d, 0, 0]), ALU.mult)
                P.tt("dve", Yn.r(Yn.t[:, d, 1], d), NBb.r(NBb.t[:, d], d), lvl.r(lvl.t[:, d, 0, 1]), ALU.mult)
                P.tt("dve", Tm.r(Tm.t[:, d], d), I2.r(), Yn.r(Yn.t[:, d], d), ALU.add)
                for lv in range(1, 7):
                    bkY = P.bank()
                    P.mm(k.ps.r(k.ps.t[:, bkY, 0:128], bkY), NBb.r(NBb.t[:, d], d), Tm.r(Tm.t[:, d, 0], d))
                    P.mm(k.ps.r(k.ps.t[:, bkY, 128:256], bkY), NAb.r(NAb.t[:, d], d), Tm.r(Tm.t[:, d, 1], d))
                    P.tt("dve", Yn.r(Yn.t[:, d], d), k.ps.r(k.ps.t[:, bkY, 0:256].rearrange("p (a m) -> p a m", m=128), bkY), lvl.r(lvl.t[:, d, lv]), ALU.mult)
                    bkZ = P.bank()
                    P.mm(k.ps.r(k.ps.t[:, bkZ, 0:128], bkZ), Tm.r(Tm.t[:, d, 1], d), Yn.r(Yn.t[:, d, 0], d))
                    P.mm(k.ps.r(k.ps.t[:, bkZ, 128:256], bkZ), Tm.r(Tm.t[:, d, 0], d), Yn.r(Yn.t[:, d, 1], d))
                    P.tt("dve", Tm.r(Tm.t[:, d], d), Tm.r(Tm.t[:, d], d), k.ps.r(k.ps.t[:, bkZ, 0:256].rearrange("p (a m) -> p a m", m=128), bkZ), ALU.add)
                bkC = P.bank()
                P.mm(k.ps.r(k.ps.t[:, bkC, 0:128], bkC), kc, Sb.r(Sb.t[:, d], d))
                P.stt(Rp.r(Rp.t[:, d], d), k.ps.r(k.ps.t[:, bkC, 0:128], bkC), negegc.r(negegc.t[:, c, d:d + 1]), v_tm.r(v_tm.t[:, c, :], c), ALU.mult, ALU.add)
                bkV = P.bank()
                P.mm(k.ps.r(k.ps.t[:, bkV, 0:128], bkV), Tm.r(Tm.t[:, d, 1], d), Rp.r(Rp.t[:, d], d))
                P.act(vn.r(vn.t[:, d], d), k.ps.r(k.ps.t[:, bkV, 0:128], bkV), AF.Identity, scale=beta.r(beta.t[:, c, d:d + 1]))
                bkO = P.bank()
                po = k.ps.r(k.ps.t[:, bkO, 0:128], bkO)
                P.mm(po, qin.r(qin.t[:, d], d), Sb.r(Sb.t[:, d], d), start=True, stop=False)
                P.mm(po, attnB.r(attnB.t[:, d], d), vn.r(vn.t[:, d], d), start=False, stop=True)
                bkS = P.bank()
                pS = k.ps.r(k.ps.t[:, bkS, 0:128], bkS)
                P.mm(pS, kout.r(kout.t[:, d], d), vn.r(vn.t[:, d], d))
                P.stt(S.r(S.t[:, d], d), S.r(S.t[:, d], d), egl.r(egl.t[:, c, d:d + 1]), pS, ALU.mult, ALU.add)
                P.copy("act", Sb.r(Sb.t[:, d], d), S.r(S.t[:, d], d))
                if first_visit:
                    P.copy("act", o_tm.r(o_tm.t[:, c, :], c), po)
                else:
                    P.tt("dve", ot.r(), po, o_tm.r(o_tm.t[:, c, :], c), ALU.add)
                    o_, i_, a_ = junk.t, ot.t, ssq.t[:, 0:1]
                    P.op("act", lambda e, o_=o_, i_=i_, a_=a_: e.activation(out=o_, in_=i_, func=AF.Square, accum_out=a_),
                         reads=[ot.r()], writes=[junk.r(), ssq.r()])
                    P.act(ssq.r(ssq.t[:, 1:2]), ssq.r(ssq.t[:, 0:1]), AF.Sqrt, bias=c_eps5.r(), scale=1.0 / 128)
                    o_, i_ = ssq.t[:, 1:2], ssq.t[:, 1:2]
                    P.op("dve", lambda e, o_=o_, i_=i_: e.reciprocal(out=o_, in_=i_), reads=[ssq.r()], writes=[ssq.r()])
                    bkZ = P.bank()
                    for kk in range(8):
                        P.mm(k.ps.r(k.ps.t[:, bkZ, 0:128], bkZ), hch(k, kk, c), wz.r(wz.t[:, kk, :]), start=(kk == 0), stop=(kk == 7))
                    P.act(sz.r(), k.ps.r(k.ps.t[:, bkZ, 0:128], bkZ), AF.Silu)
                    P.stt(ot.r(), ot.r(), ssq.r(ssq.t[:, 1:2]), rows.r(rows.t[:, 32:160]), ALU.mult, ALU.mult)
                    P.tt("dve", ogb.r(), ot.r(), sz.r(), ALU.mult)
                    bkT2 = P.bank()
                    pbT2 = k.ps.t[:, bkT2, :].bitcast(BF16)
                    P.tr(k.ps.r(pbT2[:, 0:128], bkT2), ogb.r(), k.identb.r())
                    P.copy("act", v_fm.r(v_fm.t[:, c * 128:(c + 1) * 128], c), k.ps.r(pbT2[:, 0:128], bkT2))
                if seg_end and seg < 2:
                    P.dma("sp", R(k.dram("gdn_out").ap[j, seg, d, hh], k.dram("gdn_out").keys), S.r(S.t[:, d], d))
        if nsteps == NCH:
            outproj_acc_g(k, W_out, hh * 128, lambda b: v_fm.r(v_fm.t[:, b * BLK:(b + 1) * BLK], range(4 * b, 4 * b + 4)))
    P.barrier()
    P.release(mk)


def outproj_acc_g(k, W, r0, o_region):
    P = k.P
    s, wo = load_out_w(k, W, 0, r0)
    for m in range(8):
        for b in range(NBLK):
            bk = P.bank()
            P.mm(psb(k, bk), k.wr.r(wo[:, m * 128:(m + 1) * 128], s), o_region(b))
            c = bc(b)
            P.stt(xs(k, m, b), psb(k, bk), k.mod.r(k.mod.t[:, Q_G1 + m, c:c + 1]), xs(k, m, b), ALU.mult, ALU.add)


NCORES = 8


def f32(a):
    return np.ascontiguousarray(np.asarray(a, dtype=np.float32))


def prep_inputs(inp):
    g = {k: np.asarray(v) for k, v in inp.items()}
    DEPTH = 4
    shared = {}
    shared["ident"] = np.eye(128, dtype=np.float32)
    for L in range(DEPTH):
        shared["ada_w%d" % L] = f32(g["ada_w"][L:L + 1])
        shared["ffn_w_in%d" % L] = f32(g["ffn_w_in"][L:L + 1])
        shared["ffn_w_out%d" % L] = f32(g["ffn_w_out"][L:L + 1])
    shared["ada_b"] = f32(g["ada_b"].reshape(DEPTH, 48, 128).transpose(0, 2, 1))
    shared["lng"] = f32(g["ln_g"].reshape(DEPTH, 2, 8, 128).transpose(0, 1, 3, 2))
    shared["lnb"] = f32(g["ln_b"].reshape(DEPTH, 2, 8, 128).transpose(0, 1, 3, 2))
    shared["fcw"] = f32(g["ffn_conv"].reshape(DEPTH, 9, 44, 128).transpose(0, 3, 2, 1))
    shared["lru_w_in"] = f32(g["lru_w_in"])
    shared["lru_w_out"] = f32(g["lru_w_out"])
    shared["lru_gate_w"] = f32(g["lru_gate_w"])
    sm = np.zeros((128, 10, 12), np.float32)
    sm[:, :, 0:4] = g["lru_conv"][0].reshape(4, 10, 128).transpose(2, 1, 0)
    sm[:, :, 4] = g["lru_conv_b"][0].reshape(10, 128).T
    sm[:, :, 5:9] = g["lru_gate_b"][0].reshape(4, 10, 128).transpose(2, 1, 0)
    sm[:, :, 9:11] = g["lru_lambda"][0].reshape(2, 10, 128).transpose(2, 1, 0)
    shared["lru_sm"] = sm
    tri_f = np.triu(np.ones((128, 128), np.float32))
    tri_b = np.tril(np.ones((128, 128), np.float32))
    shared["cmask"] = np.stack([tri_f, tri_b, (1 - tri_f) * -30000.0, (1 - tri_b) * -30000.0]).astype(np.float32)
    shared["ssd_w_in"] = f32(g["ssd_w_in"])
    shared["ssd_w_out"] = f32(g["ssd_w_out"])
    cw = np.zeros((128, 24, 5), np.float32)
    cw[:, :, 0:4] = g["ssd_conv"][0].reshape(4, 24, 128).transpose(2, 1, 0)
    cw[:, :, 4] = g["ssd_conv_b"][0].reshape(24, 128).T
    shared["ssd_cw"] = cw
    row = np.concatenate([g["ssd_dt_bias"][0].reshape(64), g["ssd_a_log"][0].reshape(64), g["ssd_d"][0].reshape(32)])
    shared["ssd_rows"] = f32(np.broadcast_to(row[None, :], (128, 160)))
    shared["ssd_nrm"] = f32(np.broadcast_to(g["ssd_norm"][0][None, :], (128, 2048)))
    for jj in range(2):
        shared["gdn_w_in%d" % jj] = f32(g["gdn_w_in"][jj:jj + 1])
        shared["gdn_w_out%d" % jj] = f32(g["gdn_w_out"][jj:jj + 1])
    shared["gdn_cw"] = f32(g["gdn_conv"].reshape(2, 4, 24, 128).transpose(0, 3, 2, 1))
    grow = np.concatenate([g["gdn_dt_bias"].reshape(2, 16), g["gdn_a_log"].reshape(2, 16), g["gdn_norm"].reshape(2, 128)], axis=1)
    shared["gdn_rows"] = f32(np.broadcast_to(grow[:, None, :], (2, 128, 160)))
    li = np.arange(128)[:, None]
    si = np.arange(128)[None, :]
    lv = np.zeros((2, 7, 2, 128, 128), np.float32)
    for jl in range(7):
        Bs = 2 ** jl
        mA = ((li // (2 * Bs)) == (si // (2 * Bs))) & ((li % (2 * Bs)) >= Bs) & ((si % (2 * Bs)) < Bs)
        mA = mA.astype(np.float32)
        lv[0, jl, 0] = -mA
        lv[0, jl, 1] = -mA.T
        lv[1, jl, 0] = -mA.T
        lv[1, jl, 1] = -mA
    shared["glvl"] = lv
    maps = []
    for i in range(NCORES):
        p0, p1, sb = 2 * i, 2 * i + 1, i % 4
        xin = np.concatenate([g["x_prompt"][p0].T, g["x_prompt"][p1].T, g["x_sample"][sb].T], axis=1)
        cond = np.stack([g["c_ctx"].reshape(8, 128).T, g["c"][sb].reshape(8, 128).T], axis=-1)
        m = dict(shared)
        m["xin"] = f32(xin)
        m["cond"] = f32(cond)
        m["ssd_s0"] = f32(g["state_ssd"][sb, 0].reshape(2, 8, 4, 64, 128).transpose(0, 1, 4, 2, 3).reshape(2, 8, 128, 256))
        m["gdn_s0"] = f32(g["state_gdn"][sb])
        m["lru_s0"] = f32(g["state_lru"][sb, 0].reshape(2, 10, 128).transpose(2, 1, 0))
        maps.append(m)
    return maps


def assemble(results, inp):
    BATCH, SEQ, D = 16, 256, 1024
    yp = np.zeros((BATCH, SEQ, D), np.float32)
    ys = np.zeros((4, 2048, D), np.float32)
    for i in range(NCORES):
        y = np.asarray(results[i]["yout"])
        yp[2 * i] = y[:, 0:256].T
        yp[2 * i + 1] = y[:, 256:512].T
        if i < 4:
            ys[i] = y[:, 512:].T
    nl = np.zeros((BATCH, 1, 2, 1280), np.float32)
    for i in range(NCORES):
        if "lru_out" in results[i]:
            o = np.asarray(results[i]["lru_out"])
            for pi in range(2):
                nl[2 * i + pi, 0] = o[:, :, pi, :].transpose(2, 1, 0).reshape(2, 1280)
    nssd = np.zeros((BATCH, 1, 2, 32, 64, 128), np.float32)
    for i in range(NCORES):
        if "ssd_out" in results[i]:
            o = np.asarray(results[i]["ssd_out"])
            for pi in range(2):
                nssd[2 * i + pi, 0] = o[pi].reshape(2, 32, 64, 128)
    ngdn = np.zeros((BATCH, 2, 2, 8, 128, 128), np.float32)
    for i in range(NCORES):
        if "gdn_out" in results[i]:
            o = np.asarray(results[i]["gdn_out"])
            for pi in range(2):
                ngdn[2 * i + pi] = o[:, pi]
    return yp, ys, nl, nssd, ngdn


_NC_CACHE = {}


def kernel(**inputs):
    cfg = {}
    if "nc" not in _NC_CACHE:
        _NC_CACHE["nc"] = build(cfg)
    nc, used = _NC_CACHE["nc"]
    maps = prep_inputs(inputs)
    maps = [{kk: v for kk, v in mm.items() if kk in used} for mm in maps]
    res = run_bass_kernel_spmd(nc, maps, core_ids=list(range(NCORES)))
    yp, ys, nl, nssd, ngdn = assemble(res.results, inputs)
    return (yp, ys, ngdn, nssd, nl)
```

```python
import numpy as np
import concourse.bass as bass
import concourse.mybir as mybir
from concourse.bass_utils import run_bass_kernel_spmd
from contextlib import ExitStack

F32 = mybir.dt.float32
F32R = mybir.dt.float32r
BF16 = mybir.dt.bfloat16
ALU = mybir.AluOpType
AF = mybir.ActivationFunctionType
AX = mybir.AxisListType

ENGS = ["pe", "act", "dve", "pool", "sp"]
NDS = 24
MAXEMB = 1
ARENA_WORDS = 53000


class R:
    __slots__ = ("ap", "keys")

    def __init__(self, ap, keys):
        self.ap = ap
        self.keys = keys


class Buf:
    def __init__(self, prog, name, shape, dtype, nsub=1, psum=False):
        self.name = name
        self.nsub = nsub
        if psum:
            self.t = prog.st.enter_context(prog.nc.psum_tensor(name, shape, dtype))
        else:
            n = 1
            for d in shape[1:]:
                n *= d
            esz = 2 if dtype == BF16 else 4
            words = (n * esz + 3) // 4
            off = prog.aoff
            prog.aoff += words
            assert prog.aoff <= ARENA_WORDS, (name, prog.aoff)
            prog.apeak = max(prog.apeak, prog.aoff)
            ap = prog.arena[:, off:off + words]
            if dtype == BF16:
                ap = ap.bitcast(BF16)
            ap = ap[:, 0:n]
            if len(shape) > 2:
                names = " ".join("d%d" % i for i in range(len(shape) - 1))
                kw = {"d%d" % i: shape[i + 1] for i in range(len(shape) - 1)}
                ap = ap.rearrange("p (%s) -> p %s" % (names, names), **kw)
            self.t = ap

    def r(self, ap=None, sub=None):
        if ap is None:
            ap = self.t[:] if not hasattr(self.t, "rearrange") else self.t
        if sub is None:
            keys = [(self.name, i) for i in range(self.nsub)]
        elif isinstance(sub, (list, tuple, range)):
            keys = [(self.name, i) for i in sub]
        else:
            keys = [(self.name, sub)]
        return R(ap, keys)


class Prog:
    def __init__(self, nc, st):
        self.nc = nc
        self.st = st
        self.q = {e: [] for e in ENGS}
        self.cnt = {e: 0 for e in ENGS}
        self.sem = {e: st.enter_context(nc.semaphore("s_" + e)) for e in ENGS}
        self.dsem = [st.enter_context(nc.semaphore("d%d" % i)) for i in range(NDS)]
        self.dcnt = [0] * NDS
        self.dnext = 0
        self.known = {e: {} for e in ENGS}
        self.last_w = {}
        self.readers = {}
        self.nops = 0
        self.nwaits = 0
        self.bar = {e: None for e in ENGS}
        self._bank = 0
        self.arena_t = st.enter_context(nc.sbuf_tensor("arena", [128, ARENA_WORDS], F32))
        self.arena = self.arena_t[:]
        self.aoff = 0
        self.apeak = 0

    def mark(self):
        return self.aoff

    def release(self, m):
        self.aoff = m

    def bank(self):
        b = self._bank
        self._bank = (self._bank + 1) % 8
        return b

    def barrier(self):
        snap = {e: self.cnt[e] for e in ENGS if self.cnt[e] > 0}
        for i in range(NDS):
            if self.dcnt[i] > 0:
                snap["d%d" % i] = self.dcnt[i]
        for e in ENGS:
            self.bar[e] = dict(snap)

    def buf(self, name, shape, dtype, nsub=1, psum=False):
        return Buf(self, name, shape, dtype, nsub, psum)

    def _collect(self, eng, reads, writes):
        need = {}

        def add(tok):
            if tok is None:
                return
            semid, val, snap = tok
            if eng == "pe" and semid == "pe":
                return
            if need.get(semid, (0, None))[0] < val:
                need[semid] = (val, snap)

        for k in reads:
            add(self.last_w.get(k))
        for k in writes:
            add(self.last_w.get(k))
            rd = self.readers.get(k)
            if rd:
                for tok in rd.values():
                    add(tok)
        kn = self.known[eng]
        if self.bar[eng] is not None:
            for semid, val in self.bar[eng].items():
                if eng == "pe" and semid == "pe":
                    continue
                if semid == eng and val >= self.cnt[eng] + 1:
                    continue
                if need.get(semid, (0, None))[0] < val:
                    need[semid] = (val, None)
            self.bar[eng] = None
        waits = []
        for semid, (val, snap) in need.items():
            if kn.get(semid, 0) >= val:
                continue
            waits.append((semid, val))
        for semid, (val, snap) in need.items():
            if kn.get(semid, 0) < val:
                kn[semid] = val
            if snap:
                for s2, v2 in snap.items():
                    if kn.get(s2, 0) < v2:
                        kn[s2] = v2
        return waits

    def _commit(self, tok, reads, writes):
        for k in writes:
            self.last_w[k] = tok
            self.readers[k] = {}
        for k in reads:
            if k in writes:
                continue
            self.readers.setdefault(k, {})[tok[0]] = tok

    def op(self, eng, fn, reads=(), writes=()):
        rk = [k for r in reads if r is not None for k in r.keys]
        wk = [k for r in writes if r is not None for k in r.keys]
        if eng != "pe":
            for k_ in rk:
                if k_[0] == "ps" and k_ not in wk:
                    wk.append(k_)
        waits = self._collect(eng, rk, wk)
        self.cnt[eng] += 1
        tok = (eng, self.cnt[eng], dict(self.known[eng]))
        self.q[eng].append((waits, fn, None))
        self._commit(tok, rk, wk)
        self.nops += 1
        self.nwaits += len(waits)
        return tok

    def dma(self, eng, out, in_, **kw):
        rk = list(in_.keys)
        wk = list(out.keys)
        i = self.dnext
        self.dnext = (self.dnext + 1) % NDS
        semid = "d%d" % i
        waits = self._collect(eng, rk, wk)
        kn = self.known[eng]
        if kn.get(semid, 0) < self.dcnt[i]:
            waits.append((semid, self.dcnt[i]))
            kn[semid] = self.dcnt[i]
        self.dcnt[i] += 16
        tok = (semid, self.dcnt[i], dict(kn))
        oap, iap = out.ap, in_.ap

        def fn(e):
            return e.dma_start(out=oap, in_=iap, **kw)

        self.q[eng].append((waits, fn, i))
        self._commit(tok, rk, wk)
        self.nops += 1
        self.nwaits += len(waits)
        return tok

    def _semh(self, semid):
        if semid in self.sem:
            return self.sem[semid]
        return self.dsem[int(semid[1:])]

    def emit(self, final_wait_eng="sp"):
        nc = self.nc
        finals = []
        for e in ENGS:
            if self.cnt[e] > 0:
                finals.append((e, self.cnt[e]))
        for i in range(NDS):
            if self.dcnt[i] > 0:
                finals.append(("d%d" % i, self.dcnt[i]))
        eng_objs = {}
        with nc.Block() as block:
            def mk(ename):
                def body(e):
                    for waits, fn, dsi in self.q[ename]:
                        if dsi is not None or len(waits) > MAXEMB:
                            for semid, val in waits:
                                e.wait_ge(self._semh(semid), val)
                            ins = fn(e)
                        else:
                            ins = fn(e)
                            for semid, val in waits:
                                ins._wait_ge(self._semh(semid), val)
                        if dsi is None:
                            ins.then_inc(self.sem[ename], 1)
                        else:
                            ins.then_inc(self.dsem[dsi], 16)
                    if ename == final_wait_eng:
                        for semid, val in finals:
                            e.wait_ge(self._semh(semid), val)
                return body
            block.tensor(mk("pe"))
            block.scalar(mk("act"))
            block.vector(mk("dve"))
            block.gpsimd(mk("pool"))
            block.sync(mk("sp"))

    def mm(self, out, lhsT, rhs, start=True, stop=True, extra_reads=()):
        o, l, r = out.ap, lhsT.ap, rhs.ap
        return self.op("pe", lambda e: e.matmul(o, l, r, start=start, stop=stop),
                       reads=[lhsT, rhs] + list(extra_reads) + ([] if start else [out]), writes=[out])

    def tr(self, out, in_, ident):
        o, i, d = out.ap, in_.ap, ident.ap
        return self.op("pe", lambda e: e.transpose(o, i, d), reads=[in_, ident], writes=[out])

    def act(self, out, in_, func, bias=None, scale=None, eng="act"):
        o, i = out.ap, in_.ap
        kw = {}
        rd = [in_]
        if bias is not None:
            if isinstance(bias, R):
                kw["bias"] = bias.ap
                rd.append(bias)
            else:
                kw["bias"] = float(bias)
        if scale is not None:
            if isinstance(scale, R):
                kw["scale"] = scale.ap
                rd.append(scale)
            else:
                kw["scale"] = float(scale)
        return self.op("act", lambda e: e.activation(out=o, in_=i, func=func, **kw), reads=rd, writes=[out])

    def tt(self, eng, out, in0, in1, op):
        o, a, b = out.ap, in0.ap, in1.ap
        return self.op(eng, lambda e: e.tensor_tensor(out=o, in0=a, in1=b, op=op), reads=[in0, in1], writes=[out])

    def ts(self, eng, out, in0, s1, op0, s2=None, op1=None):
        o, a = out.ap, in0.ap
        rd = [in0]
        v1 = s1
        if isinstance(s1, R):
            rd.append(s1)
            v1 = s1.ap
        v2 = s2
        if isinstance(s2, R):
            rd.append(s2)
            v2 = s2.ap
        if op1 is None:
            return self.op(eng, lambda e: e.tensor_scalar(out=o, in0=a, scalar1=v1, scalar2=None, op0=op0), reads=rd, writes=[out])
        return self.op(eng, lambda e: e.tensor_scalar(out=o, in0=a, scalar1=v1, scalar2=v2, op0=op0, op1=op1), reads=rd, writes=[out])

    def stt(self, out, in0, scalar, in1, op0, op1):
        o, a, b = out.ap, in0.ap, in1.ap
        rd = [in0, in1]
        sv = scalar
        if isinstance(scalar, R):
            rd.append(scalar)
            sv = scalar.ap
        return self.op("dve", lambda e: e.scalar_tensor_tensor(out=o, in0=a, scalar=sv, in1=b, op0=op0, op1=op1), reads=rd, writes=[out])

    def copy(self, eng, out, in_):
        o, i = out.ap, in_.ap
        if eng == "act":
            return self.op("act", lambda e: e.copy(out=o, in_=i), reads=[in_], writes=[out])
        return self.op(eng, lambda e: e.tensor_copy(out=o, in_=i), reads=[in_], writes=[out])

    def memset(self, eng, out, val):
        o = out.ap
        return self.op(eng, lambda e: e.memset(o, val), reads=[], writes=[out])


D = 1024
NT = 2560
NBLK = 5
BLK = 512
DEPTH = 4
FH = 2816
NPAIR = 22
UPW = 2 * 258 + 34 * 66
ALPHA = (2 * DEPTH) ** 0.25
LN_EPS = 1e-5
NW = 5
WSL = 1024


def bc(b):
    return 0 if b == 0 else 1


class K:
    pass


def build(cfg):
    nc = bass.Bass("TRN2", target_bir_lowering=False)
    k = K()
    k.nc = nc
    k.cfg = cfg

    shapes = {
        "xin": [D, NT], "cond": [128, 8, 2], "ident": [128, 128],
        "ada_b": [DEPTH, 128, 48], "lng": [DEPTH, 2, 128, 8], "lnb": [DEPTH, 2, 128, 8],
        "fcw": [DEPTH, 128, 44, 9],
        "lru_w_in": [1, D, 2560], "lru_w_out": [1, 1280, D], "lru_gate_w": [1, 2, 2, 10, 128, 128],
        "lru_sm": [128, 10, 12], "lru_s0": [128, 10, 2],
        "cmask": [4, 128, 128],
        "ssd_w_in": [1, D, 5184], "ssd_w_out": [1, 2048, D], "ssd_cw": [128, 24, 5], "ssd_rows": [128, 160],
        "ssd_nrm": [128, 2048], "ssd_s0": [2, 8, 128, 256],
    }
    for jj in range(2):
        shapes["gdn_w_in%d" % jj] = [1, D, 4128]
        shapes["gdn_w_out%d" % jj] = [1, D, D]
    shapes.update({"gdn_cw": [2, 128, 24, 4], "gdn_rows": [2, 128, 160], "gdn_s0": [2, 2, 8, 128, 128], "glvl": [2, 7, 2, 128, 128]})
    for L in range(DEPTH):
        shapes["ada_w%d" % L] = [1, D, 6 * D]
        shapes["ffn_w_in%d" % L] = [1, D, 2 * FH]
        shapes["ffn_w_out%d" % L] = [1, FH, D]
    oshapes = {"yout": [D, NT], "lru_out": [128, 10, 2, 2], "ssd_out": [2, 2, 2048, 128], "gdn_out": [2, 2, 2, 8, 128, 128]}
    k.shapes = shapes
    k.oshapes = oshapes
    k.decl = {}

    def dram(name):
        if name in k.decl:
            return k.decl[name]
        if name in shapes:
            t = nc.dram_tensor(name, list(shapes[name]), F32, kind="ExternalInput").ap()
        else:
            t = nc.dram_tensor(name, list(oshapes[name]), F32, kind="ExternalOutput").ap()
        k.decl[name] = R(t, [("dram_" + name, 0)])
        return k.decl[name]
    k.dram = dram

    with ExitStack() as st:
        P = Prog(nc, st)
        k.P = P
        k.x = P.buf("x", [128, 8, NT], F32, nsub=40)
        k.h = P.buf("h", [128, 8, NT], BF16, nsub=40)
        k.wr = P.buf("wr", [128, NW, WSL], BF16, nsub=NW)
        k.wnext = 0
        k.ps = P.buf("ps", [128, 8, 512], F32, nsub=8, psum=True)
        k.identf = P.buf("identf", [128, 128], F32)
        k.identb = P.buf("identb", [128, 128], BF16)
        k.onesb = P.buf("onesb", [128, 128], BF16)
        k.csil = P.buf("csil", [128, 8, 2], BF16)
        k.mod = P.buf("mod", [128, 48, 2], F32)
        k.msc = P.buf("msc", [128, 2, 8, 2], F32)
        k.lngb = P.buf("lngb", [128, DEPTH, 2, 8], F32)
        k.lnbb = P.buf("lnbb", [128, DEPTH, 2, 8], F32)
        k.adab = P.buf("adab", [128, DEPTH, 48], F32)

        prologue(k)
        for L in cfg.get("layers", list(range(DEPTH))):
            layer(k, L)
        for j in range(8):
            P.dma("sp", R(k.dram("yout").ap[j * 128:(j + 1) * 128, :], k.dram("yout").keys), k.x.r(k.x.t[:, j, :], range(j * 5, j * 5 + 5)))
        P.emit()
        print("ops", P.nops, "waits", P.nwaits, {e: P.cnt[e] for e in ENGS})
    return nc, set(n for n in k.decl if n in k.shapes)


def xs(k, j, b):
    return k.x.r(k.x.t[:, j, b * BLK:(b + 1) * BLK], j * 5 + b)


def hs(k, j, b):
    return k.h.r(k.h.t[:, j, b * BLK:(b + 1) * BLK], j * 5 + b)


def psb(k, b, n=512):
    return k.ps.r(k.ps.t[:, b, 0:n], b)


def prologue(k):
    P = k.P
    for j in range(8):
        P.dma("sp", k.x.r(k.x.t[:, j, :], range(j * 5, j * 5 + 5)), R(k.dram("xin").ap[j * 128:(j + 1) * 128, :], k.dram("xin").keys))
    P.dma("sp", k.identf.r(), k.dram("ident"))
    P.copy("dve", k.identb.r(), k.identf.r())
    P.memset("dve", k.onesb.r(), 1.0)
    ctmp = P.buf("ctmp", [128, 8, 2], F32)
    P.dma("sp", ctmp.r(), k.dram("cond"))
    P.act(k.csil.r(), ctmp.r(), AF.Silu)
    for L in range(DEPTH):
        P.dma("sp", k.lngb.r(k.lngb.t[:, L]), R(k.dram("lng").ap[L].rearrange("s p j -> p s j"), k.dram("lng").keys))
        P.dma("sp", k.lnbb.r(k.lnbb.t[:, L]), R(k.dram("lnb").ap[L].rearrange("s p j -> p s j"), k.dram("lnb").keys))
        P.dma("sp", k.adab.r(k.adab.t[:, L]), R(k.dram("ada_b").ap[L], k.dram("ada_b").keys))


def wslot(k):
    s = k.wnext
    k.wnext = (k.wnext + 1) % NW
    return s


def load_in_w(k, W, L, c0, ncols=128):
    P = k.P
    s = wslot(k)
    src = W.ap[L][:, c0:c0 + ncols].rearrange("(kk p) c -> p kk c", p=128)
    dst = k.wr.t[:, s, 0:8 * ncols].rearrange("p (kk c) -> p kk c", c=ncols)
    P.dma("pool", k.wr.r(dst, s), R(src, W.keys))
    return s, dst


def load_out_w(k, W, L, r0):
    P = k.P
    s = wslot(k)
    src = W.ap[L][r0:r0 + 128, :]
    dst = k.wr.t[:, s, 0:1024]
    P.dma("pool", k.wr.r(dst, s), R(src, W.keys))
    return s, dst


def ada(k, L):
    P = k.P
    b = P.bank()
    for q in range(48):
        s, w = load_in_w(k, k.dram("ada_w%d" % L), 0, q * 128)
        for kk in range(8):
            P.mm(k.ps.r(k.ps.t[:, b, q * 2:q * 2 + 2], b), k.wr.r(w[:, kk, :], s),
                 k.csil.r(k.csil.t[:, kk, :]), start=(kk == 0), stop=(kk == 7))
    src = k.ps.t[:, b, 0:96].rearrange("p (q c) -> p q c", c=2)
    bias = k.adab.t[:, L, :].unsqueeze(2).to_broadcast([128, 48, 2])
    P.tt("dve", k.mod.r(), k.ps.r(src, b), k.adab.r(bias), ALU.add)
    for sl in range(2):
        q0 = (1 + 3 * sl) * 8
        P.ts("dve", k.msc.r(k.msc.t[:, sl]), k.mod.r(k.mod.t[:, q0:q0 + 8, :]), 1.0, ALU.add)


def modulate(k, sl):
    P = k.P
    q_sh = (0 + 3 * sl) * 8
    for j in range(8):
        for b in range(NBLK):
            c = bc(b)
            P.act(hs(k, j, b), xs(k, j, b), AF.Identity,
                  bias=k.mod.r(k.mod.t[:, q_sh + j, c:c + 1]), scale=k.msc.r(k.msc.t[:, sl, j, c:c + 1]))
    for j in range(8):
        for b in range(NBLK):
            P.ts("dve", xs(k, j, b), xs(k, j, b), ALPHA, ALU.mult)


def layernorm(k, L, sl, lb):
    P = k.P
    xb, sq, mean, m2, var, rstd, nmr, t1 = lb["xb"], lb["sq"], lb["mean"], lb["m2"], lb["var"], lb["rstd"], lb["nmr"], lb["t1"]
    for b in range(NBLK):
        pb = b % 2
        for j in range(8):
            P.act(xb.r(xb.t[:, pb, j, :], pb * 8 + j), xs(k, j, b), AF.Identity)
            P.act(sq.r(sq.t[:, pb, j, :], pb * 8 + j), xs(k, j, b), AF.Square)
        b1 = P.bank()
        b2 = P.bank()
        for j in range(8):
            P.mm(psb(k, b1), k.onesb.r(), xb.r(xb.t[:, pb, j, :], pb * 8 + j), start=(j == 0), stop=(j == 7))
        for j in range(8):
            P.mm(psb(k, b2), k.onesb.r(), sq.r(sq.t[:, pb, j, :], pb * 8 + j), start=(j == 0), stop=(j == 7))
        P.act(mean.r(mean.t[:, pb], pb), psb(k, b1), AF.Identity, scale=1.0 / D)
        P.tt("dve", m2.r(m2.t[:, pb], pb), mean.r(mean.t[:, pb], pb), mean.r(mean.t[:, pb], pb), ALU.mult)
        P.stt(var.r(var.t[:, pb], pb), psb(k, b2), 1.0 / D, m2.r(m2.t[:, pb], pb), ALU.mult, ALU.subtract)
        P.act(var.r(var.t[:, pb], pb), var.r(var.t[:, pb], pb), AF.Sqrt, bias=lb["eps"].r())
        o, i = rstd.t[:, pb], var.t[:, pb]
        P.op("dve", lambda e, o=o, i=i: e.reciprocal(out=o, in_=i), reads=[var.r(var.t[:, pb], pb)], writes=[rstd.r(rstd.t[:, pb], pb)])
        P.stt(nmr.r(nmr.t[:, pb], pb), mean.r(mean.t[:, pb], pb), -1.0, rstd.r(rstd.t[:, pb], pb), ALU.mult, ALU.mult)
        for j in range(8):
            tb = j % 2
            P.tt("dve", t1.r(t1.t[:, tb], tb), xs(k, j, b), rstd.r(rstd.t[:, pb], pb), ALU.mult)
            P.tt("dve", t1.r(t1.t[:, tb], tb), t1.r(t1.t[:, tb], tb), nmr.r(nmr.t[:, pb], pb), ALU.add)
            P.act(xs(k, j, b), t1.r(t1.t[:, tb], tb), AF.Identity,
                  bias=k.lnbb.r(k.lnbb.t[:, L, sl, j:j + 1]), scale=k.lngb.r(k.lngb.t[:, L, sl, j:j + 1]))


def ln_scope(k, L, sl):
    P = k.P
    P.barrier()
    mk = P.mark()
    if True:
        lb = {
            "xb": P.buf("ln_xb", [128, 2, 8, BLK], BF16, nsub=16),
            "sq": P.buf("ln_sq", [128, 2, 8, BLK], BF16, nsub=16),
            "mean": P.buf("ln_mean", [128, 2, BLK], F32, nsub=2),
            "m2": P.buf("ln_m2", [128, 2, BLK], F32, nsub=2),
            "var": P.buf("ln_var", [128, 2, BLK], F32, nsub=2),
            "rstd": P.buf("ln_rstd", [128, 2, BLK], F32, nsub=2),
            "nmr": P.buf("ln_nmr", [128, 2, BLK], F32, nsub=2),
            "t1": P.buf("ln_t1", [128, 2, BLK], F32, nsub=2),
            "eps": P.buf("ln_eps", [128, 1], F32),
        }
        P.memset("dve", lb["eps"].r(), LN_EPS)
        layernorm(k, L, sl, lb)
        P.barrier()
        P.release(mk)


def ffn(k, L):
    P = k.P
    G = 2
    P.barrier()
    mk = P.mark()
    if True:
        up = P.buf("f_up", [128, 2, 2, UPW], BF16, nsub=4)
        ab = P.buf("f_ab", [128, G, NT], BF16, nsub=G * 5)
        dg = P.buf("f_dg", [128, 2, 2, 9, 128], BF16, nsub=4)
        sg = P.buf("f_sg", [128, 2, BLK], F32, nsub=2)
        fcw = P.buf("f_fcw", [128, 44, 9], F32)
        P.dma("sp", fcw.r(), R(k.dram("fcw").ap[L], k.dram("fcw").keys))
        P.memset("pool", up.r(), 0.0)
        q_g = 5 * 8
        nsg = 0
        for g0 in range(0, NPAIR, G):
            pairs = list(range(g0, min(g0 + G, NPAIR)))
            for jj, j in enumerate(pairs):
                db = j % 2
                sg_, wg = load_in_w(k, k.dram("ffn_w_in%d" % L), 0, j * 128)
                sv_, wv = load_in_w(k, k.dram("ffn_w_in%d" % L), 0, FH + j * 128)
                ws = [wg, wv]
                wss = [sg_, sv_]
                for gv in range(2):
                    tile_idx = j + gv * NPAIR
                    i0 = k.identf.t[:].unsqueeze(1).to_broadcast([128, 9, 128])
                    i1 = fcw.t[:, tile_idx, :].unsqueeze(2).to_broadcast([128, 9, 128])
                    P.tt("pool", dg.r(dg.t[:, db, gv], db * 2 + gv), k.identf.r(i0), fcw.r(i1), ALU.mult)
                for b in range(NBLK):
                    for gv in range(2):
                        bk = P.bank()
                        for kk in range(8):
                            P.mm(psb(k, bk), k.wr.r(ws[gv][:, kk, :], wss[gv]), hs(k, kk, b), start=(kk == 0), stop=(kk == 7))
                        upt = up.t[:, db, gv]
                        if b == 0:
                            dst = upt[:, 0:516].rearrange("p (s w) -> p s w", w=258)[:, :, 1:257]
                            src = k.ps.t[:, bk, :].rearrange("p (s w) -> p s w", w=256)
                        else:
                            r0 = 8 * (b - 1)
                            dst = upt[:, 516:].rearrange("p (r w) -> p r w", w=66)[:, 1 + r0:9 + r0, 1:65]
                            src = k.ps.t[:, bk, :].rearrange("p (r w) -> p r w", w=64)
                        P.copy("act" if gv == 0 else "dve", up.r(dst, db * 2 + gv), k.ps.r(src, bk))
                for b in range(NBLK):
                    bks = []
                    for gv in range(2):
                        bk = P.bank()
                        bks.append(bk)
                        upt = up.t[:, db, gv]
                        if b == 0:
                            taps = [(1, kw) for kw in range(3)]
                            outv = k.ps.t[:, bk, :].rearrange("p (s w) -> p s w", w=256)
                        else:
                            taps = [(kh, kw) for kh in range(3) for kw in range(3)]
                            outv = k.ps.t[:, bk, :].rearrange("p (r w) -> p r w", w=64)
                        for ti, (kh, kw) in enumerate(taps):
                            if b == 0:
                                rhs = upt[:, 0:516].rearrange("p (s w) -> p s w", w=258)[:, :, kw:kw + 256]
                            else:
                                r0 = 8 * (b - 1)
                                rhs = upt[:, 516:].rearrange("p (r w) -> p r w", w=66)[:, r0 + kh:r0 + kh + 8, kw:kw + 64]
                            P.mm(k.ps.r(outv, bk), dg.r(dg.t[:, db, gv, kh * 3 + kw, :], db * 2 + gv), up.r(rhs, db * 2 + gv),
                                 start=(ti == 0), stop=(ti == len(taps) - 1))
                    sb_ = nsg % 2
                    nsg += 1
                    P.act(sg.r(sg.t[:, sb_], sb_), psb(k, bks[0]), AF.Silu)
                    P.tt("dve", ab.r(ab.t[:, jj, b * BLK:(b + 1) * BLK], jj * 5 + b), psb(k, bks[1]), sg.r(sg.t[:, sb_], sb_), ALU.mult)
            wos = [load_out_w(k, k.dram("ffn_w_out%d" % L), 0, (g0 + jj) * 128) for jj in range(len(pairs))]
            for m in range(8):
                for b in range(NBLK):
                    bk = P.bank()
                    for jj in range(len(pairs)):
                        P.mm(psb(k, bk), k.wr.r(wos[jj][1][:, m * 128:(m + 1) * 128], wos[jj][0]), ab.r(ab.t[:, jj, b * BLK:(b + 1) * BLK], jj * 5 + b),
                             start=(jj == 0), stop=(jj == len(pairs) - 1))
                    c = bc(b)
                    P.stt(xs(k, m, b), psb(k, bk), k.mod.r(k.mod.t[:, q_g + m, c:c + 1]), xs(k, m, b), ALU.mult, ALU.add)
        P.barrier()
        P.release(mk)


def layer(k, L):
    cfg = k.cfg
    ada(k, L)
    modulate(k, 0)
    if cfg.get("mixers", True):
        mixer(k, L)
    ln_scope(k, L, 0)
    modulate(k, 1)
    if cfg.get("ffn", True):
        ffn(k, L)
    ln_scope(k, L, 1)


LW = 1280
SEGS = [(0, 256), (256, 256), (512, 2048)]
Q_G1 = 2 * 8


def mixer(k, L):
    kind = L % 3
    if kind == 2:
        lru(k, L // 3)
    elif kind == 1:
        ssd(k, L // 3)
    else:
        gdn(k, L // 3)


def outproj_acc(k, W, Lw, r0, ntile, o_regions):
    P = k.P
    slots = [load_out_w(k, W, Lw, r0 + i * 128) for i in range(ntile)]
    for m in range(8):
        for b in range(NBLK):
            bk = P.bank()
            for jj in range(ntile):
                s, wo = slots[jj]
                P.mm(psb(k, bk), k.wr.r(wo[:, m * 128:(m + 1) * 128], s), o_regions(jj, b), start=(jj == 0), stop=(jj == ntile - 1))
            c = bc(b)
            P.stt(xs(k, m, b), psb(k, bk), k.mod.r(k.mod.t[:, Q_G1 + m, c:c + 1]), xs(k, m, b), ALU.mult, ALU.add)


def conv1d_pad_layout():
    offs = []
    o = 0
    for (t0, n) in SEGS:
        offs.append(o)
        o += n + 3
    return offs, o


CPO, CPW = conv1d_pad_layout()


def conv_dst(xpad_t, b):
    if b == 0:
        return xpad_t[:, 0:518].rearrange("p (s w) -> p s w", w=259)[:, :, 1:257], "p (s w) -> p s w", 256
    o = CPO[2] + 1 + (b - 1) * BLK
    return xpad_t[:, o:o + BLK], None, None


def conv_rhs(xpad_t, b, kk):
    if b == 0:
        return xpad_t[:, 0:518].rearrange("p (s w) -> p s w", w=259)[:, :, kk:kk + 256]
    o = CPO[2] + (b - 1) * BLK + kk
    return xpad_t[:, o:o + BLK]


def psview(k, bk, b):
    if b == 0:
        return k.ps.t[:, bk, :].rearrange("p (s w) -> p s w", w=256)
    return k.ps.t[:, bk, :]


def lru(k, j):
    P = k.P
    G = 2
    P.barrier()
    mk = P.mark()
    xpad = P.buf("l_xpad", [128, CPW + 1], BF16)
    xrb = P.buf("l_xrb", [128, NT], BF16)
    gg = P.buf("l_gg", [128, NT], BF16)
    Ib = P.buf("l_i", [128, NT], BF16)
    A = P.buf("l_a", [128, NT], F32)
    T = P.buf("l_t", [128, NT], F32)
    H = [P.buf("l_h0", [128, NT], BF16), P.buf("l_h1", [128, NT], BF16)]
    ob = P.buf("l_o", [128, G, NT], BF16, nsub=G)
    dgl = P.buf("l_dg", [128, 4, 128], BF16)
    sm = P.buf("l_sm", [128, 10, 12], F32)
    s0 = P.buf("l_s0", [128, 10, 2], F32)
    sp = P.buf("l_sp", [128, 10, 2], F32)
    ep = P.buf("l_ep", [128, 10, 2], F32)
    one = P.buf("l_one", [128, 1], F32)
    sto = P.buf("l_sto", [128, 10, 2, 2], F32)
    P.dma("sp", sm.r(), k.dram("lru_sm"))
    P.dma("sp", s0.r(), k.dram("lru_s0"))
    P.memset("dve", one.r(), 1.0)
    P.memset("pool", xpad.r(), 0.0)
    P.act(ep.r(), sm.r(sm.t[:, :, 9:11]), AF.Exp, scale=-1.0)
    P.ts("dve", sp.r(), ep.r(), -0.2, ALU.mult, 0.25, ALU.add)
    for cst in (1.0 / 3, 0.5, 1.0):
        P.tt("dve", sp.r(), sp.r(), ep.r(), ALU.mult)
        P.ts("dve", sp.r(), sp.r(), -1.0, ALU.mult, cst, ALU.add)
    P.tt("dve", sp.r(), sp.r(), ep.r(), ALU.mult)
    P.ts("dve", sp.r(), sp.r(), -8.0, ALU.mult)

    for g0 in range(0, 10, G):
        tiles = list(range(g0, min(g0 + G, 10)))
        for jj, n in enumerate(tiles):
            s, wgb = load_in_w(k, k.dram("lru_w_in"), j, n * 128)
            sx, wxr = load_in_w(k, k.dram("lru_w_in"), j, LW + n * 128)
            s2 = wslot(k)
            gsrc = k.dram("lru_gate_w").ap[j][:, :, n].rearrange("d g kk m -> kk (d g) m")
            gw = k.wr.t[:, s2, 0:512].rearrange("p (a m) -> p a m", m=128)
            P.dma("pool", k.wr.r(gw, s2), R(gsrc, k.dram("lru_gate_w").keys))
            i0 = k.identf.t[:].unsqueeze(1).to_broadcast([128, 4, 128])
            i1 = sm.t[:, n, 0:4].unsqueeze(2).to_broadcast([128, 4, 128])
            P.tt("pool", dgl.r(), k.identf.r(i0), sm.r(i1), ALU.mult)
            for b in range(NBLK):
                bk = P.bank()
                for kk in range(8):
                    P.mm(psb(k, bk), k.wr.r(wgb[:, kk, :], s), hs(k, kk, b), start=(kk == 0), stop=(kk == 7))
                P.act(gg.r(gg.t[:, b * BLK:(b + 1) * BLK]), psb(k, bk), AF.Gelu_apprx_tanh)
            for b in range(NBLK):
                bk = P.bank()
                for kk in range(8):
                    P.mm(psb(k, bk), k.wr.r(wxr[:, kk, :], sx), hs(k, kk, b), start=(kk == 0), stop=(kk == 7))
                dst, _, _ = conv_dst(xpad.t, b)
                P.copy("dve", xpad.r(dst), k.ps.r(psview(k, bk, b), bk))
            for b in range(NBLK):
                bk = P.bank()
                for kk in range(4):
                    P.mm(k.ps.r(psview(k, bk, b), bk), dgl.r(dgl.t[:, kk, :]), xpad.r(conv_rhs(xpad.t, b, kk)), start=(kk == 0), stop=(kk == 3))
                P.act(xrb.r(xrb.t[:, b * BLK:(b + 1) * BLK]), psb(k, bk), AF.Identity, bias=sm.r(sm.t[:, n, 4:5]))
            for d in range(2):
                for b in range(NBLK):
                    for g in range(2):
                        bk = P.bank()
                        P.mm(psb(k, bk), k.wr.r(gw[:, d * 2 + g, :], s2), xrb.r(xrb.t[:, b * BLK:(b + 1) * BLK]))
                        dstb = A if g == 0 else Ib
                        P.act(dstb.r(dstb.t[:, b * BLK:(b + 1) * BLK]), psb(k, bk), AF.Sigmoid, bias=sm.r(sm.t[:, n, 5 + d * 2 + g:6 + d * 2 + g]))
                P.act(A.r(), A.r(), AF.Exp, scale=sp.r(sp.t[:, n, d:d + 1]))
                P.tt("dve", T.r(), A.r(), A.r(), ALU.mult)
                P.act(T.r(), T.r(), AF.Sqrt, bias=one.r(), scale=-1.0)
                P.tt("dve", T.r(), T.r(), Ib.r(), ALU.mult)
                P.tt("dve", T.r(), T.r(), xrb.r(), ALU.mult)
                for si, (t0, n_t) in enumerate(SEGS):
                    if d == 0:
                        o_, a_, b_ = H[0].t[:, t0:t0 + n_t], A.t[:, t0:t0 + n_t], T.t[:, t0:t0 + n_t]
                    else:
                        lo = t0 - 1 if t0 > 0 else None
                        o_, a_, b_ = H[1].t[:, t0 + n_t - 1:lo:-1], A.t[:, t0 + n_t - 1:lo:-1], T.t[:, t0 + n_t - 1:lo:-1]
                    if si == 2:
                        init = s0.t[:, n, d:d + 1]
                        rd = [A.r(), T.r(), s0.r()]
                    else:
                        init = 0.0
                        rd = [A.r(), T.r()]
                    P.op("dve", lambda e, o_=o_, a_=a_, b_=b_, init=init: e.tensor_tensor_scan(out=o_, data0=a_, data1=b_, initial=init, op0=ALU.mult, op1=ALU.add),
                         reads=rd, writes=[H[d].r()])
                    if si < 2:
                        tl = t0 + n_t - 1 if d == 0 else t0
                        P.copy("act", sto.r(sto.t[:, n, si, d:d + 1]), H[d].r(H[d].t[:, tl:tl + 1]))
            P.tt("dve", H[0].r(), H[0].r(), H[1].r(), ALU.add)
            P.tt("dve", ob.r(ob.t[:, jj, :], jj), H[0].r(), gg.r(), ALU.mult)
        outproj_acc(k, k.dram("lru_w_out"), j, g0 * 128, len(tiles), lambda jj, b: ob.r(ob.t[:, jj, b * BLK:(b + 1) * BLK], jj))
    P.dma("sp", k.dram("lru_out"), sto.r())
    P.barrier()
    P.release(mk)


NCH = 20
SEG_CH = [(0, 2), (2, 2), (4, 16)]


def seg_of_chunk(c):
    return 0 if c < 2 else (1 if c < 4 else 2)


def hch(k, kk, c):
    b = c // 4
    return k.h.r(k.h.t[:, kk, c * 128:(c + 1) * 128], kk * 5 + b)


def ssd(k, j):
    P = k.P
    P.barrier()
    mk = P.mark()
    y_tm = P.buf("s_ytm", [128, NCH, 512], BF16, nsub=NCH)
    xs_tm = P.buf("s_xstm", [128, NCH, 256], BF16, nsub=NCH)
    Bfm = P.buf("s_bfm", [128, NT], BF16)
    Cfm = P.buf("s_cfm", [128, NT], BF16)
    xpad = P.buf("s_xpad", [128, CPW + 1], BF16)
    wz = P.buf("s_wz", [128, 8, 256], BF16)
    wdt = P.buf("s_wdt", [128, 8, 64], BF16)
    dgl = P.buf("s_dg", [128, 4, 128], BF16)
    cw = P.buf("s_cw", [128, 24, 5], F32)
    rows = P.buf("s_rows", [128, 160], F32)
    nrm = P.buf("s_nrm", [128, 512], BF16)
    masks = P.buf("s_masks", [128, 2, 128], F32)
    negb = P.buf("s_negb", [128, 2, 128], BF16)
    onesf = P.buf("s_onesf", [128, 128], F32)
    one1 = P.buf("s_one1", [128, 1], F32)
    eps1 = P.buf("s_eps1", [128, 1], F32)
    sc = {nm: P.buf("s_" + nm, [128, NCH, 8], F32) for nm in ["dt", "da", "nacum", "eac", "cdec", "ce"]}
    sc["tmp"] = sc["ce"]
    fm_tmp = Cfm
    ssq = P.buf("s_ssq", [128, NCH, 2], F32)
    rstd = P.buf("s_rstd", [128, NCH], F32)
    cbT = P.buf("s_cbT", [128, 2, 2, 128], BF16, nsub=2)
    Btm = P.buf("s_btm", [128, 2, 128], BF16, nsub=2)
    dec = P.buf("s_dec", [128, 2, 4, 128], BF16, nsub=2)
    Mb = P.buf("s_M", [128, 2, 4, 128], BF16, nsub=2)
    xd = P.buf("s_xd", [128, 2, 256], BF16, nsub=2)
    xdt = P.buf("s_xdt", [128, 2, 256], BF16, nsub=2)
    ST = P.buf("s_ST", [128, 2, 256], F32, nsub=2)
    STb = P.buf("s_STb", [128, 2, 256], BF16, nsub=2)
    t1 = P.buf("s_t1", [128, 1, 256], F32, nsub=1)
    t2 = P.buf("s_t2", [128, 1, 256], F32, nsub=1)
    sz = t2
    junk = P.buf("s_junk", [128, 256], BF16)
    sto = P.buf("s_sto", [128, 2, 128], F32, nsub=2)

    P.dma("sp", cw.r(), k.dram("ssd_cw"))
    P.dma("sp", rows.r(), k.dram("ssd_rows"))
    P.dma("sp", masks.r(), R(k.dram("cmask").ap[0:2].rearrange("a p m -> p a m"), k.dram("cmask").keys))
    P.dma("pool", negb.r(), R(k.dram("cmask").ap[2:4].rearrange("a p m -> p a m"), k.dram("cmask").keys))
    P.memset("dve", onesf.r(), 1.0)
    P.memset("dve", one1.r(), 1.0)
    P.memset("dve", eps1.r(), 1e-5)
    P.memset("pool", xpad.r(), 0.0)
    P.act(rows.r(rows.t[:, 64:128]), rows.r(rows.t[:, 64:128]), AF.Exp)
    P.ts("dve", rows.r(rows.t[:, 64:128]), rows.r(rows.t[:, 64:128]), -1.0, ALU.mult)
    src = k.dram("ssd_w_in").ap[j][:, 5120:5184].rearrange("(kk p) c -> p kk c", p=128)
    P.dma("pool", wdt.r(), R(src, k.dram("ssd_w_in").keys))

    def conv_tile(col0, cidx, dst_writer):
        s, w = load_in_w(k, k.dram("ssd_w_in"), j, col0)
        i0 = k.identf.t[:].unsqueeze(1).to_broadcast([128, 4, 128])
        i1 = cw.t[:, cidx, 0:4].unsqueeze(2).to_broadcast([128, 4, 128])
        P.tt("pool", dgl.r(), k.identf.r(i0), cw.r(i1), ALU.mult)
        for b in range(NBLK):
            bk = P.bank()
            for kk in range(8):
                P.mm(psb(k, bk), k.wr.r(w[:, kk, :], s), hs(k, kk, b), start=(kk == 0), stop=(kk == 7))
            dst, _, _ = conv_dst(xpad.t, b)
            P.copy("dve", xpad.r(dst), k.ps.r(psview(k, bk, b), bk))
        for b in range(NBLK):
            bk = P.bank()
            for kk in range(4):
                P.mm(k.ps.r(psview(k, bk, b), bk), dgl.r(dgl.t[:, kk, :]), xpad.r(conv_rhs(xpad.t, b, kk)), start=(kk == 0), stop=(kk == 3))
            P.act(dst_writer(b), psb(k, bk), AF.Silu, bias=cw.r(cw.t[:, cidx, 4:5]))

    stop = k.cfg.get("ssd_stop", 99)
    for hg in range(k.cfg.get("ssd_nhg", 8)):
        g, half = hg // 2, hg % 2
        src = k.dram("ssd_w_in").ap[j][:, hg * 256:(hg + 1) * 256].rearrange("(kk p) c -> p kk c", p=128)
        P.dma("pool", wz.r(), R(src, k.dram("ssd_w_in").keys))
        if half == 0:
            P.dma("pool", nrm.r(), R(k.dram("ssd_nrm").ap[:, g * 512:(g + 1) * 512], k.dram("ssd_nrm").keys))
        if stop <= 0.5:
            continue
        bk = P.bank()
        for c in range(NCH):
            for kk in range(8):
                rhs = wdt.t[:, kk, :].rearrange("p (d h) -> p d h", d=2)[:, :, hg * 4:hg * 4 + 4]
                out = k.ps.t[:, bk, c * 8:(c + 1) * 8].rearrange("p (d h) -> p d h", d=2)
                P.mm(k.ps.r(out, bk), hch(k, kk, c), wdt.r(rhs), start=(kk == 0), stop=(kk == 7))
        psv = k.ps.t[:, bk, 0:NCH * 8].rearrange("p (c d h) -> p c d h", d=2, h=4)
        if stop <= 0.7:
            continue

        def rowbc(off):
            return rows.t[:, off:off + 64].rearrange("p (d h) -> p d h", d=2)[:, :, hg * 4:hg * 4 + 4].unsqueeze(1).to_broadcast([128, NCH, 2, 4])

        def v4(bf):
            return bf.t.rearrange("p c (d h) -> p c d h", d=2)
        tmp, dt, da = sc["tmp"], sc["dt"], sc["da"]
        P.tt("dve", tmp.r(v4(tmp)), k.ps.r(psv, bk), rows.r(rowbc(0)), ALU.add)
        P.ts("dve", dt.r(), tmp.r(), -1.0, ALU.mult)
        P.tt("dve", dt.r(), dt.r(), tmp.r(), ALU.max)
        P.act(dt.r(), dt.r(), AF.Exp, scale=-1.0)
        P.act(dt.r(), dt.r(), AF.Ln, bias=one1.r())
        P.ts("dve", tmp.r(), tmp.r(), 0.0, ALU.max)
        P.tt("dve", dt.r(), dt.r(), tmp.r(), ALU.add)
        P.tt("dve", da.r(v4(da)), dt.r(v4(dt)), rows.r(rowbc(64)), ALU.mult)
        if stop <= 0.8:
            continue
        bk2 = P.bank()
        bk3 = P.bank()
        for c in range(NCH):
            for d in range(2):
                P.mm(k.ps.r(k.ps.t[:, bk2, c * 8 + d * 4:c * 8 + d * 4 + 4], bk2), masks.r(masks.t[:, d, :]), da.r(da.t[:, c, d * 4:d * 4 + 4]))
            P.mm(k.ps.r(k.ps.t[:, bk3, c * 8:(c + 1) * 8], bk3), onesf.r(), da.r(da.t[:, c, :]))
        nacum, eac, cdec, ce = sc["nacum"], sc["eac"], sc["cdec"], sc["ce"]
        ps2 = k.ps.t[:, bk2, 0:NCH * 8].rearrange("p (c e) -> p c e", e=8)
        ps3 = k.ps.t[:, bk3, 0:NCH * 8].rearrange("p (c e) -> p c e", e=8)
        if stop <= 0.9:
            continue
        P.ts("dve", nacum.r(), k.ps.r(ps2, bk2), -1.0, ALU.mult)
        P.act(eac.r(), nacum.r(), AF.Exp, scale=-1.0)
        if stop <= 0.95:
            continue
        P.copy("dve", cdec.r(), k.ps.r(ps3, bk3))
        P.tt("dve", ce.r(), cdec.r(), nacum.r(), ALU.add)
        if stop <= 0.96:
            continue
        P.act(cdec.r(), cdec.r(), AF.Exp)
        if stop <= 0.97:
            continue
        P.act(ce.r(), ce.r(), AF.Exp)
        P.tt("dve", ce.r(), ce.r(), dt.r(), ALU.mult)
        if stop <= 1:
            continue
        for ti in range(2):
            conv_tile(2048 + hg * 256 + ti * 128, hg * 2 + ti, lambda b: fm_tmp.r(fm_tmp.t[:, b * BLK:(b + 1) * BLK]))
            for c0 in range(0, NCH, 8):
                bk = P.bank()
                pb = k.ps.t[:, bk, :].bitcast(BF16)
                n = min(8, NCH - c0)
                for ci in range(n):
                    c = c0 + ci
                    P.tr(k.ps.r(pb[:, ci * 128:(ci + 1) * 128], bk), fm_tmp.r(fm_tmp.t[:, c * 128:(c + 1) * 128]), k.identb.r())
                dst = xs_tm.t[:, c0:c0 + n, ti * 128:(ti + 1) * 128]
                srcv = pb[:, 0:n * 128].rearrange("p (c m) -> p c m", m=128)
                P.copy("act", xs_tm.r(dst, range(c0, c0 + n)), k.ps.r(srcv, bk))
        conv_tile(2048 + 2048 + g * 128, 16 + g, lambda b: Bfm.r(Bfm.t[:, b * BLK:(b + 1) * BLK]))
        conv_tile(2048 + 2560 + g * 128, 20 + g, lambda b: Cfm.r(Cfm.t[:, b * BLK:(b + 1) * BLK]))
        if stop <= 2:
            continue
        for step in range(k.cfg.get("ssd_steps", NCH)):
            for d in range(2):
                c = step if d == 0 else NCH - 1 - step
                first_visit = (c <= 9) if d == 0 else (c >= 10)
                seg = seg_of_chunk(c)
                c_first, c_n = SEG_CH[seg]
                seg_start = (c == c_first) if d == 0 else (c == c_first + c_n - 1)
                seg_end = (c == c_first + c_n - 1) if d == 0 else (c == c_first)
                pb = step % 2
                tok = slice(c * 128, (c + 1) * 128)
                if seg_start:
                    if seg == 2:
                        P.dma("sp", ST.r(ST.t[:, d], d), R(k.dram("ssd_s0").ap[d, hg], k.dram("ssd_s0").keys))
                    else:
                        P.memset("dve", ST.r(ST.t[:, d], d), 0.0)
                    P.copy("act", STb.r(STb.t[:, d], d), ST.r(ST.t[:, d], d))
                if d == 0:
                    pass
                sl = d
                bk = P.bank()
                P.mm(k.ps.r(k.ps.t[:, bk, 0:128], bk), Bfm.r(Bfm.t[:, tok]), Cfm.r(Cfm.t[:, tok]))
                P.tt("dve", cbT.r(cbT.t[:, sl, d], sl), k.ps.r(k.ps.t[:, bk, 0:128], bk), masks.r(masks.t[:, d, :]), ALU.mult)
                bkt = P.bank()
                pbb = k.ps.t[:, bkt, :].bitcast(BF16)
                P.tr(k.ps.r(pbb[:, 0:128], bkt), Bfm.r(Bfm.t[:, tok]), k.identb.r())
                P.copy("act", Btm.r(Btm.t[:, sl], sl), k.ps.r(pbb[:, 0:128], bkt))
                xsv = xs_tm.t[:, c, :].rearrange("p (q e) -> p q e", e=64)
                dtb = dt.t[:, c, d * 4:d * 4 + 4].unsqueeze(2).to_broadcast([128, 4, 64])
                ceb = ce.t[:, c, d * 4:d * 4 + 4].unsqueeze(2).to_broadcast([128, 4, 64])
                P.tt("dve", xd.r(xd.t[:, sl].rearrange("p (q e) -> p q e", e=64), sl), xs_tm.r(xsv, c), dt.r(dtb), ALU.mult)
                P.tt("dve", xdt.r(xdt.t[:, sl].rearrange("p (q e) -> p q e", e=64), sl), xs_tm.r(xsv, c), ce.r(ceb), ALU.mult)
                bk = P.bank()
                for q in range(4):
                    col = d * 4 + q
                    lhs = da.t[:, c, col:col + 1].to_broadcast([128, 128])
                    o_ = k.ps.r(k.ps.t[:, bk, q * 128:(q + 1) * 128], bk)
                    P.mm(o_, da.r(lhs), masks.r(masks.t[:, d, :]), start=True, stop=False)
                    P.mm(o_, k.identb.r(), negb.r(negb.t[:, d, :]), start=False, stop=True)
                    P.act(dec.r(dec.t[:, sl, q, :], sl), o_, AF.Exp, bias=nacum.r(nacum.t[:, c, col:col + 1]))
                cb_b = cbT.t[:, sl, d].unsqueeze(1).to_broadcast([128, 4, 128])
                P.tt("dve", Mb.r(Mb.t[:, sl], sl), dec.r(dec.t[:, sl], sl), cbT.r(cb_b, sl), ALU.mult)
                bkY = P.bank()
                for q in range(4):
                    P.mm(k.ps.r(k.ps.t[:, bkY, q * 64:(q + 1) * 64], bkY), Mb.r(Mb.t[:, sl, q, :], sl), xd.r(xd.t[:, sl, q * 64:(q + 1) * 64], sl))
                P.mm(k.ps.r(k.ps.t[:, bkY, 256:512], bkY), Cfm.r(Cfm.t[:, tok]), STb.r(STb.t[:, d], d))
                eab = eac.t[:, c, d * 4:d * 4 + 4].unsqueeze(2).to_broadcast([128, 4, 64])
                t1v = t1.t[:, 0].rearrange("p (q e) -> p q e", e=64)
                P.tt("dve", t1.r(t1v, 0), k.ps.r(k.ps.t[:, bkY, 256:512].rearrange("p (q e) -> p q e", e=64), bkY), eac.r(eab), ALU.mult)
                P.tt("dve", t1.r(t1.t[:, 0], 0), t1.r(t1.t[:, 0], 0), k.ps.r(k.ps.t[:, bkY, 0:256], bkY), ALU.add)
                ysl = y_tm.r(y_tm.t[:, c, half * 256:(half + 1) * 256], c)
                bkS = P.bank()
                P.mm(k.ps.r(k.ps.t[:, bkS, 0:256], bkS), Btm.r(Btm.t[:, sl], sl), xdt.r(xdt.t[:, sl], sl))
                cdb = cdec.t[:, c, d * 4:d * 4 + 4].unsqueeze(2).to_broadcast([128, 4, 64])
                STv = ST.t[:, d].rearrange("p (q e) -> p q e", e=64)
                P.tt("dve", ST.r(STv, d), ST.r(STv, d), cdec.r(cdb), ALU.mult)
                P.tt("dve", ST.r(ST.t[:, d], d), ST.r(ST.t[:, d], d), k.ps.r(k.ps.t[:, bkS, 0:256], bkS), ALU.add)
                P.copy("act", STb.r(STb.t[:, d], d), ST.r(ST.t[:, d], d))
                if first_visit:
                    P.copy("act", ysl, t1.r(t1.t[:, 0], 0))
                else:
                    P.tt("dve", t1.r(t1.t[:, 0], 0), t1.r(t1.t[:, 0], 0), ysl, ALU.add)
                    dsb = rows.t[:, 128 + hg * 4:128 + hg * 4 + 4].unsqueeze(2).to_broadcast([128, 4, 64])
                    P.tt("pool", t2.r(t2.t[:, 0].rearrange("p (q e) -> p q e", e=64), 0), xs_tm.r(xsv, c), rows.r(dsb), ALU.mult)
                    P.tt("dve", t1.r(t1.t[:, 0], 0), t1.r(t1.t[:, 0], 0), t2.r(t2.t[:, 0], 0), ALU.add)
                    bkZ = P.bank()
                    for kk in range(8):
                        P.mm(k.ps.r(k.ps.t[:, bkZ, 0:256], bkZ), hch(k, kk, c), wz.r(wz.t[:, kk, :]), start=(kk == 0), stop=(kk == 7))
                    P.act(t2.r(t2.t[:, 0], 0), k.ps.r(k.ps.t[:, bkZ, 0:256], bkZ), AF.Silu)
                    P.tt("dve", t1.r(t1.t[:, 0], 0), t1.r(t1.t[:, 0], 0), t2.r(t2.t[:, 0], 0), ALU.mult)
                    P.copy("dve", ysl, t1.r(t1.t[:, 0], 0))
                    o_, i_, a_ = junk.t, t1.t[:, 0], ssq.t[:, c, half:half + 1]
                    P.op("act", lambda e, o_=o_, i_=i_, a_=a_: e.activation(out=o_, in_=i_, func=AF.Square, accum_out=a_),
                         reads=[t1.r(t1.t[:, 0], 0)], writes=[junk.r(), ssq.r()])
                if seg_end and seg < 2:
                    for pr in range(2):
                        bk = P.bank()
                        P.tr(k.ps.r(k.ps.t[:, bk, 0:128], bk), ST.r(ST.t[:, d, pr * 128:(pr + 1) * 128], d), k.identf.r())
                        P.copy("dve", sto.r(sto.t[:, pr], pr), k.ps.r(k.ps.t[:, bk, 0:128], bk))
                        r0 = (hg * 4 + pr * 2) * 64
                        P.dma("sp", R(k.dram("ssd_out").ap[seg, d, r0:r0 + 128, :], k.dram("ssd_out").keys), sto.r(sto.t[:, pr], pr))
        if half == 1 and stop > 3:
            P.tt("dve", rstd.r(), ssq.r(ssq.t[:, :, 0]), ssq.r(ssq.t[:, :, 1]), ALU.add)
            P.act(rstd.r(), rstd.r(), AF.Sqrt, bias=eps1.r(), scale=1.0 / 512)
            o_, i_ = rstd.t, rstd.t
            P.op("dve", lambda e, o_=o_, i_=i_: e.reciprocal(out=o_, in_=i_), reads=[rstd.r()], writes=[rstd.r()])
            slots = [load_out_w(k, k.dram("ssd_w_out"), j, g * 512 + ti * 128) for ti in range(4)]
            ofm_t = xs_tm.t[:, 0:8, :].rearrange("p c e -> p (c e)").rearrange("p (t m) -> p t m", m=512)
            yn_t = xs_tm.t[:, 8:12, :].rearrange("p c e -> p (c e)").rearrange("p (t m) -> p t m", m=512)
            OK_ = list(range(0, 8))
            YK_ = list(range(8, 12))
            for b in range(NBLK):
                for ci in range(4):
                    c = b * 4 + ci
                    yb = ci % 2
                    P.stt(xs_tm.r(yn_t[:, yb], YK_), y_tm.r(y_tm.t[:, c, :], c), rstd.r(rstd.t[:, c:c + 1]), nrm.r(), ALU.mult, ALU.mult)
                    bk = P.bank()
                    pbb = k.ps.t[:, bk, :].bitcast(BF16)
                    for ti in range(4):
                        P.tr(k.ps.r(pbb[:, ti * 128:(ti + 1) * 128], bk), xs_tm.r(yn_t[:, yb, ti * 128:(ti + 1) * 128], YK_), k.identb.r())
                    srcv = pbb[:, 0:512].rearrange("p (t m) -> p t m", m=128)
                    P.copy("act", xs_tm.r(ofm_t[:, :, ci * 128:(ci + 1) * 128], OK_), k.ps.r(srcv, bk))
                for m in range(8):
                    bk = P.bank()
                    for ti in range(4):
                        s_, wo = slots[ti]
                        P.mm(psb(k, bk), k.wr.r(wo[:, m * 128:(m + 1) * 128], s_), xs_tm.r(ofm_t[:, ti, :], OK_), start=(ti == 0), stop=(ti == 3))
                    cc = bc(b)
                    P.stt(xs(k, m, b), psb(k, bk), k.mod.r(k.mod.t[:, Q_G1 + m, cc:cc + 1]), xs(k, m, b), ALU.mult, ALU.add)
    P.barrier()
    P.release(mk)


def gdn(k, j):
    P = k.P
    P.barrier()
    mk = P.mark()
    W_in = k.dram("gdn_w_in%d" % j)
    W_out = k.dram("gdn_w_out%d" % j)
    kq = P.buf("g_kq", [128, NCH, 2, 128], BF16, nsub=NCH)
    v_fm = P.buf("g_vfm", [128, NT], BF16, nsub=NCH)
    v_tm = P.buf("g_vtm", [128, NCH, 128], BF16, nsub=NCH)
    k_tm = P.buf("g_ktm", [128, NCH, 128], BF16, nsub=NCH)
    o_tm = P.buf("g_otm", [128, NCH, 128], BF16, nsub=NCH)
    wz = P.buf("g_wz", [128, 8, 128], BF16)
    wsm = P.buf("g_wsm", [128, 8, 32], BF16)
    cw = P.buf("g_cw", [128, 24, 4], F32)
    rows = P.buf("g_rows", [128, 160], F32)
    masks = P.buf("g_masks", [128, 2, 128], F32)
    negb = P.buf("g_negb", [128, 2, 128], BF16)
    lvl = P.buf("g_lvl", [128, 2, 7, 2, 128], BF16)
    I2 = P.buf("g_I2", [128, 2, 128], BF16)
    onesf = P.buf("g_onesf", [128, 128], F32)
    c_one = P.buf("g_c1", [128, 1], F32)
    c_eps6 = P.buf("g_c2", [128, 1], F32)
    c_eps6q = P.buf("g_c3", [128, 1], F32)
    c_eps5 = P.buf("g_c4", [128, 1], F32)
    sc = {nm: P.buf("g_" + nm, [128, NCH, 2], F32) for nm in ["beta", "g", "ngc", "negegc", "eout", "egl", "tmp"]}
    Rp = P.buf("g_Rp", [128, 2, 128], BF16, nsub=2)
    vn = P.buf("g_vn", [128, 2, 128], BF16, nsub=2)
    S = P.buf("g_S", [128, 2, 128], F32, nsub=2)
    Sb = P.buf("g_Sb", [128, 2, 128], BF16, nsub=2)
    ot = P.buf("g_ot", [128, 128], F32)
    sz = P.buf("g_sz", [128, 128], F32)
    ogb = P.buf("g_og", [128, 128], BF16)
    junk = P.buf("g_junk", [128, 128], BF16)
    ssq = P.buf("g_ssq", [128, 2], F32)

    P.dma("sp", cw.r(), R(k.dram("gdn_cw").ap[j], k.dram("gdn_cw").keys))
    P.dma("sp", rows.r(), R(k.dram("gdn_rows").ap[j], k.dram("gdn_rows").keys))
    P.dma("sp", masks.r(), R(k.dram("cmask").ap[0:2].rearrange("a p m -> p a m"), k.dram("cmask").keys))
    P.dma("pool", negb.r(), R(k.dram("cmask").ap[2:4].rearrange("a p m -> p a m"), k.dram("cmask").keys))
    for d in range(2):
        P.dma("pool", lvl.r(lvl.t[:, d]), R(k.dram("glvl").ap[d].rearrange("l a p m -> p l a m"), k.dram("glvl").keys))
    for a in range(2):
        P.copy("dve", I2.r(I2.t[:, a, :]), k.identf.r())
    P.memset("dve", onesf.r(), 1.0)
    P.memset("dve", c_one.r(), 1.0)
    P.memset("dve", c_eps6.r(), 1e-6)
    P.memset("dve", c_eps6q.r(), 128e-6)
    P.memset("dve", c_eps5.r(), 1e-5)
    P.act(rows.r(rows.t[:, 16:32]), rows.r(rows.t[:, 16:32]), AF.Exp)
    P.ts("dve", rows.r(rows.t[:, 16:32]), rows.r(rows.t[:, 16:32]), -1.0, ALU.mult)
    src = W_in.ap[0][:, 4096:4128].rearrange("(kk p) c -> p kk c", p=128)
    P.dma("pool", wsm.r(), R(src, W_in.keys))

    loc = {}

    def conv_tile(col0, cidx, post):
        xpad, dgl = loc["xpad"], loc["dgl"]
        s, w = load_in_w(k, W_in, 0, col0)
        i0 = k.identf.t[:].unsqueeze(1).to_broadcast([128, 4, 128])
        i1 = cw.t[:, cidx, 0:4].unsqueeze(2).to_broadcast([128, 4, 128])
        P.tt("pool", dgl.r(), k.identf.r(i0), cw.r(i1), ALU.mult)
        for b in range(NBLK):
            bk = P.bank()
            for kk in range(8):
                P.mm(psb(k, bk), k.wr.r(w[:, kk, :], s), hs(k, kk, b), start=(kk == 0), stop=(kk == 7))
            dst, _, _ = conv_dst(xpad.t, b)
            P.copy("dve", xpad.r(dst), k.ps.r(psview(k, bk, b), bk))
        for b in range(NBLK):
            bk = P.bank()
            for kk in range(4):
                P.mm(k.ps.r(psview(k, bk, b), bk), dgl.r(dgl.t[:, kk, :]), xpad.r(conv_rhs(xpad.t, b, kk)), start=(kk == 0), stop=(kk == 3))
            post(b, bk)

    def norm_post(which, scale, epsb):
        def post(b, bk):
            qf, sqb, rn = loc["qf"], loc["sqb"], loc["rn"]
            P.act(qf.r(), psb(k, bk), AF.Silu)
            P.act(sqb.r(), qf.r(), AF.Square)
            b2 = P.bank()
            P.mm(psb(k, b2), k.onesb.r(), sqb.r())
            P.act(rn.r(), psb(k, b2), AF.Sqrt, bias=epsb.r(), scale=scale)
            o_, i_ = rn.t, rn.t
            P.op("dve", lambda e, o_=o_, i_=i_: e.reciprocal(out=o_, in_=i_), reads=[rn.r()], writes=[rn.r()])
            dst = kq.t[:, 4 * b:4 * b + 4, which, :]
            P.tt("dve", kq.r(dst, range(4 * b, 4 * b + 4)), qf.r(qf.t.rearrange("p (c m) -> p c m", m=128)), rn.r(rn.t.rearrange("p (c m) -> p c m", m=128)), ALU.mult)
        return post

    def v_post(b, bk):
        P.act(v_fm.r(v_fm.t[:, b * BLK:(b + 1) * BLK], range(4 * b, 4 * b + 4)), psb(k, bk), AF.Silu)

    nheads = k.cfg.get("gdn_heads", 8)
    for hh in range(nheads):
        src = W_in.ap[0][:, 3072 + hh * 128:3072 + (hh + 1) * 128].rearrange("(kk p) c -> p kk c", p=128)
        P.dma("pool", wz.r(), R(src, W_in.keys))
        bk = P.bank()
        for c in range(NCH):
            for kk in range(8):
                rhs = wsm.t[:, kk, :].rearrange("p (t d h) -> p t d h", t=2, d=2)[:, :, :, hh]
                out = k.ps.t[:, bk, c * 4:(c + 1) * 4].rearrange("p (t d) -> p t d", t=2)
                P.mm(k.ps.r(out, bk), hch(k, kk, c), wsm.r(rhs), start=(kk == 0), stop=(kk == 7))
        psv = k.ps.t[:, bk, 0:NCH * 4].rearrange("p (c t d) -> p c t d", t=2, d=2)
        beta, g, ngc, negegc, eout, egl, tmp = sc["beta"], sc["g"], sc["ngc"], sc["negegc"], sc["eout"], sc["egl"], sc["tmp"]
        P.copy("dve", tmp.r(), k.ps.r(psv[:, :, 0, :], bk))
        P.act(beta.r(), tmp.r(), AF.Sigmoid)

        def rowbc(off):
            return rows.t[:, off:off + 16].rearrange("p (d h) -> p d h", d=2)[:, :, hh].unsqueeze(1).to_broadcast([128, NCH, 2])
        P.tt("dve", tmp.r(), k.ps.r(psv[:, :, 1, :], bk), rows.r(rowbc(0)), ALU.add)
        P.ts("dve", g.r(), tmp.r(), -1.0, ALU.mult)
        P.tt("dve", g.r(), g.r(), tmp.r(), ALU.max)
        P.act(g.r(), g.r(), AF.Exp, scale=-1.0)
        P.act(g.r(), g.r(), AF.Ln, bias=c_one.r())
        P.ts("dve", tmp.r(), tmp.r(), 0.0, ALU.max)
        P.tt("dve", g.r(), g.r(), tmp.r(), ALU.add)
        P.tt("dve", g.r(), g.r(), rows.r(rowbc(16)), ALU.mult)
        bk2 = P.bank()
        bk3 = P.bank()
        for c in range(NCH):
            for d in range(2):
                P.mm(k.ps.r(k.ps.t[:, bk2, c * 2 + d:c * 2 + d + 1], bk2), masks.r(masks.t[:, d, :]), g.r(g.t[:, c, d:d + 1]))
            P.mm(k.ps.r(k.ps.t[:, bk3, c * 2:c * 2 + 2], bk3), onesf.r(), g.r(g.t[:, c, :]))
        ps2 = k.ps.t[:, bk2, 0:NCH * 2].rearrange("p (c d) -> p c d", d=2)
        ps3 = k.ps.t[:, bk3, 0:NCH * 2].rearrange("p (c d) -> p c d", d=2)
        P.ts("dve", ngc.r(), k.ps.r(ps2, bk2), -1.0, ALU.mult)
        P.act(negegc.r(), ngc.r(), AF.Exp, scale=-1.0)
        P.ts("dve", negegc.r(), negegc.r(), -1.0, ALU.mult)
        P.copy("dve", egl.r(), k.ps.r(ps3, bk3))
        P.tt("dve", eout.r(), egl.r(), ngc.r(), ALU.add)
        P.act(eout.r(), eout.r(), AF.Exp)
        P.act(egl.r(), egl.r(), AF.Exp)
        P.barrier()
        mkp = P.mark()
        xpad = P.buf("g_xpad", [128, CPW + 1], BF16)
        dgl = P.buf("g_dg", [128, 4, 128], BF16)
        qf = P.buf("g_qf", [128, BLK], F32)
        sqb = P.buf("g_sq", [128, BLK], BF16)
        rn = P.buf("g_rn", [128, BLK], F32)
        loc.update(xpad=xpad, dgl=dgl, qf=qf, sqb=sqb, rn=rn)
        P.memset("pool", xpad.r(), 0.0)
        conv_tile(hh * 128, hh, norm_post(1, 128.0, c_eps6q))
        conv_tile(1024 + hh * 128, 8 + hh, norm_post(0, 1.0, c_eps6))
        conv_tile(2048 + hh * 128, 16 + hh, v_post)
        for (srcfn, dstb) in ((lambda c: v_fm.r(v_fm.t[:, c * 128:(c + 1) * 128], c), v_tm), (lambda c: kq.r(kq.t[:, c, 0, :], c), k_tm)):
            for c0 in range(0, NCH, 8):
                bk = P.bank()
                pb = k.ps.t[:, bk, :].bitcast(BF16)
                n = min(8, NCH - c0)
                for ci in range(n):
                    P.tr(k.ps.r(pb[:, ci * 128:(ci + 1) * 128], bk), srcfn(c0 + ci), k.identb.r())
                srcv = pb[:, 0:n * 128].rearrange("p (c m) -> p c m", m=128)
                P.copy("act", dstb.r(dstb.t[:, c0:c0 + n, :], range(c0, c0 + n)), k.ps.r(srcv, bk))
        P.barrier()
        P.release(mkp)
        mku = P.mark()
        GS = 3
        G = 2 * GS
        NR = 2 * G
        U = {}
        for nm in ["egcr", "dec", "NB", "NA"]:
            U[nm] = P.buf("gu_" + nm, [128, G, 128], BF16, nsub=G)
        U["Y"] = P.buf("gu_Y", [128, G, 2, 128], BF16, nsub=G)
        for nm in ["attn", "qin", "kout"]:
            U[nm] = P.buf("gu_" + nm, [128, NR, 128], BF16, nsub=NR)
        U["T"] = P.buf("gu_T", [128, NR, 2, 128], BF16, nsub=NR)
        RES = ("attn", "qin", "kout", "T")
        nsteps = k.cfg.get("gdn_steps", NCH)
        groups = [list(range(g0, min(g0 + GS, nsteps))) for g0 in range(0, nsteps, GS)]

        def ur(nm, ui, gi, sub=None):
            b_ = U[nm]
            si = (gi % 2) * G + ui if nm in RES else ui
            return b_.r(b_.t[:, si] if sub is None else b_.t[:, si, sub], si)

        def prelim_gen(gi):
            units = [(step, d) for step in groups[gi] for d in range(2)]
            cs = [(st_ if d_ == 0 else NCH - 1 - st_) for (st_, d_) in units]
            banks = {}
            for ui, (st_, d) in enumerate(units):
                c = cs[ui]
                bkR = P.bank()
                banks[("R", ui)] = bkR
                gbc = g.t[:, c, d:d + 1].to_broadcast([128, 128])
                r0 = k.ps.r(k.ps.t[:, bkR, 0:128], bkR)
                r1 = k.ps.r(k.ps.t[:, bkR, 128:256], bkR)
                P.mm(r0, g.r(gbc), masks.r(masks.t[:, d, :]))
                P.mm(r1, g.r(gbc), masks.r(masks.t[:, d, :]), start=True, stop=False)
                P.mm(r1, k.identb.r(), negb.r(negb.t[:, d, :]), start=False, stop=True)
                yield
            for ui, (st_, d) in enumerate(units):
                c = cs[ui]
                bkR = banks[("R", ui)]
                P.act(ur("egcr", ui, gi), k.ps.r(k.ps.t[:, bkR, 0:128], bkR), AF.Exp)
                P.act(ur("dec", ui, gi), k.ps.r(k.ps.t[:, bkR, 128:256], bkR), AF.Exp, bias=ngc.r(ngc.t[:, c, d:d + 1]))
                yield
            for ui, (st_, d) in enumerate(units):
                c = cs[ui]
                bkK = P.bank()
                banks[("K", ui)] = bkK
                P.mm(k.ps.r(k.ps.t[:, bkK, 0:256], bkK), kq.r(kq.t[:, c, 0, :], c), kq.r(kq.t[:, c, :, :].rearrange("p a m -> p (a m)"), c))
                yield
            for ui, (st_, d) in enumerate(units):
                c = cs[ui]
                bkK = banks[("K", ui)]
                P.stt(ur("NB", ui, gi), k.ps.r(k.ps.t[:, bkK, 0:128], bkK), beta.r(beta.t[:, c, d:d + 1]), ur("dec", ui, gi), ALU.mult, ALU.mult)
                P.tt("dve", ur("attn", ui, gi), k.ps.r(k.ps.t[:, bkK, 128:256], bkK), ur("dec", ui, gi), ALU.mult)
                yield
            for ui, (st_, d) in enumerate(units):
                bkT = P.bank()
                banks[("T", ui)] = bkT
                pbT = k.ps.t[:, bkT, :].bitcast(BF16)
                P.tr(k.ps.r(pbT[:, 0:128], bkT), ur("NB", ui, gi), k.identb.r())
                yield
            for ui, (st_, d) in enumerate(units):
                c = cs[ui]
                bkT = banks[("T", ui)]
                pbT = k.ps.t[:, bkT, :].bitcast(BF16)
                P.copy("act", ur("NA", ui, gi), k.ps.r(pbT[:, 0:128], bkT))
                P.tt("pool", ur("qin", ui, gi), kq.r(kq.t[:, c, 1, :], c), ur("egcr", ui, gi), ALU.mult)
                P.act(ur("kout", ui, gi), k_tm.r(k_tm.t[:, c, :], c), AF.Identity, scale=eout.r(eout.t[:, c, d:d + 1]))
                yield
            for ui, (st_, d) in enumerate(units):
                P.tt("pool", ur("Y", ui, gi, 0), ur("NA", ui, gi), lvl.r(lvl.t[:, d, 0, 0]), ALU.mult)
                P.tt("pool", ur("Y", ui, gi, 1), ur("NB", ui, gi), lvl.r(lvl.t[:, d, 0, 1]), ALU.mult)
                P.tt("pool", ur("T", ui, gi), I2.r(), ur("Y", ui, gi), ALU.add)
                yield
            for lv in range(1, 7):
                for ui, (st_, d) in enumerate(units):
                    bkY = P.bank()
                    banks[("Y", ui)] = bkY
                    P.mm(k.ps.r(k.ps.t[:, bkY, 0:128], bkY), ur("NB", ui, gi), ur("T", ui, gi, 0))
                    P.mm(k.ps.r(k.ps.t[:, bkY, 128:256], bkY), ur("NA", ui, gi), ur("T", ui, gi, 1))
                    yield
                for ui, (st_, d) in enumerate(units):
                    bkY = banks[("Y", ui)]
                    P.tt("dve", ur("Y", ui, gi), k.ps.r(k.ps.t[:, bkY, 0:256].rearrange("p (a m) -> p a m", m=128), bkY), lvl.r(lvl.t[:, d, lv]), ALU.mult)
                    yield
                for ui, (st_, d) in enumerate(units):
                    bkZ = P.bank()
                    banks[("Z", ui)] = bkZ
                    P.mm(k.ps.r(k.ps.t[:, bkZ, 0:128], bkZ), ur("T", ui, gi, 1), ur("Y", ui, gi, 0))
                    P.mm(k.ps.r(k.ps.t[:, bkZ, 128:256], bkZ), ur("T", ui, gi, 0), ur("Y", ui, gi, 1))
                    yield
                for ui, (st_, d) in enumerate(units):
                    bkZ = banks[("Z", ui)]
                    P.tt("dve", ur("T", ui, gi), ur("T", ui, gi), k.ps.r(k.ps.t[:, bkZ, 0:256].rearrange("p (a m) -> p a m", m=128), bkZ), ALU.add)
                    yield

        def chain_gen(gi):
            units = [(step, d) for step in groups[gi] for d in range(2)]
            for ui, (step, d) in enumerate(units):
                c = step if d == 0 else NCH - 1 - step
                first_visit = (c <= 9) if d == 0 else (c >= 10)
                if nsteps < NCH:
                    first_visit = True
                seg = seg_of_chunk(c)
                c_first, c_n = SEG_CH[seg]
                seg_start = (c == c_first) if d == 0 else (c == c_first + c_n - 1)
                seg_end = (c == c_first + c_n - 1) if d == 0 else (c == c_first)
                if seg_start:
                    if seg == 2:
                        P.dma("sp", S.r(S.t[:, d], d), R(k.dram("gdn_s0").ap[j, d, hh], k.dram("gdn_s0").keys))
                    else:
                        P.memset("dve", S.r(S.t[:, d], d), 0.0)
                    P.copy("act", Sb.r(Sb.t[:, d], d), S.r(S.t[:, d], d))
                    yield
                kc = kq.r(kq.t[:, c, 0, :], c)
                bkC = P.bank()
                P.mm(k.ps.r(k.ps.t[:, bkC, 0:128], bkC), kc, Sb.r(Sb.t[:, d], d))
                yield
                P.stt(Rp.r(Rp.t[:, d], d), k.ps.r(k.ps.t[:, bkC, 0:128], bkC), negegc.r(negegc.t[:, c, d:d + 1]), v_tm.r(v_tm.t[:, c, :], c), ALU.mult, ALU.add)
                yield
                bkV = P.bank()
                P.mm(k.ps.r(k.ps.t[:, bkV, 0:128], bkV), ur("T", ui, gi, 1), Rp.r(Rp.t[:, d], d))
                yield
                P.act(vn.r(vn.t[:, d], d), k.ps.r(k.ps.t[:, bkV, 0:128], bkV), AF.Identity, scale=beta.r(beta.t[:, c, d:d + 1]))
                yield
                bkS = P.bank()
                pS = k.ps.r(k.ps.t[:, bkS, 0:128], bkS)
                P.mm(pS, ur("kout", ui, gi), vn.r(vn.t[:, d], d))
                bkO = P.bank()
                po = k.ps.r(k.ps.t[:, bkO, 0:128], bkO)
                P.mm(po, ur("qin", ui, gi), Sb.r(Sb.t[:, d], d), start=True, stop=False)
                P.mm(po, ur("attn", ui, gi), vn.r(vn.t[:, d], d), start=False, stop=True)
                yield
                P.stt(S.r(S.t[:, d], d), S.r(S.t[:, d], d), egl.r(egl.t[:, c, d:d + 1]), pS, ALU.mult, ALU.add)
                yield
                P.copy("act", Sb.r(Sb.t[:, d], d), S.r(S.t[:, d], d))
                yield
                if first_visit:
                    P.copy("act", o_tm.r(o_tm.t[:, c, :], c), po)
                    yield
                else:
                    P.tt("dve", ot.r(), po, o_tm.r(o_tm.t[:, c, :], c), ALU.add)
                    yield
                    o_, i_, a_ = junk.t, ot.t, ssq.t[:, 0:1]
                    P.op("act", lambda e, o_=o_, i_=i_, a_=a_: e.activation(out=o_, in_=i_, func=AF.Square, accum_out=a_),
                         reads=[ot.r()], writes=[junk.r(), ssq.r()])
                    P.act(ssq.r(ssq.t[:, 1:2]), ssq.r(ssq.t[:, 0:1]), AF.Sqrt, bias=c_eps5.r(), scale=1.0 / 128)
                    yield
                    o_, i_ = ssq.t[:, 1:2], ssq.t[:, 1:2]
                    P.op("dve", lambda e, o_=o_, i_=i_: e.reciprocal(out=o_, in_=i_), reads=[ssq.r()], writes=[ssq.r()])
                    bkZ = P.bank()
                    for kk in range(8):
                        P.mm(k.ps.r(k.ps.t[:, bkZ, 0:128], bkZ), hch(k, kk, c), wz.r(wz.t[:, kk, :]), start=(kk == 0), stop=(kk == 7))
                    yield
                    P.act(sz.r(), k.ps.r(k.ps.t[:, bkZ, 0:128], bkZ), AF.Silu)
                    P.stt(ot.r(), ot.r(), ssq.r(ssq.t[:, 1:2]), rows.r(rows.t[:, 32:160]), ALU.mult, ALU.mult)
                    yield
                    P.tt("dve", ogb.r(), ot.r(), sz.r(), ALU.mult)
                    yield
                    bkT2 = P.bank()
                    pbT2 = k.ps.t[:, bkT2, :].bitcast(BF16)
                    P.tr(k.ps.r(pbT2[:, 0:128], bkT2), ogb.r(), k.identb.r())
                    yield
                    P.copy("act", v_fm.r(v_fm.t[:, c * 128:(c + 1) * 128], c), k.ps.r(pbT2[:, 0:128], bkT2))
                    yield
                if seg_end and seg < 2:
                    P.dma("sp", R(k.dram("gdn_out").ap[j, seg, d, hh], k.dram("gdn_out").keys), S.r(S.t[:, d], d))

        def run_all(gen):
            for _ in gen:
                pass

        def merge(pg, cg, ratio):
            pdone = cdone = False
            while not (pdone and cdone):
                if not cdone:
                    try:
                        next(cg)
                    except StopIteration:
                        cdone = True
                for _ in range(ratio):
                    if pdone:
                        break
                    try:
                        next(pg)
                    except StopIteration:
                        pdone = True

        run_all(prelim_gen(0))
        for gi in range(len(groups)):
            if gi + 1 < len(groups):
                merge(prelim_gen(gi + 1), chain_gen(gi), k.cfg.get("gdn_ratio", 3))
            else:
                run_all(chain_gen(gi))
        P.barrier()
        P.release(mku)
        if nsteps == NCH:
            outproj_acc_g(k, W_out, hh * 128, lambda b: v_fm.r(v_fm.t[:, b * BLK:(b + 1) * BLK], range(4 * b, 4 * b + 4)))
    P.barrier()
    P.release(mk)


def outproj_acc_g(k, W, r0, o_region):
    P = k.P
    s, wo = load_out_w(k, W, 0, r0)
    for m in range(8):
        for b in range(NBLK):
            bk = P.bank()
            P.mm(psb(k, bk), k.wr.r(wo[:, m * 128:(m + 1) * 128], s), o_region(b))
            c = bc(b)
            P.stt(xs(k, m, b), psb(k, bk), k.mod.r(k.mod.t[:, Q_G1 + m, c:c + 1]), xs(k, m, b), ALU.mult, ALU.add)


NCORES = 8


def f32(a):
    return np.ascontiguousarray(np.asarray(a, dtype=np.float32))


def prep_inputs(inp):
    g = {k: np.asarray(v) for k, v in inp.items()}
    DEPTH = 4
    shared = {}
    shared["ident"] = np.eye(128, dtype=np.float32)
    for L in range(DEPTH):
        shared["ada_w%d" % L] = f32(g["ada_w"][L:L + 1])
        shared["ffn_w_in%d" % L] = f32(g["ffn_w_in"][L:L + 1])
        shared["ffn_w_out%d" % L] = f32(g["ffn_w_out"][L:L + 1])
    shared["ada_b"] = f32(g["ada_b"].reshape(DEPTH, 48, 128).transpose(0, 2, 1))
    shared["lng"] = f32(g["ln_g"].reshape(DEPTH, 2, 8, 128).transpose(0, 1, 3, 2))
    shared["lnb"] = f32(g["ln_b"].reshape(DEPTH, 2, 8, 128).transpose(0, 1, 3, 2))
    shared["fcw"] = f32(g["ffn_conv"].reshape(DEPTH, 9, 44, 128).transpose(0, 3, 2, 1))
    shared["lru_w_in"] = f32(g["lru_w_in"])
    shared["lru_w_out"] = f32(g["lru_w_out"])
    shared["lru_gate_w"] = f32(g["lru_gate_w"])
    sm = np.zeros((128, 10, 12), np.float32)
    sm[:, :, 0:4] = g["lru_conv"][0].reshape(4, 10, 128).transpose(2, 1, 0)
    sm[:, :, 4] = g["lru_conv_b"][0].reshape(10, 128).T
    sm[:, :, 5:9] = g["lru_gate_b"][0].reshape(4, 10, 128).transpose(2, 1, 0)
    sm[:, :, 9:11] = g["lru_lambda"][0].reshape(2, 10, 128).transpose(2, 1, 0)
    shared["lru_sm"] = sm
    tri_f = np.triu(np.ones((128, 128), np.float32))
    tri_b = np.tril(np.ones((128, 128), np.float32))
    shared["cmask"] = np.stack([tri_f, tri_b, (1 - tri_f) * -30000.0, (1 - tri_b) * -30000.0]).astype(np.float32)
    shared["ssd_w_in"] = f32(g["ssd_w_in"])
    shared["ssd_w_out"] = f32(g["ssd_w_out"])
    cw = np.zeros((128, 24, 5), np.float32)
    cw[:, :, 0:4] = g["ssd_conv"][0].reshape(4, 24, 128).transpose(2, 1, 0)
    cw[:, :, 4] = g["ssd_conv_b"][0].reshape(24, 128).T
    shared["ssd_cw"] = cw
    row = np.concatenate([g["ssd_dt_bias"][0].reshape(64), g["ssd_a_log"][0].reshape(64), g["ssd_d"][0].reshape(32)])
    shared["ssd_rows"] = f32(np.broadcast_to(row[None, :], (128, 160)))
    shared["ssd_nrm"] = f32(np.broadcast_to(g["ssd_norm"][0][None, :], (128, 2048)))
    for jj in range(2):
        shared["gdn_w_in%d" % jj] = f32(g["gdn_w_in"][jj:jj + 1])
        shared["gdn_w_out%d" % jj] = f32(g["gdn_w_out"][jj:jj + 1])
    shared["gdn_cw"] = f32(g["gdn_conv"].reshape(2, 4, 24, 128).transpose(0, 3, 2, 1))
    grow = np.concatenate([g["gdn_dt_bias"].reshape(2, 16), g["gdn_a_log"].reshape(2, 16), g["gdn_norm"].reshape(2, 128)], axis=1)
    shared["gdn_rows"] = f32(np.broadcast_to(grow[:, None, :], (2, 128, 160)))
    li = np.arange(128)[:, None]
    si = np.arange(128)[None, :]
    lv = np.zeros((2, 7, 2, 128, 128), np.float32)
    for jl in range(7):
        Bs = 2 ** jl
        mA = ((li // (2 * Bs)) == (si // (2 * Bs))) & ((li % (2 * Bs)) >= Bs) & ((si % (2 * Bs)) < Bs)
        mA = mA.astype(np.float32)
        lv[0, jl, 0] = -mA
        lv[0, jl, 1] = -mA.T
        lv[1, jl, 0] = -mA.T
        lv[1, jl, 1] = -mA
    shared["glvl"] = lv
    maps = []
    for i in range(NCORES):
        p0, p1, sb = 2 * i, 2 * i + 1, i % 4
        xin = np.concatenate([g["x_prompt"][p0].T, g["x_prompt"][p1].T, g["x_sample"][sb].T], axis=1)
        cond = np.stack([g["c_ctx"].reshape(8, 128).T, g["c"][sb].reshape(8, 128).T], axis=-1)
        m = dict(shared)
        m["xin"] = f32(xin)
        m["cond"] = f32(cond)
        m["ssd_s0"] = f32(g["state_ssd"][sb, 0].reshape(2, 8, 4, 64, 128).transpose(0, 1, 4, 2, 3).reshape(2, 8, 128, 256))
        m["gdn_s0"] = f32(g["state_gdn"][sb])
        m["lru_s0"] = f32(g["state_lru"][sb, 0].reshape(2, 10, 128).transpose(2, 1, 0))
        maps.append(m)
    return maps


def assemble(results, inp):
    BATCH, SEQ, D = 16, 256, 1024
    yp = np.zeros((BATCH, SEQ, D), np.float32)
    ys = np.zeros((4, 2048, D), np.float32)
    for i in range(NCORES):
        y = np.asarray(results[i]["yout"])
        yp[2 * i] = y[:, 0:256].T
        yp[2 * i + 1] = y[:, 256:512].T
        if i < 4:
            ys[i] = y[:, 512:].T
    nl = np.zeros((BATCH, 1, 2, 1280), np.float32)
    for i in range(NCORES):
        if "lru_out" in results[i]:
            o = np.asarray(results[i]["lru_out"])
            for pi in range(2):
                nl[2 * i + pi, 0] = o[:, :, pi, :].transpose(2, 1, 0).reshape(2, 1280)
    nssd = np.zeros((BATCH, 1, 2, 32, 64, 128), np.float32)
    for i in range(NCORES):
        if "ssd_out" in results[i]:
            o = np.asarray(results[i]["ssd_out"])
            for pi in range(2):
                nssd[2 * i + pi, 0] = o[pi].reshape(2, 32, 64, 128)
    ngdn = np.zeros((BATCH, 2, 2, 8, 128, 128), np.float32)
    for i in range(NCORES):
        if "gdn_out" in results[i]:
            o = np.asarray(results[i]["gdn_out"])
            for pi in range(2):
                ngdn[2 * i + pi] = o[:, pi]
    return yp, ys, nl, nssd, ngdn


_NC_CACHE = {}


def kernel(**inputs):
    cfg = {}
    if "nc" not in _NC_CACHE:
        _NC_CACHE["nc"] = build(cfg)
    nc, used = _NC_CACHE["nc"]
    maps = prep_inputs(inputs)
    maps = [{kk: v for kk, v in mm.items() if kk in used} for mm in maps]
    res = run_bass_kernel_spmd(nc, maps, core_ids=list(range(NCORES)))
    yp, ys, nl, nssd, ngdn = assemble(res.results, inputs)
    return (yp, ys, ngdn, nssd, nl)
```

```python
import numpy as np
import concourse.bass as bass
import concourse.mybir as mybir
from concourse.bass_utils import run_bass_kernel_spmd
from contextlib import ExitStack

F32 = mybir.dt.float32
F32R = mybir.dt.float32r
BF16 = mybir.dt.bfloat16
ALU = mybir.AluOpType
AF = mybir.ActivationFunctionType
AX = mybir.AxisListType

ENGS = ["pe", "act", "dve", "pool", "sp"]
NDS = 24
MAXEMB = 1
ARENA_WORDS = 53000


class R:
    __slots__ = ("ap", "keys")

    def __init__(self, ap, keys):
        self.ap = ap
        self.keys = keys


class Buf:
    def __init__(self, prog, name, shape, dtype, nsub=1, psum=False):
        self.name = name
        self.nsub = nsub
        if psum:
            self.t = prog.st.enter_context(prog.nc.psum_tensor(name, shape, dtype))
        else:
            n = 1
            for d in shape[1:]:
                n *= d
            esz = 2 if dtype == BF16 else 4
            words = (n * esz + 3) // 4
            off = prog.aoff
            prog.aoff += words
            assert prog.aoff <= ARENA_WORDS, (name, prog.aoff)
            prog.apeak = max(prog.apeak, prog.aoff)
            ap = prog.arena[:, off:off + words]
            if dtype == BF16:
                ap = ap.bitcast(BF16)
            ap = ap[:, 0:n]
            if len(shape) > 2:
                names = " ".join("d%d" % i for i in range(len(shape) - 1))
                kw = {"d%d" % i: shape[i + 1] for i in range(len(shape) - 1)}
                ap = ap.rearrange("p (%s) -> p %s" % (names, names), **kw)
            self.t = ap

    def r(self, ap=None, sub=None):
        if ap is None:
            ap = self.t[:] if not hasattr(self.t, "rearrange") else self.t
        if sub is None:
            keys = [(self.name, i) for i in range(self.nsub)]
        elif isinstance(sub, (list, tuple, range)):
            keys = [(self.name, i) for i in sub]
        else:
            keys = [(self.name, sub)]
        return R(ap, keys)


class Prog:
    def __init__(self, nc, st):
        self.nc = nc
        self.st = st
        self.q = {e: [] for e in ENGS}
        self.cnt = {e: 0 for e in ENGS}
        self.sem = {e: st.enter_context(nc.semaphore("s_" + e)) for e in ENGS}
        self.dsem = [st.enter_context(nc.semaphore("d%d" % i)) for i in range(NDS)]
        self.dcnt = [0] * NDS
        self.dnext = 0
        self.known = {e: {} for e in ENGS}
        self.last_w = {}
        self.readers = {}
        self.nops = 0
        self.nwaits = 0
        self.bar = {e: None for e in ENGS}
        self._bank = 0
        self.arena_t = st.enter_context(nc.sbuf_tensor("arena", [128, ARENA_WORDS], F32))
        self.arena = self.arena_t[:]
        self.aoff = 0
        self.apeak = 0

    def mark(self):
        return self.aoff

    def release(self, m):
        self.aoff = m

    def bank(self):
        b = self._bank
        self._bank = (self._bank + 1) % 8
        return b

    def barrier(self):
        snap = {e: self.cnt[e] for e in ENGS if self.cnt[e] > 0}
        for i in range(NDS):
            if self.dcnt[i] > 0:
                snap["d%d" % i] = self.dcnt[i]
        for e in ENGS:
            self.bar[e] = dict(snap)

    def buf(self, name, shape, dtype, nsub=1, psum=False):
        return Buf(self, name, shape, dtype, nsub, psum)

    def _collect(self, eng, reads, writes):
        need = {}

        def add(tok):
            if tok is None:
                return
            semid, val, snap = tok
            if eng == "pe" and semid == "pe":
                return
            if need.get(semid, (0, None))[0] < val:
                need[semid] = (val, snap)

        for k in reads:
            add(self.last_w.get(k))
        for k in writes:
            add(self.last_w.get(k))
            rd = self.readers.get(k)
            if rd:
                for tok in rd.values():
                    add(tok)
        kn = self.known[eng]
        if self.bar[eng] is not None:
            for semid, val in self.bar[eng].items():
                if eng == "pe" and semid == "pe":
                    continue
                if semid == eng and val >= self.cnt[eng] + 1:
                    continue
                if need.get(semid, (0, None))[0] < val:
                    need[semid] = (val, None)
            self.bar[eng] = None
        waits = []
        for semid, (val, snap) in need.items():
            if kn.get(semid, 0) >= val:
                continue
            waits.append((semid, val))
        for semid, (val, snap) in need.items():
            if kn.get(semid, 0) < val:
                kn[semid] = val
            if snap:
                for s2, v2 in snap.items():
                    if kn.get(s2, 0) < v2:
                        kn[s2] = v2
        return waits

    def _commit(self, tok, reads, writes):
        for k in writes:
            self.last_w[k] = tok
            self.readers[k] = {}
        for k in reads:
            if k in writes:
                continue
            self.readers.setdefault(k, {})[tok[0]] = tok

    def op(self, eng, fn, reads=(), writes=()):
        rk = [k for r in reads if r is not None for k in r.keys]
        wk = [k for r in writes if r is not None for k in r.keys]
        if eng != "pe":
            for k_ in rk:
                if k_[0] == "ps" and k_ not in wk:
                    wk.append(k_)
        waits = self._collect(eng, rk, wk)
        self.cnt[eng] += 1
        tok = (eng, self.cnt[eng], dict(self.known[eng]))
        self.q[eng].append((waits, fn, None))
        self._commit(tok, rk, wk)
        self.nops += 1
        self.nwaits += len(waits)
        return tok

    def dma(self, eng, out, in_, **kw):
        rk = list(in_.keys)
        wk = list(out.keys)
        i = self.dnext
        self.dnext = (self.dnext + 1) % NDS
        semid = "d%d" % i
        waits = self._collect(eng, rk, wk)
        kn = self.known[eng]
        if kn.get(semid, 0) < self.dcnt[i]:
            waits.append((semid, self.dcnt[i]))
            kn[semid] = self.dcnt[i]
        self.dcnt[i] += 16
        tok = (semid, self.dcnt[i], dict(kn))
        oap, iap = out.ap, in_.ap

        def fn(e):
            return e.dma_start(out=oap, in_=iap, **kw)

        self.q[eng].append((waits, fn, i))
        self._commit(tok, rk, wk)
        self.nops += 1
        self.nwaits += len(waits)
        return tok

    def _semh(self, semid):
        if semid in self.sem:
            return self.sem[semid]
        return self.dsem[int(semid[1:])]

    def emit(self, final_wait_eng="sp"):
        nc = self.nc
        finals = []
        for e in ENGS:
            if self.cnt[e] > 0:
                finals.append((e, self.cnt[e]))
        for i in range(NDS):
            if self.dcnt[i] > 0:
                finals.append(("d%d" % i, self.dcnt[i]))
        eng_objs = {}
        with nc.Block() as block:
            def mk(ename):
                def body(e):
                    for waits, fn, dsi in self.q[ename]:
                        if dsi is not None or len(waits) > MAXEMB:
                            for semid, val in waits:
                                e.wait_ge(self._semh(semid), val)
                            ins = fn(e)
                        else:
                            ins = fn(e)
                            for semid, val in waits:
                                ins._wait_ge(self._semh(semid), val)
                        if dsi is None:
                            ins.then_inc(self.sem[ename], 1)
                        else:
                            ins.then_inc(self.dsem[dsi], 16)
                    if ename == final_wait_eng:
                        for semid, val in finals:
                            e.wait_ge(self._semh(semid), val)
                return body
            block.tensor(mk("pe"))
            block.scalar(mk("act"))
            block.vector(mk("dve"))
            block.gpsimd(mk("pool"))
            block.sync(mk("sp"))

    def mm(self, out, lhsT, rhs, start=True, stop=True, extra_reads=()):
        o, l, r = out.ap, lhsT.ap, rhs.ap
        return self.op("pe", lambda e: e.matmul(o, l, r, start=start, stop=stop),
                       reads=[lhsT, rhs] + list(extra_reads) + ([] if start else [out]), writes=[out])

    def tr(self, out, in_, ident):
        o, i, d = out.ap, in_.ap, ident.ap
        return self.op("pe", lambda e: e.transpose(o, i, d), reads=[in_, ident], writes=[out])

    def act(self, out, in_, func, bias=None, scale=None, eng="act"):
        o, i = out.ap, in_.ap
        kw = {}
        rd = [in_]
        if bias is not None:
            if isinstance(bias, R):
                kw["bias"] = bias.ap
                rd.append(bias)
            else:
                kw["bias"] = float(bias)
        if scale is not None:
            if isinstance(scale, R):
                kw["scale"] = scale.ap
                rd.append(scale)
            else:
                kw["scale"] = float(scale)
        return self.op("act", lambda e: e.activation(out=o, in_=i, func=func, **kw), reads=rd, writes=[out])

    def tt(self, eng, out, in0, in1, op):
        o, a, b = out.ap, in0.ap, in1.ap
        return self.op(eng, lambda e: e.tensor_tensor(out=o, in0=a, in1=b, op=op), reads=[in0, in1], writes=[out])

    def ts(self, eng, out, in0, s1, op0, s2=None, op1=None):
        o, a = out.ap, in0.ap
        rd = [in0]
        v1 = s1
        if isinstance(s1, R):
            rd.append(s1)
            v1 = s1.ap
        v2 = s2
        if isinstance(s2, R):
            rd.append(s2)
            v2 = s2.ap
        if op1 is None:
            return self.op(eng, lambda e: e.tensor_scalar(out=o, in0=a, scalar1=v1, scalar2=None, op0=op0), reads=rd, writes=[out])
        return self.op(eng, lambda e: e.tensor_scalar(out=o, in0=a, scalar1=v1, scalar2=v2, op0=op0, op1=op1), reads=rd, writes=[out])

    def stt(self, out, in0, scalar, in1, op0, op1):
        o, a, b = out.ap, in0.ap, in1.ap
        rd = [in0, in1]
        sv = scalar
        if isinstance(scalar, R):
            rd.append(scalar)
            sv = scalar.ap
        return self.op("dve", lambda e: e.scalar_tensor_tensor(out=o, in0=a, scalar=sv, in1=b, op0=op0, op1=op1), reads=rd, writes=[out])

    def copy(self, eng, out, in_):
        o, i = out.ap, in_.ap
        if eng == "act":
            return self.op("act", lambda e: e.copy(out=o, in_=i), reads=[in_], writes=[out])
        return self.op(eng, lambda e: e.tensor_copy(out=o, in_=i), reads=[in_], writes=[out])

    def memset(self, eng, out, val):
        o = out.ap
        return self.op(eng, lambda e: e.memset(o, val), reads=[], writes=[out])


D = 1024
NT = 2560
NBLK = 5
BLK = 512
DEPTH = 4
FH = 2816
NPAIR = 22
UPW = 2 * 258 + 34 * 66
ALPHA = (2 * DEPTH) ** 0.25
LN_EPS = 1e-5
NW = 5
WSL = 1024


def bc(b):
    return 0 if b == 0 else 1


class K:
    pass


def build(cfg):
    nc = bass.Bass("TRN2", target_bir_lowering=False)
    k = K()
    k.nc = nc
    k.cfg = cfg

    shapes = {
        "xin": [D, NT], "cond": [128, 8, 2], "ident": [128, 128],
        "ada_b": [DEPTH, 128, 48], "lng": [DEPTH, 2, 128, 8], "lnb": [DEPTH, 2, 128, 8],
        "fcw": [DEPTH, 128, 44, 9],
        "lru_w_in": [1, D, 2560], "lru_w_out": [1, 1280, D], "lru_gate_w": [1, 2, 2, 10, 128, 128],
        "lru_sm": [128, 10, 12], "lru_s0": [128, 10, 2],
        "cmask": [4, 128, 128],
        "ssd_w_in": [1, D, 5184], "ssd_w_out": [1, 2048, D], "ssd_cw": [128, 24, 5], "ssd_rows": [128, 160],
        "ssd_nrm": [128, 2048], "ssd_s0": [2, 8, 128, 256],
    }
    for jj in range(2):
        shapes["gdn_w_in%d" % jj] = [1, D, 4128]
        shapes["gdn_w_out%d" % jj] = [1, D, D]
    shapes.update({"gdn_cw": [2, 128, 24, 4], "gdn_rows": [2, 128, 160], "gdn_s0": [2, 2, 8, 128, 128], "glvl": [2, 7, 2, 128, 128]})
    for L in range(DEPTH):
        shapes["ada_w%d" % L] = [1, D, 6 * D]
        shapes["ffn_w_in%d" % L] = [1, D, 2 * FH]
        shapes["ffn_w_out%d" % L] = [1, FH, D]
    oshapes = {"yout": [D, NT], "lru_out": [128, 10, 2, 2], "ssd_out": [2, 2, 2048, 128], "gdn_out": [2, 2, 2, 8, 128, 128]}
    k.shapes = shapes
    k.oshapes = oshapes
    k.decl = {}

    def dram(name):
        if name in k.decl:
            return k.decl[name]
        if name in shapes:
            t = nc.dram_tensor(name, list(shapes[name]), F32, kind="ExternalInput").ap()
        else:
            t = nc.dram_tensor(name, list(oshapes[name]), F32, kind="ExternalOutput").ap()
        k.decl[name] = R(t, [("dram_" + name, 0)])
        return k.decl[name]
    k.dram = dram

    with ExitStack() as st:
        P = Prog(nc, st)
        k.P = P
        k.x = P.buf("x", [128, 8, NT], F32, nsub=40)
        k.h = P.buf("h", [128, 8, NT], BF16, nsub=40)
        k.wr = P.buf("wr", [128, NW, WSL], BF16, nsub=NW)
        k.wnext = 0
        k.ps = P.buf("ps", [128, 8, 512], F32, nsub=8, psum=True)
        k.identf = P.buf("identf", [128, 128], F32)
        k.identb = P.buf("identb", [128, 128], BF16)
        k.onesb = P.buf("onesb", [128, 128], BF16)
        k.csil = P.buf("csil", [128, 8, 2], BF16)
        k.mod = P.buf("mod", [128, 48, 2], F32)
        k.msc = P.buf("msc", [128, 2, 8, 2], F32)
        k.lngb = P.buf("lngb", [128, DEPTH, 2, 8], F32)
        k.lnbb = P.buf("lnbb", [128, DEPTH, 2, 8], F32)
        k.adab = P.buf("adab", [128, DEPTH, 48], F32)

        prologue(k)
        for L in cfg.get("layers", list(range(DEPTH))):
            layer(k, L)
        for j in range(8):
            P.dma("sp", R(k.dram("yout").ap[j * 128:(j + 1) * 128, :], k.dram("yout").keys), k.x.r(k.x.t[:, j, :], range(j * 5, j * 5 + 5)))
        P.emit()
        print("ops", P.nops, "waits", P.nwaits, {e: P.cnt[e] for e in ENGS}, "apeak", P.apeak)
    return nc, set(n for n in k.decl if n in k.shapes)


def xs(k, j, b):
    return k.x.r(k.x.t[:, j, b * BLK:(b + 1) * BLK], j * 5 + b)


def hs(k, j, b):
    return k.h.r(k.h.t[:, j, b * BLK:(b + 1) * BLK], j * 5 + b)


def psb(k, b, n=512):
    return k.ps.r(k.ps.t[:, b, 0:n], b)


def prologue(k):
    P = k.P
    for j in range(8):
        P.dma("sp", k.x.r(k.x.t[:, j, :], range(j * 5, j * 5 + 5)), R(k.dram("xin").ap[j * 128:(j + 1) * 128, :], k.dram("xin").keys))
    P.dma("sp", k.identf.r(), k.dram("ident"))
    P.copy("dve", k.identb.r(), k.identf.r())
    P.memset("dve", k.onesb.r(), 1.0)
    ctmp = P.buf("ctmp", [128, 8, 2], F32)
    P.dma("sp", ctmp.r(), k.dram("cond"))
    P.act(k.csil.r(), ctmp.r(), AF.Silu)
    for L in range(DEPTH):
        P.dma("sp", k.lngb.r(k.lngb.t[:, L]), R(k.dram("lng").ap[L].rearrange("s p j -> p s j"), k.dram("lng").keys))
        P.dma("sp", k.lnbb.r(k.lnbb.t[:, L]), R(k.dram("lnb").ap[L].rearrange("s p j -> p s j"), k.dram("lnb").keys))
        P.dma("sp", k.adab.r(k.adab.t[:, L]), R(k.dram("ada_b").ap[L], k.dram("ada_b").keys))


def wslot(k):
    s = k.wnext
    k.wnext = (k.wnext + 1) % NW
    return s


def load_in_w(k, W, L, c0, ncols=128):
    P = k.P
    s = wslot(k)
    src = W.ap[L][:, c0:c0 + ncols].rearrange("(kk p) c -> p kk c", p=128)
    dst = k.wr.t[:, s, 0:8 * ncols].rearrange("p (kk c) -> p kk c", c=ncols)
    P.dma("pool", k.wr.r(dst, s), R(src, W.keys))
    return s, dst


def load_out_w(k, W, L, r0):
    P = k.P
    s = wslot(k)
    src = W.ap[L][r0:r0 + 128, :]
    dst = k.wr.t[:, s, 0:1024]
    P.dma("pool", k.wr.r(dst, s), R(src, W.keys))
    return s, dst


def ada(k, L):
    P = k.P
    b = P.bank()
    for q in range(48):
        s, w = load_in_w(k, k.dram("ada_w%d" % L), 0, q * 128)
        for kk in range(8):
            P.mm(k.ps.r(k.ps.t[:, b, q * 2:q * 2 + 2], b), k.wr.r(w[:, kk, :], s),
                 k.csil.r(k.csil.t[:, kk, :]), start=(kk == 0), stop=(kk == 7))
    src = k.ps.t[:, b, 0:96].rearrange("p (q c) -> p q c", c=2)
    bias = k.adab.t[:, L, :].unsqueeze(2).to_broadcast([128, 48, 2])
    P.tt("dve", k.mod.r(), k.ps.r(src, b), k.adab.r(bias), ALU.add)
    for sl in range(2):
        q0 = (1 + 3 * sl) * 8
        P.ts("dve", k.msc.r(k.msc.t[:, sl]), k.mod.r(k.mod.t[:, q0:q0 + 8, :]), 1.0, ALU.add)


def modulate(k, sl):
    P = k.P
    q_sh = (0 + 3 * sl) * 8
    for j in range(8):
        for b in range(NBLK):
            c = bc(b)
            P.act(hs(k, j, b), xs(k, j, b), AF.Identity,
                  bias=k.mod.r(k.mod.t[:, q_sh + j, c:c + 1]), scale=k.msc.r(k.msc.t[:, sl, j, c:c + 1]))
    for j in range(8):
        for b in range(NBLK):
            P.ts("dve", xs(k, j, b), xs(k, j, b), ALPHA, ALU.mult)


def layernorm(k, L, sl, lb):
    P = k.P
    xb, sq, mean, m2, var, rstd, nmr, t1 = lb["xb"], lb["sq"], lb["mean"], lb["m2"], lb["var"], lb["rstd"], lb["nmr"], lb["t1"]
    for b in range(NBLK):
        pb = b % 2
        for j in range(8):
            P.act(xb.r(xb.t[:, pb, j, :], pb * 8 + j), xs(k, j, b), AF.Identity)
            P.act(sq.r(sq.t[:, pb, j, :], pb * 8 + j), xs(k, j, b), AF.Square)
        b1 = P.bank()
        b2 = P.bank()
        for j in range(8):
            P.mm(psb(k, b1), k.onesb.r(), xb.r(xb.t[:, pb, j, :], pb * 8 + j), start=(j == 0), stop=(j == 7))
        for j in range(8):
            P.mm(psb(k, b2), k.onesb.r(), sq.r(sq.t[:, pb, j, :], pb * 8 + j), start=(j == 0), stop=(j == 7))
        P.act(mean.r(mean.t[:, pb], pb), psb(k, b1), AF.Identity, scale=1.0 / D)
        P.tt("dve", m2.r(m2.t[:, pb], pb), mean.r(mean.t[:, pb], pb), mean.r(mean.t[:, pb], pb), ALU.mult)
        P.stt(var.r(var.t[:, pb], pb), psb(k, b2), 1.0 / D, m2.r(m2.t[:, pb], pb), ALU.mult, ALU.subtract)
        P.act(var.r(var.t[:, pb], pb), var.r(var.t[:, pb], pb), AF.Ln, bias=lb["eps"].r())
        P.act(rstd.r(rstd.t[:, pb], pb), var.r(var.t[:, pb], pb), AF.Exp, scale=-0.5)
        P.stt(nmr.r(nmr.t[:, pb], pb), mean.r(mean.t[:, pb], pb), -1.0, rstd.r(rstd.t[:, pb], pb), ALU.mult, ALU.mult)
        for j in range(8):
            tb = j % 2
            P.tt("dve", t1.r(t1.t[:, tb], tb), xs(k, j, b), rstd.r(rstd.t[:, pb], pb), ALU.mult)
            P.tt("dve", t1.r(t1.t[:, tb], tb), t1.r(t1.t[:, tb], tb), nmr.r(nmr.t[:, pb], pb), ALU.add)
            P.act(xs(k, j, b), t1.r(t1.t[:, tb], tb), AF.Identity,
                  bias=k.lnbb.r(k.lnbb.t[:, L, sl, j:j + 1]), scale=k.lngb.r(k.lngb.t[:, L, sl, j:j + 1]))


def ln_scope(k, L, sl):
    P = k.P
    P.barrier()
    mk = P.mark()
    if True:
        lb = {
            "xb": P.buf("ln_xb", [128, 2, 8, BLK], BF16, nsub=16),
            "sq": P.buf("ln_sq", [128, 2, 8, BLK], BF16, nsub=16),
            "mean": P.buf("ln_mean", [128, 2, BLK], F32, nsub=2),
            "m2": P.buf("ln_m2", [128, 2, BLK], F32, nsub=2),
            "var": P.buf("ln_var", [128, 2, BLK], F32, nsub=2),
            "rstd": P.buf("ln_rstd", [128, 2, BLK], F32, nsub=2),
            "nmr": P.buf("ln_nmr", [128, 2, BLK], F32, nsub=2),
            "t1": P.buf("ln_t1", [128, 2, BLK], F32, nsub=2),
            "eps": P.buf("ln_eps", [128, 1], F32),
        }
        P.memset("dve", lb["eps"].r(), LN_EPS)
        layernorm(k, L, sl, lb)
        P.barrier()
        P.release(mk)


def ffn(k, L):
    P = k.P
    G = 2
    P.barrier()
    mk = P.mark()
    if True:
        up = P.buf("f_up", [128, 2, 2, UPW], BF16, nsub=4)
        ab = P.buf("f_ab", [128, G, NT], BF16, nsub=G * 5)
        dg = P.buf("f_dg", [128, 2, 2, 9, 128], BF16, nsub=4)
        sg = P.buf("f_sg", [128, 2, BLK], F32, nsub=2)
        fcw = P.buf("f_fcw", [128, 44, 9], F32)
        P.dma("sp", fcw.r(), R(k.dram("fcw").ap[L], k.dram("fcw").keys))
        P.memset("pool", up.r(), 0.0)
        q_g = 5 * 8
        nsg = 0
        for g0 in range(0, NPAIR, G):
            pairs = list(range(g0, min(g0 + G, NPAIR)))
            for jj, j in enumerate(pairs):
                db = j % 2
                sg_, wg = load_in_w(k, k.dram("ffn_w_in%d" % L), 0, j * 128)
                sv_, wv = load_in_w(k, k.dram("ffn_w_in%d" % L), 0, FH + j * 128)
                ws = [wg, wv]
                wss = [sg_, sv_]
                for gv in range(2):
                    tile_idx = j + gv * NPAIR
                    i0 = k.identf.t[:].unsqueeze(1).to_broadcast([128, 9, 128])
                    i1 = fcw.t[:, tile_idx, :].unsqueeze(2).to_broadcast([128, 9, 128])
                    P.tt("pool", dg.r(dg.t[:, db, gv], db * 2 + gv), k.identf.r(i0), fcw.r(i1), ALU.mult)
                for b in range(NBLK):
                    for gv in range(2):
                        bk = P.bank()
                        for kk in range(8):
                            P.mm(psb(k, bk), k.wr.r(ws[gv][:, kk, :], wss[gv]), hs(k, kk, b), start=(kk == 0), stop=(kk == 7))
                        upt = up.t[:, db, gv]
                        if b == 0:
                            dst = upt[:, 0:516].rearrange("p (s w) -> p s w", w=258)[:, :, 1:257]
                            src = k.ps.t[:, bk, :].rearrange("p (s w) -> p s w", w=256)
                        else:
                            r0 = 8 * (b - 1)
                            dst = upt[:, 516:].rearrange("p (r w) -> p r w", w=66)[:, 1 + r0:9 + r0, 1:65]
                            src = k.ps.t[:, bk, :].rearrange("p (r w) -> p r w", w=64)
                        P.copy("act" if gv == 0 else "dve", up.r(dst, db * 2 + gv), k.ps.r(src, bk))
                for b in range(NBLK):
                    bks = []
                    for gv in range(2):
                        bk = P.bank()
                        bks.append(bk)
                        upt = up.t[:, db, gv]
                        if b == 0:
                            taps = [(1, kw) for kw in range(3)]
                            outv = k.ps.t[:, bk, :].rearrange("p (s w) -> p s w", w=256)
                        else:
                            taps = [(kh, kw) for kh in range(3) for kw in range(3)]
                            outv = k.ps.t[:, bk, :].rearrange("p (r w) -> p r w", w=64)
                        for ti, (kh, kw) in enumerate(taps):
                            if b == 0:
                                rhs = upt[:, 0:516].rearrange("p (s w) -> p s w", w=258)[:, :, kw:kw + 256]
                            else:
                                r0 = 8 * (b - 1)
                                rhs = upt[:, 516:].rearrange("p (r w) -> p r w", w=66)[:, r0 + kh:r0 + kh + 8, kw:kw + 64]
                            P.mm(k.ps.r(outv, bk), dg.r(dg.t[:, db, gv, kh * 3 + kw, :], db * 2 + gv), up.r(rhs, db * 2 + gv),
                                 start=(ti == 0), stop=(ti == len(taps) - 1))
                    sb_ = nsg % 2
                    nsg += 1
                    P.act(sg.r(sg.t[:, sb_], sb_), psb(k, bks[0]), AF.Silu)
                    P.tt("dve", ab.r(ab.t[:, jj, b * BLK:(b + 1) * BLK], jj * 5 + b), psb(k, bks[1]), sg.r(sg.t[:, sb_], sb_), ALU.mult)
            wos = [load_out_w(k, k.dram("ffn_w_out%d" % L), 0, (g0 + jj) * 128) for jj in range(len(pairs))]
            for m in range(8):
                for b in range(NBLK):
                    bk = P.bank()
                    for jj in range(len(pairs)):
                        P.mm(psb(k, bk), k.wr.r(wos[jj][1][:, m * 128:(m + 1) * 128], wos[jj][0]), ab.r(ab.t[:, jj, b * BLK:(b + 1) * BLK], jj * 5 + b),
                             start=(jj == 0), stop=(jj == len(pairs) - 1))
                    c = bc(b)
                    P.stt(xs(k, m, b), psb(k, bk), k.mod.r(k.mod.t[:, q_g + m, c:c + 1]), xs(k, m, b), ALU.mult, ALU.add)
        P.barrier()
        P.release(mk)


def layer(k, L):
    cfg = k.cfg
    ada(k, L)
    modulate(k, 0)
    if cfg.get("mixers", True):
        mixer(k, L)
    ln_scope(k, L, 0)
    modulate(k, 1)
    if cfg.get("ffn", True):
        ffn(k, L)
    ln_scope(k, L, 1)


LW = 1280
SEGS = [(0, 256), (256, 256), (512, 2048)]
Q_G1 = 2 * 8


def mixer(k, L):
    kind = L % 3
    if kind == 2:
        lru(k, L // 3)
    elif kind == 1:
        ssd(k, L // 3)
    else:
        gdn(k, L // 3)


def outproj_acc(k, W, Lw, r0, ntile, o_regions):
    P = k.P
    slots = [load_out_w(k, W, Lw, r0 + i * 128) for i in range(ntile)]
    for m in range(8):
        for b in range(NBLK):
            bk = P.bank()
            for jj in range(ntile):
                s, wo = slots[jj]
                P.mm(psb(k, bk), k.wr.r(wo[:, m * 128:(m + 1) * 128], s), o_regions(jj, b), start=(jj == 0), stop=(jj == ntile - 1))
            c = bc(b)
            P.stt(xs(k, m, b), psb(k, bk), k.mod.r(k.mod.t[:, Q_G1 + m, c:c + 1]), xs(k, m, b), ALU.mult, ALU.add)


def conv1d_pad_layout():
    offs = []
    o = 0
    for (t0, n) in SEGS:
        offs.append(o)
        o += n + 3
    return offs, o


CPO, CPW = conv1d_pad_layout()


def conv_dst(xpad_t, b):
    if b == 0:
        return xpad_t[:, 0:518].rearrange("p (s w) -> p s w", w=259)[:, :, 1:257], "p (s w) -> p s w", 256
    o = CPO[2] + 1 + (b - 1) * BLK
    return xpad_t[:, o:o + BLK], None, None


def conv_rhs(xpad_t, b, kk):
    if b == 0:
        return xpad_t[:, 0:518].rearrange("p (s w) -> p s w", w=259)[:, :, kk:kk + 256]
    o = CPO[2] + (b - 1) * BLK + kk
    return xpad_t[:, o:o + BLK]


def psview(k, bk, b):
    if b == 0:
        return k.ps.t[:, bk, :].rearrange("p (s w) -> p s w", w=256)
    return k.ps.t[:, bk, :]


def lru(k, j):
    P = k.P
    G = 2
    P.barrier()
    mk = P.mark()
    xpad = P.buf("l_xpad", [128, CPW + 1], BF16)
    xrb = P.buf("l_xrb", [128, NT], BF16)
    gg = P.buf("l_gg", [128, NT], BF16)
    Ib = P.buf("l_i", [128, NT], BF16)
    A = P.buf("l_a", [128, NT], F32)
    T = P.buf("l_t", [128, NT], F32)
    H = [P.buf("l_h0", [128, NT], BF16), P.buf("l_h1", [128, NT], BF16)]
    ob = P.buf("l_o", [128, G, NT], BF16, nsub=G)
    dgl = P.buf("l_dg", [128, 4, 128], BF16)
    sm = P.buf("l_sm", [128, 10, 12], F32)
    s0 = P.buf("l_s0", [128, 10, 2], F32)
    sp = P.buf("l_sp", [128, 10, 2], F32)
    ep = P.buf("l_ep", [128, 10, 2], F32)
    one = P.buf("l_one", [128, 1], F32)
    sto = P.buf("l_sto", [128, 10, 2, 2], F32)
    P.dma("sp", sm.r(), k.dram("lru_sm"))
    P.dma("sp", s0.r(), k.dram("lru_s0"))
    P.memset("dve", one.r(), 1.0)
    P.memset("pool", xpad.r(), 0.0)
    P.act(ep.r(), sm.r(sm.t[:, :, 9:11]), AF.Exp, scale=-1.0)
    P.ts("dve", sp.r(), ep.r(), -0.2, ALU.mult, 0.25, ALU.add)
    for cst in (1.0 / 3, 0.5, 1.0):
        P.tt("dve", sp.r(), sp.r(), ep.r(), ALU.mult)
        P.ts("dve", sp.r(), sp.r(), -1.0, ALU.mult, cst, ALU.add)
    P.tt("dve", sp.r(), sp.r(), ep.r(), ALU.mult)
    P.ts("dve", sp.r(), sp.r(), -8.0, ALU.mult)

    for g0 in range(0, 10, G):
        tiles = list(range(g0, min(g0 + G, 10)))
        for jj, n in enumerate(tiles):
            s, wgb = load_in_w(k, k.dram("lru_w_in"), j, n * 128)
            sx, wxr = load_in_w(k, k.dram("lru_w_in"), j, LW + n * 128)
            s2 = wslot(k)
            gsrc = k.dram("lru_gate_w").ap[j][:, :, n].rearrange("d g kk m -> kk (d g) m")
            gw = k.wr.t[:, s2, 0:512].rearrange("p (a m) -> p a m", m=128)
            P.dma("pool", k.wr.r(gw, s2), R(gsrc, k.dram("lru_gate_w").keys))
            i0 = k.identf.t[:].unsqueeze(1).to_broadcast([128, 4, 128])
            i1 = sm.t[:, n, 0:4].unsqueeze(2).to_broadcast([128, 4, 128])
            P.tt("pool", dgl.r(), k.identf.r(i0), sm.r(i1), ALU.mult)
            for b in range(NBLK):
                bk = P.bank()
                for kk in range(8):
                    P.mm(psb(k, bk), k.wr.r(wgb[:, kk, :], s), hs(k, kk, b), start=(kk == 0), stop=(kk == 7))
                P.act(gg.r(gg.t[:, b * BLK:(b + 1) * BLK]), psb(k, bk), AF.Gelu_apprx_tanh)
            for b in range(NBLK):
                bk = P.bank()
                for kk in range(8):
                    P.mm(psb(k, bk), k.wr.r(wxr[:, kk, :], sx), hs(k, kk, b), start=(kk == 0), stop=(kk == 7))
                dst, _, _ = conv_dst(xpad.t, b)
                P.copy("dve", xpad.r(dst), k.ps.r(psview(k, bk, b), bk))
            for b in range(NBLK):
                bk = P.bank()
                for kk in range(4):
                    P.mm(k.ps.r(psview(k, bk, b), bk), dgl.r(dgl.t[:, kk, :]), xpad.r(conv_rhs(xpad.t, b, kk)), start=(kk == 0), stop=(kk == 3))
                P.act(xrb.r(xrb.t[:, b * BLK:(b + 1) * BLK]), psb(k, bk), AF.Identity, bias=sm.r(sm.t[:, n, 4:5]))
            for d in range(2):
                for b in range(NBLK):
                    for g in range(2):
                        bk = P.bank()
                        P.mm(psb(k, bk), k.wr.r(gw[:, d * 2 + g, :], s2), xrb.r(xrb.t[:, b * BLK:(b + 1) * BLK]))
                        dstb = A if g == 0 else Ib
                        P.act(dstb.r(dstb.t[:, b * BLK:(b + 1) * BLK]), psb(k, bk), AF.Sigmoid, bias=sm.r(sm.t[:, n, 5 + d * 2 + g:6 + d * 2 + g]))
                P.act(A.r(), A.r(), AF.Exp, scale=sp.r(sp.t[:, n, d:d + 1]))
                P.tt("dve", T.r(), A.r(), A.r(), ALU.mult)
                P.act(T.r(), T.r(), AF.Sqrt, bias=one.r(), scale=-1.0)
                P.tt("dve", T.r(), T.r(), Ib.r(), ALU.mult)
                P.tt("dve", T.r(), T.r(), xrb.r(), ALU.mult)
                for si, (t0, n_t) in enumerate(SEGS):
                    if d == 0:
                        o_, a_, b_ = H[0].t[:, t0:t0 + n_t], A.t[:, t0:t0 + n_t], T.t[:, t0:t0 + n_t]
                    else:
                        lo = t0 - 1 if t0 > 0 else None
                        o_, a_, b_ = H[1].t[:, t0 + n_t - 1:lo:-1], A.t[:, t0 + n_t - 1:lo:-1], T.t[:, t0 + n_t - 1:lo:-1]
                    if si == 2:
                        init = s0.t[:, n, d:d + 1]
                        rd = [A.r(), T.r(), s0.r()]
                    else:
                        init = 0.0
                        rd = [A.r(), T.r()]
                    P.op("dve", lambda e, o_=o_, a_=a_, b_=b_, init=init: e.tensor_tensor_scan(out=o_, data0=a_, data1=b_, initial=init, op0=ALU.mult, op1=ALU.add),
                         reads=rd, writes=[H[d].r()])
                    if si < 2:
                        tl = t0 + n_t - 1 if d == 0 else t0
                        P.copy("act", sto.r(sto.t[:, n, si, d:d + 1]), H[d].r(H[d].t[:, tl:tl + 1]))
            P.tt("dve", H[0].r(), H[0].r(), H[1].r(), ALU.add)
            P.tt("dve", ob.r(ob.t[:, jj, :], jj), H[0].r(), gg.r(), ALU.mult)
        outproj_acc(k, k.dram("lru_w_out"), j, g0 * 128, len(tiles), lambda jj, b: ob.r(ob.t[:, jj, b * BLK:(b + 1) * BLK], jj))
    P.dma("sp", k.dram("lru_out"), sto.r())
    P.barrier()
    P.release(mk)


NCH = 20
SEG_CH = [(0, 2), (2, 2), (4, 16)]


def seg_of_chunk(c):
    return 0 if c < 2 else (1 if c < 4 else 2)


def hch(k, kk, c):
    b = c // 4
    return k.h.r(k.h.t[:, kk, c * 128:(c + 1) * 128], kk * 5 + b)


def ssd(k, j):
    P = k.P
    P.barrier()
    mk = P.mark()
    y_tm = P.buf("s_ytm", [128, NCH, 512], BF16, nsub=NCH)
    xs_tm = P.buf("s_xstm", [128, NCH, 256], BF16, nsub=NCH)
    Bfm = P.buf("s_bfm", [128, NT], BF16)
    Cfm = P.buf("s_cfm", [128, NT], BF16)
    wz = P.buf("s_wz", [128, 8, 256], BF16)
    wdt = P.buf("s_wdt", [128, 8, 64], BF16)
    cw = P.buf("s_cw", [128, 24, 5], F32)
    rows = P.buf("s_rows", [128, 160], F32)
    nrm = P.buf("s_nrm", [128, 512], BF16)
    masks = P.buf("s_masks", [128, 2, 128], F32)
    negb = P.buf("s_negb", [128, 2, 128], BF16)
    onesf = P.buf("s_onesf", [128, 128], F32)
    one1 = P.buf("s_one1", [128, 1], F32)
    eps1 = P.buf("s_eps1", [128, 1], F32)
    sc = {nm: P.buf("s_" + nm, [128, NCH, 8], F32) for nm in ["dt", "da", "nacum", "eac", "cdec", "ce"]}
    sc["tmp"] = sc["ce"]
    fm_tmp = Cfm
    ssq = P.buf("s_ssq", [128, NCH, 2], F32)
    rstd = P.buf("s_rstd", [128, NCH], F32)
    cbT = P.buf("s_cbT", [128, 2, 2, 128], BF16, nsub=2)
    dec = P.buf("s_dec", [128, 2, 4, 128], BF16, nsub=2)
    ST = P.buf("s_ST", [128, 2, 256], F32, nsub=2)
    STb = P.buf("s_STb", [128, 2, 256], BF16, nsub=2)
    t1 = P.buf("s_t1", [128, 1, 256], F32, nsub=1)
    t2 = P.buf("s_t2", [128, 1, 256], F32, nsub=1)
    sz = t2
    sig = P.buf("s_sig", [128, BLK], BF16)
    ncb = P.buf("s_ncb", [128, 24], F32)
    junk = sig
    sto = P.buf("s_sto", [128, 2, 128], F32, nsub=2)

    P.dma("sp", cw.r(), k.dram("ssd_cw"))
    P.dma("sp", rows.r(), k.dram("ssd_rows"))
    P.dma("sp", masks.r(), R(k.dram("cmask").ap[0:2].rearrange("a p m -> p a m"), k.dram("cmask").keys))
    P.dma("pool", negb.r(), R(k.dram("cmask").ap[2:4].rearrange("a p m -> p a m"), k.dram("cmask").keys))
    P.memset("dve", onesf.r(), 1.0)
    P.memset("dve", one1.r(), 1.0)
    P.ts("dve", ncb.r(), cw.r(cw.t[:, :, 4]), -1.0, ALU.mult)
    P.memset("dve", eps1.r(), 1e-5)
    P.act(rows.r(rows.t[:, 64:128]), rows.r(rows.t[:, 64:128]), AF.Exp)
    P.ts("dve", rows.r(rows.t[:, 64:128]), rows.r(rows.t[:, 64:128]), -1.0, ALU.mult)
    src = k.dram("ssd_w_in").ap[j][:, 5120:5184].rearrange("(kk p) c -> p kk c", p=128)
    P.dma("pool", wdt.r(), R(src, k.dram("ssd_w_in").keys))

    loc = {}

    def conv_tile(col0, cidx, dst_writer):
        xpad, dgl = loc["xpad"], loc["dgl"]
        s, w = load_in_w(k, k.dram("ssd_w_in"), j, col0)
        i0 = k.identf.t[:].unsqueeze(1).to_broadcast([128, 4, 128])
        i1 = cw.t[:, cidx, 0:4].unsqueeze(2).to_broadcast([128, 4, 128])
        P.tt("pool", dgl.r(), k.identf.r(i0), cw.r(i1), ALU.mult)
        for b in range(NBLK):
            bk = P.bank()
            for kk in range(8):
                P.mm(psb(k, bk), k.wr.r(w[:, kk, :], s), hs(k, kk, b), start=(kk == 0), stop=(kk == 7))
            dst, _, _ = conv_dst(xpad.t, b)
            P.copy("dve", xpad.r(dst), k.ps.r(psview(k, bk, b), bk))
        for b in range(NBLK):
            bk = P.bank()
            for kk in range(4):
                P.mm(k.ps.r(psview(k, bk, b), bk), dgl.r(dgl.t[:, kk, :]), xpad.r(conv_rhs(xpad.t, b, kk)), start=(kk == 0), stop=(kk == 3))
            P.act(sig.r(), psb(k, bk), AF.Exp, bias=ncb.r(ncb.t[:, cidx:cidx + 1]), scale=-1.0)
            P.act(sig.r(), sig.r(), AF.Ln, bias=one1.r())
            P.act(sig.r(), sig.r(), AF.Exp, scale=-1.0)
            P.stt(dst_writer(b), psb(k, bk), cw.r(cw.t[:, cidx, 4:5]), sig.r(), ALU.add, ALU.mult)

    stop = k.cfg.get("ssd_stop", 99)
    for hg in range(k.cfg.get("ssd_nhg", 8)):
        g, half = hg // 2, hg % 2
        src = k.dram("ssd_w_in").ap[j][:, hg * 256:(hg + 1) * 256].rearrange("(kk p) c -> p kk c", p=128)
        P.dma("pool", wz.r(), R(src, k.dram("ssd_w_in").keys))
        if half == 0:
            P.dma("pool", nrm.r(), R(k.dram("ssd_nrm").ap[:, g * 512:(g + 1) * 512], k.dram("ssd_nrm").keys))
        if stop <= 0.5:
            continue
        bk = P.bank()
        for c in range(NCH):
            for kk in range(8):
                rhs = wdt.t[:, kk, :].rearrange("p (d h) -> p d h", d=2)[:, :, hg * 4:hg * 4 + 4]
                out = k.ps.t[:, bk, c * 8:(c + 1) * 8].rearrange("p (d h) -> p d h", d=2)
                P.mm(k.ps.r(out, bk), hch(k, kk, c), wdt.r(rhs), start=(kk == 0), stop=(kk == 7))
        psv = k.ps.t[:, bk, 0:NCH * 8].rearrange("p (c d h) -> p c d h", d=2, h=4)
        if stop <= 0.7:
            continue

        def rowbc(off):
            return rows.t[:, off:off + 64].rearrange("p (d h) -> p d h", d=2)[:, :, hg * 4:hg * 4 + 4].unsqueeze(1).to_broadcast([128, NCH, 2, 4])

        def v4(bf):
            return bf.t.rearrange("p c (d h) -> p c d h", d=2)
        tmp, dt, da = sc["tmp"], sc["dt"], sc["da"]
        P.tt("dve", tmp.r(v4(tmp)), k.ps.r(psv, bk), rows.r(rowbc(0)), ALU.add)
        P.ts("dve", dt.r(), tmp.r(), -1.0, ALU.mult)
        P.tt("dve", dt.r(), dt.r(), tmp.r(), ALU.max)
        P.act(dt.r(), dt.r(), AF.Exp, scale=-1.0)
        P.act(dt.r(), dt.r(), AF.Ln, bias=one1.r())
        P.ts("dve", tmp.r(), tmp.r(), 0.0, ALU.max)
        P.tt("dve", dt.r(), dt.r(), tmp.r(), ALU.add)
        P.tt("dve", da.r(v4(da)), dt.r(v4(dt)), rows.r(rowbc(64)), ALU.mult)
        if stop <= 0.8:
            continue
        bk2 = P.bank()
        bk3 = P.bank()
        for c in range(NCH):
            for d in range(2):
                P.mm(k.ps.r(k.ps.t[:, bk2, c * 8 + d * 4:c * 8 + d * 4 + 4], bk2), masks.r(masks.t[:, d, :]), da.r(da.t[:, c, d * 4:d * 4 + 4]))
            P.mm(k.ps.r(k.ps.t[:, bk3, c * 8:(c + 1) * 8], bk3), onesf.r(), da.r(da.t[:, c, :]))
        nacum, eac, cdec, ce = sc["nacum"], sc["eac"], sc["cdec"], sc["ce"]
        ps2 = k.ps.t[:, bk2, 0:NCH * 8].rearrange("p (c e) -> p c e", e=8)
        ps3 = k.ps.t[:, bk3, 0:NCH * 8].rearrange("p (c e) -> p c e", e=8)
        if stop <= 0.9:
            continue
        P.ts("dve", nacum.r(), k.ps.r(ps2, bk2), -1.0, ALU.mult)
        P.act(eac.r(), nacum.r(), AF.Exp, scale=-1.0)
        if stop <= 0.95:
            continue
        P.copy("dve", cdec.r(), k.ps.r(ps3, bk3))
        P.tt("dve", ce.r(), cdec.r(), nacum.r(), ALU.add)
        if stop <= 0.96:
            continue
        P.act(cdec.r(), cdec.r(), AF.Exp)
        if stop <= 0.97:
            continue
        P.act(ce.r(), ce.r(), AF.Exp)
        P.tt("dve", ce.r(), ce.r(), dt.r(), ALU.mult)
        if stop <= 1:
            continue
        P.barrier()
        mkp = P.mark()
        loc["xpad"] = P.buf("s_xpad", [128, CPW + 1], BF16)
        loc["dgl"] = P.buf("s_dg", [128, 4, 128], BF16)
        P.memset("pool", loc["xpad"].r(), 0.0)
        for ti in range(2):
            conv_tile(2048 + hg * 256 + ti * 128, hg * 2 + ti, lambda b: fm_tmp.r(fm_tmp.t[:, b * BLK:(b + 1) * BLK]))
            for c0 in range(0, NCH, 8):
                bk = P.bank()
                pb = k.ps.t[:, bk, :].bitcast(BF16)
                n = min(8, NCH - c0)
                for ci in range(n):
                    c = c0 + ci
                    P.tr(k.ps.r(pb[:, ci * 128:(ci + 1) * 128], bk), fm_tmp.r(fm_tmp.t[:, c * 128:(c + 1) * 128]), k.identb.r())
                dst = xs_tm.t[:, c0:c0 + n, ti * 128:(ti + 1) * 128]
                srcv = pb[:, 0:n * 128].rearrange("p (c m) -> p c m", m=128)
                P.copy("act", xs_tm.r(dst, range(c0, c0 + n)), k.ps.r(srcv, bk))
        conv_tile(2048 + 2048 + g * 128, 16 + g, lambda b: Bfm.r(Bfm.t[:, b * BLK:(b + 1) * BLK]))
        conv_tile(2048 + 2560 + g * 128, 20 + g, lambda b: Cfm.r(Cfm.t[:, b * BLK:(b + 1) * BLK]))
        if stop <= 2:
            continue
        P.barrier()
        P.release(mkp)
        mku = P.mark()
        Btm = P.buf("s_btm", [128, 4, 128], BF16, nsub=4)
        Mb = P.buf("s_M", [128, 4, 4, 128], BF16, nsub=4)
        xd = P.buf("s_xd", [128, 4, 256], BF16, nsub=4)
        xdt = P.buf("s_xdt", [128, 4, 256], BF16, nsub=4)
        nsteps = k.cfg.get("ssd_steps", NCH)

        def prelim_gen(step):
            banks = {}
            units = [(d, (step if d == 0 else NCH - 1 - step), (step % 2) * 2 + d) for d in range(2)]
            for (d, c, sl) in units:
                tok = slice(c * 128, (c + 1) * 128)
                bk = P.bank()
                banks[("cb", d)] = bk
                P.mm(k.ps.r(k.ps.t[:, bk, 0:128], bk), Bfm.r(Bfm.t[:, tok]), Cfm.r(Cfm.t[:, tok]))
                bkt = P.bank()
                banks[("bt", d)] = bkt
                pbb = k.ps.t[:, bkt, :].bitcast(BF16)
                P.tr(k.ps.r(pbb[:, 0:128], bkt), Bfm.r(Bfm.t[:, tok]), k.identb.r())
                yield
            for (d, c, sl) in units:
                bk = banks[("cb", d)]
                bkt = banks[("bt", d)]
                pbb = k.ps.t[:, bkt, :].bitcast(BF16)
                P.tt("dve", cbT.r(cbT.t[:, d, d], d), k.ps.r(k.ps.t[:, bk, 0:128], bk), masks.r(masks.t[:, d, :]), ALU.mult)
                P.copy("act", Btm.r(Btm.t[:, sl], sl), k.ps.r(pbb[:, 0:128], bkt))
                xsv = xs_tm.t[:, c, :].rearrange("p (q e) -> p q e", e=64)
                dtb = dt.t[:, c, d * 4:d * 4 + 4].unsqueeze(2).to_broadcast([128, 4, 64])
                ceb = ce.t[:, c, d * 4:d * 4 + 4].unsqueeze(2).to_broadcast([128, 4, 64])
                P.tt("pool", xd.r(xd.t[:, sl].rearrange("p (q e) -> p q e", e=64), sl), xs_tm.r(xsv, c), dt.r(dtb), ALU.mult)
                P.tt("pool", xdt.r(xdt.t[:, sl].rearrange("p (q e) -> p q e", e=64), sl), xs_tm.r(xsv, c), ce.r(ceb), ALU.mult)
                yield
            for (d, c, sl) in units:
                bk = P.bank()
                banks[("R", d)] = bk
                for q in range(4):
                    col = d * 4 + q
                    lhs = da.t[:, c, col:col + 1].to_broadcast([128, 128])
                    o_ = k.ps.r(k.ps.t[:, bk, q * 128:(q + 1) * 128], bk)
                    P.mm(o_, da.r(lhs), masks.r(masks.t[:, d, :]), start=True, stop=False)
                    P.mm(o_, k.identb.r(), negb.r(negb.t[:, d, :]), start=False, stop=True)
                yield
            for (d, c, sl) in units:
                bk = banks[("R", d)]
                for q in range(4):
                    col = d * 4 + q
                    P.act(dec.r(dec.t[:, d, q, :], d), k.ps.r(k.ps.t[:, bk, q * 128:(q + 1) * 128], bk), AF.Exp, bias=nacum.r(nacum.t[:, c, col:col + 1]))
                    yield
            for (d, c, sl) in units:
                cb_b = cbT.t[:, d, d].unsqueeze(1).to_broadcast([128, 4, 128])
                P.tt("dve", Mb.r(Mb.t[:, sl], sl), dec.r(dec.t[:, d], d), cbT.r(cb_b, d), ALU.mult)
                yield

        def chain_gen(step):
            for d in range(2):
                c = step if d == 0 else NCH - 1 - step
                sl = (step % 2) * 2 + d
                first_visit = (c <= 9) if d == 0 else (c >= 10)
                if nsteps < NCH:
                    first_visit = True
                seg = seg_of_chunk(c)
                c_first, c_n = SEG_CH[seg]
                seg_start = (c == c_first) if d == 0 else (c == c_first + c_n - 1)
                seg_end = (c == c_first + c_n - 1) if d == 0 else (c == c_first)
                tok = slice(c * 128, (c + 1) * 128)
                if seg_start:
                    if seg == 2:
                        P.dma("sp", ST.r(ST.t[:, d], d), R(k.dram("ssd_s0").ap[d, hg], k.dram("ssd_s0").keys))
                    else:
                        P.memset("dve", ST.r(ST.t[:, d], d), 0.0)
                    P.copy("act", STb.r(STb.t[:, d], d), ST.r(ST.t[:, d], d))
                    yield
                xsv = xs_tm.t[:, c, :].rearrange("p (q e) -> p q e", e=64)
                bkY = P.bank()
                for q in range(4):
                    P.mm(k.ps.r(k.ps.t[:, bkY, q * 64:(q + 1) * 64], bkY), Mb.r(Mb.t[:, sl, q, :], sl), xd.r(xd.t[:, sl, q * 64:(q + 1) * 64], sl))
                P.mm(k.ps.r(k.ps.t[:, bkY, 256:512], bkY), Cfm.r(Cfm.t[:, tok]), STb.r(STb.t[:, d], d))
                bkS = P.bank()
                P.mm(k.ps.r(k.ps.t[:, bkS, 0:256], bkS), Btm.r(Btm.t[:, sl], sl), xdt.r(xdt.t[:, sl], sl))
                yield
                cdb = cdec.t[:, c, d * 4:d * 4 + 4].unsqueeze(2).to_broadcast([128, 4, 64])
                STv = ST.t[:, d].rearrange("p (q e) -> p q e", e=64)
                P.tt("dve", ST.r(STv, d), ST.r(STv, d), cdec.r(cdb), ALU.mult)
                P.tt("dve", ST.r(ST.t[:, d], d), ST.r(ST.t[:, d], d), k.ps.r(k.ps.t[:, bkS, 0:256], bkS), ALU.add)
                yield
                P.copy("act", STb.r(STb.t[:, d], d), ST.r(ST.t[:, d], d))
                yield
                eab = eac.t[:, c, d * 4:d * 4 + 4].unsqueeze(2).to_broadcast([128, 4, 64])
                t1v = t1.t[:, 0].rearrange("p (q e) -> p q e", e=64)
                P.tt("dve", t1.r(t1v, 0), k.ps.r(k.ps.t[:, bkY, 256:512].rearrange("p (q e) -> p q e", e=64), bkY), eac.r(eab), ALU.mult)
                P.tt("dve", t1.r(t1.t[:, 0], 0), t1.r(t1.t[:, 0], 0), k.ps.r(k.ps.t[:, bkY, 0:256], bkY), ALU.add)
                yield
                ysl = y_tm.r(y_tm.t[:, c, half * 256:(half + 1) * 256], c)
                if first_visit:
                    P.copy("act", ysl, t1.r(t1.t[:, 0], 0))
                    yield
                else:
                    P.tt("dve", t1.r(t1.t[:, 0], 0), t1.r(t1.t[:, 0], 0), ysl, ALU.add)
                    dsb = rows.t[:, 128 + hg * 4:128 + hg * 4 + 4].unsqueeze(2).to_broadcast([128, 4, 64])
                    P.tt("pool", t2.r(t2.t[:, 0].rearrange("p (q e) -> p q e", e=64), 0), xs_tm.r(xsv, c), rows.r(dsb), ALU.mult)
                    yield
                    P.tt("dve", t1.r(t1.t[:, 0], 0), t1.r(t1.t[:, 0], 0), t2.r(t2.t[:, 0], 0), ALU.add)
                    bkZ = P.bank()
                    for kk in range(8):
                        P.mm(k.ps.r(k.ps.t[:, bkZ, 0:256], bkZ), hch(k, kk, c), wz.r(wz.t[:, kk, :]), start=(kk == 0), stop=(kk == 7))
                    yield
                    P.act(t2.r(t2.t[:, 0], 0), k.ps.r(k.ps.t[:, bkZ, 0:256], bkZ), AF.Exp, scale=-1.0)
                    P.act(t2.r(t2.t[:, 0], 0), t2.r(t2.t[:, 0], 0), AF.Ln, bias=one1.r())
                    P.act(t2.r(t2.t[:, 0], 0), t2.r(t2.t[:, 0], 0), AF.Exp, scale=-1.0)
                    yield
                    P.tt("dve", t1.r(t1.t[:, 0], 0), t1.r(t1.t[:, 0], 0), t2.r(t2.t[:, 0], 0), ALU.mult)
                    P.tt("dve", t1.r(t1.t[:, 0], 0), t1.r(t1.t[:, 0], 0), k.ps.r(k.ps.t[:, bkZ, 0:256], bkZ), ALU.mult)
                    yield
                    P.copy("dve", ysl, t1.r(t1.t[:, 0], 0))
                    o_, i_, a_ = junk.t[:, 0:256], t1.t[:, 0], ssq.t[:, c, half:half + 1]
                    P.op("act", lambda e, o_=o_, i_=i_, a_=a_: e.activation(out=o_, in_=i_, func=AF.Square, accum_out=a_),
                         reads=[t1.r(t1.t[:, 0], 0)], writes=[junk.r(), ssq.r()])
                    yield
                if seg_end and seg < 2:
                    for pr in range(2):
                        bk = P.bank()
                        P.tr(k.ps.r(k.ps.t[:, bk, 0:128], bk), ST.r(ST.t[:, d, pr * 128:(pr + 1) * 128], d), k.identf.r())
                        P.copy("dve", sto.r(sto.t[:, pr], pr), k.ps.r(k.ps.t[:, bk, 0:128], bk))
                        r0 = (hg * 4 + pr * 2) * 64
                        P.dma("sp", R(k.dram("ssd_out").ap[seg, d, r0:r0 + 128, :], k.dram("ssd_out").keys), sto.r(sto.t[:, pr], pr))
                    yield

        def run_all(gen):
            for _ in gen:
                pass

        def merge(pg, cg, ratio):
            pdone = cdone = False
            while not (pdone and cdone):
                if not cdone:
                    try:
                        next(cg)
                    except StopIteration:
                        cdone = True
                for _ in range(ratio):
                    if pdone:
                        break
                    try:
                        next(pg)
                    except StopIteration:
                        pdone = True

        run_all(prelim_gen(0))
        for step in range(nsteps):
            if step + 1 < nsteps:
                merge(prelim_gen(step + 1), chain_gen(step), k.cfg.get("ssd_ratio", 1))
            else:
                run_all(chain_gen(step))
        P.barrier()
        P.release(mku)
        if half == 1 and stop > 3:
            P.tt("dve", rstd.r(), ssq.r(ssq.t[:, :, 0]), ssq.r(ssq.t[:, :, 1]), ALU.add)
            P.act(rstd.r(), rstd.r(), AF.Ln, bias=eps1.r(), scale=1.0 / 512)
            P.act(rstd.r(), rstd.r(), AF.Exp, scale=-0.5)
            slots = [load_out_w(k, k.dram("ssd_w_out"), j, g * 512 + ti * 128) for ti in range(4)]
            ofm_t = xs_tm.t[:, 0:8, :].rearrange("p c e -> p (c e)").rearrange("p (t m) -> p t m", m=512)
            yn_t = xs_tm.t[:, 8:12, :].rearrange("p c e -> p (c e)").rearrange("p (t m) -> p t m", m=512)
            OK_ = list(range(0, 8))
            YK_ = list(range(8, 12))
            for b in range(NBLK):
                for ci in range(4):
                    c = b * 4 + ci
                    yb = ci % 2
                    P.stt(xs_tm.r(yn_t[:, yb], YK_), y_tm.r(y_tm.t[:, c, :], c), rstd.r(rstd.t[:, c:c + 1]), nrm.r(), ALU.mult, ALU.mult)
                    bk = P.bank()
                    pbb = k.ps.t[:, bk, :].bitcast(BF16)
                    for ti in range(4):
                        P.tr(k.ps.r(pbb[:, ti * 128:(ti + 1) * 128], bk), xs_tm.r(yn_t[:, yb, ti * 128:(ti + 1) * 128], YK_), k.identb.r())
                    srcv = pbb[:, 0:512].rearrange("p (t m) -> p t m", m=128)
                    P.copy("act", xs_tm.r(ofm_t[:, :, ci * 128:(ci + 1) * 128], OK_), k.ps.r(srcv, bk))
                for m in range(8):
                    bk = P.bank()
                    for ti in range(4):
                        s_, wo = slots[ti]
                        P.mm(psb(k, bk), k.wr.r(wo[:, m * 128:(m + 1) * 128], s_), xs_tm.r(ofm_t[:, ti, :], OK_), start=(ti == 0), stop=(ti == 3))
                    cc = bc(b)
                    P.stt(xs(k, m, b), psb(k, bk), k.mod.r(k.mod.t[:, Q_G1 + m, cc:cc + 1]), xs(k, m, b), ALU.mult, ALU.add)
    P.barrier()
    P.release(mk)


def gdn(k, j):
    P = k.P
    P.barrier()
    mk = P.mark()
    W_in = k.dram("gdn_w_in%d" % j)
    W_out = k.dram("gdn_w_out%d" % j)
    kq = P.buf("g_kq", [128, NCH, 2, 128], BF16, nsub=NCH)
    v_fm = P.buf("g_vfm", [128, NT], BF16, nsub=NCH)
    v_tm = P.buf("g_vtm", [128, NCH, 128], BF16, nsub=NCH)
    k_tm = P.buf("g_ktm", [128, NCH, 128], BF16, nsub=NCH)
    o_tm = P.buf("g_otm", [128, NCH, 128], BF16, nsub=NCH)
    wz = P.buf("g_wz", [128, 8, 128], BF16)
    wsm = P.buf("g_wsm", [128, 8, 32], BF16)
    cw = P.buf("g_cw", [128, 24, 4], F32)
    rows = P.buf("g_rows", [128, 160], F32)
    masks = P.buf("g_masks", [128, 2, 128], F32)
    negb = P.buf("g_negb", [128, 2, 128], BF16)
    lvl = P.buf("g_lvl", [128, 2, 7, 2, 128], BF16)
    I2 = P.buf("g_I2", [128, 2, 128], BF16)
    onesf = P.buf("g_onesf", [128, 128], F32)
    c_one = P.buf("g_c1", [128, 1], F32)
    c_eps6 = P.buf("g_c2", [128, 1], F32)
    c_eps6q = P.buf("g_c3", [128, 1], F32)
    c_eps5 = P.buf("g_c4", [128, 1], F32)
    sc = {nm: P.buf("g_" + nm, [128, NCH, 2], F32) for nm in ["beta", "g", "ngc", "negegc", "eout", "egl", "tmp"]}
    Rp = P.buf("g_Rp", [128, 2, 128], BF16, nsub=2)
    vn = P.buf("g_vn", [128, 2, 128], BF16, nsub=2)
    S = P.buf("g_S", [128, 2, 128], F32, nsub=2)
    Sb = P.buf("g_Sb", [128, 2, 128], BF16, nsub=2)
    ot = P.buf("g_ot", [128, 128], F32)
    sz = P.buf("g_sz", [128, 128], F32)
    ogb = P.buf("g_og", [128, 128], BF16)
    junk = P.buf("g_junk", [128, 128], BF16)
    ssq = P.buf("g_ssq", [128, 2], F32)

    P.dma("sp", cw.r(), R(k.dram("gdn_cw").ap[j], k.dram("gdn_cw").keys))
    P.dma("sp", rows.r(), R(k.dram("gdn_rows").ap[j], k.dram("gdn_rows").keys))
    P.dma("sp", masks.r(), R(k.dram("cmask").ap[0:2].rearrange("a p m -> p a m"), k.dram("cmask").keys))
    P.dma("pool", negb.r(), R(k.dram("cmask").ap[2:4].rearrange("a p m -> p a m"), k.dram("cmask").keys))
    for d in range(2):
        P.dma("pool", lvl.r(lvl.t[:, d]), R(k.dram("glvl").ap[d].rearrange("l a p m -> p l a m"), k.dram("glvl").keys))
    for a in range(2):
        P.copy("dve", I2.r(I2.t[:, a, :]), k.identf.r())
    P.memset("dve", onesf.r(), 1.0)
    P.memset("dve", c_one.r(), 1.0)
    P.memset("dve", c_eps6.r(), 1e-6)
    P.memset("dve", c_eps6q.r(), 128e-6)
    P.memset("dve", c_eps5.r(), 1e-5)
    P.act(rows.r(rows.t[:, 16:32]), rows.r(rows.t[:, 16:32]), AF.Exp)
    P.ts("dve", rows.r(rows.t[:, 16:32]), rows.r(rows.t[:, 16:32]), -1.0, ALU.mult)
    src = W_in.ap[0][:, 4096:4128].rearrange("(kk p) c -> p kk c", p=128)
    P.dma("pool", wsm.r(), R(src, W_in.keys))

    loc = {}

    def conv_tile(col0, cidx, post):
        xpad, dgl = loc["xpad"], loc["dgl"]
        s, w = load_in_w(k, W_in, 0, col0)
        i0 = k.identf.t[:].unsqueeze(1).to_broadcast([128, 4, 128])
        i1 = cw.t[:, cidx, 0:4].unsqueeze(2).to_broadcast([128, 4, 128])
        P.tt("pool", dgl.r(), k.identf.r(i0), cw.r(i1), ALU.mult)
        for b in range(NBLK):
            bk = P.bank()
            for kk in range(8):
                P.mm(psb(k, bk), k.wr.r(w[:, kk, :], s), hs(k, kk, b), start=(kk == 0), stop=(kk == 7))
            dst, _, _ = conv_dst(xpad.t, b)
            P.copy("dve", xpad.r(dst), k.ps.r(psview(k, bk, b), bk))
        for b in range(NBLK):
            bk = P.bank()
            for kk in range(4):
                P.mm(k.ps.r(psview(k, bk, b), bk), dgl.r(dgl.t[:, kk, :]), xpad.r(conv_rhs(xpad.t, b, kk)), start=(kk == 0), stop=(kk == 3))
            post(b, bk)

    def norm_post(which, scale, epsb):
        def post(b, bk):
            qf, sqb, rn = loc["qf"], loc["sqb"], loc["rn"]
            P.act(qf.r(), psb(k, bk), AF.Exp, scale=-1.0)
            P.act(qf.r(), qf.r(), AF.Ln, bias=c_one.r())
            P.act(qf.r(), qf.r(), AF.Exp, scale=-1.0)
            P.tt("dve", qf.r(), psb(k, bk), qf.r(), ALU.mult)
            P.act(sqb.r(), qf.r(), AF.Square)
            b2 = P.bank()
            P.mm(psb(k, b2), k.onesb.r(), sqb.r())
            P.act(rn.r(), psb(k, b2), AF.Ln, bias=epsb.r(), scale=scale)
            P.act(rn.r(), rn.r(), AF.Exp, scale=-0.5)
            dst = kq.t[:, 4 * b:4 * b + 4, which, :]
            P.tt("dve", kq.r(dst, range(4 * b, 4 * b + 4)), qf.r(qf.t.rearrange("p (c m) -> p c m", m=128)), rn.r(rn.t.rearrange("p (c m) -> p c m", m=128)), ALU.mult)
        return post

    def v_post(b, bk):
        qf = loc["qf"]
        P.act(qf.r(), psb(k, bk), AF.Exp, scale=-1.0)
        P.act(qf.r(), qf.r(), AF.Ln, bias=c_one.r())
        P.act(qf.r(), qf.r(), AF.Exp, scale=-1.0)
        P.tt("dve", v_fm.r(v_fm.t[:, b * BLK:(b + 1) * BLK], range(4 * b, 4 * b + 4)), psb(k, bk), qf.r(), ALU.mult)

    nheads = k.cfg.get("gdn_heads", 8)
    for hh in range(nheads):
        src = W_in.ap[0][:, 3072 + hh * 128:3072 + (hh + 1) * 128].rearrange("(kk p) c -> p kk c", p=128)
        P.dma("pool", wz.r(), R(src, W_in.keys))
        bk = P.bank()
        for c in range(NCH):
            for kk in range(8):
                rhs = wsm.t[:, kk, :].rearrange("p (t d h) -> p t d h", t=2, d=2)[:, :, :, hh]
                out = k.ps.t[:, bk, c * 4:(c + 1) * 4].rearrange("p (t d) -> p t d", t=2)
                P.mm(k.ps.r(out, bk), hch(k, kk, c), wsm.r(rhs), start=(kk == 0), stop=(kk == 7))
        psv = k.ps.t[:, bk, 0:NCH * 4].rearrange("p (c t d) -> p c t d", t=2, d=2)
        beta, g, ngc, negegc, eout, egl, tmp = sc["beta"], sc["g"], sc["ngc"], sc["negegc"], sc["eout"], sc["egl"], sc["tmp"]
        P.copy("dve", tmp.r(), k.ps.r(psv[:, :, 0, :], bk))
        P.act(beta.r(), tmp.r(), AF.Exp, scale=-1.0)
        P.act(beta.r(), beta.r(), AF.Ln, bias=c_one.r())
        P.act(beta.r(), beta.r(), AF.Exp, scale=-1.0)

        def rowbc(off):
            return rows.t[:, off:off + 16].rearrange("p (d h) -> p d h", d=2)[:, :, hh].unsqueeze(1).to_broadcast([128, NCH, 2])
        P.tt("dve", tmp.r(), k.ps.r(psv[:, :, 1, :], bk), rows.r(rowbc(0)), ALU.add)
        P.ts("dve", g.r(), tmp.r(), -1.0, ALU.mult)
        P.tt("dve", g.r(), g.r(), tmp.r(), ALU.max)
        P.act(g.r(), g.r(), AF.Exp, scale=-1.0)
        P.act(g.r(), g.r(), AF.Ln, bias=c_one.r())
        P.ts("dve", tmp.r(), tmp.r(), 0.0, ALU.max)
        P.tt("dve", g.r(), g.r(), tmp.r(), ALU.add)
        P.tt("dve", g.r(), g.r(), rows.r(rowbc(16)), ALU.mult)
        bk2 = P.bank()
        bk3 = P.bank()
        for c in range(NCH):
            for d in range(2):
                P.mm(k.ps.r(k.ps.t[:, bk2, c * 2 + d:c * 2 + d + 1], bk2), masks.r(masks.t[:, d, :]), g.r(g.t[:, c, d:d + 1]))
            P.mm(k.ps.r(k.ps.t[:, bk3, c * 2:c * 2 + 2], bk3), onesf.r(), g.r(g.t[:, c, :]))
        ps2 = k.ps.t[:, bk2, 0:NCH * 2].rearrange("p (c d) -> p c d", d=2)
        ps3 = k.ps.t[:, bk3, 0:NCH * 2].rearrange("p (c d) -> p c d", d=2)
        P.ts("dve", ngc.r(), k.ps.r(ps2, bk2), -1.0, ALU.mult)
        P.act(negegc.r(), ngc.r(), AF.Exp, scale=-1.0)
        P.ts("dve", negegc.r(), negegc.r(), -1.0, ALU.mult)
        P.copy("dve", egl.r(), k.ps.r(ps3, bk3))
        P.tt("dve", eout.r(), egl.r(), ngc.r(), ALU.add)
        P.act(eout.r(), eout.r(), AF.Exp)
        P.act(egl.r(), egl.r(), AF.Exp)
        P.barrier()
        mkp = P.mark()
        xpad = P.buf("g_xpad", [128, CPW + 1], BF16)
        dgl = P.buf("g_dg", [128, 4, 128], BF16)
        qf = P.buf("g_qf", [128, BLK], F32)
        sqb = P.buf("g_sq", [128, BLK], BF16)
        rn = P.buf("g_rn", [128, BLK], F32)
        loc.update(xpad=xpad, dgl=dgl, qf=qf, sqb=sqb, rn=rn)
        P.memset("pool", xpad.r(), 0.0)
        conv_tile(hh * 128, hh, norm_post(1, 128.0, c_eps6q))
        conv_tile(1024 + hh * 128, 8 + hh, norm_post(0, 1.0, c_eps6))
        conv_tile(2048 + hh * 128, 16 + hh, v_post)
        for (srcfn, dstb) in ((lambda c: v_fm.r(v_fm.t[:, c * 128:(c + 1) * 128], c), v_tm), (lambda c: kq.r(kq.t[:, c, 0, :], c), k_tm)):
            for c0 in range(0, NCH, 8):
                bk = P.bank()
                pb = k.ps.t[:, bk, :].bitcast(BF16)
                n = min(8, NCH - c0)
                for ci in range(n):
                    P.tr(k.ps.r(pb[:, ci * 128:(ci + 1) * 128], bk), srcfn(c0 + ci), k.identb.r())
                srcv = pb[:, 0:n * 128].rearrange("p (c m) -> p c m", m=128)
                P.copy("act", dstb.r(dstb.t[:, c0:c0 + n, :], range(c0, c0 + n)), k.ps.r(srcv, bk))
        P.barrier()
        P.release(mkp)
        mku = P.mark()
        GS = 3
        G = 2 * GS
        NR = 2 * G
        U = {}
        for nm in ["egcr", "dec", "NB", "NA"]:
            U[nm] = P.buf("gu_" + nm, [128, G, 128], BF16, nsub=G)
        U["Y"] = P.buf("gu_Y", [128, G, 2, 128], BF16, nsub=G)
        for nm in ["attn", "qin", "kout"]:
            U[nm] = P.buf("gu_" + nm, [128, NR, 128], BF16, nsub=NR)
        U["T"] = P.buf("gu_T", [128, NR, 2, 128], BF16, nsub=NR)
        RES = ("attn", "qin", "kout", "T")
        nsteps = k.cfg.get("gdn_steps", NCH)
        groups = [list(range(g0, min(g0 + GS, nsteps))) for g0 in range(0, nsteps, GS)]

        def ur(nm, ui, gi, sub=None):
            b_ = U[nm]
            si = (gi % 2) * G + ui if nm in RES else ui
            return b_.r(b_.t[:, si] if sub is None else b_.t[:, si, sub], si)

        def prelim_gen(gi):
            units = [(step, d) for step in groups[gi] for d in range(2)]
            cs = [(st_ if d_ == 0 else NCH - 1 - st_) for (st_, d_) in units]
            banks = {}
            for ui, (st_, d) in enumerate(units):
                c = cs[ui]
                bkR = P.bank()
                banks[("R", ui)] = bkR
                gbc = g.t[:, c, d:d + 1].to_broadcast([128, 128])
                r0 = k.ps.r(k.ps.t[:, bkR, 0:128], bkR)
                r1 = k.ps.r(k.ps.t[:, bkR, 128:256], bkR)
                P.mm(r0, g.r(gbc), masks.r(masks.t[:, d, :]))
                P.mm(r1, g.r(gbc), masks.r(masks.t[:, d, :]), start=True, stop=False)
                P.mm(r1, k.identb.r(), negb.r(negb.t[:, d, :]), start=False, stop=True)
                yield
            for ui, (st_, d) in enumerate(units):
                c = cs[ui]
                bkR = banks[("R", ui)]
                P.act(ur("egcr", ui, gi), k.ps.r(k.ps.t[:, bkR, 0:128], bkR), AF.Exp)
                P.act(ur("dec", ui, gi), k.ps.r(k.ps.t[:, bkR, 128:256], bkR), AF.Exp, bias=ngc.r(ngc.t[:, c, d:d + 1]))
                yield
            for ui, (st_, d) in enumerate(units):
                c = cs[ui]
                bkK = P.bank()
                banks[("K", ui)] = bkK
                P.mm(k.ps.r(k.ps.t[:, bkK, 0:256], bkK), kq.r(kq.t[:, c, 0, :], c), kq.r(kq.t[:, c, :, :].rearrange("p a m -> p (a m)"), c))
                yield
            for ui, (st_, d) in enumerate(units):
                c = cs[ui]
                bkK = banks[("K", ui)]
                P.stt(ur("NB", ui, gi), k.ps.r(k.ps.t[:, bkK, 0:128], bkK), beta.r(beta.t[:, c, d:d + 1]), ur("dec", ui, gi), ALU.mult, ALU.mult)
                P.tt("dve", ur("attn", ui, gi), k.ps.r(k.ps.t[:, bkK, 128:256], bkK), ur("dec", ui, gi), ALU.mult)
                yield
            for ui, (st_, d) in enumerate(units):
                bkT = P.bank()
                banks[("T", ui)] = bkT
                pbT = k.ps.t[:, bkT, :].bitcast(BF16)
                P.tr(k.ps.r(pbT[:, 0:128], bkT), ur("NB", ui, gi), k.identb.r())
                yield
            for ui, (st_, d) in enumerate(units):
                c = cs[ui]
                bkT = banks[("T", ui)]
                pbT = k.ps.t[:, bkT, :].bitcast(BF16)
                P.copy("act", ur("NA", ui, gi), k.ps.r(pbT[:, 0:128], bkT))
                P.tt("pool", ur("qin", ui, gi), kq.r(kq.t[:, c, 1, :], c), ur("egcr", ui, gi), ALU.mult)
                P.act(ur("kout", ui, gi), k_tm.r(k_tm.t[:, c, :], c), AF.Identity, scale=eout.r(eout.t[:, c, d:d + 1]))
                yield
            for ui, (st_, d) in enumerate(units):
                P.tt("pool", ur("Y", ui, gi, 0), ur("NA", ui, gi), lvl.r(lvl.t[:, d, 0, 0]), ALU.mult)
                P.tt("pool", ur("Y", ui, gi, 1), ur("NB", ui, gi), lvl.r(lvl.t[:, d, 0, 1]), ALU.mult)
                P.tt("pool", ur("T", ui, gi), I2.r(), ur("Y", ui, gi), ALU.add)
                yield
            for lv in range(1, 7):
                for ui, (st_, d) in enumerate(units):
                    bkY = P.bank()
                    banks[("Y", ui)] = bkY
                    P.mm(k.ps.r(k.ps.t[:, bkY, 0:128], bkY), ur("NB", ui, gi), ur("T", ui, gi, 0))
                    P.mm(k.ps.r(k.ps.t[:, bkY, 128:256], bkY), ur("NA", ui, gi), ur("T", ui, gi, 1))
                    yield
                for ui, (st_, d) in enumerate(units):
                    bkY = banks[("Y", ui)]
                    P.tt("dve", ur("Y", ui, gi), k.ps.r(k.ps.t[:, bkY, 0:256].rearrange("p (a m) -> p a m", m=128), bkY), lvl.r(lvl.t[:, d, lv]), ALU.mult)
                    yield
                for ui, (st_, d) in enumerate(units):
                    bkZ = P.bank()
                    banks[("Z", ui)] = bkZ
                    P.mm(k.ps.r(k.ps.t[:, bkZ, 0:128], bkZ), ur("T", ui, gi, 1), ur("Y", ui, gi, 0))
                    P.mm(k.ps.r(k.ps.t[:, bkZ, 128:256], bkZ), ur("T", ui, gi, 0), ur("Y", ui, gi, 1))
                    yield
                for ui, (st_, d) in enumerate(units):
                    bkZ = banks[("Z", ui)]
                    P.tt("dve", ur("T", ui, gi), ur("T", ui, gi), k.ps.r(k.ps.t[:, bkZ, 0:256].rearrange("p (a m) -> p a m", m=128), bkZ), ALU.add)
                    yield

        def chain_gen(gi):
            units = [(step, d) for step in groups[gi] for d in range(2)]
            for ui, (step, d) in enumerate(units):
                c = step if d == 0 else NCH - 1 - step
                first_visit = (c <= 9) if d == 0 else (c >= 10)
                if nsteps < NCH:
                    first_visit = True
                seg = seg_of_chunk(c)
                c_first, c_n = SEG_CH[seg]
                seg_start = (c == c_first) if d == 0 else (c == c_first + c_n - 1)
                seg_end = (c == c_first + c_n - 1) if d == 0 else (c == c_first)
                if seg_start:
                    if seg == 2:
                        P.dma("sp", S.r(S.t[:, d], d), R(k.dram("gdn_s0").ap[j, d, hh], k.dram("gdn_s0").keys))
                    else:
                        P.memset("dve", S.r(S.t[:, d], d), 0.0)
                    P.copy("act", Sb.r(Sb.t[:, d], d), S.r(S.t[:, d], d))
                    yield
                kc = kq.r(kq.t[:, c, 0, :], c)
                bkC = P.bank()
                P.mm(k.ps.r(k.ps.t[:, bkC, 0:128], bkC), kc, Sb.r(Sb.t[:, d], d))
                yield
                P.stt(Rp.r(Rp.t[:, d], d), k.ps.r(k.ps.t[:, bkC, 0:128], bkC), negegc.r(negegc.t[:, c, d:d + 1]), v_tm.r(v_tm.t[:, c, :], c), ALU.mult, ALU.add)
                yield
                bkV = P.bank()
                P.mm(k.ps.r(k.ps.t[:, bkV, 0:128], bkV), ur("T", ui, gi, 1), Rp.r(Rp.t[:, d], d))
                yield
                P.act(vn.r(vn.t[:, d], d), k.ps.r(k.ps.t[:, bkV, 0:128], bkV), AF.Identity, scale=beta.r(beta.t[:, c, d:d + 1]))
                yield
                bkS = P.bank()
                pS = k.ps.r(k.ps.t[:, bkS, 0:128], bkS)
                P.mm(pS, ur("kout", ui, gi), vn.r(vn.t[:, d], d))
                bkO = P.bank()
                po = k.ps.r(k.ps.t[:, bkO, 0:128], bkO)
                P.mm(po, ur("qin", ui, gi), Sb.r(Sb.t[:, d], d), start=True, stop=False)
                P.mm(po, ur("attn", ui, gi), vn.r(vn.t[:, d], d), start=False, stop=True)
                yield
                P.stt(S.r(S.t[:, d], d), S.r(S.t[:, d], d), egl.r(egl.t[:, c, d:d + 1]), pS, ALU.mult, ALU.add)
                yield
                P.copy("act", Sb.r(Sb.t[:, d], d), S.r(S.t[:, d], d))
                yield
                if first_visit:
                    P.copy("act", o_tm.r(o_tm.t[:, c, :], c), po)
                    yield
                else:
                    P.tt("dve", ot.r(), po, o_tm.r(o_tm.t[:, c, :], c), ALU.add)
                    yield
                    o_, i_, a_ = junk.t, ot.t, ssq.t[:, 0:1]
                    P.op("act", lambda e, o_=o_, i_=i_, a_=a_: e.activation(out=o_, in_=i_, func=AF.Square, accum_out=a_),
                         reads=[ot.r()], writes=[junk.r(), ssq.r()])
                    P.act(ssq.r(ssq.t[:, 1:2]), ssq.r(ssq.t[:, 0:1]), AF.Ln, bias=c_eps5.r(), scale=1.0 / 128)
                    P.act(ssq.r(ssq.t[:, 1:2]), ssq.r(ssq.t[:, 1:2]), AF.Exp, scale=-0.5)
                    yield
                    bkZ = P.bank()
                    for kk in range(8):
                        P.mm(k.ps.r(k.ps.t[:, bkZ, 0:128], bkZ), hch(k, kk, c), wz.r(wz.t[:, kk, :]), start=(kk == 0), stop=(kk == 7))
                    yield
                    P.act(sz.r(), k.ps.r(k.ps.t[:, bkZ, 0:128], bkZ), AF.Exp, scale=-1.0)
                    P.act(sz.r(), sz.r(), AF.Ln, bias=c_one.r())
                    P.act(sz.r(), sz.r(), AF.Exp, scale=-1.0)
                    P.stt(ot.r(), ot.r(), ssq.r(ssq.t[:, 1:2]), rows.r(rows.t[:, 32:160]), ALU.mult, ALU.mult)
                    yield
                    P.tt("dve", ot.r(), ot.r(), sz.r(), ALU.mult)
                    P.tt("dve", ogb.r(), ot.r(), k.ps.r(k.ps.t[:, bkZ, 0:128], bkZ), ALU.mult)
                    yield
                    bkT2 = P.bank()
                    pbT2 = k.ps.t[:, bkT2, :].bitcast(BF16)
                    P.tr(k.ps.r(pbT2[:, 0:128], bkT2), ogb.r(), k.identb.r())
                    yield
                    P.copy("act", v_fm.r(v_fm.t[:, c * 128:(c + 1) * 128], c), k.ps.r(pbT2[:, 0:128], bkT2))
                    yield
                if seg_end and seg < 2:
                    P.dma("sp", R(k.dram("gdn_out").ap[j, seg, d, hh], k.dram("gdn_out").keys), S.r(S.t[:, d], d))

        def run_all(gen):
            for _ in gen:
                pass

        def merge(pg, cg, ratio):
            pdone = cdone = False
            while not (pdone and cdone):
                if not cdone:
                    try:
                        next(cg)
                    except StopIteration:
                        cdone = True
                for _ in range(ratio):
                    if pdone:
                        break
                    try:
                        next(pg)
                    except StopIteration:
                        pdone = True

        run_all(prelim_gen(0))
        for gi in range(len(groups)):
            if gi + 1 < len(groups):
                merge(prelim_gen(gi + 1), chain_gen(gi), k.cfg.get("gdn_ratio", 3))
            else:
                run_all(chain_gen(gi))
        P.barrier()
        P.release(mku)
        if nsteps == NCH:
            outproj_acc_g(k, W_out, hh * 128, lambda b: v_fm.r(v_fm.t[:, b * BLK:(b + 1) * BLK], range(4 * b, 4 * b + 4)))
    P.barrier()
    P.release(mk)


def outproj_acc_g(k, W, r0, o_region):
    P = k.P
    s, wo = load_out_w(k, W, 0, r0)
    for m in range(8):
        for b in range(NBLK):
            bk = P.bank()
            P.mm(psb(k, bk), k.wr.r(wo[:, m * 128:(m + 1) * 128], s), o_region(b))
            c = bc(b)
            P.stt(xs(k, m, b), psb(k, bk), k.mod.r(k.mod.t[:, Q_G1 + m, c:c + 1]), xs(k, m, b), ALU.mult, ALU.add)


NCORES = 8


def f32(a):
    return np.ascontiguousarray(np.asarray(a, dtype=np.float32))


def prep_inputs(inp):
    g = {k: np.asarray(v) for k, v in inp.items()}
    DEPTH = 4
    shared = {}
    shared["ident"] = np.eye(128, dtype=np.float32)
    for L in range(DEPTH):
        shared["ada_w%d" % L] = f32(g["ada_w"][L:L + 1])
        shared["ffn_w_in%d" % L] = f32(g["ffn_w_in"][L:L + 1])
        shared["ffn_w_out%d" % L] = f32(g["ffn_w_out"][L:L + 1])
    shared["ada_b"] = f32(g["ada_b"].reshape(DEPTH, 48, 128).transpose(0, 2, 1))
    shared["lng"] = f32(g["ln_g"].reshape(DEPTH, 2, 8, 128).transpose(0, 1, 3, 2))
    shared["lnb"] = f32(g["ln_b"].reshape(DEPTH, 2, 8, 128).transpose(0, 1, 3, 2))
    shared["fcw"] = f32(g["ffn_conv"].reshape(DEPTH, 9, 44, 128).transpose(0, 3, 2, 1))
    shared["lru_w_in"] = f32(g["lru_w_in"])
    shared["lru_w_out"] = f32(g["lru_w_out"])
    shared["lru_gate_w"] = f32(g["lru_gate_w"])
    sm = np.zeros((128, 10, 12), np.float32)
    sm[:, :, 0:4] = g["lru_conv"][0].reshape(4, 10, 128).transpose(2, 1, 0)
    sm[:, :, 4] = g["lru_conv_b"][0].reshape(10, 128).T
    sm[:, :, 5:9] = g["lru_gate_b"][0].reshape(4, 10, 128).transpose(2, 1, 0)
    sm[:, :, 9:11] = g["lru_lambda"][0].reshape(2, 10, 128).transpose(2, 1, 0)
    shared["lru_sm"] = sm
    tri_f = np.triu(np.ones((128, 128), np.float32))
    tri_b = np.tril(np.ones((128, 128), np.float32))
    shared["cmask"] = np.stack([tri_f, tri_b, (1 - tri_f) * -30000.0, (1 - tri_b) * -30000.0]).astype(np.float32)
    shared["ssd_w_in"] = f32(g["ssd_w_in"])
    shared["ssd_w_out"] = f32(g["ssd_w_out"])
    cw = np.zeros((128, 24, 5), np.float32)
    cw[:, :, 0:4] = g["ssd_conv"][0].reshape(4, 24, 128).transpose(2, 1, 0)
    cw[:, :, 4] = g["ssd_conv_b"][0].reshape(24, 128).T
    shared["ssd_cw"] = cw
    row = np.concatenate([g["ssd_dt_bias"][0].reshape(64), g["ssd_a_log"][0].reshape(64), g["ssd_d"][0].reshape(32)])
    shared["ssd_rows"] = f32(np.broadcast_to(row[None, :], (128, 160)))
    shared["ssd_nrm"] = f32(np.broadcast_to(g["ssd_norm"][0][None, :], (128, 2048)))
    for jj in range(2):
        shared["gdn_w_in%d" % jj] = f32(g["gdn_w_in"][jj:jj + 1])
        shared["gdn_w_out%d" % jj] = f32(g["gdn_w_out"][jj:jj + 1])
    shared["gdn_cw"] = f32(g["gdn_conv"].reshape(2, 4, 24, 128).transpose(0, 3, 2, 1))
    grow = np.concatenate([g["gdn_dt_bias"].reshape(2, 16), g["gdn_a_log"].reshape(2, 16), g["gdn_norm"].reshape(2, 128)], axis=1)
    shared["gdn_rows"] = f32(np.broadcast_to(grow[:, None, :], (2, 128, 160)))
    li = np.arange(128)[:, None]
    si = np.arange(128)[None, :]
    lv = np.zeros((2, 7, 2, 128, 128), np.float32)
    for jl in range(7):
        Bs = 2 ** jl
        mA = ((li // (2 * Bs)) == (si // (2 * Bs))) & ((li % (2 * Bs)) >= Bs) & ((si % (2 * Bs)) < Bs)
        mA = mA.astype(np.float32)
        lv[0, jl, 0] = -mA
        lv[0, jl, 1] = -mA.T
        lv[1, jl, 0] = -mA.T
        lv[1, jl, 1] = -mA
    shared["glvl"] = lv
    maps = []
    for i in range(NCORES):
        p0, p1, sb = 2 * i, 2 * i + 1, i % 4
        xin = np.concatenate([g["x_prompt"][p0].T, g["x_prompt"][p1].T, g["x_sample"][sb].T], axis=1)
        cond = np.stack([g["c_ctx"].reshape(8, 128).T, g["c"][sb].reshape(8, 128).T], axis=-1)
        m = dict(shared)
        m["xin"] = f32(xin)
        m["cond"] = f32(cond)
        m["ssd_s0"] = f32(g["state_ssd"][sb, 0].reshape(2, 8, 4, 64, 128).transpose(0, 1, 4, 2, 3).reshape(2, 8, 128, 256))
        m["gdn_s0"] = f32(g["state_gdn"][sb])
        m["lru_s0"] = f32(g["state_lru"][sb, 0].reshape(2, 10, 128).transpose(2, 1, 0))
        maps.append(m)
    return maps


def assemble(results, inp):
    BATCH, SEQ, D = 16, 256, 1024
    yp = np.zeros((BATCH, SEQ, D), np.float32)
    ys = np.zeros((4, 2048, D), np.float32)
    for i in range(NCORES):
        y = np.asarray(results[i]["yout"])
        yp[2 * i] = y[:, 0:256].T
        yp[2 * i + 1] = y[:, 256:512].T
        if i < 4:
            ys[i] = y[:, 512:].T
    nl = np.zeros((BATCH, 1, 2, 1280), np.float32)
    for i in range(NCORES):
        if "lru_out" in results[i]:
            o = np.asarray(results[i]["lru_out"])
            for pi in range(2):
                nl[2 * i + pi, 0] = o[:, :, pi, :].transpose(2, 1, 0).reshape(2, 1280)
    nssd = np.zeros((BATCH, 1, 2, 32, 64, 128), np.float32)
    for i in range(NCORES):
        if "ssd_out" in results[i]:
            o = np.asarray(results[i]["ssd_out"])
            for pi in range(2):
                nssd[2 * i + pi, 0] = o[pi].reshape(2, 32, 64, 128)
    ngdn = np.zeros((BATCH, 2, 2, 8, 128, 128), np.float32)
    for i in range(NCORES):
        if "gdn_out" in results[i]:
            o = np.asarray(results[i]["gdn_out"])
            for pi in range(2):
                ngdn[2 * i + pi] = o[:, pi]
    return yp, ys, nl, nssd, ngdn


_NC_CACHE = {}


def kernel(**inputs):
    cfg = {}
    if "nc" not in _NC_CACHE:
        _NC_CACHE["nc"] = build(cfg)
    nc, used = _NC_CACHE["nc"]
    maps = prep_inputs(inputs)
    maps = [{kk: v for kk, v in mm.items() if kk in used} for mm in maps]
    res = run_bass_kernel_spmd(nc, maps, core_ids=list(range(NCORES)))
    yp, ys, nl, nssd, ngdn = assemble(res.results, inputs)
    return (yp, ys, ngdn, nssd, nl)
```

```python
import numpy as np
import concourse.bass as bass
import concourse.mybir as mybir
from concourse.bass_utils import run_bass_kernel_spmd
from contextlib import ExitStack

F32 = mybir.dt.float32
F32R = mybir.dt.float32r
BF16 = mybir.dt.bfloat16
ALU = mybir.AluOpType
AF = mybir.ActivationFunctionType
AX = mybir.AxisListType

ENGS = ["pe", "act", "dve", "pool", "sp"]
NDS = 24
MAXEMB = 1
ARENA_WORDS = 53000


class R:
    __slots__ = ("ap", "keys")

    def __init__(self, ap, keys):
        self.ap = ap
        self.keys = keys


class Buf:
    def __init__(self, prog, name, shape, dtype, nsub=1, psum=False):
        self.name = name
        self.nsub = nsub
        if psum:
            self.t = prog.st.enter_context(prog.nc.psum_tensor(name, shape, dtype))
        else:
            n = 1
            for d in shape[1:]:
                n *= d
            esz = 2 if dtype == BF16 else 4
            words = (n * esz + 3) // 4
            off = prog.aoff
            prog.aoff += words
            assert prog.aoff <= ARENA_WORDS, (name, prog.aoff)
            prog.apeak = max(prog.apeak, prog.aoff)
            ap = prog.arena[:, off:off + words]
            if dtype == BF16:
                ap = ap.bitcast(BF16)
            ap = ap[:, 0:n]
            if len(shape) > 2:
                names = " ".join("d%d" % i for i in range(len(shape) - 1))
                kw = {"d%d" % i: shape[i + 1] for i in range(len(shape) - 1)}
                ap = ap.rearrange("p (%s) -> p %s" % (names, names), **kw)
            self.t = ap

    def r(self, ap=None, sub=None):
        if ap is None:
            ap = self.t[:] if not hasattr(self.t, "rearrange") else self.t
        if sub is None:
            keys = [(self.name, i) for i in range(self.nsub)]
        elif isinstance(sub, (list, tuple, range)):
            keys = [(self.name, i) for i in sub]
        else:
            keys = [(self.name, sub)]
        return R(ap, keys)


class Prog:
    def __init__(self, nc, st):
        self.nc = nc
        self.st = st
        self.q = {e: [] for e in ENGS}
        self.cnt = {e: 0 for e in ENGS}
        self.sem = {e: st.enter_context(nc.semaphore("s_" + e)) for e in ENGS}
        self.dsem = [st.enter_context(nc.semaphore("d%d" % i)) for i in range(NDS)]
        self.dcnt = [0] * NDS
        self.dnext = 0
        self.known = {e: {} for e in ENGS}
        self.last_w = {}
        self.readers = {}
        self.nops = 0
        self.nwaits = 0
        self.bar = {e: None for e in ENGS}
        self._bank = 0
        self.arena_t = st.enter_context(nc.sbuf_tensor("arena", [128, ARENA_WORDS], F32))
        self.arena = self.arena_t[:]
        self.aoff = 0
        self.apeak = 0

    def mark(self):
        return self.aoff

    def release(self, m):
        self.aoff = m

    def bank(self):
        b = self._bank
        self._bank = (self._bank + 1) % 8
        return b

    def barrier(self):
        snap = {e: self.cnt[e] for e in ENGS if self.cnt[e] > 0}
        for i in range(NDS):
            if self.dcnt[i] > 0:
                snap["d%d" % i] = self.dcnt[i]
        for e in ENGS:
            self.bar[e] = dict(snap)

    def buf(self, name, shape, dtype, nsub=1, psum=False):
        return Buf(self, name, shape, dtype, nsub, psum)

    def _collect(self, eng, reads, writes):
        need = {}

        def add(tok):
            if tok is None:
                return
            semid, val, snap = tok
            if eng == "pe" and semid == "pe":
                return
            if need.get(semid, (0, None))[0] < val:
                need[semid] = (val, snap)

        for k in reads:
            add(self.last_w.get(k))
        for k in writes:
            add(self.last_w.get(k))
            rd = self.readers.get(k)
            if rd:
                for tok in rd.values():
                    add(tok)
        kn = self.known[eng]
        if self.bar[eng] is not None:
            for semid, val in self.bar[eng].items():
                if eng == "pe" and semid == "pe":
                    continue
                if semid == eng and val >= self.cnt[eng] + 1:
                    continue
                if need.get(semid, (0, None))[0] < val:
                    need[semid] = (val, None)
            self.bar[eng] = None
        waits = []
        for semid, (val, snap) in need.items():
            if kn.get(semid, 0) >= val:
                continue
            waits.append((semid, val))
        for semid, (val, snap) in need.items():
            if kn.get(semid, 0) < val:
                kn[semid] = val
            if snap:
                for s2, v2 in snap.items():
                    if kn.get(s2, 0) < v2:
                        kn[s2] = v2
        return waits

    def _commit(self, tok, reads, writes):
        for k in writes:
            self.last_w[k] = tok
            self.readers[k] = {}
        for k in reads:
            if k in writes:
                continue
            self.readers.setdefault(k, {})[tok[0]] = tok

    def op(self, eng, fn, reads=(), writes=()):
        rk = [k for r in reads if r is not None for k in r.keys]
        wk = [k for r in writes if r is not None for k in r.keys]
        if eng != "pe":
            for k_ in rk:
                if k_[0] == "ps" and k_ not in wk:
                    wk.append(k_)
        waits = self._collect(eng, rk, wk)
        self.cnt[eng] += 1
        tok = (eng, self.cnt[eng], dict(self.known[eng]))
        self.q[eng].append((waits, fn, None))
        self._commit(tok, rk, wk)
        self.nops += 1
        self.nwaits += len(waits)
        return tok

    def dma(self, eng, out, in_, **kw):
        rk = list(in_.keys)
        wk = list(out.keys)
        i = self.dnext
        self.dnext = (self.dnext + 1) % NDS
        semid = "d%d" % i
        waits = self._collect(eng, rk, wk)
        kn = self.known[eng]
        if kn.get(semid, 0) < self.dcnt[i]:
            waits.append((semid, self.dcnt[i]))
            kn[semid] = self.dcnt[i]
        self.dcnt[i] += 16
        tok = (semid, self.dcnt[i], dict(kn))
        oap, iap = out.ap, in_.ap

        def fn(e):
            return e.dma_start(out=oap, in_=iap, **kw)

        self.q[eng].append((waits, fn, i))
        self._commit(tok, rk, wk)
        self.nops += 1
        self.nwaits += len(waits)
        return tok

    def _semh(self, semid):
        if semid in self.sem:
            return self.sem[semid]
        return self.dsem[int(semid[1:])]

    def emit(self, final_wait_eng="sp"):
        nc = self.nc
        finals = []
        for e in ENGS:
            if self.cnt[e] > 0:
                finals.append((e, self.cnt[e]))
        for i in range(NDS):
            if self.dcnt[i] > 0:
                finals.append(("d%d" % i, self.dcnt[i]))
        eng_objs = {}
        with nc.Block() as block:
            def mk(ename):
                def body(e):
                    for waits, fn, dsi in self.q[ename]:
                        if dsi is not None or len(waits) > MAXEMB:
                            for semid, val in waits:
                                e.wait_ge(self._semh(semid), val)
                            ins = fn(e)
                        else:
                            ins = fn(e)
                            for semid, val in waits:
                                ins._wait_ge(self._semh(semid), val)
                        if dsi is None:
                            ins.then_inc(self.sem[ename], 1)
                        else:
                            ins.then_inc(self.dsem[dsi], 16)
                    if ename == final_wait_eng:
                        for semid, val in finals:
                            e.wait_ge(self._semh(semid), val)
                return body
            block.tensor(mk("pe"))
            block.scalar(mk("act"))
            block.vector(mk("dve"))
            block.gpsimd(mk("pool"))
            block.sync(mk("sp"))

    def mm(self, out, lhsT, rhs, start=True, stop=True, extra_reads=()):
        o, l, r = out.ap, lhsT.ap, rhs.ap
        return self.op("pe", lambda e: e.matmul(o, l, r, start=start, stop=stop),
                       reads=[lhsT, rhs] + list(extra_reads) + ([] if start else [out]), writes=[out])

    def tr(self, out, in_, ident):
        o, i, d = out.ap, in_.ap, ident.ap
        return self.op("pe", lambda e: e.transpose(o, i, d), reads=[in_, ident], writes=[out])

    def act(self, out, in_, func, bias=None, scale=None, eng="act"):
        o, i = out.ap, in_.ap
        kw = {}
        rd = [in_]
        if bias is not None:
            if isinstance(bias, R):
                kw["bias"] = bias.ap
                rd.append(bias)
            else:
                kw["bias"] = float(bias)
        if scale is not None:
            if isinstance(scale, R):
                kw["scale"] = scale.ap
                rd.append(scale)
            else:
                kw["scale"] = float(scale)
        return self.op("act", lambda e: e.activation(out=o, in_=i, func=func, **kw), reads=rd, writes=[out])

    def tt(self, eng, out, in0, in1, op):
        o, a, b = out.ap, in0.ap, in1.ap
        return self.op(eng, lambda e: e.tensor_tensor(out=o, in0=a, in1=b, op=op), reads=[in0, in1], writes=[out])

    def ts(self, eng, out, in0, s1, op0, s2=None, op1=None):
        o, a = out.ap, in0.ap
        rd = [in0]
        v1 = s1
        if isinstance(s1, R):
            rd.append(s1)
            v1 = s1.ap
        v2 = s2
        if isinstance(s2, R):
            rd.append(s2)
            v2 = s2.ap
        if op1 is None:
            return self.op(eng, lambda e: e.tensor_scalar(out=o, in0=a, scalar1=v1, scalar2=None, op0=op0), reads=rd, writes=[out])
        return self.op(eng, lambda e: e.tensor_scalar(out=o, in0=a, scalar1=v1, scalar2=v2, op0=op0, op1=op1), reads=rd, writes=[out])

    def stt(self, out, in0, scalar, in1, op0, op1):
        o, a, b = out.ap, in0.ap, in1.ap
        rd = [in0, in1]
        sv = scalar
        if isinstance(scalar, R):
            rd.append(scalar)
            sv = scalar.ap
        return self.op("dve", lambda e: e.scalar_tensor_tensor(out=o, in0=a, scalar=sv, in1=b, op0=op0, op1=op1), reads=rd, writes=[out])

    def copy(self, eng, out, in_):
        o, i = out.ap, in_.ap
        if eng == "act":
            return self.op("act", lambda e: e.copy(out=o, in_=i), reads=[in_], writes=[out])
        return self.op(eng, lambda e: e.tensor_copy(out=o, in_=i), reads=[in_], writes=[out])

    def memset(self, eng, out, val):
        o = out.ap
        return self.op(eng, lambda e: e.memset(o, val), reads=[], writes=[out])


D = 1024
NT = 2560
NBLK = 5
BLK = 512
DEPTH = 4
FH = 2816
NPAIR = 22
UPW = 2 * 258 + 34 * 66
ALPHA = (2 * DEPTH) ** 0.25
LN_EPS = 1e-5
NW = 5
WSL = 1024


def bc(b):
    return 0 if b == 0 else 1


class K:
    pass


def build(cfg):
    nc = bass.Bass("TRN2", target_bir_lowering=False)
    k = K()
    k.nc = nc
    k.cfg = cfg

    shapes = {
        "xin": [D, NT], "cond": [128, 8, 2], "ident": [128, 128],
        "ada_b": [DEPTH, 128, 48], "lng": [DEPTH, 2, 128, 8], "lnb": [DEPTH, 2, 128, 8],
        "fcw": [DEPTH, 128, 44, 9],
        "lru_w_in": [1, D, 2560], "lru_w_out": [1, 1280, D], "lru_gate_w": [1, 2, 2, 10, 128, 128],
        "lru_sm": [128, 10, 12], "lru_s0": [128, 10, 2],
        "cmask": [4, 128, 128],
        "ssd_w_in": [1, D, 5184], "ssd_w_out": [1, 2048, D], "ssd_cw": [128, 24, 5], "ssd_rows": [128, 160],
        "ssd_nrm": [128, 2048], "ssd_s0": [2, 8, 128, 256],
    }
    for jj in range(2):
        shapes["gdn_w_in%d" % jj] = [1, D, 4128]
        shapes["gdn_w_out%d" % jj] = [1, D, D]
    shapes.update({"gdn_cw": [2, 128, 24, 4], "gdn_rows": [2, 128, 160], "gdn_s0": [2, 2, 8, 128, 128], "glvl": [2, 7, 2, 128, 128]})
    for L in range(DEPTH):
        shapes["ada_w%d" % L] = [1, D, 6 * D]
        shapes["ffn_w_in%d" % L] = [1, D, 2 * FH]
        shapes["ffn_w_out%d" % L] = [1, FH, D]
    oshapes = {"yout": [D, NT], "lru_out": [128, 10, 2, 2], "ssd_out": [2, 2, 2048, 128], "gdn_out": [2, 2, 2, 8, 128, 128]}
    k.shapes = shapes
    k.oshapes = oshapes
    k.decl = {}

    def dram(name):
        if name in k.decl:
            return k.decl[name]
        if name in shapes:
            t = nc.dram_tensor(name, list(shapes[name]), F32, kind="ExternalInput").ap()
        else:
            t = nc.dram_tensor(name, list(oshapes[name]), F32, kind="ExternalOutput").ap()
        k.decl[name] = R(t, [("dram_" + name, 0)])
        return k.decl[name]
    k.dram = dram

    with ExitStack() as st:
        P = Prog(nc, st)
        k.P = P
        k.x = P.buf("x", [128, 8, NT], F32, nsub=40)
        k.h = P.buf("h", [128, 8, NT], BF16, nsub=40)
        k.wr = P.buf("wr", [128, NW, WSL], BF16, nsub=NW)
        k.wnext = 0
        k.ps = P.buf("ps", [128, 8, 512], F32, nsub=8, psum=True)
        k.identf = P.buf("identf", [128, 128], F32)
        k.identb = P.buf("identb", [128, 128], BF16)
        k.onesb = P.buf("onesb", [128, 128], BF16)
        k.csil = P.buf("csil", [128, 8, 2], BF16)
        k.mod = P.buf("mod", [128, 48, 2], F32)
        k.msc = P.buf("msc", [128, 2, 8, 2], F32)
        k.lngb = P.buf("lngb", [128, DEPTH, 2, 8], F32)
        k.lnbb = P.buf("lnbb", [128, DEPTH, 2, 8], F32)
        k.adab = P.buf("adab", [128, DEPTH, 48], F32)

        prologue(k)
        for L in cfg.get("layers", list(range(DEPTH))):
            layer(k, L)
        for j in range(8):
            P.dma("sp", R(k.dram("yout").ap[j * 128:(j + 1) * 128, :], k.dram("yout").keys), k.x.r(k.x.t[:, j, :], range(j * 5, j * 5 + 5)))
        P.emit()
        print("ops", P.nops, "waits", P.nwaits, {e: P.cnt[e] for e in ENGS}, "apeak", P.apeak)
    return nc, set(n for n in k.decl if n in k.shapes)


def xs(k, j, b):
    return k.x.r(k.x.t[:, j, b * BLK:(b + 1) * BLK], j * 5 + b)


def hs(k, j, b):
    return k.h.r(k.h.t[:, j, b * BLK:(b + 1) * BLK], j * 5 + b)


def psb(k, b, n=512):
    return k.ps.r(k.ps.t[:, b, 0:n], b)


def prologue(k):
    P = k.P
    for j in range(8):
        P.dma("sp", k.x.r(k.x.t[:, j, :], range(j * 5, j * 5 + 5)), R(k.dram("xin").ap[j * 128:(j + 1) * 128, :], k.dram("xin").keys))
    P.dma("sp", k.identf.r(), k.dram("ident"))
    P.copy("dve", k.identb.r(), k.identf.r())
    P.memset("dve", k.onesb.r(), 1.0)
    ctmp = P.buf("ctmp", [128, 8, 2], F32)
    P.dma("sp", ctmp.r(), k.dram("cond"))
    P.act(k.csil.r(), ctmp.r(), AF.Silu)
    for L in range(DEPTH):
        P.dma("sp", k.lngb.r(k.lngb.t[:, L]), R(k.dram("lng").ap[L].rearrange("s p j -> p s j"), k.dram("lng").keys))
        P.dma("sp", k.lnbb.r(k.lnbb.t[:, L]), R(k.dram("lnb").ap[L].rearrange("s p j -> p s j"), k.dram("lnb").keys))
        P.dma("sp", k.adab.r(k.adab.t[:, L]), R(k.dram("ada_b").ap[L], k.dram("ada_b").keys))


def wslot(k):
    s = k.wnext
    k.wnext = (k.wnext + 1) % NW
    return s


def load_in_w(k, W, L, c0, ncols=128):
    P = k.P
    s = wslot(k)
    src = W.ap[L][:, c0:c0 + ncols].rearrange("(kk p) c -> p kk c", p=128)
    dst = k.wr.t[:, s, 0:8 * ncols].rearrange("p (kk c) -> p kk c", c=ncols)
    P.dma("pool", k.wr.r(dst, s), R(src, W.keys))
    return s, dst


def load_out_w(k, W, L, r0):
    P = k.P
    s = wslot(k)
    src = W.ap[L][r0:r0 + 128, :]
    dst = k.wr.t[:, s, 0:1024]
    P.dma("pool", k.wr.r(dst, s), R(src, W.keys))
    return s, dst


def ada(k, L):
    P = k.P
    b = P.bank()
    for q in range(48):
        s, w = load_in_w(k, k.dram("ada_w%d" % L), 0, q * 128)
        for kk in range(8):
            P.mm(k.ps.r(k.ps.t[:, b, q * 2:q * 2 + 2], b), k.wr.r(w[:, kk, :], s),
                 k.csil.r(k.csil.t[:, kk, :]), start=(kk == 0), stop=(kk == 7))
    src = k.ps.t[:, b, 0:96].rearrange("p (q c) -> p q c", c=2)
    bias = k.adab.t[:, L, :].unsqueeze(2).to_broadcast([128, 48, 2])
    P.tt("dve", k.mod.r(), k.ps.r(src, b), k.adab.r(bias), ALU.add)
    for sl in range(2):
        q0 = (1 + 3 * sl) * 8
        P.ts("dve", k.msc.r(k.msc.t[:, sl]), k.mod.r(k.mod.t[:, q0:q0 + 8, :]), 1.0, ALU.add)


def modulate(k, sl):
    P = k.P
    q_sh = (0 + 3 * sl) * 8
    for j in range(8):
        for b in range(NBLK):
            c = bc(b)
            P.act(hs(k, j, b), xs(k, j, b), AF.Identity,
                  bias=k.mod.r(k.mod.t[:, q_sh + j, c:c + 1]), scale=k.msc.r(k.msc.t[:, sl, j, c:c + 1]))
    for j in range(8):
        for b in range(NBLK):
            P.ts("dve", xs(k, j, b), xs(k, j, b), ALPHA, ALU.mult)


def layernorm(k, L, sl, lb):
    P = k.P
    xb, sq, mean, m2, var, rstd, nmr, t1 = lb["xb"], lb["sq"], lb["mean"], lb["m2"], lb["var"], lb["rstd"], lb["nmr"], lb["t1"]
    for b in range(NBLK):
        pb = b % 2
        for j in range(8):
            P.act(xb.r(xb.t[:, pb, j, :], pb * 8 + j), xs(k, j, b), AF.Identity)
            P.act(sq.r(sq.t[:, pb, j, :], pb * 8 + j), xs(k, j, b), AF.Square)
        b1 = P.bank()
        b2 = P.bank()
        for j in range(8):
            P.mm(psb(k, b1), k.onesb.r(), xb.r(xb.t[:, pb, j, :], pb * 8 + j), start=(j == 0), stop=(j == 7))
        for j in range(8):
            P.mm(psb(k, b2), k.onesb.r(), sq.r(sq.t[:, pb, j, :], pb * 8 + j), start=(j == 0), stop=(j == 7))
        P.act(mean.r(mean.t[:, pb], pb), psb(k, b1), AF.Identity, scale=1.0 / D)
        P.tt("dve", m2.r(m2.t[:, pb], pb), mean.r(mean.t[:, pb], pb), mean.r(mean.t[:, pb], pb), ALU.mult)
        P.stt(var.r(var.t[:, pb], pb), psb(k, b2), 1.0 / D, m2.r(m2.t[:, pb], pb), ALU.mult, ALU.subtract)
        P.act(var.r(var.t[:, pb], pb), var.r(var.t[:, pb], pb), AF.Ln, bias=lb["eps"].r())
        P.act(rstd.r(rstd.t[:, pb], pb), var.r(var.t[:, pb], pb), AF.Exp, scale=-0.5)
        P.stt(nmr.r(nmr.t[:, pb], pb), mean.r(mean.t[:, pb], pb), -1.0, rstd.r(rstd.t[:, pb], pb), ALU.mult, ALU.mult)
        for j in range(8):
            tb = j % 2
            P.tt("dve", t1.r(t1.t[:, tb], tb), xs(k, j, b), rstd.r(rstd.t[:, pb], pb), ALU.mult)
            P.tt("dve", t1.r(t1.t[:, tb], tb), t1.r(t1.t[:, tb], tb), nmr.r(nmr.t[:, pb], pb), ALU.add)
            P.act(xs(k, j, b), t1.r(t1.t[:, tb], tb), AF.Identity,
                  bias=k.lnbb.r(k.lnbb.t[:, L, sl, j:j + 1]), scale=k.lngb.r(k.lngb.t[:, L, sl, j:j + 1]))


def ln_scope(k, L, sl):
    P = k.P
    P.barrier()
    mk = P.mark()
    if True:
        lb = {
            "xb": P.buf("ln_xb", [128, 2, 8, BLK], BF16, nsub=16),
            "sq": P.buf("ln_sq", [128, 2, 8, BLK], BF16, nsub=16),
            "mean": P.buf("ln_mean", [128, 2, BLK], F32, nsub=2),
            "m2": P.buf("ln_m2", [128, 2, BLK], F32, nsub=2),
            "var": P.buf("ln_var", [128, 2, BLK], F32, nsub=2),
            "rstd": P.buf("ln_rstd", [128, 2, BLK], F32, nsub=2),
            "nmr": P.buf("ln_nmr", [128, 2, BLK], F32, nsub=2),
            "t1": P.buf("ln_t1", [128, 2, BLK], F32, nsub=2),
            "eps": P.buf("ln_eps", [128, 1], F32),
        }
        P.memset("dve", lb["eps"].r(), LN_EPS)
        layernorm(k, L, sl, lb)
        P.barrier()
        P.release(mk)


def ffn(k, L):
    P = k.P
    G = 2
    P.barrier()
    mk = P.mark()
    if True:
        up = P.buf("f_up", [128, 2, 2, UPW], BF16, nsub=4)
        ab = P.buf("f_ab", [128, G, NT], BF16, nsub=G * 5)
        dg = P.buf("f_dg", [128, 2, 2, 9, 128], BF16, nsub=4)
        sg = P.buf("f_sg", [128, 2, BLK], F32, nsub=2)
        fcw = P.buf("f_fcw", [128, 44, 9], F32)
        P.dma("sp", fcw.r(), R(k.dram("fcw").ap[L], k.dram("fcw").keys))
        P.memset("pool", up.r(), 0.0)
        q_g = 5 * 8
        nsg = 0
        for g0 in range(0, NPAIR, G):
            pairs = list(range(g0, min(g0 + G, NPAIR)))
            for jj, j in enumerate(pairs):
                db = j % 2
                sg_, wg = load_in_w(k, k.dram("ffn_w_in%d" % L), 0, j * 128)
                sv_, wv = load_in_w(k, k.dram("ffn_w_in%d" % L), 0, FH + j * 128)
                ws = [wg, wv]
                wss = [sg_, sv_]
                for gv in range(2):
                    tile_idx = j + gv * NPAIR
                    i0 = k.identf.t[:].unsqueeze(1).to_broadcast([128, 9, 128])
                    i1 = fcw.t[:, tile_idx, :].unsqueeze(2).to_broadcast([128, 9, 128])
                    P.tt("pool", dg.r(dg.t[:, db, gv], db * 2 + gv), k.identf.r(i0), fcw.r(i1), ALU.mult)
                for b in range(NBLK):
                    for gv in range(2):
                        bk = P.bank()
                        for kk in range(8):
                            P.mm(psb(k, bk), k.wr.r(ws[gv][:, kk, :], wss[gv]), hs(k, kk, b), start=(kk == 0), stop=(kk == 7))
                        upt = up.t[:, db, gv]
                        if b == 0:
                            dst = upt[:, 0:516].rearrange("p (s w) -> p s w", w=258)[:, :, 1:257]
                            src = k.ps.t[:, bk, :].rearrange("p (s w) -> p s w", w=256)
                        else:
                            r0 = 8 * (b - 1)
                            dst = upt[:, 516:].rearrange("p (r w) -> p r w", w=66)[:, 1 + r0:9 + r0, 1:65]
                            src = k.ps.t[:, bk, :].rearrange("p (r w) -> p r w", w=64)
                        P.copy("act" if gv == 0 else "dve", up.r(dst, db * 2 + gv), k.ps.r(src, bk))
                for b in range(NBLK):
                    bks = []
                    for gv in range(2):
                        bk = P.bank()
                        bks.append(bk)
                        upt = up.t[:, db, gv]
                        if b == 0:
                            taps = [(1, kw) for kw in range(3)]
                            outv = k.ps.t[:, bk, :].rearrange("p (s w) -> p s w", w=256)
                        else:
                            taps = [(kh, kw) for kh in range(3) for kw in range(3)]
                            outv = k.ps.t[:, bk, :].rearrange("p (r w) -> p r w", w=64)
                        for ti, (kh, kw) in enumerate(taps):
                            if b == 0:
                                rhs = upt[:, 0:516].rearrange("p (s w) -> p s w", w=258)[:, :, kw:kw + 256]
                            else:
                                r0 = 8 * (b - 1)
                                rhs = upt[:, 516:].rearrange("p (r w) -> p r w", w=66)[:, r0 + kh:r0 + kh + 8, kw:kw + 64]
                            P.mm(k.ps.r(outv, bk), dg.r(dg.t[:, db, gv, kh * 3 + kw, :], db * 2 + gv), up.r(rhs, db * 2 + gv),
                                 start=(ti == 0), stop=(ti == len(taps) - 1))
                    sb_ = nsg % 2
                    nsg += 1
                    P.act(sg.r(sg.t[:, sb_], sb_), psb(k, bks[0]), AF.Silu)
                    P.tt("dve", ab.r(ab.t[:, jj, b * BLK:(b + 1) * BLK], jj * 5 + b), psb(k, bks[1]), sg.r(sg.t[:, sb_], sb_), ALU.mult)
            wos = [load_out_w(k, k.dram("ffn_w_out%d" % L), 0, (g0 + jj) * 128) for jj in range(len(pairs))]
            for m in range(8):
                for b in range(NBLK):
                    bk = P.bank()
                    for jj in range(len(pairs)):
                        P.mm(psb(k, bk), k.wr.r(wos[jj][1][:, m * 128:(m + 1) * 128], wos[jj][0]), ab.r(ab.t[:, jj, b * BLK:(b + 1) * BLK], jj * 5 + b),
                             start=(jj == 0), stop=(jj == len(pairs) - 1))
                    c = bc(b)
                    P.stt(xs(k, m, b), psb(k, bk), k.mod.r(k.mod.t[:, q_g + m, c:c + 1]), xs(k, m, b), ALU.mult, ALU.add)
        P.barrier()
        P.release(mk)


def layer(k, L):
    cfg = k.cfg
    ada(k, L)
    modulate(k, 0)
    if cfg.get("mixers", True):
        mixer(k, L)
    ln_scope(k, L, 0)
    modulate(k, 1)
    if cfg.get("ffn", True):
        ffn(k, L)
    ln_scope(k, L, 1)


LW = 1280
SEGS = [(0, 256), (256, 256), (512, 2048)]
Q_G1 = 2 * 8


def mixer(k, L):
    kind = L % 3
    if kind == 2:
        lru(k, L // 3)
    elif kind == 1:
        ssd(k, L // 3)
    else:
        gdn(k, L // 3)


def outproj_acc(k, W, Lw, r0, ntile, o_regions):
    P = k.P
    slots = [load_out_w(k, W, Lw, r0 + i * 128) for i in range(ntile)]
    for m in range(8):
        for b in range(NBLK):
            bk = P.bank()
            for jj in range(ntile):
                s, wo = slots[jj]
                P.mm(psb(k, bk), k.wr.r(wo[:, m * 128:(m + 1) * 128], s), o_regions(jj, b), start=(jj == 0), stop=(jj == ntile - 1))
            c = bc(b)
            P.stt(xs(k, m, b), psb(k, bk), k.mod.r(k.mod.t[:, Q_G1 + m, c:c + 1]), xs(k, m, b), ALU.mult, ALU.add)


def conv1d_pad_layout():
    offs = []
    o = 0
    for (t0, n) in SEGS:
        offs.append(o)
        o += n + 3
    return offs, o


CPO, CPW = conv1d_pad_layout()


def conv_dst(xpad_t, b):
    if b == 0:
        return xpad_t[:, 0:518].rearrange("p (s w) -> p s w", w=259)[:, :, 1:257], "p (s w) -> p s w", 256
    o = CPO[2] + 1 + (b - 1) * BLK
    return xpad_t[:, o:o + BLK], None, None


def conv_rhs(xpad_t, b, kk):
    if b == 0:
        return xpad_t[:, 0:518].rearrange("p (s w) -> p s w", w=259)[:, :, kk:kk + 256]
    o = CPO[2] + (b - 1) * BLK + kk
    return xpad_t[:, o:o + BLK]


def psview(k, bk, b):
    if b == 0:
        return k.ps.t[:, bk, :].rearrange("p (s w) -> p s w", w=256)
    return k.ps.t[:, bk, :]


def lru(k, j):
    P = k.P
    G = 2
    P.barrier()
    mk = P.mark()
    xpad = P.buf("l_xpad", [128, CPW + 1], BF16)
    xrb = P.buf("l_xrb", [128, NT], BF16)
    gg = P.buf("l_gg", [128, NT], BF16)
    Ib = P.buf("l_i", [128, NT], BF16)
    A = P.buf("l_a", [128, NT], F32)
    T = P.buf("l_t", [128, NT], F32)
    H = [P.buf("l_h0", [128, NT], BF16), P.buf("l_h1", [128, NT], BF16)]
    ob = P.buf("l_o", [128, G, NT], BF16, nsub=G)
    dgl = P.buf("l_dg", [128, 4, 128], BF16)
    sm = P.buf("l_sm", [128, 10, 12], F32)
    s0 = P.buf("l_s0", [128, 10, 2], F32)
    sp = P.buf("l_sp", [128, 10, 2], F32)
    ep = P.buf("l_ep", [128, 10, 2], F32)
    one = P.buf("l_one", [128, 1], F32)
    sto = P.buf("l_sto", [128, 10, 2, 2], F32)
    P.dma("sp", sm.r(), k.dram("lru_sm"))
    P.dma("sp", s0.r(), k.dram("lru_s0"))
    P.memset("dve", one.r(), 1.0)
    P.memset("pool", xpad.r(), 0.0)
    P.act(ep.r(), sm.r(sm.t[:, :, 9:11]), AF.Exp, scale=-1.0)
    P.ts("dve", sp.r(), ep.r(), -0.2, ALU.mult, 0.25, ALU.add)
    for cst in (1.0 / 3, 0.5, 1.0):
        P.tt("dve", sp.r(), sp.r(), ep.r(), ALU.mult)
        P.ts("dve", sp.r(), sp.r(), -1.0, ALU.mult, cst, ALU.add)
    P.tt("dve", sp.r(), sp.r(), ep.r(), ALU.mult)
    P.ts("dve", sp.r(), sp.r(), -8.0, ALU.mult)

    for g0 in range(0, 10, G):
        tiles = list(range(g0, min(g0 + G, 10)))
        for jj, n in enumerate(tiles):
            s, wgb = load_in_w(k, k.dram("lru_w_in"), j, n * 128)
            sx, wxr = load_in_w(k, k.dram("lru_w_in"), j, LW + n * 128)
            s2 = wslot(k)
            gsrc = k.dram("lru_gate_w").ap[j][:, :, n].rearrange("d g kk m -> kk (d g) m")
            gw = k.wr.t[:, s2, 0:512].rearrange("p (a m) -> p a m", m=128)
            P.dma("pool", k.wr.r(gw, s2), R(gsrc, k.dram("lru_gate_w").keys))
            i0 = k.identf.t[:].unsqueeze(1).to_broadcast([128, 4, 128])
            i1 = sm.t[:, n, 0:4].unsqueeze(2).to_broadcast([128, 4, 128])
            P.tt("pool", dgl.r(), k.identf.r(i0), sm.r(i1), ALU.mult)
            for b in range(NBLK):
                bk = P.bank()
                for kk in range(8):
                    P.mm(psb(k, bk), k.wr.r(wgb[:, kk, :], s), hs(k, kk, b), start=(kk == 0), stop=(kk == 7))
                P.act(gg.r(gg.t[:, b * BLK:(b + 1) * BLK]), psb(k, bk), AF.Gelu_apprx_tanh)
            for b in range(NBLK):
                bk = P.bank()
                for kk in range(8):
                    P.mm(psb(k, bk), k.wr.r(wxr[:, kk, :], sx), hs(k, kk, b), start=(kk == 0), stop=(kk == 7))
                dst, _, _ = conv_dst(xpad.t, b)
                P.copy("dve", xpad.r(dst), k.ps.r(psview(k, bk, b), bk))
            for b in range(NBLK):
                bk = P.bank()
                for kk in range(4):
                    P.mm(k.ps.r(psview(k, bk, b), bk), dgl.r(dgl.t[:, kk, :]), xpad.r(conv_rhs(xpad.t, b, kk)), start=(kk == 0), stop=(kk == 3))
                P.act(xrb.r(xrb.t[:, b * BLK:(b + 1) * BLK]), psb(k, bk), AF.Identity, bias=sm.r(sm.t[:, n, 4:5]))
            for d in range(2):
                for b in range(NBLK):
                    for g in range(2):
                        bk = P.bank()
                        P.mm(psb(k, bk), k.wr.r(gw[:, d * 2 + g, :], s2), xrb.r(xrb.t[:, b * BLK:(b + 1) * BLK]))
                        dstb = A if g == 0 else Ib
                        P.act(dstb.r(dstb.t[:, b * BLK:(b + 1) * BLK]), psb(k, bk), AF.Sigmoid, bias=sm.r(sm.t[:, n, 5 + d * 2 + g:6 + d * 2 + g]))
                P.act(A.r(), A.r(), AF.Exp, scale=sp.r(sp.t[:, n, d:d + 1]))
                P.tt("dve", T.r(), A.r(), A.r(), ALU.mult)
                P.act(T.r(), T.r(), AF.Sqrt, bias=one.r(), scale=-1.0)
                P.tt("dve", T.r(), T.r(), Ib.r(), ALU.mult)
                P.tt("dve", T.r(), T.r(), xrb.r(), ALU.mult)
                for si, (t0, n_t) in enumerate(SEGS):
                    if d == 0:
                        o_, a_, b_ = H[0].t[:, t0:t0 + n_t], A.t[:, t0:t0 + n_t], T.t[:, t0:t0 + n_t]
                    else:
                        lo = t0 - 1 if t0 > 0 else None
                        o_, a_, b_ = H[1].t[:, t0 + n_t - 1:lo:-1], A.t[:, t0 + n_t - 1:lo:-1], T.t[:, t0 + n_t - 1:lo:-1]
                    if si == 2:
                        init = s0.t[:, n, d:d + 1]
                        rd = [A.r(), T.r(), s0.r()]
                    else:
                        init = 0.0
                        rd = [A.r(), T.r()]
                    P.op("dve", lambda e, o_=o_, a_=a_, b_=b_, init=init: e.tensor_tensor_scan(out=o_, data0=a_, data1=b_, initial=init, op0=ALU.mult, op1=ALU.add),
                         reads=rd, writes=[H[d].r()])
                    if si < 2:
                        tl = t0 + n_t - 1 if d == 0 else t0
                        P.copy("act", sto.r(sto.t[:, n, si, d:d + 1]), H[d].r(H[d].t[:, tl:tl + 1]))
            P.tt("dve", H[0].r(), H[0].r(), H[1].r(), ALU.add)
            P.tt("dve", ob.r(ob.t[:, jj, :], jj), H[0].r(), gg.r(), ALU.mult)
        outproj_acc(k, k.dram("lru_w_out"), j, g0 * 128, len(tiles), lambda jj, b: ob.r(ob.t[:, jj, b * BLK:(b + 1) * BLK], jj))
    P.dma("sp", k.dram("lru_out"), sto.r())
    P.barrier()
    P.release(mk)


NCH = 20
SEG_CH = [(0, 2), (2, 2), (4, 16)]


def seg_of_chunk(c):
    return 0 if c < 2 else (1 if c < 4 else 2)


def hch(k, kk, c):
    b = c // 4
    return k.h.r(k.h.t[:, kk, c * 128:(c + 1) * 128], kk * 5 + b)


def ssd(k, j):
    P = k.P
    P.barrier()
    mk = P.mark()
    y_tm = P.buf("s_ytm", [128, NCH, 512], BF16, nsub=NCH)
    xs_tm = P.buf("s_xstm", [128, NCH, 256], BF16, nsub=NCH)
    Bfm = P.buf("s_bfm", [128, NT], BF16)
    Cfm = P.buf("s_cfm", [128, NT], BF16)
    wz = P.buf("s_wz", [128, 8, 256], BF16)
    wdt = P.buf("s_wdt", [128, 8, 64], BF16)
    cw = P.buf("s_cw", [128, 24, 5], F32)
    rows = P.buf("s_rows", [128, 160], F32)
    nrm = P.buf("s_nrm", [128, 512], BF16)
    masks = P.buf("s_masks", [128, 2, 128], F32)
    negb = P.buf("s_negb", [128, 2, 128], BF16)
    onesf = P.buf("s_onesf", [128, 128], F32)
    one1 = P.buf("s_one1", [128, 1], F32)
    eps1 = P.buf("s_eps1", [128, 1], F32)
    sc = {nm: P.buf("s_" + nm, [128, NCH, 8], F32) for nm in ["dt", "da", "nacum", "eac", "cdec", "ce"]}
    sc["tmp"] = sc["ce"]
    fm_tmp = Cfm
    ssq = P.buf("s_ssq", [128, NCH, 2], F32)
    rstd = P.buf("s_rstd", [128, NCH], F32)
    cbT = P.buf("s_cbT", [128, 2, 2, 128], BF16, nsub=2)
    dec = P.buf("s_dec", [128, 2, 4, 128], BF16, nsub=2)
    ST = P.buf("s_ST", [128, 2, 256], F32, nsub=2)
    STb = P.buf("s_STb", [128, 2, 256], BF16, nsub=2)
    t1 = P.buf("s_t1", [128, 1, 256], F32, nsub=1)
    t2 = P.buf("s_t2", [128, 1, 256], F32, nsub=1)
    sz = t2
    sig = P.buf("s_sig", [128, BLK], BF16)
    ncb = P.buf("s_ncb", [128, 24], F32)
    junk = sig
    sto = P.buf("s_sto", [128, 2, 128], F32, nsub=2)

    P.dma("sp", cw.r(), k.dram("ssd_cw"))
    P.dma("sp", rows.r(), k.dram("ssd_rows"))
    P.dma("sp", masks.r(), R(k.dram("cmask").ap[0:2].rearrange("a p m -> p a m"), k.dram("cmask").keys))
    P.dma("pool", negb.r(), R(k.dram("cmask").ap[2:4].rearrange("a p m -> p a m"), k.dram("cmask").keys))
    P.memset("dve", onesf.r(), 1.0)
    P.memset("dve", one1.r(), 1.0)
    P.ts("dve", ncb.r(), cw.r(cw.t[:, :, 4]), -1.0, ALU.mult)
    P.memset("dve", eps1.r(), 1e-5)
    P.act(rows.r(rows.t[:, 64:128]), rows.r(rows.t[:, 64:128]), AF.Exp)
    P.ts("dve", rows.r(rows.t[:, 64:128]), rows.r(rows.t[:, 64:128]), -1.0, ALU.mult)
    src = k.dram("ssd_w_in").ap[j][:, 5120:5184].rearrange("(kk p) c -> p kk c", p=128)
    P.dma("pool", wdt.r(), R(src, k.dram("ssd_w_in").keys))

    loc = {}

    def conv_tile(col0, cidx, dst_writer):
        xpad, dgl = loc["xpad"], loc["dgl"]
        s, w = load_in_w(k, k.dram("ssd_w_in"), j, col0)
        i0 = k.identf.t[:].unsqueeze(1).to_broadcast([128, 4, 128])
        i1 = cw.t[:, cidx, 0:4].unsqueeze(2).to_broadcast([128, 4, 128])
        P.tt("pool", dgl.r(), k.identf.r(i0), cw.r(i1), ALU.mult)
        for b in range(NBLK):
            bk = P.bank()
            for kk in range(8):
                P.mm(psb(k, bk), k.wr.r(w[:, kk, :], s), hs(k, kk, b), start=(kk == 0), stop=(kk == 7))
            dst, _, _ = conv_dst(xpad.t, b)
            P.copy("dve", xpad.r(dst), k.ps.r(psview(k, bk, b), bk))
        for b in range(NBLK):
            bk = P.bank()
            for kk in range(4):
                P.mm(k.ps.r(psview(k, bk, b), bk), dgl.r(dgl.t[:, kk, :]), xpad.r(conv_rhs(xpad.t, b, kk)), start=(kk == 0), stop=(kk == 3))
            P.act(sig.r(), psb(k, bk), AF.Exp, bias=ncb.r(ncb.t[:, cidx:cidx + 1]), scale=-1.0)
            P.act(sig.r(), sig.r(), AF.Ln, bias=one1.r())
            P.act(sig.r(), sig.r(), AF.Exp, scale=-1.0)
            P.stt(dst_writer(b), psb(k, bk), cw.r(cw.t[:, cidx, 4:5]), sig.r(), ALU.add, ALU.mult)

    stop = k.cfg.get("ssd_stop", 99)
    for hg in range(k.cfg.get("ssd_nhg", 8)):
        g, half = hg // 2, hg % 2
        src = k.dram("ssd_w_in").ap[j][:, hg * 256:(hg + 1) * 256].rearrange("(kk p) c -> p kk c", p=128)
        P.dma("pool", wz.r(), R(src, k.dram("ssd_w_in").keys))
        if half == 0:
            P.dma("pool", nrm.r(), R(k.dram("ssd_nrm").ap[:, g * 512:(g + 1) * 512], k.dram("ssd_nrm").keys))
        if stop <= 0.5:
            continue
        bk = P.bank()
        for c in range(NCH):
            for kk in range(8):
                rhs = wdt.t[:, kk, :].rearrange("p (d h) -> p d h", d=2)[:, :, hg * 4:hg * 4 + 4]
                out = k.ps.t[:, bk, c * 8:(c + 1) * 8].rearrange("p (d h) -> p d h", d=2)
                P.mm(k.ps.r(out, bk), hch(k, kk, c), wdt.r(rhs), start=(kk == 0), stop=(kk == 7))
        psv = k.ps.t[:, bk, 0:NCH * 8].rearrange("p (c d h) -> p c d h", d=2, h=4)
        if stop <= 0.7:
            continue

        def rowbc(off):
            return rows.t[:, off:off + 64].rearrange("p (d h) -> p d h", d=2)[:, :, hg * 4:hg * 4 + 4].unsqueeze(1).to_broadcast([128, NCH, 2, 4])

        def v4(bf):
            return bf.t.rearrange("p c (d h) -> p c d h", d=2)
        tmp, dt, da = sc["tmp"], sc["dt"], sc["da"]
        P.tt("dve", tmp.r(v4(tmp)), k.ps.r(psv, bk), rows.r(rowbc(0)), ALU.add)
        P.ts("dve", dt.r(), tmp.r(), -1.0, ALU.mult)
        P.tt("dve", dt.r(), dt.r(), tmp.r(), ALU.max)
        P.act(dt.r(), dt.r(), AF.Exp, scale=-1.0)
        P.act(dt.r(), dt.r(), AF.Ln, bias=one1.r())
        P.ts("dve", tmp.r(), tmp.r(), 0.0, ALU.max)
        P.tt("dve", dt.r(), dt.r(), tmp.r(), ALU.add)
        P.tt("dve", da.r(v4(da)), dt.r(v4(dt)), rows.r(rowbc(64)), ALU.mult)
        if stop <= 0.8:
            continue
        bk2 = P.bank()
        bk3 = P.bank()
        for c in range(NCH):
            for d in range(2):
                P.mm(k.ps.r(k.ps.t[:, bk2, c * 8 + d * 4:c * 8 + d * 4 + 4], bk2), masks.r(masks.t[:, d, :]), da.r(da.t[:, c, d * 4:d * 4 + 4]))
            P.mm(k.ps.r(k.ps.t[:, bk3, c * 8:(c + 1) * 8], bk3), onesf.r(), da.r(da.t[:, c, :]))
        nacum, eac, cdec, ce = sc["nacum"], sc["eac"], sc["cdec"], sc["ce"]
        ps2 = k.ps.t[:, bk2, 0:NCH * 8].rearrange("p (c e) -> p c e", e=8)
        ps3 = k.ps.t[:, bk3, 0:NCH * 8].rearrange("p (c e) -> p c e", e=8)
        if stop <= 0.9:
            continue
        P.ts("dve", nacum.r(), k.ps.r(ps2, bk2), -1.0, ALU.mult)
        P.act(eac.r(), nacum.r(), AF.Exp, scale=-1.0)
        if stop <= 0.95:
            continue
        P.copy("dve", cdec.r(), k.ps.r(ps3, bk3))
        P.tt("dve", ce.r(), cdec.r(), nacum.r(), ALU.add)
        if stop <= 0.96:
            continue
        P.act(cdec.r(), cdec.r(), AF.Exp)
        if stop <= 0.97:
            continue
        P.act(ce.r(), ce.r(), AF.Exp)
        P.tt("dve", ce.r(), ce.r(), dt.r(), ALU.mult)
        if stop <= 1:
            continue
        P.barrier()
        mkp = P.mark()
        loc["xpad"] = P.buf("s_xpad", [128, CPW + 1], BF16)
        loc["dgl"] = P.buf("s_dg", [128, 4, 128], BF16)
        P.memset("pool", loc["xpad"].r(), 0.0)
        for ti in range(2):
            conv_tile(2048 + hg * 256 + ti * 128, hg * 2 + ti, lambda b: fm_tmp.r(fm_tmp.t[:, b * BLK:(b + 1) * BLK]))
            for c0 in range(0, NCH, 8):
                bk = P.bank()
                pb = k.ps.t[:, bk, :].bitcast(BF16)
                n = min(8, NCH - c0)
                for ci in range(n):
                    c = c0 + ci
                    P.tr(k.ps.r(pb[:, ci * 128:(ci + 1) * 128], bk), fm_tmp.r(fm_tmp.t[:, c * 128:(c + 1) * 128]), k.identb.r())
                dst = xs_tm.t[:, c0:c0 + n, ti * 128:(ti + 1) * 128]
                srcv = pb[:, 0:n * 128].rearrange("p (c m) -> p c m", m=128)
                P.copy("act", xs_tm.r(dst, range(c0, c0 + n)), k.ps.r(srcv, bk))
        conv_tile(2048 + 2048 + g * 128, 16 + g, lambda b: Bfm.r(Bfm.t[:, b * BLK:(b + 1) * BLK]))
        conv_tile(2048 + 2560 + g * 128, 20 + g, lambda b: Cfm.r(Cfm.t[:, b * BLK:(b + 1) * BLK]))
        if stop <= 2:
            continue
        P.barrier()
        P.release(mkp)
        mku = P.mark()
        Btm = P.buf("s_btm", [128, 4, 128], BF16, nsub=4)
        Mb = P.buf("s_M", [128, 4, 4, 128], BF16, nsub=4)
        xd = P.buf("s_xd", [128, 4, 256], BF16, nsub=4)
        xdt = P.buf("s_xdt", [128, 4, 256], BF16, nsub=4)
        nsteps = k.cfg.get("ssd_steps", NCH)

        def prelim_gen(step):
            banks = {}
            units = [(d, (step if d == 0 else NCH - 1 - step), (step % 2) * 2 + d) for d in range(2)]
            for (d, c, sl) in units:
                tok = slice(c * 128, (c + 1) * 128)
                bk = d * 3
                banks[("cb", d)] = bk
                P.mm(k.ps.r(k.ps.t[:, bk, 0:128], bk), Bfm.r(Bfm.t[:, tok]), Cfm.r(Cfm.t[:, tok]))
                bkt = d * 3 + 1
                banks[("bt", d)] = bkt
                pbb = k.ps.t[:, bkt, :].bitcast(BF16)
                P.tr(k.ps.r(pbb[:, 0:128], bkt), Bfm.r(Bfm.t[:, tok]), k.identb.r())
                yield
            for (d, c, sl) in units:
                bk = banks[("cb", d)]
                bkt = banks[("bt", d)]
                pbb = k.ps.t[:, bkt, :].bitcast(BF16)
                P.tt("dve", cbT.r(cbT.t[:, d, d], d), k.ps.r(k.ps.t[:, bk, 0:128], bk), masks.r(masks.t[:, d, :]), ALU.mult)
                P.copy("act", Btm.r(Btm.t[:, sl], sl), k.ps.r(pbb[:, 0:128], bkt))
                xsv = xs_tm.t[:, c, :].rearrange("p (q e) -> p q e", e=64)
                dtb = dt.t[:, c, d * 4:d * 4 + 4].unsqueeze(2).to_broadcast([128, 4, 64])
                ceb = ce.t[:, c, d * 4:d * 4 + 4].unsqueeze(2).to_broadcast([128, 4, 64])
                P.tt("pool", xd.r(xd.t[:, sl].rearrange("p (q e) -> p q e", e=64), sl), xs_tm.r(xsv, c), dt.r(dtb), ALU.mult)
                P.tt("pool", xdt.r(xdt.t[:, sl].rearrange("p (q e) -> p q e", e=64), sl), xs_tm.r(xsv, c), ce.r(ceb), ALU.mult)
                yield
            for (d, c, sl) in units:
                bk = d * 3 + 2
                banks[("R", d)] = bk
                for q in range(4):
                    col = d * 4 + q
                    lhs = da.t[:, c, col:col + 1].to_broadcast([128, 128])
                    o_ = k.ps.r(k.ps.t[:, bk, q * 128:(q + 1) * 128], bk)
                    P.mm(o_, da.r(lhs), masks.r(masks.t[:, d, :]), start=True, stop=False)
                    P.mm(o_, k.identb.r(), negb.r(negb.t[:, d, :]), start=False, stop=True)
                yield
            for (d, c, sl) in units:
                bk = banks[("R", d)]
                for q in range(4):
                    col = d * 4 + q
                    P.act(dec.r(dec.t[:, d, q, :], d), k.ps.r(k.ps.t[:, bk, q * 128:(q + 1) * 128], bk), AF.Exp, bias=nacum.r(nacum.t[:, c, col:col + 1]))
                    yield
            for (d, c, sl) in units:
                cb_b = cbT.t[:, d, d].unsqueeze(1).to_broadcast([128, 4, 128])
                P.tt("dve", Mb.r(Mb.t[:, sl], sl), dec.r(dec.t[:, d], d), cbT.r(cb_b, d), ALU.mult)
                yield

        cbs = [0]

        def cbank():
            cbs[0] ^= 1
            return 6 + cbs[0]

        def chain_gen(step):
            for d in range(2):
                c = step if d == 0 else NCH - 1 - step
                sl = (step % 2) * 2 + d
                first_visit = (c <= 9) if d == 0 else (c >= 10)
                if nsteps < NCH:
                    first_visit = True
                seg = seg_of_chunk(c)
                c_first, c_n = SEG_CH[seg]
                seg_start = (c == c_first) if d == 0 else (c == c_first + c_n - 1)
                seg_end = (c == c_first + c_n - 1) if d == 0 else (c == c_first)
                tok = slice(c * 128, (c + 1) * 128)
                if seg_start:
                    if seg == 2:
                        P.dma("sp", ST.r(ST.t[:, d], d), R(k.dram("ssd_s0").ap[d, hg], k.dram("ssd_s0").keys))
                    else:
                        P.memset("dve", ST.r(ST.t[:, d], d), 0.0)
                    P.copy("act", STb.r(STb.t[:, d], d), ST.r(ST.t[:, d], d))
                    yield
                xsv = xs_tm.t[:, c, :].rearrange("p (q e) -> p q e", e=64)
                bkY = cbank()
                for q in range(4):
                    P.mm(k.ps.r(k.ps.t[:, bkY, q * 64:(q + 1) * 64], bkY), Mb.r(Mb.t[:, sl, q, :], sl), xd.r(xd.t[:, sl, q * 64:(q + 1) * 64], sl))
                P.mm(k.ps.r(k.ps.t[:, bkY, 256:512], bkY), Cfm.r(Cfm.t[:, tok]), STb.r(STb.t[:, d], d))
                bkS = cbank()
                P.mm(k.ps.r(k.ps.t[:, bkS, 0:256], bkS), Btm.r(Btm.t[:, sl], sl), xdt.r(xdt.t[:, sl], sl))
                yield
                cdb = cdec.t[:, c, d * 4:d * 4 + 4].unsqueeze(2).to_broadcast([128, 4, 64])
                STv = ST.t[:, d].rearrange("p (q e) -> p q e", e=64)
                P.tt("dve", ST.r(STv, d), ST.r(STv, d), cdec.r(cdb), ALU.mult)
                P.tt("dve", ST.r(ST.t[:, d], d), ST.r(ST.t[:, d], d), k.ps.r(k.ps.t[:, bkS, 0:256], bkS), ALU.add)
                yield
                P.copy("act", STb.r(STb.t[:, d], d), ST.r(ST.t[:, d], d))
                yield
                eab = eac.t[:, c, d * 4:d * 4 + 4].unsqueeze(2).to_broadcast([128, 4, 64])
                t1v = t1.t[:, 0].rearrange("p (q e) -> p q e", e=64)
                P.tt("dve", t1.r(t1v, 0), k.ps.r(k.ps.t[:, bkY, 256:512].rearrange("p (q e) -> p q e", e=64), bkY), eac.r(eab), ALU.mult)
                P.tt("dve", t1.r(t1.t[:, 0], 0), t1.r(t1.t[:, 0], 0), k.ps.r(k.ps.t[:, bkY, 0:256], bkY), ALU.add)
                yield
                ysl = y_tm.r(y_tm.t[:, c, half * 256:(half + 1) * 256], c)
                if first_visit:
                    P.copy("act", ysl, t1.r(t1.t[:, 0], 0))
                    yield
                else:
                    P.tt("dve", t1.r(t1.t[:, 0], 0), t1.r(t1.t[:, 0], 0), ysl, ALU.add)
                    dsb = rows.t[:, 128 + hg * 4:128 + hg * 4 + 4].unsqueeze(2).to_broadcast([128, 4, 64])
                    P.tt("pool", t2.r(t2.t[:, 0].rearrange("p (q e) -> p q e", e=64), 0), xs_tm.r(xsv, c), rows.r(dsb), ALU.mult)
                    yield
                    P.tt("dve", t1.r(t1.t[:, 0], 0), t1.r(t1.t[:, 0], 0), t2.r(t2.t[:, 0], 0), ALU.add)
                    bkZ = cbank()
                    for kk in range(8):
                        P.mm(k.ps.r(k.ps.t[:, bkZ, 0:256], bkZ), hch(k, kk, c), wz.r(wz.t[:, kk, :]), start=(kk == 0), stop=(kk == 7))
                    yield
                    P.act(t2.r(t2.t[:, 0], 0), k.ps.r(k.ps.t[:, bkZ, 0:256], bkZ), AF.Exp, scale=-1.0)
                    P.act(t2.r(t2.t[:, 0], 0), t2.r(t2.t[:, 0], 0), AF.Ln, bias=one1.r())
                    P.act(t2.r(t2.t[:, 0], 0), t2.r(t2.t[:, 0], 0), AF.Exp, scale=-1.0)
                    yield
                    P.tt("dve", t1.r(t1.t[:, 0], 0), t1.r(t1.t[:, 0], 0), t2.r(t2.t[:, 0], 0), ALU.mult)
                    P.tt("dve", t1.r(t1.t[:, 0], 0), t1.r(t1.t[:, 0], 0), k.ps.r(k.ps.t[:, bkZ, 0:256], bkZ), ALU.mult)
                    yield
                    P.copy("dve", ysl, t1.r(t1.t[:, 0], 0))
                    o_, i_, a_ = junk.t[:, 0:256], t1.t[:, 0], ssq.t[:, c, half:half + 1]
                    P.op("act", lambda e, o_=o_, i_=i_, a_=a_: e.activation(out=o_, in_=i_, func=AF.Square, accum_out=a_),
                         reads=[t1.r(t1.t[:, 0], 0)], writes=[junk.r(), ssq.r()])
                    yield
                if seg_end and seg < 2:
                    for pr in range(2):
                        bk = cbank()
                        P.tr(k.ps.r(k.ps.t[:, bk, 0:128], bk), ST.r(ST.t[:, d, pr * 128:(pr + 1) * 128], d), k.identf.r())
                        P.copy("dve", sto.r(sto.t[:, pr], pr), k.ps.r(k.ps.t[:, bk, 0:128], bk))
                        r0 = (hg * 4 + pr * 2) * 64
                        P.dma("sp", R(k.dram("ssd_out").ap[seg, d, r0:r0 + 128, :], k.dram("ssd_out").keys), sto.r(sto.t[:, pr], pr))
                    yield

        def run_all(gen):
            for _ in gen:
                pass

        def merge(pg, cg, ratio):
            pdone = cdone = False
            while not (pdone and cdone):
                if not cdone:
                    try:
                        next(cg)
                    except StopIteration:
                        cdone = True
                for _ in range(ratio):
                    if pdone:
                        break
                    try:
                        next(pg)
                    except StopIteration:
                        pdone = True

        run_all(prelim_gen(0))
        for step in range(nsteps):
            if step + 1 < nsteps:
                merge(prelim_gen(step + 1), chain_gen(step), k.cfg.get("ssd_ratio", 1))
            else:
                run_all(chain_gen(step))
        P.barrier()
        P.release(mku)
        if half == 1 and stop > 3:
            P.tt("dve", rstd.r(), ssq.r(ssq.t[:, :, 0]), ssq.r(ssq.t[:, :, 1]), ALU.add)
            P.act(rstd.r(), rstd.r(), AF.Ln, bias=eps1.r(), scale=1.0 / 512)
            P.act(rstd.r(), rstd.r(), AF.Exp, scale=-0.5)
            slots = [load_out_w(k, k.dram("ssd_w_out"), j, g * 512 + ti * 128) for ti in range(4)]
            ofm_t = xs_tm.t[:, 0:8, :].rearrange("p c e -> p (c e)").rearrange("p (t m) -> p t m", m=512)
            yn_t = xs_tm.t[:, 8:12, :].rearrange("p c e -> p (c e)").rearrange("p (t m) -> p t m", m=512)
            OK_ = list(range(0, 8))
            YK_ = list(range(8, 12))
            for b in range(NBLK):
                for ci in range(4):
                    c = b * 4 + ci
                    yb = ci % 2
                    P.stt(xs_tm.r(yn_t[:, yb], YK_), y_tm.r(y_tm.t[:, c, :], c), rstd.r(rstd.t[:, c:c + 1]), nrm.r(), ALU.mult, ALU.mult)
                    bk = P.bank()
                    pbb = k.ps.t[:, bk, :].bitcast(BF16)
                    for ti in range(4):
                        P.tr(k.ps.r(pbb[:, ti * 128:(ti + 1) * 128], bk), xs_tm.r(yn_t[:, yb, ti * 128:(ti + 1) * 128], YK_), k.identb.r())
                    srcv = pbb[:, 0:512].rearrange("p (t m) -> p t m", m=128)
                    P.copy("act", xs_tm.r(ofm_t[:, :, ci * 128:(ci + 1) * 128], OK_), k.ps.r(srcv, bk))
                for m in range(8):
                    bk = P.bank()
                    for ti in range(4):
                        s_, wo = slots[ti]
                        P.mm(psb(k, bk), k.wr.r(wo[:, m * 128:(m + 1) * 128], s_), xs_tm.r(ofm_t[:, ti, :], OK_), start=(ti == 0), stop=(ti == 3))
                    cc = bc(b)
                    P.stt(xs(k, m, b), psb(k, bk), k.mod.r(k.mod.t[:, Q_G1 + m, cc:cc + 1]), xs(k, m, b), ALU.mult, ALU.add)
    P.barrier()
    P.release(mk)


def gdn(k, j):
    P = k.P
    P.barrier()
    mk = P.mark()
    W_in = k.dram("gdn_w_in%d" % j)
    W_out = k.dram("gdn_w_out%d" % j)
    kq = P.buf("g_kq", [128, NCH, 2, 128], BF16, nsub=NCH)
    v_fm = P.buf("g_vfm", [128, NT], BF16, nsub=NCH)
    v_tm = P.buf("g_vtm", [128, NCH, 128], BF16, nsub=NCH)
    k_tm = P.buf("g_ktm", [128, NCH, 128], BF16, nsub=NCH)
    o_tm = P.buf("g_otm", [128, NCH, 128], BF16, nsub=NCH)
    wz = P.buf("g_wz", [128, 8, 128], BF16)
    wsm = P.buf("g_wsm", [128, 8, 32], BF16)
    cw = P.buf("g_cw", [128, 24, 4], F32)
    rows = P.buf("g_rows", [128, 160], F32)
    masks = P.buf("g_masks", [128, 2, 128], F32)
    negb = P.buf("g_negb", [128, 2, 128], BF16)
    lvl = P.buf("g_lvl", [128, 2, 7, 2, 128], BF16)
    I2 = P.buf("g_I2", [128, 2, 128], BF16)
    onesf = P.buf("g_onesf", [128, 128], F32)
    c_one = P.buf("g_c1", [128, 1], F32)
    c_eps6 = P.buf("g_c2", [128, 1], F32)
    c_eps6q = P.buf("g_c3", [128, 1], F32)
    c_eps5 = P.buf("g_c4", [128, 1], F32)
    sc = {nm: P.buf("g_" + nm, [128, NCH, 2], F32) for nm in ["beta", "g", "ngc", "negegc", "eout", "egl", "tmp"]}
    Rp = P.buf("g_Rp", [128, 2, 128], BF16, nsub=2)
    vn = P.buf("g_vn", [128, 2, 128], BF16, nsub=2)
    S = P.buf("g_S", [128, 2, 128], F32, nsub=2)
    Sb = P.buf("g_Sb", [128, 2, 128], BF16, nsub=2)
    ot = P.buf("g_ot", [128, 128], F32)
    sz = P.buf("g_sz", [128, 128], F32)
    ogb = P.buf("g_og", [128, 128], BF16)
    junk = P.buf("g_junk", [128, 128], BF16)
    ssq = P.buf("g_ssq", [128, 2], F32)

    P.dma("sp", cw.r(), R(k.dram("gdn_cw").ap[j], k.dram("gdn_cw").keys))
    P.dma("sp", rows.r(), R(k.dram("gdn_rows").ap[j], k.dram("gdn_rows").keys))
    P.dma("sp", masks.r(), R(k.dram("cmask").ap[0:2].rearrange("a p m -> p a m"), k.dram("cmask").keys))
    P.dma("pool", negb.r(), R(k.dram("cmask").ap[2:4].rearrange("a p m -> p a m"), k.dram("cmask").keys))
    for d in range(2):
        P.dma("pool", lvl.r(lvl.t[:, d]), R(k.dram("glvl").ap[d].rearrange("l a p m -> p l a m"), k.dram("glvl").keys))
    for a in range(2):
        P.copy("dve", I2.r(I2.t[:, a, :]), k.identf.r())
    P.memset("dve", onesf.r(), 1.0)
    P.memset("dve", c_one.r(), 1.0)
    P.memset("dve", c_eps6.r(), 1e-6)
    P.memset("dve", c_eps6q.r(), 128e-6)
    P.memset("dve", c_eps5.r(), 1e-5)
    P.act(rows.r(rows.t[:, 16:32]), rows.r(rows.t[:, 16:32]), AF.Exp)
    P.ts("dve", rows.r(rows.t[:, 16:32]), rows.r(rows.t[:, 16:32]), -1.0, ALU.mult)
    src = W_in.ap[0][:, 4096:4128].rearrange("(kk p) c -> p kk c", p=128)
    P.dma("pool", wsm.r(), R(src, W_in.keys))

    loc = {}

    def conv_tile(col0, cidx, post):
        xpad, dgl = loc["xpad"], loc["dgl"]
        s, w = load_in_w(k, W_in, 0, col0)
        i0 = k.identf.t[:].unsqueeze(1).to_broadcast([128, 4, 128])
        i1 = cw.t[:, cidx, 0:4].unsqueeze(2).to_broadcast([128, 4, 128])
        P.tt("pool", dgl.r(), k.identf.r(i0), cw.r(i1), ALU.mult)
        for b in range(NBLK):
            bk = P.bank()
            for kk in range(8):
                P.mm(psb(k, bk), k.wr.r(w[:, kk, :], s), hs(k, kk, b), start=(kk == 0), stop=(kk == 7))
            dst, _, _ = conv_dst(xpad.t, b)
            P.copy("dve", xpad.r(dst), k.ps.r(psview(k, bk, b), bk))
        for b in range(NBLK):
            bk = P.bank()
            for kk in range(4):
                P.mm(k.ps.r(psview(k, bk, b), bk), dgl.r(dgl.t[:, kk, :]), xpad.r(conv_rhs(xpad.t, b, kk)), start=(kk == 0), stop=(kk == 3))
            post(b, bk)

    def norm_post(which, scale, epsb):
        def post(b, bk):
            qf, sqb, rn = loc["qf"], loc["sqb"], loc["rn"]
            P.act(qf.r(), psb(k, bk), AF.Exp, scale=-1.0)
            P.act(qf.r(), qf.r(), AF.Ln, bias=c_one.r())
            P.act(qf.r(), qf.r(), AF.Exp, scale=-1.0)
            P.tt("dve", qf.r(), psb(k, bk), qf.r(), ALU.mult)
            P.act(sqb.r(), qf.r(), AF.Square)
            b2 = P.bank()
            P.mm(psb(k, b2), k.onesb.r(), sqb.r())
            P.act(rn.r(), psb(k, b2), AF.Ln, bias=epsb.r(), scale=scale)
            P.act(rn.r(), rn.r(), AF.Exp, scale=-0.5)
            dst = kq.t[:, 4 * b:4 * b + 4, which, :]
            P.tt("dve", kq.r(dst, range(4 * b, 4 * b + 4)), qf.r(qf.t.rearrange("p (c m) -> p c m", m=128)), rn.r(rn.t.rearrange("p (c m) -> p c m", m=128)), ALU.mult)
        return post

    def v_post(b, bk):
        qf = loc["qf"]
        P.act(qf.r(), psb(k, bk), AF.Exp, scale=-1.0)
        P.act(qf.r(), qf.r(), AF.Ln, bias=c_one.r())
        P.act(qf.r(), qf.r(), AF.Exp, scale=-1.0)
        P.tt("dve", v_fm.r(v_fm.t[:, b * BLK:(b + 1) * BLK], range(4 * b, 4 * b + 4)), psb(k, bk), qf.r(), ALU.mult)

    nheads = k.cfg.get("gdn_heads", 8)
    for hh in range(nheads):
        src = W_in.ap[0][:, 3072 + hh * 128:3072 + (hh + 1) * 128].rearrange("(kk p) c -> p kk c", p=128)
        P.dma("pool", wz.r(), R(src, W_in.keys))
        bk = P.bank()
        for c in range(NCH):
            for kk in range(8):
                rhs = wsm.t[:, kk, :].rearrange("p (t d h) -> p t d h", t=2, d=2)[:, :, :, hh]
                out = k.ps.t[:, bk, c * 4:(c + 1) * 4].rearrange("p (t d) -> p t d", t=2)
                P.mm(k.ps.r(out, bk), hch(k, kk, c), wsm.r(rhs), start=(kk == 0), stop=(kk == 7))
        psv = k.ps.t[:, bk, 0:NCH * 4].rearrange("p (c t d) -> p c t d", t=2, d=2)
        beta, g, ngc, negegc, eout, egl, tmp = sc["beta"], sc["g"], sc["ngc"], sc["negegc"], sc["eout"], sc["egl"], sc["tmp"]
        P.copy("dve", tmp.r(), k.ps.r(psv[:, :, 0, :], bk))
        P.act(beta.r(), tmp.r(), AF.Exp, scale=-1.0)
        P.act(beta.r(), beta.r(), AF.Ln, bias=c_one.r())
        P.act(beta.r(), beta.r(), AF.Exp, scale=-1.0)

        def rowbc(off):
            return rows.t[:, off:off + 16].rearrange("p (d h) -> p d h", d=2)[:, :, hh].unsqueeze(1).to_broadcast([128, NCH, 2])
        P.tt("dve", tmp.r(), k.ps.r(psv[:, :, 1, :], bk), rows.r(rowbc(0)), ALU.add)
        P.ts("dve", g.r(), tmp.r(), -1.0, ALU.mult)
        P.tt("dve", g.r(), g.r(), tmp.r(), ALU.max)
        P.act(g.r(), g.r(), AF.Exp, scale=-1.0)
        P.act(g.r(), g.r(), AF.Ln, bias=c_one.r())
        P.ts("dve", tmp.r(), tmp.r(), 0.0, ALU.max)
        P.tt("dve", g.r(), g.r(), tmp.r(), ALU.add)
        P.tt("dve", g.r(), g.r(), rows.r(rowbc(16)), ALU.mult)
        bk2 = P.bank()
        bk3 = P.bank()
        for c in range(NCH):
            for d in range(2):
                P.mm(k.ps.r(k.ps.t[:, bk2, c * 2 + d:c * 2 + d + 1], bk2), masks.r(masks.t[:, d, :]), g.r(g.t[:, c, d:d + 1]))
            P.mm(k.ps.r(k.ps.t[:, bk3, c * 2:c * 2 + 2], bk3), onesf.r(), g.r(g.t[:, c, :]))
        ps2 = k.ps.t[:, bk2, 0:NCH * 2].rearrange("p (c d) -> p c d", d=2)
        ps3 = k.ps.t[:, bk3, 0:NCH * 2].rearrange("p (c d) -> p c d", d=2)
        P.ts("dve", ngc.r(), k.ps.r(ps2, bk2), -1.0, ALU.mult)
        P.act(negegc.r(), ngc.r(), AF.Exp, scale=-1.0)
        P.ts("dve", negegc.r(), negegc.r(), -1.0, ALU.mult)
        P.copy("dve", egl.r(), k.ps.r(ps3, bk3))
        P.tt("dve", eout.r(), egl.r(), ngc.r(), ALU.add)
        P.act(eout.r(), eout.r(), AF.Exp)
        P.act(egl.r(), egl.r(), AF.Exp)
        P.barrier()
        mkp = P.mark()
        xpad = P.buf("g_xpad", [128, CPW + 1], BF16)
        dgl = P.buf("g_dg", [128, 4, 128], BF16)
        qf = P.buf("g_qf", [128, BLK], F32)
        sqb = P.buf("g_sq", [128, BLK], BF16)
        rn = P.buf("g_rn", [128, BLK], F32)
        loc.update(xpad=xpad, dgl=dgl, qf=qf, sqb=sqb, rn=rn)
        P.memset("pool", xpad.r(), 0.0)
        conv_tile(hh * 128, hh, norm_post(1, 128.0, c_eps6q))
        conv_tile(1024 + hh * 128, 8 + hh, norm_post(0, 1.0, c_eps6))
        conv_tile(2048 + hh * 128, 16 + hh, v_post)
        for (srcfn, dstb) in ((lambda c: v_fm.r(v_fm.t[:, c * 128:(c + 1) * 128], c), v_tm), (lambda c: kq.r(kq.t[:, c, 0, :], c), k_tm)):
            for c0 in range(0, NCH, 8):
                bk = P.bank()
                pb = k.ps.t[:, bk, :].bitcast(BF16)
                n = min(8, NCH - c0)
                for ci in range(n):
                    P.tr(k.ps.r(pb[:, ci * 128:(ci + 1) * 128], bk), srcfn(c0 + ci), k.identb.r())
                srcv = pb[:, 0:n * 128].rearrange("p (c m) -> p c m", m=128)
                P.copy("act", dstb.r(dstb.t[:, c0:c0 + n, :], range(c0, c0 + n)), k.ps.r(srcv, bk))
        P.barrier()
        P.release(mkp)
        mku = P.mark()
        GS = 3
        G = 2 * GS
        NR = 2 * G
        U = {}
        for nm in ["egcr", "dec", "NB", "NA"]:
            U[nm] = P.buf("gu_" + nm, [128, G, 128], BF16, nsub=G)
        U["Y"] = P.buf("gu_Y", [128, G, 2, 128], BF16, nsub=G)
        for nm in ["attn", "qin", "kout"]:
            U[nm] = P.buf("gu_" + nm, [128, NR, 128], BF16, nsub=NR)
        U["T"] = P.buf("gu_T", [128, NR, 2, 128], BF16, nsub=NR)
        RES = ("attn", "qin", "kout", "T")
        nsteps = k.cfg.get("gdn_steps", NCH)
        groups = [list(range(g0, min(g0 + GS, nsteps))) for g0 in range(0, nsteps, GS)]

        def ur(nm, ui, gi, sub=None):
            b_ = U[nm]
            si = (gi % 2) * G + ui if nm in RES else ui
            return b_.r(b_.t[:, si] if sub is None else b_.t[:, si, sub], si)

        def prelim_gen(gi):
            units = [(step, d) for step in groups[gi] for d in range(2)]
            cs = [(st_ if d_ == 0 else NCH - 1 - st_) for (st_, d_) in units]
            banks = {}
            for ui, (st_, d) in enumerate(units):
                c = cs[ui]
                bkR = ui
                banks[("R", ui)] = bkR
                gbc = g.t[:, c, d:d + 1].to_broadcast([128, 128])
                r0 = k.ps.r(k.ps.t[:, bkR, 0:128], bkR)
                r1 = k.ps.r(k.ps.t[:, bkR, 128:256], bkR)
                P.mm(r0, g.r(gbc), masks.r(masks.t[:, d, :]))
                P.mm(r1, g.r(gbc), masks.r(masks.t[:, d, :]), start=True, stop=False)
                P.mm(r1, k.identb.r(), negb.r(negb.t[:, d, :]), start=False, stop=True)
                yield
            for ui, (st_, d) in enumerate(units):
                c = cs[ui]
                bkR = banks[("R", ui)]
                P.act(ur("egcr", ui, gi), k.ps.r(k.ps.t[:, bkR, 0:128], bkR), AF.Exp)
                P.act(ur("dec", ui, gi), k.ps.r(k.ps.t[:, bkR, 128:256], bkR), AF.Exp, bias=ngc.r(ngc.t[:, c, d:d + 1]))
                yield
            for ui, (st_, d) in enumerate(units):
                c = cs[ui]
                bkK = ui
                banks[("K", ui)] = bkK
                P.mm(k.ps.r(k.ps.t[:, bkK, 0:256], bkK), kq.r(kq.t[:, c, 0, :], c), kq.r(kq.t[:, c, :, :].rearrange("p a m -> p (a m)"), c))
                yield
            for ui, (st_, d) in enumerate(units):
                c = cs[ui]
                bkK = banks[("K", ui)]
                P.stt(ur("NB", ui, gi), k.ps.r(k.ps.t[:, bkK, 0:128], bkK), beta.r(beta.t[:, c, d:d + 1]), ur("dec", ui, gi), ALU.mult, ALU.mult)
                P.tt("dve", ur("attn", ui, gi), k.ps.r(k.ps.t[:, bkK, 128:256], bkK), ur("dec", ui, gi), ALU.mult)
                yield
            for ui, (st_, d) in enumerate(units):
                bkT = ui
                banks[("T", ui)] = bkT
                pbT = k.ps.t[:, bkT, :].bitcast(BF16)
                P.tr(k.ps.r(pbT[:, 0:128], bkT), ur("NB", ui, gi), k.identb.r())
                yield
            for ui, (st_, d) in enumerate(units):
                c = cs[ui]
                bkT = banks[("T", ui)]
                pbT = k.ps.t[:, bkT, :].bitcast(BF16)
                P.copy("act", ur("NA", ui, gi), k.ps.r(pbT[:, 0:128], bkT))
                P.tt("pool", ur("qin", ui, gi), kq.r(kq.t[:, c, 1, :], c), ur("egcr", ui, gi), ALU.mult)
                P.act(ur("kout", ui, gi), k_tm.r(k_tm.t[:, c, :], c), AF.Identity, scale=eout.r(eout.t[:, c, d:d + 1]))
                yield
            for ui, (st_, d) in enumerate(units):
                P.tt("pool", ur("Y", ui, gi, 0), ur("NA", ui, gi), lvl.r(lvl.t[:, d, 0, 0]), ALU.mult)
                P.tt("pool", ur("Y", ui, gi, 1), ur("NB", ui, gi), lvl.r(lvl.t[:, d, 0, 1]), ALU.mult)
                P.tt("pool", ur("T", ui, gi), I2.r(), ur("Y", ui, gi), ALU.add)
                yield
            for lv in range(1, 7):
                for ui, (st_, d) in enumerate(units):
                    bkY = ui
                    banks[("Y", ui)] = bkY
                    P.mm(k.ps.r(k.ps.t[:, bkY, 0:128], bkY), ur("NB", ui, gi), ur("T", ui, gi, 0))
                    P.mm(k.ps.r(k.ps.t[:, bkY, 128:256], bkY), ur("NA", ui, gi), ur("T", ui, gi, 1))
                    yield
                for ui, (st_, d) in enumerate(units):
                    bkY = banks[("Y", ui)]
                    P.tt("dve", ur("Y", ui, gi), k.ps.r(k.ps.t[:, bkY, 0:256].rearrange("p (a m) -> p a m", m=128), bkY), lvl.r(lvl.t[:, d, lv]), ALU.mult)
                    yield
                for ui, (st_, d) in enumerate(units):
                    bkZ = ui
                    banks[("Z", ui)] = bkZ
                    P.mm(k.ps.r(k.ps.t[:, bkZ, 0:128], bkZ), ur("T", ui, gi, 1), ur("Y", ui, gi, 0))
                    P.mm(k.ps.r(k.ps.t[:, bkZ, 128:256], bkZ), ur("T", ui, gi, 0), ur("Y", ui, gi, 1))
                    yield
                for ui, (st_, d) in enumerate(units):
                    bkZ = banks[("Z", ui)]
                    P.tt("dve", ur("T", ui, gi), ur("T", ui, gi), k.ps.r(k.ps.t[:, bkZ, 0:256].rearrange("p (a m) -> p a m", m=128), bkZ), ALU.add)
                    yield

        cbs = [0]

        def cbank():
            cbs[0] ^= 1
            return 6 + cbs[0]

        def chain_gen(gi):
            units = [(step, d) for step in groups[gi] for d in range(2)]
            for ui, (step, d) in enumerate(units):
                c = step if d == 0 else NCH - 1 - step
                first_visit = (c <= 9) if d == 0 else (c >= 10)
                if nsteps < NCH:
                    first_visit = True
                seg = seg_of_chunk(c)
                c_first, c_n = SEG_CH[seg]
                seg_start = (c == c_first) if d == 0 else (c == c_first + c_n - 1)
                seg_end = (c == c_first + c_n - 1) if d == 0 else (c == c_first)
                if seg_start:
                    if seg == 2:
                        P.dma("sp", S.r(S.t[:, d], d), R(k.dram("gdn_s0").ap[j, d, hh], k.dram("gdn_s0").keys))
                    else:
                        P.memset("dve", S.r(S.t[:, d], d), 0.0)
                    P.copy("act", Sb.r(Sb.t[:, d], d), S.r(S.t[:, d], d))
                    yield
                kc = kq.r(kq.t[:, c, 0, :], c)
                bkC = cbank()
                P.mm(k.ps.r(k.ps.t[:, bkC, 0:128], bkC), kc, Sb.r(Sb.t[:, d], d))
                yield
                P.stt(Rp.r(Rp.t[:, d], d), k.ps.r(k.ps.t[:, bkC, 0:128], bkC), negegc.r(negegc.t[:, c, d:d + 1]), v_tm.r(v_tm.t[:, c, :], c), ALU.mult, ALU.add)
                yield
                bkV = cbank()
                P.mm(k.ps.r(k.ps.t[:, bkV, 0:128], bkV), ur("T", ui, gi, 1), Rp.r(Rp.t[:, d], d))
                yield
                P.act(vn.r(vn.t[:, d], d), k.ps.r(k.ps.t[:, bkV, 0:128], bkV), AF.Identity, scale=beta.r(beta.t[:, c, d:d + 1]))
                yield
                bkS = cbank()
                pS = k.ps.r(k.ps.t[:, bkS, 0:128], bkS)
                P.mm(pS, ur("kout", ui, gi), vn.r(vn.t[:, d], d))
                bkO = cbank()
                po = k.ps.r(k.ps.t[:, bkO, 0:128], bkO)
                P.mm(po, ur("qin", ui, gi), Sb.r(Sb.t[:, d], d), start=True, stop=False)
                P.mm(po, ur("attn", ui, gi), vn.r(vn.t[:, d], d), start=False, stop=True)
                yield
                P.stt(S.r(S.t[:, d], d), S.r(S.t[:, d], d), egl.r(egl.t[:, c, d:d + 1]), pS, ALU.mult, ALU.add)
                yield
                P.copy("act", Sb.r(Sb.t[:, d], d), S.r(S.t[:, d], d))
                yield
                if first_visit:
                    P.copy("act", o_tm.r(o_tm.t[:, c, :], c), po)
                    yield
                else:
                    P.tt("dve", ot.r(), po, o_tm.r(o_tm.t[:, c, :], c), ALU.add)
                    yield
                    o_, i_, a_ = junk.t, ot.t, ssq.t[:, 0:1]
                    P.op("act", lambda e, o_=o_, i_=i_, a_=a_: e.activation(out=o_, in_=i_, func=AF.Square, accum_out=a_),
                         reads=[ot.r()], writes=[junk.r(), ssq.r()])
                    P.act(ssq.r(ssq.t[:, 1:2]), ssq.r(ssq.t[:, 0:1]), AF.Ln, bias=c_eps5.r(), scale=1.0 / 128)
                    P.act(ssq.r(ssq.t[:, 1:2]), ssq.r(ssq.t[:, 1:2]), AF.Exp, scale=-0.5)
                    yield
                    bkZ = cbank()
                    for kk in range(8):
                        P.mm(k.ps.r(k.ps.t[:, bkZ, 0:128], bkZ), hch(k, kk, c), wz.r(wz.t[:, kk, :]), start=(kk == 0), stop=(kk == 7))
                    yield
                    P.act(sz.r(), k.ps.r(k.ps.t[:, bkZ, 0:128], bkZ), AF.Exp, scale=-1.0)
                    P.act(sz.r(), sz.r(), AF.Ln, bias=c_one.r())
                    P.act(sz.r(), sz.r(), AF.Exp, scale=-1.0)
                    P.stt(ot.r(), ot.r(), ssq.r(ssq.t[:, 1:2]), rows.r(rows.t[:, 32:160]), ALU.mult, ALU.mult)
                    yield
                    P.tt("dve", ot.r(), ot.r(), sz.r(), ALU.mult)
                    P.tt("dve", ogb.r(), ot.r(), k.ps.r(k.ps.t[:, bkZ, 0:128], bkZ), ALU.mult)
                    yield
                    bkT2 = cbank()
                    pbT2 = k.ps.t[:, bkT2, :].bitcast(BF16)
                    P.tr(k.ps.r(pbT2[:, 0:128], bkT2), ogb.r(), k.identb.r())
                    yield
                    P.copy("act", v_fm.r(v_fm.t[:, c * 128:(c + 1) * 128], c), k.ps.r(pbT2[:, 0:128], bkT2))
                    yield
                if seg_end and seg < 2:
                    P.dma("sp", R(k.dram("gdn_out").ap[j, seg, d, hh], k.dram("gdn_out").keys), S.r(S.t[:, d], d))

        def run_all(gen):
            for _ in gen:
                pass

        def merge(pg, cg, ratio):
            pdone = cdone = False
            while not (pdone and cdone):
                if not cdone:
                    try:
                        next(cg)
                    except StopIteration:
                        cdone = True
                for _ in range(ratio):
                    if pdone:
                        break
                    try:
                        next(pg)
                    except StopIteration:
                        pdone = True

        run_all(prelim_gen(0))
        for gi in range(len(groups)):
            if gi + 1 < len(groups):
                merge(prelim_gen(gi + 1), chain_gen(gi), k.cfg.get("gdn_ratio", 3))
            else:
                run_all(chain_gen(gi))
        P.barrier()
        P.release(mku)
        if nsteps == NCH:
            outproj_acc_g(k, W_out, hh * 128, lambda b: v_fm.r(v_fm.t[:, b * BLK:(b + 1) * BLK], range(4 * b, 4 * b + 4)))
    P.barrier()
    P.release(mk)


def outproj_acc_g(k, W, r0, o_region):
    P = k.P
    s, wo = load_out_w(k, W, 0, r0)
    for m in range(8):
        for b in range(NBLK):
            bk = P.bank()
            P.mm(psb(k, bk), k.wr.r(wo[:, m * 128:(m + 1) * 128], s), o_region(b))
            c = bc(b)
            P.stt(xs(k, m, b), psb(k, bk), k.mod.r(k.mod.t[:, Q_G1 + m, c:c + 1]), xs(k, m, b), ALU.mult, ALU.add)


NCORES = 8


def f32(a):
    return np.ascontiguousarray(np.asarray(a, dtype=np.float32))


def prep_inputs(inp):
    g = {k: np.asarray(v) for k, v in inp.items()}
    DEPTH = 4
    shared = {}
    shared["ident"] = np.eye(128, dtype=np.float32)
    for L in range(DEPTH):
        shared["ada_w%d" % L] = f32(g["ada_w"][L:L + 1])
        shared["ffn_w_in%d" % L] = f32(g["ffn_w_in"][L:L + 1])
        shared["ffn_w_out%d" % L] = f32(g["ffn_w_out"][L:L + 1])
    shared["ada_b"] = f32(g["ada_b"].reshape(DEPTH, 48, 128).transpose(0, 2, 1))
    shared["lng"] = f32(g["ln_g"].reshape(DEPTH, 2, 8, 128).transpose(0, 1, 3, 2))
    shared["lnb"] = f32(g["ln_b"].reshape(DEPTH, 2, 8, 128).transpose(0, 1, 3, 2))
    shared["fcw"] = f32(g["ffn_conv"].reshape(DEPTH, 9, 44, 128).transpose(0, 3, 2, 1))
    shared["lru_w_in"] = f32(g["lru_w_in"])
    shared["lru_w_out"] = f32(g["lru_w_out"])
    shared["lru_gate_w"] = f32(g["lru_gate_w"])
    sm = np.zeros((128, 10, 12), np.float32)
    sm[:, :, 0:4] = g["lru_conv"][0].reshape(4, 10, 128).transpose(2, 1, 0)
    sm[:, :, 4] = g["lru_conv_b"][0].reshape(10, 128).T
    sm[:, :, 5:9] = g["lru_gate_b"][0].reshape(4, 10, 128).transpose(2, 1, 0)
    sm[:, :, 9:11] = g["lru_lambda"][0].reshape(2, 10, 128).transpose(2, 1, 0)
    shared["lru_sm"] = sm
    tri_f = np.triu(np.ones((128, 128), np.float32))
    tri_b = np.tril(np.ones((128, 128), np.float32))
    shared["cmask"] = np.stack([tri_f, tri_b, (1 - tri_f) * -30000.0, (1 - tri_b) * -30000.0]).astype(np.float32)
    shared["ssd_w_in"] = f32(g["ssd_w_in"])
    shared["ssd_w_out"] = f32(g["ssd_w_out"])
    cw = np.zeros((128, 24, 5), np.float32)
    cw[:, :, 0:4] = g["ssd_conv"][0].reshape(4, 24, 128).transpose(2, 1, 0)
    cw[:, :, 4] = g["ssd_conv_b"][0].reshape(24, 128).T
    shared["ssd_cw"] = cw
    row = np.concatenate([g["ssd_dt_bias"][0].reshape(64), g["ssd_a_log"][0].reshape(64), g["ssd_d"][0].reshape(32)])
    shared["ssd_rows"] = f32(np.broadcast_to(row[None, :], (128, 160)))
    shared["ssd_nrm"] = f32(np.broadcast_to(g["ssd_norm"][0][None, :], (128, 2048)))
    for jj in range(2):
        shared["gdn_w_in%d" % jj] = f32(g["gdn_w_in"][jj:jj + 1])
        shared["gdn_w_out%d" % jj] = f32(g["gdn_w_out"][jj:jj + 1])
    shared["gdn_cw"] = f32(g["gdn_conv"].reshape(2, 4, 24, 128).transpose(0, 3, 2, 1))
    grow = np.concatenate([g["gdn_dt_bias"].reshape(2, 16), g["gdn_a_log"].reshape(2, 16), g["gdn_norm"].reshape(2, 128)], axis=1)
    shared["gdn_rows"] = f32(np.broadcast_to(grow[:, None, :], (2, 128, 160)))
    li = np.arange(128)[:, None]
    si = np.arange(128)[None, :]
    lv = np.zeros((2, 7, 2, 128, 128), np.float32)
    for jl in range(7):
        Bs = 2 ** jl
        mA = ((li // (2 * Bs)) == (si // (2 * Bs))) & ((li % (2 * Bs)) >= Bs) & ((si % (2 * Bs)) < Bs)
        mA = mA.astype(np.float32)
        lv[0, jl, 0] = -mA
        lv[0, jl, 1] = -mA.T
        lv[1, jl, 0] = -mA.T
        lv[1, jl, 1] = -mA
    shared["glvl"] = lv
    maps = []
    for i in range(NCORES):
        p0, p1, sb = 2 * i, 2 * i + 1, i % 4
        xin = np.concatenate([g["x_prompt"][p0].T, g["x_prompt"][p1].T, g["x_sample"][sb].T], axis=1)
        cond = np.stack([g["c_ctx"].reshape(8, 128).T, g["c"][sb].reshape(8, 128).T], axis=-1)
        m = dict(shared)
        m["xin"] = f32(xin)
        m["cond"] = f32(cond)
        m["ssd_s0"] = f32(g["state_ssd"][sb, 0].reshape(2, 8, 4, 64, 128).transpose(0, 1, 4, 2, 3).reshape(2, 8, 128, 256))
        m["gdn_s0"] = f32(g["state_gdn"][sb])
        m["lru_s0"] = f32(g["state_lru"][sb, 0].reshape(2, 10, 128).transpose(2, 1, 0))
        maps.append(m)
    return maps


def assemble(results, inp):
    BATCH, SEQ, D = 16, 256, 1024
    yp = np.zeros((BATCH, SEQ, D), np.float32)
    ys = np.zeros((4, 2048, D), np.float32)
    for i in range(NCORES):
        y = np.asarray(results[i]["yout"])
        yp[2 * i] = y[:, 0:256].T
        yp[2 * i + 1] = y[:, 256:512].T
        if i < 4:
            ys[i] = y[:, 512:].T
    nl = np.zeros((BATCH, 1, 2, 1280), np.float32)
    for i in range(NCORES):
        if "lru_out" in results[i]:
            o = np.asarray(results[i]["lru_out"])
            for pi in range(2):
                nl[2 * i + pi, 0] = o[:, :, pi, :].transpose(2, 1, 0).reshape(2, 1280)
    nssd = np.zeros((BATCH, 1, 2, 32, 64, 128), np.float32)
    for i in range(NCORES):
        if "ssd_out" in results[i]:
            o = np.asarray(results[i]["ssd_out"])
            for pi in range(2):
                nssd[2 * i + pi, 0] = o[pi].reshape(2, 32, 64, 128)
    ngdn = np.zeros((BATCH, 2, 2, 8, 128, 128), np.float32)
    for i in range(NCORES):
        if "gdn_out" in results[i]:
            o = np.asarray(results[i]["gdn_out"])
            for pi in range(2):
                ngdn[2 * i + pi] = o[:, pi]
    return yp, ys, nl, nssd, ngdn


_NC_CACHE = {}


def kernel(**inputs):
    cfg = {}
    if "nc" not in _NC_CACHE:
        _NC_CACHE["nc"] = build(cfg)
    nc, used = _NC_CACHE["nc"]
    maps = prep_inputs(inputs)
    maps = [{kk: v for kk, v in mm.items() if kk in used} for mm in maps]
    res = run_bass_kernel_spmd(nc, maps, core_ids=list(range(NCORES)))
    yp, ys, nl, nssd, ngdn = assemble(res.results, inputs)
    return (yp, ys, ngdn, nssd, nl)
```

```python
import numpy as np
import concourse.bass as bass
import concourse.mybir as mybir
from concourse.bass_utils import run_bass_kernel_spmd
from contextlib import ExitStack

F32 = mybir.dt.float32
F32R = mybir.dt.float32r
BF16 = mybir.dt.bfloat16
ALU = mybir.AluOpType
AF = mybir.ActivationFunctionType
AX = mybir.AxisListType

ENGS = ["pe", "act", "dve", "pool", "sp"]
NDS = 24
MAXEMB = 1
ARENA_WORDS = 53000


class R:
    __slots__ = ("ap", "keys")

    def __init__(self, ap, keys):
        self.ap = ap
        self.keys = keys


class Buf:
    def __init__(self, prog, name, shape, dtype, nsub=1, psum=False):
        self.name = name
        self.nsub = nsub
        if psum:
            self.t = prog.st.enter_context(prog.nc.psum_tensor(name, shape, dtype))
        else:
            n = 1
            for d in shape[1:]:
                n *= d
            esz = 2 if dtype == BF16 else 4
            words = (n * esz + 3) // 4
            off = prog.aoff
            prog.aoff += words
            assert prog.aoff <= ARENA_WORDS, (name, prog.aoff)
            prog.apeak = max(prog.apeak, prog.aoff)
            ap = prog.arena[:, off:off + words]
            if dtype == BF16:
                ap = ap.bitcast(BF16)
            ap = ap[:, 0:n]
            if len(shape) > 2:
                names = " ".join("d%d" % i for i in range(len(shape) - 1))
                kw = {"d%d" % i: shape[i + 1] for i in range(len(shape) - 1)}
                ap = ap.rearrange("p (%s) -> p %s" % (names, names), **kw)
            self.t = ap

    def r(self, ap=None, sub=None):
        if ap is None:
            ap = self.t[:] if not hasattr(self.t, "rearrange") else self.t
        if sub is None:
            keys = [(self.name, i) for i in range(self.nsub)]
        elif isinstance(sub, (list, tuple, range)):
            keys = [(self.name, i) for i in sub]
        else:
            keys = [(self.name, sub)]
        return R(ap, keys)


class Prog:
    def __init__(self, nc, st):
        self.nc = nc
        self.st = st
        self.q = {e: [] for e in ENGS}
        self.cnt = {e: 0 for e in ENGS}
        self.sem = {e: st.enter_context(nc.semaphore("s_" + e)) for e in ENGS}
        self.dsem = [st.enter_context(nc.semaphore("d%d" % i)) for i in range(NDS)]
        self.dcnt = [0] * NDS
        self.dnext = 0
        self.known = {e: {} for e in ENGS}
        self.last_w = {}
        self.readers = {}
        self.nops = 0
        self.nwaits = 0
        self.bar = {e: None for e in ENGS}
        self._bank = 0
        self.arena_t = st.enter_context(nc.sbuf_tensor("arena", [128, ARENA_WORDS], F32))
        self.arena = self.arena_t[:]
        self.aoff = 0
        self.apeak = 0

    def mark(self):
        return self.aoff

    def release(self, m):
        self.aoff = m

    def bank(self):
        b = self._bank
        self._bank = (self._bank + 1) % 8
        return b

    def barrier(self):
        snap = {e: self.cnt[e] for e in ENGS if self.cnt[e] > 0}
        for i in range(NDS):
            if self.dcnt[i] > 0:
                snap["d%d" % i] = self.dcnt[i]
        for e in ENGS:
            self.bar[e] = dict(snap)

    def buf(self, name, shape, dtype, nsub=1, psum=False):
        return Buf(self, name, shape, dtype, nsub, psum)

    def _collect(self, eng, reads, writes):
        need = {}

        def add(tok):
            if tok is None:
                return
            semid, val, snap = tok
            if eng == "pe" and semid == "pe":
                return
            if need.get(semid, (0, None))[0] < val:
                need[semid] = (val, snap)

        for k in reads:
            add(self.last_w.get(k))
        for k in writes:
            add(self.last_w.get(k))
            rd = self.readers.get(k)
            if rd:
                for tok in rd.values():
                    add(tok)
        kn = self.known[eng]
        if self.bar[eng] is not None:
            for semid, val in self.bar[eng].items():
                if eng == "pe" and semid == "pe":
                    continue
                if semid == eng and val >= self.cnt[eng] + 1:
                    continue
                if need.get(semid, (0, None))[0] < val:
                    need[semid] = (val, None)
            self.bar[eng] = None
        waits = []
        for semid, (val, snap) in need.items():
            if kn.get(semid, 0) >= val:
                continue
            waits.append((semid, val))
        for semid, (val, snap) in need.items():
            if kn.get(semid, 0) < val:
                kn[semid] = val
            if snap:
                for s2, v2 in snap.items():
                    if kn.get(s2, 0) < v2:
                        kn[s2] = v2
        return waits

    def _commit(self, tok, reads, writes):
        for k in writes:
            self.last_w[k] = tok
            self.readers[k] = {}
        for k in reads:
            if k in writes:
                continue
            self.readers.setdefault(k, {})[tok[0]] = tok

    def op(self, eng, fn, reads=(), writes=()):
        rk = [k for r in reads if r is not None for k in r.keys]
        wk = [k for r in writes if r is not None for k in r.keys]
        if eng != "pe":
            for k_ in rk:
                if k_[0] == "ps" and k_ not in wk:
                    wk.append(k_)
        waits = self._collect(eng, rk, wk)
        self.cnt[eng] += 1
        tok = (eng, self.cnt[eng], dict(self.known[eng]))
        self.q[eng].append((waits, fn, None))
        self._commit(tok, rk, wk)
        self.nops += 1
        self.nwaits += len(waits)
        return tok

    def dma(self, eng, out, in_, **kw):
        rk = list(in_.keys)
        wk = list(out.keys)
        i = self.dnext
        self.dnext = (self.dnext + 1) % NDS
        semid = "d%d" % i
        waits = self._collect(eng, rk, wk)
        kn = self.known[eng]
        if kn.get(semid, 0) < self.dcnt[i]:
            waits.append((semid, self.dcnt[i]))
            kn[semid] = self.dcnt[i]
        self.dcnt[i] += 16
        tok = (semid, self.dcnt[i], dict(kn))
        oap, iap = out.ap, in_.ap

        def fn(e):
            return e.dma_start(out=oap, in_=iap, **kw)

        self.q[eng].append((waits, fn, i))
        self._commit(tok, rk, wk)
        self.nops += 1
        self.nwaits += len(waits)
        return tok

    def _semh(self, semid):
        if semid in self.sem:
            return self.sem[semid]
        return self.dsem[int(semid[1:])]

    def emit(self, final_wait_eng="sp"):
        nc = self.nc
        finals = []
        for e in ENGS:
            if self.cnt[e] > 0:
                finals.append((e, self.cnt[e]))
        for i in range(NDS):
            if self.dcnt[i] > 0:
                finals.append(("d%d" % i, self.dcnt[i]))
        eng_objs = {}
        with nc.Block() as block:
            def mk(ename):
                def body(e):
                    for waits, fn, dsi in self.q[ename]:
                        if dsi is not None or len(waits) > MAXEMB:
                            for semid, val in waits:
                                e.wait_ge(self._semh(semid), val)
                            ins = fn(e)
                        else:
                            ins = fn(e)
                            for semid, val in waits:
                                ins._wait_ge(self._semh(semid), val)
                        if dsi is None:
                            ins.then_inc(self.sem[ename], 1)
                        else:
                            ins.then_inc(self.dsem[dsi], 16)
                    if ename == final_wait_eng:
                        for semid, val in finals:
                            e.wait_ge(self._semh(semid), val)
                return body
            block.tensor(mk("pe"))
            block.scalar(mk("act"))
            block.vector(mk("dve"))
            block.gpsimd(mk("pool"))
            block.sync(mk("sp"))

    def mm(self, out, lhsT, rhs, start=True, stop=True, extra_reads=()):
        o, l, r = out.ap, lhsT.ap, rhs.ap
        return self.op("pe", lambda e: e.matmul(o, l, r, start=start, stop=stop),
                       reads=[lhsT, rhs] + list(extra_reads) + ([] if start else [out]), writes=[out])

    def tr(self, out, in_, ident):
        o, i, d = out.ap, in_.ap, ident.ap
        return self.op("pe", lambda e: e.transpose(o, i, d), reads=[in_, ident], writes=[out])

    def act(self, out, in_, func, bias=None, scale=None, eng="act"):
        o, i = out.ap, in_.ap
        kw = {}
        rd = [in_]
        if bias is not None:
            if isinstance(bias, R):
                kw["bias"] = bias.ap
                rd.append(bias)
            else:
                kw["bias"] = float(bias)
        if scale is not None:
            if isinstance(scale, R):
                kw["scale"] = scale.ap
                rd.append(scale)
            else:
                kw["scale"] = float(scale)
        return self.op("act", lambda e: e.activation(out=o, in_=i, func=func, **kw), reads=rd, writes=[out])

    def tt(self, eng, out, in0, in1, op):
        o, a, b = out.ap, in0.ap, in1.ap
        return self.op(eng, lambda e: e.tensor_tensor(out=o, in0=a, in1=b, op=op), reads=[in0, in1], writes=[out])

    def ts(self, eng, out, in0, s1, op0, s2=None, op1=None):
        o, a = out.ap, in0.ap
        rd = [in0]
        v1 = s1
        if isinstance(s1, R):
            rd.append(s1)
            v1 = s1.ap
        v2 = s2
        if isinstance(s2, R):
            rd.append(s2)
            v2 = s2.ap
        if op1 is None:
            return self.op(eng, lambda e: e.tensor_scalar(out=o, in0=a, scalar1=v1, scalar2=None, op0=op0), reads=rd, writes=[out])
        return self.op(eng, lambda e: e.tensor_scalar(out=o, in0=a, scalar1=v1, scalar2=v2, op0=op0, op1=op1), reads=rd, writes=[out])

    def stt(self, out, in0, scalar, in1, op0, op1):
        o, a, b = out.ap, in0.ap, in1.ap
        rd = [in0, in1]
        sv = scalar
        if isinstance(scalar, R):
            rd.append(scalar)
            sv = scalar.ap
        return self.op("dve", lambda e: e.scalar_tensor_tensor(out=o, in0=a, scalar=sv, in1=b, op0=op0, op1=op1), reads=rd, writes=[out])

    def copy(self, eng, out, in_):
        o, i = out.ap, in_.ap
        if eng == "act":
            return self.op("act", lambda e: e.copy(out=o, in_=i), reads=[in_], writes=[out])
        return self.op(eng, lambda e: e.tensor_copy(out=o, in_=i), reads=[in_], writes=[out])

    def memset(self, eng, out, val):
        o = out.ap
        return self.op(eng, lambda e: e.memset(o, val), reads=[], writes=[out])


D = 1024
NT = 2560
NBLK = 5
BLK = 512
DEPTH = 4
FH = 2816
NPAIR = 22
UPW = 2 * 258 + 34 * 66
ALPHA = (2 * DEPTH) ** 0.25
LN_EPS = 1e-5
NW = 5
WSL = 1024


def bc(b):
    return 0 if b == 0 else 1


class K:
    pass


def build(cfg):
    nc = bass.Bass("TRN2", target_bir_lowering=False)
    k = K()
    k.nc = nc
    k.cfg = cfg

    shapes = {
        "xin": [D, NT], "cond": [128, 8, 2], "ident": [128, 128],
        "ada_b": [DEPTH, 128, 48], "lng": [DEPTH, 2, 128, 8], "lnb": [DEPTH, 2, 128, 8],
        "fcw": [DEPTH, 128, 44, 9],
        "lru_w_in": [1, D, 2560], "lru_w_out": [1, 1280, D], "lru_gate_w": [1, 2, 2, 10, 128, 128],
        "lru_sm": [128, 10, 12], "lru_s0": [128, 10, 2],
        "cmask": [4, 128, 128],
        "ssd_w_in": [1, D, 5184], "ssd_w_out": [1, 2048, D], "ssd_cw": [128, 24, 5], "ssd_rows": [128, 160],
        "ssd_nrm": [128, 2048], "ssd_s0": [2, 8, 128, 256],
    }
    for jj in range(2):
        shapes["gdn_w_in%d" % jj] = [1, D, 4128]
        shapes["gdn_w_out%d" % jj] = [1, D, D]
    shapes.update({"gdn_cw": [2, 128, 24, 4], "gdn_rows": [2, 128, 160], "gdn_s0": [2, 2, 8, 128, 128], "glvl": [2, 7, 2, 128, 128]})
    for L in range(DEPTH):
        shapes["ada_w%d" % L] = [1, D, 6 * D]
        shapes["ffn_w_in%d" % L] = [1, D, 2 * FH]
        shapes["ffn_w_out%d" % L] = [1, FH, D]
    oshapes = {"yout": [D, NT], "lru_out": [128, 10, 2, 2], "ssd_out": [2, 2, 2048, 128], "gdn_out": [2, 2, 2, 8, 128, 128]}
    k.shapes = shapes
    k.oshapes = oshapes
    k.decl = {}

    def dram(name):
        if name in k.decl:
            return k.decl[name]
        if name in shapes:
            t = nc.dram_tensor(name, list(shapes[name]), F32, kind="ExternalInput").ap()
        else:
            t = nc.dram_tensor(name, list(oshapes[name]), F32, kind="ExternalOutput").ap()
        k.decl[name] = R(t, [("dram_" + name, 0)])
        return k.decl[name]
    k.dram = dram

    with ExitStack() as st:
        P = Prog(nc, st)
        k.P = P
        k.x = P.buf("x", [128, 8, NT], F32, nsub=40)
        k.h = P.buf("h", [128, 8, NT], BF16, nsub=40)
        k.wr = P.buf("wr", [128, NW, WSL], BF16, nsub=NW)
        k.wnext = 0
        k.ps = P.buf("ps", [128, 8, 512], F32, nsub=8, psum=True)
        k.identf = P.buf("identf", [128, 128], F32)
        k.identb = P.buf("identb", [128, 128], BF16)
        k.onesb = P.buf("onesb", [128, 128], BF16)
        k.csil = P.buf("csil", [128, 8, 2], BF16)
        k.mod = P.buf("mod", [128, 48, 2], F32)
        k.msc = P.buf("msc", [128, 2, 8, 2], F32)
        k.lngb = P.buf("lngb", [128, DEPTH, 2, 8], F32)
        k.lnbb = P.buf("lnbb", [128, DEPTH, 2, 8], F32)
        k.adab = P.buf("adab", [128, DEPTH, 48], F32)

        prologue(k)
        for L in cfg.get("layers", list(range(DEPTH))):
            layer(k, L)
        for j in range(8):
            P.dma("sp", R(k.dram("yout").ap[j * 128:(j + 1) * 128, :], k.dram("yout").keys), k.x.r(k.x.t[:, j, :], range(j * 5, j * 5 + 5)))
        P.emit()
        print("ops", P.nops, "waits", P.nwaits, {e: P.cnt[e] for e in ENGS}, "apeak", P.apeak)
    return nc, set(n for n in k.decl if n in k.shapes)


def xs(k, j, b):
    return k.x.r(k.x.t[:, j, b * BLK:(b + 1) * BLK], j * 5 + b)


def hs(k, j, b):
    return k.h.r(k.h.t[:, j, b * BLK:(b + 1) * BLK], j * 5 + b)


def psb(k, b, n=512):
    return k.ps.r(k.ps.t[:, b, 0:n], b)


def prologue(k):
    P = k.P
    for j in range(8):
        P.dma("sp", k.x.r(k.x.t[:, j, :], range(j * 5, j * 5 + 5)), R(k.dram("xin").ap[j * 128:(j + 1) * 128, :], k.dram("xin").keys))
    P.dma("sp", k.identf.r(), k.dram("ident"))
    P.copy("dve", k.identb.r(), k.identf.r())
    P.memset("dve", k.onesb.r(), 1.0)
    ctmp = P.buf("ctmp", [128, 8, 2], F32)
    P.dma("sp", ctmp.r(), k.dram("cond"))
    P.act(k.csil.r(), ctmp.r(), AF.Silu)
    for L in range(DEPTH):
        P.dma("sp", k.lngb.r(k.lngb.t[:, L]), R(k.dram("lng").ap[L].rearrange("s p j -> p s j"), k.dram("lng").keys))
        P.dma("sp", k.lnbb.r(k.lnbb.t[:, L]), R(k.dram("lnb").ap[L].rearrange("s p j -> p s j"), k.dram("lnb").keys))
        P.dma("sp", k.adab.r(k.adab.t[:, L]), R(k.dram("ada_b").ap[L], k.dram("ada_b").keys))


def wslot(k):
    s = k.wnext
    k.wnext = (k.wnext + 1) % NW
    return s


def load_in_w(k, W, L, c0, ncols=128):
    P = k.P
    s = wslot(k)
    src = W.ap[L][:, c0:c0 + ncols].rearrange("(kk p) c -> p kk c", p=128)
    dst = k.wr.t[:, s, 0:8 * ncols].rearrange("p (kk c) -> p kk c", c=ncols)
    P.dma("pool", k.wr.r(dst, s), R(src, W.keys))
    return s, dst


def load_out_w(k, W, L, r0):
    P = k.P
    s = wslot(k)
    src = W.ap[L][r0:r0 + 128, :]
    dst = k.wr.t[:, s, 0:1024]
    P.dma("pool", k.wr.r(dst, s), R(src, W.keys))
    return s, dst


def ada(k, L):
    P = k.P
    b = P.bank()
    for q in range(48):
        s, w = load_in_w(k, k.dram("ada_w%d" % L), 0, q * 128)
        for kk in range(8):
            P.mm(k.ps.r(k.ps.t[:, b, q * 2:q * 2 + 2], b), k.wr.r(w[:, kk, :], s),
                 k.csil.r(k.csil.t[:, kk, :]), start=(kk == 0), stop=(kk == 7))
    src = k.ps.t[:, b, 0:96].rearrange("p (q c) -> p q c", c=2)
    bias = k.adab.t[:, L, :].unsqueeze(2).to_broadcast([128, 48, 2])
    P.tt("dve", k.mod.r(), k.ps.r(src, b), k.adab.r(bias), ALU.add)
    for sl in range(2):
        q0 = (1 + 3 * sl) * 8
        P.ts("dve", k.msc.r(k.msc.t[:, sl]), k.mod.r(k.mod.t[:, q0:q0 + 8, :]), 1.0, ALU.add)


def modulate(k, sl):
    P = k.P
    q_sh = (0 + 3 * sl) * 8
    for j in range(8):
        for b in range(NBLK):
            c = bc(b)
            P.act(hs(k, j, b), xs(k, j, b), AF.Identity,
                  bias=k.mod.r(k.mod.t[:, q_sh + j, c:c + 1]), scale=k.msc.r(k.msc.t[:, sl, j, c:c + 1]))
    for j in range(8):
        for b in range(NBLK):
            P.ts("dve", xs(k, j, b), xs(k, j, b), ALPHA, ALU.mult)


def layernorm(k, L, sl, lb):
    P = k.P
    xb, sq, mean, m2, var, rstd, nmr, t1 = lb["xb"], lb["sq"], lb["mean"], lb["m2"], lb["var"], lb["rstd"], lb["nmr"], lb["t1"]
    for b in range(NBLK):
        pb = b % 2
        for j in range(8):
            P.act(xb.r(xb.t[:, pb, j, :], pb * 8 + j), xs(k, j, b), AF.Identity)
            P.act(sq.r(sq.t[:, pb, j, :], pb * 8 + j), xs(k, j, b), AF.Square)
        b1 = P.bank()
        b2 = P.bank()
        for j in range(8):
            P.mm(psb(k, b1), k.onesb.r(), xb.r(xb.t[:, pb, j, :], pb * 8 + j), start=(j == 0), stop=(j == 7))
        for j in range(8):
            P.mm(psb(k, b2), k.onesb.r(), sq.r(sq.t[:, pb, j, :], pb * 8 + j), start=(j == 0), stop=(j == 7))
        P.act(mean.r(mean.t[:, pb], pb), psb(k, b1), AF.Identity, scale=1.0 / D)
        P.tt("dve", m2.r(m2.t[:, pb], pb), mean.r(mean.t[:, pb], pb), mean.r(mean.t[:, pb], pb), ALU.mult)
        P.stt(var.r(var.t[:, pb], pb), psb(k, b2), 1.0 / D, m2.r(m2.t[:, pb], pb), ALU.mult, ALU.subtract)
        P.act(var.r(var.t[:, pb], pb), var.r(var.t[:, pb], pb), AF.Ln, bias=lb["eps"].r())
        P.act(rstd.r(rstd.t[:, pb], pb), var.r(var.t[:, pb], pb), AF.Exp, scale=-0.5)
        P.stt(nmr.r(nmr.t[:, pb], pb), mean.r(mean.t[:, pb], pb), -1.0, rstd.r(rstd.t[:, pb], pb), ALU.mult, ALU.mult)
        for j in range(8):
            tb = j % 2
            P.tt("dve", t1.r(t1.t[:, tb], tb), xs(k, j, b), rstd.r(rstd.t[:, pb], pb), ALU.mult)
            P.tt("dve", t1.r(t1.t[:, tb], tb), t1.r(t1.t[:, tb], tb), nmr.r(nmr.t[:, pb], pb), ALU.add)
            P.act(xs(k, j, b), t1.r(t1.t[:, tb], tb), AF.Identity,
                  bias=k.lnbb.r(k.lnbb.t[:, L, sl, j:j + 1]), scale=k.lngb.r(k.lngb.t[:, L, sl, j:j + 1]))


def ln_scope(k, L, sl):
    P = k.P
    P.barrier()
    mk = P.mark()
    if True:
        lb = {
            "xb": P.buf("ln_xb", [128, 2, 8, BLK], BF16, nsub=16),
            "sq": P.buf("ln_sq", [128, 2, 8, BLK], BF16, nsub=16),
            "mean": P.buf("ln_mean", [128, 2, BLK], F32, nsub=2),
            "m2": P.buf("ln_m2", [128, 2, BLK], F32, nsub=2),
            "var": P.buf("ln_var", [128, 2, BLK], F32, nsub=2),
            "rstd": P.buf("ln_rstd", [128, 2, BLK], F32, nsub=2),
            "nmr": P.buf("ln_nmr", [128, 2, BLK], F32, nsub=2),
            "t1": P.buf("ln_t1", [128, 2, BLK], F32, nsub=2),
            "eps": P.buf("ln_eps", [128, 1], F32),
        }
        P.memset("dve", lb["eps"].r(), LN_EPS)
        layernorm(k, L, sl, lb)
        P.barrier()
        P.release(mk)


def ffn(k, L):
    P = k.P
    G = 2
    P.barrier()
    mk = P.mark()
    if True:
        up = P.buf("f_up", [128, 2, 2, UPW], BF16, nsub=4)
        ab = P.buf("f_ab", [128, G, NT], BF16, nsub=G * 5)
        dg = P.buf("f_dg", [128, 2, 2, 9, 128], BF16, nsub=4)
        sg = P.buf("f_sg", [128, 2, BLK], F32, nsub=2)
        fcw = P.buf("f_fcw", [128, 44, 9], F32)
        P.dma("sp", fcw.r(), R(k.dram("fcw").ap[L], k.dram("fcw").keys))
        P.memset("pool", up.r(), 0.0)
        q_g = 5 * 8
        nsg = 0
        for g0 in range(0, NPAIR, G):
            pairs = list(range(g0, min(g0 + G, NPAIR)))
            for jj, j in enumerate(pairs):
                db = j % 2
                sg_, wg = load_in_w(k, k.dram("ffn_w_in%d" % L), 0, j * 128)
                sv_, wv = load_in_w(k, k.dram("ffn_w_in%d" % L), 0, FH + j * 128)
                ws = [wg, wv]
                wss = [sg_, sv_]
                for gv in range(2):
                    tile_idx = j + gv * NPAIR
                    i0 = k.identf.t[:].unsqueeze(1).to_broadcast([128, 9, 128])
                    i1 = fcw.t[:, tile_idx, :].unsqueeze(2).to_broadcast([128, 9, 128])
                    P.tt("pool", dg.r(dg.t[:, db, gv], db * 2 + gv), k.identf.r(i0), fcw.r(i1), ALU.mult)
                for b in range(NBLK):
                    for gv in range(2):
                        bk = P.bank()
                        for kk in range(8):
                            P.mm(psb(k, bk), k.wr.r(ws[gv][:, kk, :], wss[gv]), hs(k, kk, b), start=(kk == 0), stop=(kk == 7))
                        upt = up.t[:, db, gv]
                        if b == 0:
                            dst = upt[:, 0:516].rearrange("p (s w) -> p s w", w=258)[:, :, 1:257]
                            src = k.ps.t[:, bk, :].rearrange("p (s w) -> p s w", w=256)
                        else:
                            r0 = 8 * (b - 1)
                            dst = upt[:, 516:].rearrange("p (r w) -> p r w", w=66)[:, 1 + r0:9 + r0, 1:65]
                            src = k.ps.t[:, bk, :].rearrange("p (r w) -> p r w", w=64)
                        P.copy("act" if gv == 0 else "dve", up.r(dst, db * 2 + gv), k.ps.r(src, bk))
                for b in range(NBLK):
                    bks = []
                    for gv in range(2):
                        bk = P.bank()
                        bks.append(bk)
                        upt = up.t[:, db, gv]
                        if b == 0:
                            taps = [(1, kw) for kw in range(3)]
                            outv = k.ps.t[:, bk, :].rearrange("p (s w) -> p s w", w=256)
                        else:
                            taps = [(kh, kw) for kh in range(3) for kw in range(3)]
                            outv = k.ps.t[:, bk, :].rearrange("p (r w) -> p r w", w=64)
                        for ti, (kh, kw) in enumerate(taps):
                            if b == 0:
                                rhs = upt[:, 0:516].rearrange("p (s w) -> p s w", w=258)[:, :, kw:kw + 256]
                            else:
                                r0 = 8 * (b - 1)
                                rhs = upt[:, 516:].rearrange("p (r w) -> p r w", w=66)[:, r0 + kh:r0 + kh + 8, kw:kw + 64]
                            P.mm(k.ps.r(outv, bk), dg.r(dg.t[:, db, gv, kh * 3 + kw, :], db * 2 + gv), up.r(rhs, db * 2 + gv),
                                 start=(ti == 0), stop=(ti == len(taps) - 1))
                    sb_ = nsg % 2
                    nsg += 1
                    P.act(sg.r(sg.t[:, sb_], sb_), psb(k, bks[0]), AF.Silu)
                    P.tt("dve", ab.r(ab.t[:, jj, b * BLK:(b + 1) * BLK], jj * 5 + b), psb(k, bks[1]), sg.r(sg.t[:, sb_], sb_), ALU.mult)
            wos = [load_out_w(k, k.dram("ffn_w_out%d" % L), 0, (g0 + jj) * 128) for jj in range(len(pairs))]
            for m in range(8):
                for b in range(NBLK):
                    bk = P.bank()
                    for jj in range(len(pairs)):
                        P.mm(psb(k, bk), k.wr.r(wos[jj][1][:, m * 128:(m + 1) * 128], wos[jj][0]), ab.r(ab.t[:, jj, b * BLK:(b + 1) * BLK], jj * 5 + b),
                             start=(jj == 0), stop=(jj == len(pairs) - 1))
                    c = bc(b)
                    P.stt(xs(k, m, b), psb(k, bk), k.mod.r(k.mod.t[:, q_g + m, c:c + 1]), xs(k, m, b), ALU.mult, ALU.add)
        P.barrier()
        P.release(mk)


def layer(k, L):
    cfg = k.cfg
    ada(k, L)
    modulate(k, 0)
    if cfg.get("mixers", True):
        mixer(k, L)
    ln_scope(k, L, 0)
    modulate(k, 1)
    if cfg.get("ffn", True):
        ffn(k, L)
    ln_scope(k, L, 1)


LW = 1280
SEGS = [(0, 256), (256, 256), (512, 2048)]
Q_G1 = 2 * 8


def mixer(k, L):
    kind = L % 3
    if kind == 2:
        lru(k, L // 3)
    elif kind == 1:
        ssd(k, L // 3)
    else:
        gdn(k, L // 3)


def outproj_acc(k, W, Lw, r0, ntile, o_regions):
    P = k.P
    slots = [load_out_w(k, W, Lw, r0 + i * 128) for i in range(ntile)]
    for m in range(8):
        for b in range(NBLK):
            bk = P.bank()
            for jj in range(ntile):
                s, wo = slots[jj]
                P.mm(psb(k, bk), k.wr.r(wo[:, m * 128:(m + 1) * 128], s), o_regions(jj, b), start=(jj == 0), stop=(jj == ntile - 1))
            c = bc(b)
            P.stt(xs(k, m, b), psb(k, bk), k.mod.r(k.mod.t[:, Q_G1 + m, c:c + 1]), xs(k, m, b), ALU.mult, ALU.add)


def conv1d_pad_layout():
    offs = []
    o = 0
    for (t0, n) in SEGS:
        offs.append(o)
        o += n + 3
    return offs, o


CPO, CPW = conv1d_pad_layout()


def conv_dst(xpad_t, b):
    if b == 0:
        return xpad_t[:, 0:518].rearrange("p (s w) -> p s w", w=259)[:, :, 1:257], "p (s w) -> p s w", 256
    o = CPO[2] + 1 + (b - 1) * BLK
    return xpad_t[:, o:o + BLK], None, None


def conv_rhs(xpad_t, b, kk):
    if b == 0:
        return xpad_t[:, 0:518].rearrange("p (s w) -> p s w", w=259)[:, :, kk:kk + 256]
    o = CPO[2] + (b - 1) * BLK + kk
    return xpad_t[:, o:o + BLK]


def psview(k, bk, b):
    if b == 0:
        return k.ps.t[:, bk, :].rearrange("p (s w) -> p s w", w=256)
    return k.ps.t[:, bk, :]


def lru(k, j):
    P = k.P
    G = 2
    P.barrier()
    mk = P.mark()
    xpad = P.buf("l_xpad", [128, CPW + 1], BF16)
    xrb = P.buf("l_xrb", [128, NT], BF16)
    gg = P.buf("l_gg", [128, NT], BF16)
    Ib = P.buf("l_i", [128, NT], BF16)
    A = P.buf("l_a", [128, NT], F32)
    T = P.buf("l_t", [128, NT], F32)
    H = [P.buf("l_h0", [128, NT], BF16), P.buf("l_h1", [128, NT], BF16)]
    ob = P.buf("l_o", [128, G, NT], BF16, nsub=G)
    dgl = P.buf("l_dg", [128, 4, 128], BF16)
    sm = P.buf("l_sm", [128, 10, 12], F32)
    s0 = P.buf("l_s0", [128, 10, 2], F32)
    sp = P.buf("l_sp", [128, 10, 2], F32)
    ep = P.buf("l_ep", [128, 10, 2], F32)
    one = P.buf("l_one", [128, 1], F32)
    sto = P.buf("l_sto", [128, 10, 2, 2], F32)
    P.dma("sp", sm.r(), k.dram("lru_sm"))
    P.dma("sp", s0.r(), k.dram("lru_s0"))
    P.memset("dve", one.r(), 1.0)
    P.memset("pool", xpad.r(), 0.0)
    P.act(ep.r(), sm.r(sm.t[:, :, 9:11]), AF.Exp, scale=-1.0)
    P.ts("dve", sp.r(), ep.r(), -0.2, ALU.mult, 0.25, ALU.add)
    for cst in (1.0 / 3, 0.5, 1.0):
        P.tt("dve", sp.r(), sp.r(), ep.r(), ALU.mult)
        P.ts("dve", sp.r(), sp.r(), -1.0, ALU.mult, cst, ALU.add)
    P.tt("dve", sp.r(), sp.r(), ep.r(), ALU.mult)
    P.ts("dve", sp.r(), sp.r(), -8.0, ALU.mult)

    for g0 in range(0, 10, G):
        tiles = list(range(g0, min(g0 + G, 10)))
        for jj, n in enumerate(tiles):
            s, wgb = load_in_w(k, k.dram("lru_w_in"), j, n * 128)
            sx, wxr = load_in_w(k, k.dram("lru_w_in"), j, LW + n * 128)
            s2 = wslot(k)
            gsrc = k.dram("lru_gate_w").ap[j][:, :, n].rearrange("d g kk m -> kk (d g) m")
            gw = k.wr.t[:, s2, 0:512].rearrange("p (a m) -> p a m", m=128)
            P.dma("pool", k.wr.r(gw, s2), R(gsrc, k.dram("lru_gate_w").keys))
            i0 = k.identf.t[:].unsqueeze(1).to_broadcast([128, 4, 128])
            i1 = sm.t[:, n, 0:4].unsqueeze(2).to_broadcast([128, 4, 128])
            P.tt("pool", dgl.r(), k.identf.r(i0), sm.r(i1), ALU.mult)
            for b in range(NBLK):
                bk = P.bank()
                for kk in range(8):
                    P.mm(psb(k, bk), k.wr.r(wgb[:, kk, :], s), hs(k, kk, b), start=(kk == 0), stop=(kk == 7))
                P.act(gg.r(gg.t[:, b * BLK:(b + 1) * BLK]), psb(k, bk), AF.Gelu_apprx_tanh)
            for b in range(NBLK):
                bk = P.bank()
                for kk in range(8):
                    P.mm(psb(k, bk), k.wr.r(wxr[:, kk, :], sx), hs(k, kk, b), start=(kk == 0), stop=(kk == 7))
                dst, _, _ = conv_dst(xpad.t, b)
                P.copy("dve", xpad.r(dst), k.ps.r(psview(k, bk, b), bk))
            for b in range(NBLK):
                bk = P.bank()
                for kk in range(4):
                    P.mm(k.ps.r(psview(k, bk, b), bk), dgl.r(dgl.t[:, kk, :]), xpad.r(conv_rhs(xpad.t, b, kk)), start=(kk == 0), stop=(kk == 3))
                P.act(xrb.r(xrb.t[:, b * BLK:(b + 1) * BLK]), psb(k, bk), AF.Identity, bias=sm.r(sm.t[:, n, 4:5]))
            for d in range(2):
                for b in range(NBLK):
                    for g in range(2):
                        bk = P.bank()
                        P.mm(psb(k, bk), k.wr.r(gw[:, d * 2 + g, :], s2), xrb.r(xrb.t[:, b * BLK:(b + 1) * BLK]))
                        dstb = A if g == 0 else Ib
                        P.act(dstb.r(dstb.t[:, b * BLK:(b + 1) * BLK]), psb(k, bk), AF.Sigmoid, bias=sm.r(sm.t[:, n, 5 + d * 2 + g:6 + d * 2 + g]))
                P.act(A.r(), A.r(), AF.Exp, scale=sp.r(sp.t[:, n, d:d + 1]))
                P.tt("dve", T.r(), A.r(), A.r(), ALU.mult)
                P.act(T.r(), T.r(), AF.Sqrt, bias=one.r(), scale=-1.0)
                P.tt("dve", T.r(), T.r(), Ib.r(), ALU.mult)
                P.tt("dve", T.r(), T.r(), xrb.r(), ALU.mult)
                for si, (t0, n_t) in enumerate(SEGS):
                    if d == 0:
                        o_, a_, b_ = H[0].t[:, t0:t0 + n_t], A.t[:, t0:t0 + n_t], T.t[:, t0:t0 + n_t]
                    else:
                        lo = t0 - 1 if t0 > 0 else None
                        o_, a_, b_ = H[1].t[:, t0 + n_t - 1:lo:-1], A.t[:, t0 + n_t - 1:lo:-1], T.t[:, t0 + n_t - 1:lo:-1]
                    if si == 2:
                        init = s0.t[:, n, d:d + 1]
                        rd = [A.r(), T.r(), s0.r()]
                    else:
                        init = 0.0
                        rd = [A.r(), T.r()]
                    P.op("dve", lambda e, o_=o_, a_=a_, b_=b_, init=init: e.tensor_tensor_scan(out=o_, data0=a_, data1=b_, initial=init, op0=ALU.mult, op1=ALU.add),
                         reads=rd, writes=[H[d].r()])
                    if si < 2:
                        tl = t0 + n_t - 1 if d == 0 else t0
                        P.copy("act", sto.r(sto.t[:, n, si, d:d + 1]), H[d].r(H[d].t[:, tl:tl + 1]))
            P.tt("dve", H[0].r(), H[0].r(), H[1].r(), ALU.add)
            P.tt("dve", ob.r(ob.t[:, jj, :], jj), H[0].r(), gg.r(), ALU.mult)
        outproj_acc(k, k.dram("lru_w_out"), j, g0 * 128, len(tiles), lambda jj, b: ob.r(ob.t[:, jj, b * BLK:(b + 1) * BLK], jj))
    P.dma("sp", k.dram("lru_out"), sto.r())
    P.barrier()
    P.release(mk)


NCH = 20
SEG_CH = [(0, 2), (2, 2), (4, 16)]


def seg_of_chunk(c):
    return 0 if c < 2 else (1 if c < 4 else 2)


def hch(k, kk, c):
    b = c // 4
    return k.h.r(k.h.t[:, kk, c * 128:(c + 1) * 128], kk * 5 + b)


def ssd(k, j):
    P = k.P
    P.barrier()
    mk = P.mark()
    y_tm = P.buf("s_ytm", [128, NCH, 512], BF16, nsub=NCH)
    xs_tm = P.buf("s_xstm", [128, NCH, 256], BF16, nsub=NCH)
    Bfm = P.buf("s_bfm", [128, NT], BF16)
    Cfm = P.buf("s_cfm", [128, NT], BF16)
    wz = P.buf("s_wz", [128, 8, 256], BF16)
    wdt = P.buf("s_wdt", [128, 8, 64], BF16)
    cw = P.buf("s_cw", [128, 24, 5], F32)
    rows = P.buf("s_rows", [128, 160], F32)
    nrm = P.buf("s_nrm", [128, 512], BF16)
    masks = P.buf("s_masks", [128, 2, 128], F32)
    negb = P.buf("s_negb", [128, 2, 128], BF16)
    onesf = P.buf("s_onesf", [128, 128], F32)
    one1 = P.buf("s_one1", [128, 1], F32)
    eps1 = P.buf("s_eps1", [128, 1], F32)
    sc = {nm: P.buf("s_" + nm, [128, NCH, 8], F32) for nm in ["dt", "da", "nacum", "eac", "cdec", "ce"]}
    sc["tmp"] = sc["ce"]
    fm_tmp = Cfm
    ssq = P.buf("s_ssq", [128, NCH, 2], F32)
    rstd = P.buf("s_rstd", [128, NCH], F32)
    cbT = P.buf("s_cbT", [128, 2, 2, 128], BF16, nsub=2)
    dec = P.buf("s_dec", [128, 2, 4, 128], BF16, nsub=2)
    ST = P.buf("s_ST", [128, 2, 256], F32, nsub=2)
    STb = P.buf("s_STb", [128, 2, 256], BF16, nsub=2)
    t1 = P.buf("s_t1", [128, 1, 256], F32, nsub=1)
    t2 = P.buf("s_t2", [128, 1, 256], F32, nsub=1)
    sz = t2
    sig = P.buf("s_sig", [128, BLK], BF16)
    ncb = P.buf("s_ncb", [128, 24], F32)
    junk = sig
    sto = P.buf("s_sto", [128, 2, 128], F32, nsub=2)

    P.dma("sp", cw.r(), k.dram("ssd_cw"))
    P.dma("sp", rows.r(), k.dram("ssd_rows"))
    P.dma("sp", masks.r(), R(k.dram("cmask").ap[0:2].rearrange("a p m -> p a m"), k.dram("cmask").keys))
    P.dma("pool", negb.r(), R(k.dram("cmask").ap[2:4].rearrange("a p m -> p a m"), k.dram("cmask").keys))
    P.memset("dve", onesf.r(), 1.0)
    P.memset("dve", one1.r(), 1.0)
    P.ts("dve", ncb.r(), cw.r(cw.t[:, :, 4]), -1.0, ALU.mult)
    P.memset("dve", eps1.r(), 1e-5)
    P.act(rows.r(rows.t[:, 64:128]), rows.r(rows.t[:, 64:128]), AF.Exp)
    P.ts("dve", rows.r(rows.t[:, 64:128]), rows.r(rows.t[:, 64:128]), -1.0, ALU.mult)
    src = k.dram("ssd_w_in").ap[j][:, 5120:5184].rearrange("(kk p) c -> p kk c", p=128)
    P.dma("pool", wdt.r(), R(src, k.dram("ssd_w_in").keys))

    loc = {}

    def conv_tile(col0, cidx, dst_writer):
        xpad, dgl = loc["xpad"], loc["dgl"]
        s, w = load_in_w(k, k.dram("ssd_w_in"), j, col0)
        i0 = k.identf.t[:].unsqueeze(1).to_broadcast([128, 4, 128])
        i1 = cw.t[:, cidx, 0:4].unsqueeze(2).to_broadcast([128, 4, 128])
        P.tt("pool", dgl.r(), k.identf.r(i0), cw.r(i1), ALU.mult)
        for b in range(NBLK):
            bk = P.bank()
            for kk in range(8):
                P.mm(psb(k, bk), k.wr.r(w[:, kk, :], s), hs(k, kk, b), start=(kk == 0), stop=(kk == 7))
            dst, _, _ = conv_dst(xpad.t, b)
            P.copy("dve", xpad.r(dst), k.ps.r(psview(k, bk, b), bk))
        for b in range(NBLK):
            bk = P.bank()
            for kk in range(4):
                P.mm(k.ps.r(psview(k, bk, b), bk), dgl.r(dgl.t[:, kk, :]), xpad.r(conv_rhs(xpad.t, b, kk)), start=(kk == 0), stop=(kk == 3))
            P.act(sig.r(), psb(k, bk), AF.Exp, bias=ncb.r(ncb.t[:, cidx:cidx + 1]), scale=-1.0)
            P.act(sig.r(), sig.r(), AF.Ln, bias=one1.r())
            P.act(sig.r(), sig.r(), AF.Exp, scale=-1.0)
            P.stt(dst_writer(b), psb(k, bk), cw.r(cw.t[:, cidx, 4:5]), sig.r(), ALU.add, ALU.mult)

    stop = k.cfg.get("ssd_stop", 99)
    for hg in range(k.cfg.get("ssd_nhg", 8)):
        g, half = hg // 2, hg % 2
        src = k.dram("ssd_w_in").ap[j][:, hg * 256:(hg + 1) * 256].rearrange("(kk p) c -> p kk c", p=128)
        P.dma("pool", wz.r(), R(src, k.dram("ssd_w_in").keys))
        if half == 0:
            P.dma("pool", nrm.r(), R(k.dram("ssd_nrm").ap[:, g * 512:(g + 1) * 512], k.dram("ssd_nrm").keys))
        if stop <= 0.5:
            continue
        bk = P.bank()
        for c in range(NCH):
            for kk in range(8):
                rhs = wdt.t[:, kk, :].rearrange("p (d h) -> p d h", d=2)[:, :, hg * 4:hg * 4 + 4]
                out = k.ps.t[:, bk, c * 8:(c + 1) * 8].rearrange("p (d h) -> p d h", d=2)
                P.mm(k.ps.r(out, bk), hch(k, kk, c), wdt.r(rhs), start=(kk == 0), stop=(kk == 7))
        psv = k.ps.t[:, bk, 0:NCH * 8].rearrange("p (c d h) -> p c d h", d=2, h=4)
        if stop <= 0.7:
            continue

        def rowbc(off):
            return rows.t[:, off:off + 64].rearrange("p (d h) -> p d h", d=2)[:, :, hg * 4:hg * 4 + 4].unsqueeze(1).to_broadcast([128, NCH, 2, 4])

        def v4(bf):
            return bf.t.rearrange("p c (d h) -> p c d h", d=2)
        tmp, dt, da = sc["tmp"], sc["dt"], sc["da"]
        P.tt("dve", tmp.r(v4(tmp)), k.ps.r(psv, bk), rows.r(rowbc(0)), ALU.add)
        P.ts("dve", dt.r(), tmp.r(), -1.0, ALU.mult)
        P.tt("dve", dt.r(), dt.r(), tmp.r(), ALU.max)
        P.act(dt.r(), dt.r(), AF.Exp, scale=-1.0)
        P.act(dt.r(), dt.r(), AF.Ln, bias=one1.r())
        P.ts("dve", tmp.r(), tmp.r(), 0.0, ALU.max)
        P.tt("dve", dt.r(), dt.r(), tmp.r(), ALU.add)
        P.tt("dve", da.r(v4(da)), dt.r(v4(dt)), rows.r(rowbc(64)), ALU.mult)
        if stop <= 0.8:
            continue
        bk2 = P.bank()
        bk3 = P.bank()
        for c in range(NCH):
            for d in range(2):
                P.mm(k.ps.r(k.ps.t[:, bk2, c * 8 + d * 4:c * 8 + d * 4 + 4], bk2), masks.r(masks.t[:, d, :]), da.r(da.t[:, c, d * 4:d * 4 + 4]))
            P.mm(k.ps.r(k.ps.t[:, bk3, c * 8:(c + 1) * 8], bk3), onesf.r(), da.r(da.t[:, c, :]))
        nacum, eac, cdec, ce = sc["nacum"], sc["eac"], sc["cdec"], sc["ce"]
        ps2 = k.ps.t[:, bk2, 0:NCH * 8].rearrange("p (c e) -> p c e", e=8)
        ps3 = k.ps.t[:, bk3, 0:NCH * 8].rearrange("p (c e) -> p c e", e=8)
        if stop <= 0.9:
            continue
        P.ts("dve", nacum.r(), k.ps.r(ps2, bk2), -1.0, ALU.mult)
        P.act(eac.r(), nacum.r(), AF.Exp, scale=-1.0)
        if stop <= 0.95:
            continue
        P.copy("dve", cdec.r(), k.ps.r(ps3, bk3))
        P.tt("dve", ce.r(), cdec.r(), nacum.r(), ALU.add)
        if stop <= 0.96:
            continue
        P.act(cdec.r(), cdec.r(), AF.Exp)
        if stop <= 0.97:
            continue
        P.act(ce.r(), ce.r(), AF.Exp)
        P.tt("dve", ce.r(), ce.r(), dt.r(), ALU.mult)
        if stop <= 1:
            continue
        P.barrier()
        mkp = P.mark()
        loc["xpad"] = P.buf("s_xpad", [128, CPW + 1], BF16)
        loc["dgl"] = P.buf("s_dg", [128, 4, 128], BF16)
        P.memset("pool", loc["xpad"].r(), 0.0)
        for ti in range(2):
            conv_tile(2048 + hg * 256 + ti * 128, hg * 2 + ti, lambda b: fm_tmp.r(fm_tmp.t[:, b * BLK:(b + 1) * BLK]))
            for c0 in range(0, NCH, 8):
                bk = P.bank()
                pb = k.ps.t[:, bk, :].bitcast(BF16)
                n = min(8, NCH - c0)
                for ci in range(n):
                    c = c0 + ci
                    P.tr(k.ps.r(pb[:, ci * 128:(ci + 1) * 128], bk), fm_tmp.r(fm_tmp.t[:, c * 128:(c + 1) * 128]), k.identb.r())
                dst = xs_tm.t[:, c0:c0 + n, ti * 128:(ti + 1) * 128]
                srcv = pb[:, 0:n * 128].rearrange("p (c m) -> p c m", m=128)
                P.copy("act", xs_tm.r(dst, range(c0, c0 + n)), k.ps.r(srcv, bk))
        conv_tile(2048 + 2048 + g * 128, 16 + g, lambda b: Bfm.r(Bfm.t[:, b * BLK:(b + 1) * BLK]))
        conv_tile(2048 + 2560 + g * 128, 20 + g, lambda b: Cfm.r(Cfm.t[:, b * BLK:(b + 1) * BLK]))
        if stop <= 2:
            continue
        P.barrier()
        P.release(mkp)
        mku = P.mark()
        Btm = P.buf("s_btm", [128, 4, 128], BF16, nsub=4)
        Mb = P.buf("s_M", [128, 4, 4, 128], BF16, nsub=4)
        xd = P.buf("s_xd", [128, 4, 256], BF16, nsub=4)
        xdt = P.buf("s_xdt", [128, 4, 256], BF16, nsub=4)
        nsteps = k.cfg.get("ssd_steps", NCH)

        def prelim_gen(step):
            banks = {}
            units = [(d, (step if d == 0 else NCH - 1 - step), (step % 2) * 2 + d) for d in range(2)]
            for (d, c, sl) in units:
                tok = slice(c * 128, (c + 1) * 128)
                bk = d * 3
                banks[("cb", d)] = bk
                P.mm(k.ps.r(k.ps.t[:, bk, 0:128], bk), Bfm.r(Bfm.t[:, tok]), Cfm.r(Cfm.t[:, tok]))
                bkt = d * 3 + 1
                banks[("bt", d)] = bkt
                pbb = k.ps.t[:, bkt, :].bitcast(BF16)
                P.tr(k.ps.r(pbb[:, 0:128], bkt), Bfm.r(Bfm.t[:, tok]), k.identb.r())
                yield
            for (d, c, sl) in units:
                bk = banks[("cb", d)]
                bkt = banks[("bt", d)]
                pbb = k.ps.t[:, bkt, :].bitcast(BF16)
                P.tt("dve", cbT.r(cbT.t[:, d, d], d), k.ps.r(k.ps.t[:, bk, 0:128], bk), masks.r(masks.t[:, d, :]), ALU.mult)
                P.copy("act", Btm.r(Btm.t[:, sl], sl), k.ps.r(pbb[:, 0:128], bkt))
                xsv = xs_tm.t[:, c, :].rearrange("p (q e) -> p q e", e=64)
                dtb = dt.t[:, c, d * 4:d * 4 + 4].unsqueeze(2).to_broadcast([128, 4, 64])
                ceb = ce.t[:, c, d * 4:d * 4 + 4].unsqueeze(2).to_broadcast([128, 4, 64])
                P.tt("pool", xd.r(xd.t[:, sl].rearrange("p (q e) -> p q e", e=64), sl), xs_tm.r(xsv, c), dt.r(dtb), ALU.mult)
                P.tt("pool", xdt.r(xdt.t[:, sl].rearrange("p (q e) -> p q e", e=64), sl), xs_tm.r(xsv, c), ce.r(ceb), ALU.mult)
                yield
            for (d, c, sl) in units:
                bk = d * 3 + 2
                banks[("R", d)] = bk
                for q in range(4):
                    col = d * 4 + q
                    lhs = da.t[:, c, col:col + 1].to_broadcast([128, 128])
                    o_ = k.ps.r(k.ps.t[:, bk, q * 128:(q + 1) * 128], bk)
                    P.mm(o_, da.r(lhs), masks.r(masks.t[:, d, :]), start=True, stop=False)
                    P.mm(o_, k.identb.r(), negb.r(negb.t[:, d, :]), start=False, stop=True)
                yield
            for (d, c, sl) in units:
                bk = banks[("R", d)]
                for q in range(4):
                    col = d * 4 + q
                    P.act(dec.r(dec.t[:, d, q, :], d), k.ps.r(k.ps.t[:, bk, q * 128:(q + 1) * 128], bk), AF.Exp, bias=nacum.r(nacum.t[:, c, col:col + 1]))
                    yield
            for (d, c, sl) in units:
                cb_b = cbT.t[:, d, d].unsqueeze(1).to_broadcast([128, 4, 128])
                P.tt("dve", Mb.r(Mb.t[:, sl], sl), dec.r(dec.t[:, d], d), cbT.r(cb_b, d), ALU.mult)
                yield

        cbs = [0]

        def cbank():
            cbs[0] ^= 1
            return 6 + cbs[0]

        def chain_gen(step):
            for d in range(2):
                c = step if d == 0 else NCH - 1 - step
                sl = (step % 2) * 2 + d
                first_visit = (c <= 9) if d == 0 else (c >= 10)
                if nsteps < NCH:
                    first_visit = True
                seg = seg_of_chunk(c)
                c_first, c_n = SEG_CH[seg]
                seg_start = (c == c_first) if d == 0 else (c == c_first + c_n - 1)
                seg_end = (c == c_first + c_n - 1) if d == 0 else (c == c_first)
                tok = slice(c * 128, (c + 1) * 128)
                if seg_start:
                    if seg == 2:
                        P.dma("sp", ST.r(ST.t[:, d], d), R(k.dram("ssd_s0").ap[d, hg], k.dram("ssd_s0").keys))
                    else:
                        P.memset("dve", ST.r(ST.t[:, d], d), 0.0)
                    P.copy("act", STb.r(STb.t[:, d], d), ST.r(ST.t[:, d], d))
                    yield
                xsv = xs_tm.t[:, c, :].rearrange("p (q e) -> p q e", e=64)
                bkY = cbank()
                for q in range(4):
                    P.mm(k.ps.r(k.ps.t[:, bkY, q * 64:(q + 1) * 64], bkY), Mb.r(Mb.t[:, sl, q, :], sl), xd.r(xd.t[:, sl, q * 64:(q + 1) * 64], sl))
                P.mm(k.ps.r(k.ps.t[:, bkY, 256:512], bkY), Cfm.r(Cfm.t[:, tok]), STb.r(STb.t[:, d], d))
                bkS = cbank()
                P.mm(k.ps.r(k.ps.t[:, bkS, 0:256], bkS), Btm.r(Btm.t[:, sl], sl), xdt.r(xdt.t[:, sl], sl))
                yield
                cdb = cdec.t[:, c, d * 4:d * 4 + 4].unsqueeze(2).to_broadcast([128, 4, 64])
                STv = ST.t[:, d].rearrange("p (q e) -> p q e", e=64)
                P.tt("dve", ST.r(STv, d), ST.r(STv, d), cdec.r(cdb), ALU.mult)
                P.tt("dve", ST.r(ST.t[:, d], d), ST.r(ST.t[:, d], d), k.ps.r(k.ps.t[:, bkS, 0:256], bkS), ALU.add)
                yield
                P.copy("act", STb.r(STb.t[:, d], d), ST.r(ST.t[:, d], d))
                yield
                eab = eac.t[:, c, d * 4:d * 4 + 4].unsqueeze(2).to_broadcast([128, 4, 64])
                t1v = t1.t[:, 0].rearrange("p (q e) -> p q e", e=64)
                P.tt("dve", t1.r(t1v, 0), k.ps.r(k.ps.t[:, bkY, 256:512].rearrange("p (q e) -> p q e", e=64), bkY), eac.r(eab), ALU.mult)
                P.tt("dve", t1.r(t1.t[:, 0], 0), t1.r(t1.t[:, 0], 0), k.ps.r(k.ps.t[:, bkY, 0:256], bkY), ALU.add)
                yield
                ysl = y_tm.r(y_tm.t[:, c, half * 256:(half + 1) * 256], c)
                if first_visit:
                    P.copy("act", ysl, t1.r(t1.t[:, 0], 0))
                    yield
                else:
                    P.tt("dve", t1.r(t1.t[:, 0], 0), t1.r(t1.t[:, 0], 0), ysl, ALU.add)
                    dsb = rows.t[:, 128 + hg * 4:128 + hg * 4 + 4].unsqueeze(2).to_broadcast([128, 4, 64])
                    P.tt("pool", t2.r(t2.t[:, 0].rearrange("p (q e) -> p q e", e=64), 0), xs_tm.r(xsv, c), rows.r(dsb), ALU.mult)
                    yield
                    P.tt("dve", t1.r(t1.t[:, 0], 0), t1.r(t1.t[:, 0], 0), t2.r(t2.t[:, 0], 0), ALU.add)
                    bkZ = cbank()
                    for kk in range(8):
                        P.mm(k.ps.r(k.ps.t[:, bkZ, 0:256], bkZ), hch(k, kk, c), wz.r(wz.t[:, kk, :]), start=(kk == 0), stop=(kk == 7))
                    yield
                    P.act(t2.r(t2.t[:, 0], 0), k.ps.r(k.ps.t[:, bkZ, 0:256], bkZ), AF.Exp, scale=-1.0)
                    P.act(t2.r(t2.t[:, 0], 0), t2.r(t2.t[:, 0], 0), AF.Ln, bias=one1.r())
                    P.act(t2.r(t2.t[:, 0], 0), t2.r(t2.t[:, 0], 0), AF.Exp, scale=-1.0)
                    yield
                    P.tt("dve", t1.r(t1.t[:, 0], 0), t1.r(t1.t[:, 0], 0), t2.r(t2.t[:, 0], 0), ALU.mult)
                    P.tt("dve", t1.r(t1.t[:, 0], 0), t1.r(t1.t[:, 0], 0), k.ps.r(k.ps.t[:, bkZ, 0:256], bkZ), ALU.mult)
                    yield
                    P.copy("dve", ysl, t1.r(t1.t[:, 0], 0))
                    o_, i_, a_ = junk.t[:, 0:256], t1.t[:, 0], ssq.t[:, c, half:half + 1]
                    P.op("act", lambda e, o_=o_, i_=i_, a_=a_: e.activation(out=o_, in_=i_, func=AF.Square, accum_out=a_),
                         reads=[t1.r(t1.t[:, 0], 0)], writes=[junk.r(), ssq.r()])
                    yield
                if seg_end and seg < 2:
                    for pr in range(2):
                        bk = cbank()
                        P.tr(k.ps.r(k.ps.t[:, bk, 0:128], bk), ST.r(ST.t[:, d, pr * 128:(pr + 1) * 128], d), k.identf.r())
                        P.copy("dve", sto.r(sto.t[:, pr], pr), k.ps.r(k.ps.t[:, bk, 0:128], bk))
                        r0 = (hg * 4 + pr * 2) * 64
                        P.dma("sp", R(k.dram("ssd_out").ap[seg, d, r0:r0 + 128, :], k.dram("ssd_out").keys), sto.r(sto.t[:, pr], pr))
                    yield

        def run_all(gen):
            for _ in gen:
                pass

        def merge(pg, cg, ratio):
            pdone = cdone = False
            while not (pdone and cdone):
                if not cdone:
                    try:
                        next(cg)
                    except StopIteration:
                        cdone = True
                for _ in range(ratio):
                    if pdone:
                        break
                    try:
                        next(pg)
                    except StopIteration:
                        pdone = True

        run_all(prelim_gen(0))
        for step in range(nsteps):
            if step + 1 < nsteps:
                merge(prelim_gen(step + 1), chain_gen(step), k.cfg.get("ssd_ratio", 1))
            else:
                run_all(chain_gen(step))
        P.barrier()
        P.release(mku)
        if half == 1 and stop > 3:
            P.tt("dve", rstd.r(), ssq.r(ssq.t[:, :, 0]), ssq.r(ssq.t[:, :, 1]), ALU.add)
            P.act(rstd.r(), rstd.r(), AF.Ln, bias=eps1.r(), scale=1.0 / 512)
            P.act(rstd.r(), rstd.r(), AF.Exp, scale=-0.5)
            slots = [load_out_w(k, k.dram("ssd_w_out"), j, g * 512 + ti * 128) for ti in range(4)]
            ofm_t = xs_tm.t[:, 0:8, :].rearrange("p c e -> p (c e)").rearrange("p (t m) -> p t m", m=512)
            yn_t = xs_tm.t[:, 8:12, :].rearrange("p c e -> p (c e)").rearrange("p (t m) -> p t m", m=512)
            OK_ = list(range(0, 8))
            YK_ = list(range(8, 12))
            for b in range(NBLK):
                for ci in range(4):
                    c = b * 4 + ci
                    yb = ci % 2
                    P.stt(xs_tm.r(yn_t[:, yb], YK_), y_tm.r(y_tm.t[:, c, :], c), rstd.r(rstd.t[:, c:c + 1]), nrm.r(), ALU.mult, ALU.mult)
                    bk = P.bank()
                    pbb = k.ps.t[:, bk, :].bitcast(BF16)
                    for ti in range(4):
                        P.tr(k.ps.r(pbb[:, ti * 128:(ti + 1) * 128], bk), xs_tm.r(yn_t[:, yb, ti * 128:(ti + 1) * 128], YK_), k.identb.r())
                    srcv = pbb[:, 0:512].rearrange("p (t m) -> p t m", m=128)
                    P.copy("act", xs_tm.r(ofm_t[:, :, ci * 128:(ci + 1) * 128], OK_), k.ps.r(srcv, bk))
                for m in range(8):
                    bk = P.bank()
                    for ti in range(4):
                        s_, wo = slots[ti]
                        P.mm(psb(k, bk), k.wr.r(wo[:, m * 128:(m + 1) * 128], s_), xs_tm.r(ofm_t[:, ti, :], OK_), start=(ti == 0), stop=(ti == 3))
                    cc = bc(b)
                    P.stt(xs(k, m, b), psb(k, bk), k.mod.r(k.mod.t[:, Q_G1 + m, cc:cc + 1]), xs(k, m, b), ALU.mult, ALU.add)
    P.barrier()
    P.release(mk)


def gdn(k, j):
    P = k.P
    P.barrier()
    mk = P.mark()
    W_in = k.dram("gdn_w_in%d" % j)
    W_out = k.dram("gdn_w_out%d" % j)
    kq = P.buf("g_kq", [128, NCH, 2, 128], BF16, nsub=NCH)
    v_fm = P.buf("g_vfm", [128, NT], BF16, nsub=NCH)
    v_tm = P.buf("g_vtm", [128, NCH, 128], BF16, nsub=NCH)
    k_tm = P.buf("g_ktm", [128, NCH, 128], BF16, nsub=NCH)
    o_tm = P.buf("g_otm", [128, NCH, 128], BF16, nsub=NCH)
    wz = P.buf("g_wz", [128, 8, 128], BF16)
    wsm = P.buf("g_wsm", [128, 8, 32], BF16)
    cw = P.buf("g_cw", [128, 24, 4], F32)
    rows = P.buf("g_rows", [128, 160], F32)
    masks = P.buf("g_masks", [128, 2, 128], F32)
    negb = P.buf("g_negb", [128, 2, 128], BF16)
    lvl = P.buf("g_lvl", [128, 2, 7, 2, 128], BF16)
    I2 = P.buf("g_I2", [128, 2, 128], BF16)
    onesf = P.buf("g_onesf", [128, 128], F32)
    c_one = P.buf("g_c1", [128, 1], F32)
    c_eps6 = P.buf("g_c2", [128, 1], F32)
    c_eps6q = P.buf("g_c3", [128, 1], F32)
    c_eps5 = P.buf("g_c4", [128, 1], F32)
    sc = {nm: P.buf("g_" + nm, [128, NCH, 2], F32) for nm in ["beta", "g", "ngc", "negegc", "eout", "egl", "tmp"]}
    Rp = P.buf("g_Rp", [128, 2, 128], BF16, nsub=2)
    vn = P.buf("g_vn", [128, 2, 128], BF16, nsub=2)
    S = P.buf("g_S", [128, 2, 128], F32, nsub=2)
    Sb = P.buf("g_Sb", [128, 2, 128], BF16, nsub=2)
    ot2 = P.buf("g_ot", [128, 2, 128], F32, nsub=2)
    sz2 = P.buf("g_sz", [128, 2, 128], F32, nsub=2)
    ogb2 = P.buf("g_og", [128, 2, 128], BF16, nsub=2)
    junk2 = P.buf("g_junk", [128, 2, 128], BF16, nsub=2)
    ssq2 = P.buf("g_ssq", [128, 2, 2], F32, nsub=2)

    P.dma("sp", cw.r(), R(k.dram("gdn_cw").ap[j], k.dram("gdn_cw").keys))
    P.dma("sp", rows.r(), R(k.dram("gdn_rows").ap[j], k.dram("gdn_rows").keys))
    P.dma("sp", masks.r(), R(k.dram("cmask").ap[0:2].rearrange("a p m -> p a m"), k.dram("cmask").keys))
    P.dma("pool", negb.r(), R(k.dram("cmask").ap[2:4].rearrange("a p m -> p a m"), k.dram("cmask").keys))
    for d in range(2):
        P.dma("pool", lvl.r(lvl.t[:, d]), R(k.dram("glvl").ap[d].rearrange("l a p m -> p l a m"), k.dram("glvl").keys))
    for a in range(2):
        P.copy("dve", I2.r(I2.t[:, a, :]), k.identf.r())
    P.memset("dve", onesf.r(), 1.0)
    P.memset("dve", c_one.r(), 1.0)
    P.memset("dve", c_eps6.r(), 1e-6)
    P.memset("dve", c_eps6q.r(), 128e-6)
    P.memset("dve", c_eps5.r(), 1e-5)
    P.act(rows.r(rows.t[:, 16:32]), rows.r(rows.t[:, 16:32]), AF.Exp)
    P.ts("dve", rows.r(rows.t[:, 16:32]), rows.r(rows.t[:, 16:32]), -1.0, ALU.mult)
    src = W_in.ap[0][:, 4096:4128].rearrange("(kk p) c -> p kk c", p=128)
    P.dma("pool", wsm.r(), R(src, W_in.keys))

    loc = {}

    def conv_tile(col0, cidx, post):
        xpad, dgl = loc["xpad"], loc["dgl"]
        s, w = load_in_w(k, W_in, 0, col0)
        i0 = k.identf.t[:].unsqueeze(1).to_broadcast([128, 4, 128])
        i1 = cw.t[:, cidx, 0:4].unsqueeze(2).to_broadcast([128, 4, 128])
        P.tt("pool", dgl.r(), k.identf.r(i0), cw.r(i1), ALU.mult)
        for b in range(NBLK):
            bk = P.bank()
            for kk in range(8):
                P.mm(psb(k, bk), k.wr.r(w[:, kk, :], s), hs(k, kk, b), start=(kk == 0), stop=(kk == 7))
            dst, _, _ = conv_dst(xpad.t, b)
            P.copy("dve", xpad.r(dst), k.ps.r(psview(k, bk, b), bk))
        for b in range(NBLK):
            bk = P.bank()
            for kk in range(4):
                P.mm(k.ps.r(psview(k, bk, b), bk), dgl.r(dgl.t[:, kk, :]), xpad.r(conv_rhs(xpad.t, b, kk)), start=(kk == 0), stop=(kk == 3))
            post(b, bk)

    def norm_post(which, scale, epsb):
        def post(b, bk):
            qf, sqb, rn = loc["qf"], loc["sqb"], loc["rn"]
            P.act(qf.r(), psb(k, bk), AF.Exp, scale=-1.0)
            P.act(qf.r(), qf.r(), AF.Ln, bias=c_one.r())
            P.act(qf.r(), qf.r(), AF.Exp, scale=-1.0)
            P.tt("dve", qf.r(), psb(k, bk), qf.r(), ALU.mult)
            P.act(sqb.r(), qf.r(), AF.Square)
            b2 = P.bank()
            P.mm(psb(k, b2), k.onesb.r(), sqb.r())
            P.act(rn.r(), psb(k, b2), AF.Ln, bias=epsb.r(), scale=scale)
            P.act(rn.r(), rn.r(), AF.Exp, scale=-0.5)
            dst = kq.t[:, 4 * b:4 * b + 4, which, :]
            P.tt("dve", kq.r(dst, range(4 * b, 4 * b + 4)), qf.r(qf.t.rearrange("p (c m) -> p c m", m=128)), rn.r(rn.t.rearrange("p (c m) -> p c m", m=128)), ALU.mult)
        return post

    def v_post(b, bk):
        qf = loc["qf"]
        P.act(qf.r(), psb(k, bk), AF.Exp, scale=-1.0)
        P.act(qf.r(), qf.r(), AF.Ln, bias=c_one.r())
        P.act(qf.r(), qf.r(), AF.Exp, scale=-1.0)
        P.tt("dve", v_fm.r(v_fm.t[:, b * BLK:(b + 1) * BLK], range(4 * b, 4 * b + 4)), psb(k, bk), qf.r(), ALU.mult)

    nheads = k.cfg.get("gdn_heads", 8)
    for hh in range(nheads):
        src = W_in.ap[0][:, 3072 + hh * 128:3072 + (hh + 1) * 128].rearrange("(kk p) c -> p kk c", p=128)
        P.dma("pool", wz.r(), R(src, W_in.keys))
        bk = P.bank()
        for c in range(NCH):
            for kk in range(8):
                rhs = wsm.t[:, kk, :].rearrange("p (t d h) -> p t d h", t=2, d=2)[:, :, :, hh]
                out = k.ps.t[:, bk, c * 4:(c + 1) * 4].rearrange("p (t d) -> p t d", t=2)
                P.mm(k.ps.r(out, bk), hch(k, kk, c), wsm.r(rhs), start=(kk == 0), stop=(kk == 7))
        psv = k.ps.t[:, bk, 0:NCH * 4].rearrange("p (c t d) -> p c t d", t=2, d=2)
        beta, g, ngc, negegc, eout, egl, tmp = sc["beta"], sc["g"], sc["ngc"], sc["negegc"], sc["eout"], sc["egl"], sc["tmp"]
        P.copy("dve", tmp.r(), k.ps.r(psv[:, :, 0, :], bk))
        P.act(beta.r(), tmp.r(), AF.Exp, scale=-1.0)
        P.act(beta.r(), beta.r(), AF.Ln, bias=c_one.r())
        P.act(beta.r(), beta.r(), AF.Exp, scale=-1.0)

        def rowbc(off):
            return rows.t[:, off:off + 16].rearrange("p (d h) -> p d h", d=2)[:, :, hh].unsqueeze(1).to_broadcast([128, NCH, 2])
        P.tt("dve", tmp.r(), k.ps.r(psv[:, :, 1, :], bk), rows.r(rowbc(0)), ALU.add)
        P.ts("dve", g.r(), tmp.r(), -1.0, ALU.mult)
        P.tt("dve", g.r(), g.r(), tmp.r(), ALU.max)
        P.act(g.r(), g.r(), AF.Exp, scale=-1.0)
        P.act(g.r(), g.r(), AF.Ln, bias=c_one.r())
        P.ts("dve", tmp.r(), tmp.r(), 0.0, ALU.max)
        P.tt("dve", g.r(), g.r(), tmp.r(), ALU.add)
        P.tt("dve", g.r(), g.r(), rows.r(rowbc(16)), ALU.mult)
        bk2 = P.bank()
        bk3 = P.bank()
        for c in range(NCH):
            for d in range(2):
                P.mm(k.ps.r(k.ps.t[:, bk2, c * 2 + d:c * 2 + d + 1], bk2), masks.r(masks.t[:, d, :]), g.r(g.t[:, c, d:d + 1]))
            P.mm(k.ps.r(k.ps.t[:, bk3, c * 2:c * 2 + 2], bk3), onesf.r(), g.r(g.t[:, c, :]))
        ps2 = k.ps.t[:, bk2, 0:NCH * 2].rearrange("p (c d) -> p c d", d=2)
        ps3 = k.ps.t[:, bk3, 0:NCH * 2].rearrange("p (c d) -> p c d", d=2)
        P.ts("dve", ngc.r(), k.ps.r(ps2, bk2), -1.0, ALU.mult)
        P.act(negegc.r(), ngc.r(), AF.Exp, scale=-1.0)
        P.ts("dve", negegc.r(), negegc.r(), -1.0, ALU.mult)
        P.copy("dve", egl.r(), k.ps.r(ps3, bk3))
        P.tt("dve", eout.r(), egl.r(), ngc.r(), ALU.add)
        P.act(eout.r(), eout.r(), AF.Exp)
        P.act(egl.r(), egl.r(), AF.Exp)
        P.barrier()
        mkp = P.mark()
        xpad = P.buf("g_xpad", [128, CPW + 1], BF16)
        dgl = P.buf("g_dg", [128, 4, 128], BF16)
        qf = P.buf("g_qf", [128, BLK], F32)
        sqb = P.buf("g_sq", [128, BLK], BF16)
        rn = P.buf("g_rn", [128, BLK], F32)
        loc.update(xpad=xpad, dgl=dgl, qf=qf, sqb=sqb, rn=rn)
        P.memset("pool", xpad.r(), 0.0)
        conv_tile(hh * 128, hh, norm_post(1, 128.0, c_eps6q))
        conv_tile(1024 + hh * 128, 8 + hh, norm_post(0, 1.0, c_eps6))
        conv_tile(2048 + hh * 128, 16 + hh, v_post)
        for (srcfn, dstb) in ((lambda c: v_fm.r(v_fm.t[:, c * 128:(c + 1) * 128], c), v_tm), (lambda c: kq.r(kq.t[:, c, 0, :], c), k_tm)):
            for c0 in range(0, NCH, 8):
                bk = P.bank()
                pb = k.ps.t[:, bk, :].bitcast(BF16)
                n = min(8, NCH - c0)
                for ci in range(n):
                    P.tr(k.ps.r(pb[:, ci * 128:(ci + 1) * 128], bk), srcfn(c0 + ci), k.identb.r())
                srcv = pb[:, 0:n * 128].rearrange("p (c m) -> p c m", m=128)
                P.copy("act", dstb.r(dstb.t[:, c0:c0 + n, :], range(c0, c0 + n)), k.ps.r(srcv, bk))
        P.barrier()
        P.release(mkp)
        mku = P.mark()
        GS = 2
        G = 2 * GS
        NR = 2 * G
        U = {}
        for nm in ["egcr", "dec", "NB", "NA"]:
            U[nm] = P.buf("gu_" + nm, [128, G, 128], BF16, nsub=G)
        U["Y"] = P.buf("gu_Y", [128, G, 2, 128], BF16, nsub=G)
        for nm in ["attn", "qin", "kout"]:
            U[nm] = P.buf("gu_" + nm, [128, NR, 128], BF16, nsub=NR)
        U["T"] = P.buf("gu_T", [128, NR, 2, 128], BF16, nsub=NR)
        RES = ("attn", "qin", "kout", "T")
        nsteps = k.cfg.get("gdn_steps", NCH)
        groups = [list(range(g0, min(g0 + GS, nsteps))) for g0 in range(0, nsteps, GS)]

        def ur(nm, ui, gi, sub=None):
            b_ = U[nm]
            si = (gi % 2) * G + ui if nm in RES else ui
            return b_.r(b_.t[:, si] if sub is None else b_.t[:, si, sub], si)

        def prelim_gen(gi):
            units = [(step, d) for step in groups[gi] for d in range(2)]
            cs = [(st_ if d_ == 0 else NCH - 1 - st_) for (st_, d_) in units]
            banks = {}
            for ui, (st_, d) in enumerate(units):
                c = cs[ui]
                bkR = ui
                banks[("R", ui)] = bkR
                gbc = g.t[:, c, d:d + 1].to_broadcast([128, 128])
                r0 = k.ps.r(k.ps.t[:, bkR, 0:128], bkR)
                r1 = k.ps.r(k.ps.t[:, bkR, 128:256], bkR)
                P.mm(r0, g.r(gbc), masks.r(masks.t[:, d, :]))
                P.mm(r1, g.r(gbc), masks.r(masks.t[:, d, :]), start=True, stop=False)
                P.mm(r1, k.identb.r(), negb.r(negb.t[:, d, :]), start=False, stop=True)
                yield
            for ui, (st_, d) in enumerate(units):
                c = cs[ui]
                bkR = banks[("R", ui)]
                P.act(ur("egcr", ui, gi), k.ps.r(k.ps.t[:, bkR, 0:128], bkR), AF.Exp)
                P.act(ur("dec", ui, gi), k.ps.r(k.ps.t[:, bkR, 128:256], bkR), AF.Exp, bias=ngc.r(ngc.t[:, c, d:d + 1]))
                yield
            for ui, (st_, d) in enumerate(units):
                c = cs[ui]
                bkK = ui
                banks[("K", ui)] = bkK
                P.mm(k.ps.r(k.ps.t[:, bkK, 0:256], bkK), kq.r(kq.t[:, c, 0, :], c), kq.r(kq.t[:, c, :, :].rearrange("p a m -> p (a m)"), c))
                yield
            for ui, (st_, d) in enumerate(units):
                c = cs[ui]
                bkK = banks[("K", ui)]
                P.stt(ur("NB", ui, gi), k.ps.r(k.ps.t[:, bkK, 0:128], bkK), beta.r(beta.t[:, c, d:d + 1]), ur("dec", ui, gi), ALU.mult, ALU.mult)
                P.tt("dve", ur("attn", ui, gi), k.ps.r(k.ps.t[:, bkK, 128:256], bkK), ur("dec", ui, gi), ALU.mult)
                yield
            for ui, (st_, d) in enumerate(units):
                bkT = ui
                banks[("T", ui)] = bkT
                pbT = k.ps.t[:, bkT, :].bitcast(BF16)
                P.tr(k.ps.r(pbT[:, 0:128], bkT), ur("NB", ui, gi), k.identb.r())
                yield
            for ui, (st_, d) in enumerate(units):
                c = cs[ui]
                bkT = banks[("T", ui)]
                pbT = k.ps.t[:, bkT, :].bitcast(BF16)
                P.copy("act", ur("NA", ui, gi), k.ps.r(pbT[:, 0:128], bkT))
                P.tt("pool", ur("qin", ui, gi), kq.r(kq.t[:, c, 1, :], c), ur("egcr", ui, gi), ALU.mult)
                P.act(ur("kout", ui, gi), k_tm.r(k_tm.t[:, c, :], c), AF.Identity, scale=eout.r(eout.t[:, c, d:d + 1]))
                yield
            for ui, (st_, d) in enumerate(units):
                P.tt("pool", ur("Y", ui, gi, 0), ur("NA", ui, gi), lvl.r(lvl.t[:, d, 0, 0]), ALU.mult)
                P.tt("pool", ur("Y", ui, gi, 1), ur("NB", ui, gi), lvl.r(lvl.t[:, d, 0, 1]), ALU.mult)
                P.tt("pool", ur("T", ui, gi), I2.r(), ur("Y", ui, gi), ALU.add)
                yield
            for lv in range(1, 7):
                last = (lv == 6)
                for ui, (st_, d) in enumerate(units):
                    bkY = ui
                    banks[("Y", ui)] = bkY
                    if not last:
                        P.mm(k.ps.r(k.ps.t[:, bkY, 0:128], bkY), ur("NB", ui, gi), ur("T", ui, gi, 0))
                    P.mm(k.ps.r(k.ps.t[:, bkY, 128:256], bkY), ur("NA", ui, gi), ur("T", ui, gi, 1))
                    yield
                for ui, (st_, d) in enumerate(units):
                    bkY = banks[("Y", ui)]
                    if not last:
                        P.tt("dve", ur("Y", ui, gi), k.ps.r(k.ps.t[:, bkY, 0:256].rearrange("p (a m) -> p a m", m=128), bkY), lvl.r(lvl.t[:, d, lv]), ALU.mult)
                    else:
                        P.tt("dve", ur("Y", ui, gi, 1), k.ps.r(k.ps.t[:, bkY, 128:256], bkY), lvl.r(lvl.t[:, d, lv, 1]), ALU.mult)
                    yield
                for ui, (st_, d) in enumerate(units):
                    bkZ = ui
                    banks[("Z", ui)] = bkZ
                    if not last:
                        P.mm(k.ps.r(k.ps.t[:, bkZ, 0:128], bkZ), ur("T", ui, gi, 1), ur("Y", ui, gi, 0))
                    P.mm(k.ps.r(k.ps.t[:, bkZ, 128:256], bkZ), ur("T", ui, gi, 0), ur("Y", ui, gi, 1))
                    yield
                for ui, (st_, d) in enumerate(units):
                    bkZ = banks[("Z", ui)]
                    if not last:
                        P.tt("dve", ur("T", ui, gi), ur("T", ui, gi), k.ps.r(k.ps.t[:, bkZ, 0:256].rearrange("p (a m) -> p a m", m=128), bkZ), ALU.add)
                    else:
                        P.tt("dve", ur("T", ui, gi, 1), ur("T", ui, gi, 1), k.ps.r(k.ps.t[:, bkZ, 128:256], bkZ), ALU.add)
                    yield

        cbs = [0, 0]

        def chain_gen(gi):
            units = [(step, d) for step in groups[gi] for d in range(2)]
            for s_i in range(len(groups[gi])):
                gens = [unit_chain(gi, 2 * s_i + d_, units[2 * s_i + d_][0], d_) for d_ in range(2)]
                alive = [True, True]
                while alive[0] or alive[1]:
                    for d_ in range(2):
                        if alive[d_]:
                            try:
                                next(gens[d_])
                            except StopIteration:
                                alive[d_] = False
                    yield

        def unit_chain(gi, ui, step, d):
            def cbank():
                cbs[d] ^= 1
                return 4 + 2 * d + cbs[d]
            class _V:
                def __init__(self, buf):
                    self.buf = buf
                    self.t = buf.t[:, d]

                def r(self, ap=None):
                    return self.buf.r(self.t if ap is None else ap, d)
            ot, sz, ogb, junk, ssq = _V(ot2), _V(sz2), _V(ogb2), _V(junk2), _V(ssq2)
            if True:
                c = step if d == 0 else NCH - 1 - step
                first_visit = (c <= 9) if d == 0 else (c >= 10)
                if nsteps < NCH:
                    first_visit = True
                seg = seg_of_chunk(c)
                c_first, c_n = SEG_CH[seg]
                seg_start = (c == c_first) if d == 0 else (c == c_first + c_n - 1)
                seg_end = (c == c_first + c_n - 1) if d == 0 else (c == c_first)
                if seg_start:
                    if seg == 2:
                        P.dma("sp", S.r(S.t[:, d], d), R(k.dram("gdn_s0").ap[j, d, hh], k.dram("gdn_s0").keys))
                    else:
                        P.memset("dve", S.r(S.t[:, d], d), 0.0)
                    P.copy("act", Sb.r(Sb.t[:, d], d), S.r(S.t[:, d], d))
                    yield
                kc = kq.r(kq.t[:, c, 0, :], c)
                bkC = cbank()
                P.mm(k.ps.r(k.ps.t[:, bkC, 0:128], bkC), kc, Sb.r(Sb.t[:, d], d))
                yield
                P.stt(Rp.r(Rp.t[:, d], d), k.ps.r(k.ps.t[:, bkC, 0:128], bkC), negegc.r(negegc.t[:, c, d:d + 1]), v_tm.r(v_tm.t[:, c, :], c), ALU.mult, ALU.add)
                yield
                bkV = cbank()
                P.mm(k.ps.r(k.ps.t[:, bkV, 0:128], bkV), ur("T", ui, gi, 1), Rp.r(Rp.t[:, d], d))
                yield
                P.act(vn.r(vn.t[:, d], d), k.ps.r(k.ps.t[:, bkV, 0:128], bkV), AF.Identity, scale=beta.r(beta.t[:, c, d:d + 1]))
                yield
                bkS = cbank()
                pS = k.ps.r(k.ps.t[:, bkS, 0:128], bkS)
                P.mm(pS, ur("kout", ui, gi), vn.r(vn.t[:, d], d))
                bkO = cbank()
                po = k.ps.r(k.ps.t[:, bkO, 0:128], bkO)
                P.mm(po, ur("qin", ui, gi), Sb.r(Sb.t[:, d], d), start=True, stop=False)
                P.mm(po, ur("attn", ui, gi), vn.r(vn.t[:, d], d), start=False, stop=True)
                yield
                P.stt(S.r(S.t[:, d], d), S.r(S.t[:, d], d), egl.r(egl.t[:, c, d:d + 1]), pS, ALU.mult, ALU.add)
                yield
                P.copy("act", Sb.r(Sb.t[:, d], d), S.r(S.t[:, d], d))
                yield
                if first_visit:
                    P.copy("act", o_tm.r(o_tm.t[:, c, :], c), po)
                    yield
                else:
                    P.tt("dve", ot.r(), po, o_tm.r(o_tm.t[:, c, :], c), ALU.add)
                    yield
                    o_, i_, a_ = junk.t, ot.t, ssq.t[:, 0:1]
                    P.op("act", lambda e, o_=o_, i_=i_, a_=a_: e.activation(out=o_, in_=i_, func=AF.Square, accum_out=a_),
                         reads=[ot.r()], writes=[junk.r(), ssq.r()])
                    P.act(ssq.r(ssq.t[:, 1:2]), ssq.r(ssq.t[:, 0:1]), AF.Ln, bias=c_eps5.r(), scale=1.0 / 128)
                    P.act(ssq.r(ssq.t[:, 1:2]), ssq.r(ssq.t[:, 1:2]), AF.Exp, scale=-0.5)
                    yield
                    bkZ = cbank()
                    for kk in range(8):
                        P.mm(k.ps.r(k.ps.t[:, bkZ, 0:128], bkZ), hch(k, kk, c), wz.r(wz.t[:, kk, :]), start=(kk == 0), stop=(kk == 7))
                    yield
                    P.act(sz.r(), k.ps.r(k.ps.t[:, bkZ, 0:128], bkZ), AF.Exp, scale=-1.0)
                    P.act(sz.r(), sz.r(), AF.Ln, bias=c_one.r())
                    P.act(sz.r(), sz.r(), AF.Exp, scale=-1.0)
                    P.stt(ot.r(), ot.r(), ssq.r(ssq.t[:, 1:2]), rows.r(rows.t[:, 32:160]), ALU.mult, ALU.mult)
                    yield
                    P.tt("dve", ot.r(), ot.r(), sz.r(), ALU.mult)
                    P.tt("dve", ogb.r(), ot.r(), k.ps.r(k.ps.t[:, bkZ, 0:128], bkZ), ALU.mult)
                    yield
                    bkT2 = cbank()
                    pbT2 = k.ps.t[:, bkT2, :].bitcast(BF16)
                    P.tr(k.ps.r(pbT2[:, 0:128], bkT2), ogb.r(), k.identb.r())
                    yield
                    P.copy("act", v_fm.r(v_fm.t[:, c * 128:(c + 1) * 128], c), k.ps.r(pbT2[:, 0:128], bkT2))
                    yield
                if seg_end and seg < 2:
                    P.dma("sp", R(k.dram("gdn_out").ap[j, seg, d, hh], k.dram("gdn_out").keys), S.r(S.t[:, d], d))

        def run_all(gen):
            for _ in gen:
                pass

        def merge(pg, cg, ratio):
            pdone = cdone = False
            while not (pdone and cdone):
                if not cdone:
                    try:
                        next(cg)
                    except StopIteration:
                        cdone = True
                for _ in range(ratio):
                    if pdone:
                        break
                    try:
                        next(pg)
                    except StopIteration:
                        pdone = True

        run_all(prelim_gen(0))
        for gi in range(len(groups)):
            if gi + 1 < len(groups):
                merge(prelim_gen(gi + 1), chain_gen(gi), k.cfg.get("gdn_ratio", 4))
            else:
                run_all(chain_gen(gi))
        P.barrier()
        P.release(mku)
        if nsteps == NCH:
            outproj_acc_g(k, W_out, hh * 128, lambda b: v_fm.r(v_fm.t[:, b * BLK:(b + 1) * BLK], range(4 * b, 4 * b + 4)))
    P.barrier()
    P.release(mk)


def outproj_acc_g(k, W, r0, o_region):
    P = k.P
    s, wo = load_out_w(k, W, 0, r0)
    for m in range(8):
        for b in range(NBLK):
            bk = P.bank()
            P.mm(psb(k, bk), k.wr.r(wo[:, m * 128:(m + 1) * 128], s), o_region(b))
            c = bc(b)
            P.stt(xs(k, m, b), psb(k, bk), k.mod.r(k.mod.t[:, Q_G1 + m, c:c + 1]), xs(k, m, b), ALU.mult, ALU.add)


NCORES = 8


def f32(a):
    return np.ascontiguousarray(np.asarray(a, dtype=np.float32))


def prep_inputs(inp):
    g = {k: np.asarray(v) for k, v in inp.items()}
    DEPTH = 4
    shared = {}
    shared["ident"] = np.eye(128, dtype=np.float32)
    for L in range(DEPTH):
        shared["ada_w%d" % L] = f32(g["ada_w"][L:L + 1])
        shared["ffn_w_in%d" % L] = f32(g["ffn_w_in"][L:L + 1])
        shared["ffn_w_out%d" % L] = f32(g["ffn_w_out"][L:L + 1])
    shared["ada_b"] = f32(g["ada_b"].reshape(DEPTH, 48, 128).transpose(0, 2, 1))
    shared["lng"] = f32(g["ln_g"].reshape(DEPTH, 2, 8, 128).transpose(0, 1, 3, 2))
    shared["lnb"] = f32(g["ln_b"].reshape(DEPTH, 2, 8, 128).transpose(0, 1, 3, 2))
    shared["fcw"] = f32(g["ffn_conv"].reshape(DEPTH, 9, 44, 128).transpose(0, 3, 2, 1))
    shared["lru_w_in"] = f32(g["lru_w_in"])
    shared["lru_w_out"] = f32(g["lru_w_out"])
    shared["lru_gate_w"] = f32(g["lru_gate_w"])
    sm = np.zeros((128, 10, 12), np.float32)
    sm[:, :, 0:4] = g["lru_conv"][0].reshape(4, 10, 128).transpose(2, 1, 0)
    sm[:, :, 4] = g["lru_conv_b"][0].reshape(10, 128).T
    sm[:, :, 5:9] = g["lru_gate_b"][0].reshape(4, 10, 128).transpose(2, 1, 0)
    sm[:, :, 9:11] = g["lru_lambda"][0].reshape(2, 10, 128).transpose(2, 1, 0)
    shared["lru_sm"] = sm
    tri_f = np.triu(np.ones((128, 128), np.float32))
    tri_b = np.tril(np.ones((128, 128), np.float32))
    shared["cmask"] = np.stack([tri_f, tri_b, (1 - tri_f) * -30000.0, (1 - tri_b) * -30000.0]).astype(np.float32)
    shared["ssd_w_in"] = f32(g["ssd_w_in"])
    shared["ssd_w_out"] = f32(g["ssd_w_out"])
    cw = np.zeros((128, 24, 5), np.float32)
    cw[:, :, 0:4] = g["ssd_conv"][0].reshape(4, 24, 128).transpose(2, 1, 0)
    cw[:, :, 4] = g["ssd_conv_b"][0].reshape(24, 128).T
    shared["ssd_cw"] = cw
    row = np.concatenate([g["ssd_dt_bias"][0].reshape(64), g["ssd_a_log"][0].reshape(64), g["ssd_d"][0].reshape(32)])
    shared["ssd_rows"] = f32(np.broadcast_to(row[None, :], (128, 160)))
    shared["ssd_nrm"] = f32(np.broadcast_to(g["ssd_norm"][0][None, :], (128, 2048)))
    for jj in range(2):
        shared["gdn_w_in%d" % jj] = f32(g["gdn_w_in"][jj:jj + 1])
        shared["gdn_w_out%d" % jj] = f32(g["gdn_w_out"][jj:jj + 1])
    shared["gdn_cw"] = f32(g["gdn_conv"].reshape(2, 4, 24, 128).transpose(0, 3, 2, 1))
    grow = np.concatenate([g["gdn_dt_bias"].reshape(2, 16), g["gdn_a_log"].reshape(2, 16), g["gdn_norm"].reshape(2, 128)], axis=1)
    shared["gdn_rows"] = f32(np.broadcast_to(grow[:, None, :], (2, 128, 160)))
    li = np.arange(128)[:, None]
    si = np.arange(128)[None, :]
    lv = np.zeros((2, 7, 2, 128, 128), np.float32)
    for jl in range(7):
        Bs = 2 ** jl
        mA = ((li // (2 * Bs)) == (si // (2 * Bs))) & ((li % (2 * Bs)) >= Bs) & ((si % (2 * Bs)) < Bs)
        mA = mA.astype(np.float32)
        lv[0, jl, 0] = -mA
        lv[0, jl, 1] = -mA.T
        lv[1, jl, 0] = -mA.T
        lv[1, jl, 1] = -mA
    shared["glvl"] = lv
    maps = []
    for i in range(NCORES):
        p0, p1, sb = 2 * i, 2 * i + 1, i % 4
        xin = np.concatenate([g["x_prompt"][p0].T, g["x_prompt"][p1].T, g["x_sample"][sb].T], axis=1)
        cond = np.stack([g["c_ctx"].reshape(8, 128).T, g["c"][sb].reshape(8, 128).T], axis=-1)
        m = dict(shared)
        m["xin"] = f32(xin)
        m["cond"] = f32(cond)
        m["ssd_s0"] = f32(g["state_ssd"][sb, 0].reshape(2, 8, 4, 64, 128).transpose(0, 1, 4, 2, 3).reshape(2, 8, 128, 256))
        m["gdn_s0"] = f32(g["state_gdn"][sb])
        m["lru_s0"] = f32(g["state_lru"][sb, 0].reshape(2, 10, 128).transpose(2, 1, 0))
        maps.append(m)
    return maps


def assemble(results, inp):
    BATCH, SEQ, D = 16, 256, 1024
    yp = np.zeros((BATCH, SEQ, D), np.float32)
    ys = np.zeros((4, 2048, D), np.float32)
    for i in range(NCORES):
        y = np.asarray(results[i]["yout"])
        yp[2 * i] = y[:, 0:256].T
        yp[2 * i + 1] = y[:, 256:512].T
        if i < 4:
            ys[i] = y[:, 512:].T
    nl = np.zeros((BATCH, 1, 2, 1280), np.float32)
    for i in range(NCORES):
        if "lru_out" in results[i]:
            o = np.asarray(results[i]["lru_out"])
            for pi in range(2):
                nl[2 * i + pi, 0] = o[:, :, pi, :].transpose(2, 1, 0).reshape(2, 1280)
    nssd = np.zeros((BATCH, 1, 2, 32, 64, 128), np.float32)
    for i in range(NCORES):
        if "ssd_out" in results[i]:
            o = np.asarray(results[i]["ssd_out"])
            for pi in range(2):
                nssd[2 * i + pi, 0] = o[pi].reshape(2, 32, 64, 128)
    ngdn = np.zeros((BATCH, 2, 2, 8, 128, 128), np.float32)
    for i in range(NCORES):
        if "gdn_out" in results[i]:
            o = np.asarray(results[i]["gdn_out"])
            for pi in range(2):
                ngdn[2 * i + pi] = o[:, pi]
    return yp, ys, nl, nssd, ngdn


_NC_CACHE = {}


def kernel(**inputs):
    cfg = {}
    if "nc" not in _NC_CACHE:
        _NC_CACHE["nc"] = build(cfg)
    nc, used = _NC_CACHE["nc"]
    maps = prep_inputs(inputs)
    maps = [{kk: v for kk, v in mm.items() if kk in used} for mm in maps]
    res = run_bass_kernel_spmd(nc, maps, core_ids=list(range(NCORES)))
    yp, ys, nl, nssd, ngdn = assemble(res.results, inputs)
    return (yp, ys, ngdn, nssd, nl)
```

```python
import numpy as np
import concourse.bass as bass
import concourse.mybir as mybir
from concourse.bass_utils import run_bass_kernel_spmd
from contextlib import ExitStack

F32 = mybir.dt.float32
F32R = mybir.dt.float32r
BF16 = mybir.dt.bfloat16
ALU = mybir.AluOpType
AF = mybir.ActivationFunctionType
AX = mybir.AxisListType

ENGS = ["pe", "act", "dve", "pool", "sp"]
NDS = 24
MAXEMB = 1
ARENA_WORDS = 53000


class R:
    __slots__ = ("ap", "keys")

    def __init__(self, ap, keys):
        self.ap = ap
        self.keys = keys


class Buf:
    def __init__(self, prog, name, shape, dtype, nsub=1, psum=False):
        self.name = name
        self.nsub = nsub
        if psum:
            self.t = prog.st.enter_context(prog.nc.psum_tensor(name, shape, dtype))
        else:
            n = 1
            for d in shape[1:]:
                n *= d
            esz = 2 if dtype == BF16 else 4
            words = (n * esz + 3) // 4
            off = prog.aoff
            prog.aoff += words
            assert prog.aoff <= ARENA_WORDS, (name, prog.aoff)
            prog.apeak = max(prog.apeak, prog.aoff)
            ap = prog.arena[:, off:off + words]
            if dtype == BF16:
                ap = ap.bitcast(BF16)
            ap = ap[:, 0:n]
            if len(shape) > 2:
                names = " ".join("d%d" % i for i in range(len(shape) - 1))
                kw = {"d%d" % i: shape[i + 1] for i in range(len(shape) - 1)}
                ap = ap.rearrange("p (%s) -> p %s" % (names, names), **kw)
            self.t = ap

    def r(self, ap=None, sub=None):
        if ap is None:
            ap = self.t[:] if not hasattr(self.t, "rearrange") else self.t
        if sub is None:
            keys = [(self.name, i) for i in range(self.nsub)]
        elif isinstance(sub, (list, tuple, range)):
            keys = [(self.name, i) for i in sub]
        else:
            keys = [(self.name, sub)]
        return R(ap, keys)


class Prog:
    def __init__(self, nc, st):
        self.nc = nc
        self.st = st
        self.q = {e: [] for e in ENGS}
        self.cnt = {e: 0 for e in ENGS}
        self.sem = {e: st.enter_context(nc.semaphore("s_" + e)) for e in ENGS}
        self.dsem = [st.enter_context(nc.semaphore("d%d" % i)) for i in range(NDS)]
        self.dcnt = [0] * NDS
        self.dnext = 0
        self.known = {e: {} for e in ENGS}
        self.last_w = {}
        self.readers = {}
        self.nops = 0
        self.nwaits = 0
        self.bar = {e: None for e in ENGS}
        self._bank = 0
        self.arena_t = st.enter_context(nc.sbuf_tensor("arena", [128, ARENA_WORDS], F32))
        self.arena = self.arena_t[:]
        self.aoff = 0
        self.apeak = 0

    def mark(self):
        return self.aoff

    def release(self, m):
        self.aoff = m

    def bank(self):
        b = self._bank
        self._bank = (self._bank + 1) % 8
        return b

    def barrier(self):
        snap = {e: self.cnt[e] for e in ENGS if self.cnt[e] > 0}
        for i in range(NDS):
            if self.dcnt[i] > 0:
                snap["d%d" % i] = self.dcnt[i]
        for e in ENGS:
            self.bar[e] = dict(snap)

    def buf(self, name, shape, dtype, nsub=1, psum=False):
        return Buf(self, name, shape, dtype, nsub, psum)

    def _collect(self, eng, reads, writes):
        need = {}

        def add(tok):
            if tok is None:
                return
            semid, val, snap = tok
            if eng == "pe" and semid == "pe":
                return
            if need.get(semid, (0, None))[0] < val:
                need[semid] = (val, snap)

        for k in reads:
            add(self.last_w.get(k))
        for k in writes:
            add(self.last_w.get(k))
            rd = self.readers.get(k)
            if rd:
                for tok in rd.values():
                    add(tok)
        kn = self.known[eng]
        if self.bar[eng] is not None:
            for semid, val in self.bar[eng].items():
                if eng == "pe" and semid == "pe":
                    continue
                if semid == eng and val >= self.cnt[eng] + 1:
                    continue
                if need.get(semid, (0, None))[0] < val:
                    need[semid] = (val, None)
            self.bar[eng] = None
        waits = []
        for semid, (val, snap) in need.items():
            if kn.get(semid, 0) >= val:
                continue
            waits.append((semid, val))
        for semid, (val, snap) in need.items():
            if kn.get(semid, 0) < val:
                kn[semid] = val
            if snap:
                for s2, v2 in snap.items():
                    if kn.get(s2, 0) < v2:
                        kn[s2] = v2
        return waits

    def _commit(self, tok, reads, writes):
        for k in writes:
            self.last_w[k] = tok
            self.readers[k] = {}
        for k in reads:
            if k in writes:
                continue
            self.readers.setdefault(k, {})[tok[0]] = tok

    def op(self, eng, fn, reads=(), writes=()):
        rk = [k for r in reads if r is not None for k in r.keys]
        wk = [k for r in writes if r is not None for k in r.keys]
        if eng != "pe":
            for k_ in rk:
                if k_[0] == "ps" and k_ not in wk:
                    wk.append(k_)
        waits = self._collect(eng, rk, wk)
        self.cnt[eng] += 1
        tok = (eng, self.cnt[eng], dict(self.known[eng]))
        self.q[eng].append((waits, fn, None))
        self._commit(tok, rk, wk)
        self.nops += 1
        self.nwaits += len(waits)
        return tok

    def dma(self, eng, out, in_, **kw):
        rk = list(in_.keys)
        wk = list(out.keys)
        i = self.dnext
        self.dnext = (self.dnext + 1) % NDS
        semid = "d%d" % i
        waits = self._collect(eng, rk, wk)
        kn = self.known[eng]
        if kn.get(semid, 0) < self.dcnt[i]:
            waits.append((semid, self.dcnt[i]))
            kn[semid] = self.dcnt[i]
        self.dcnt[i] += 16
        tok = (semid, self.dcnt[i], dict(kn))
        oap, iap = out.ap, in_.ap

        def fn(e):
            return e.dma_start(out=oap, in_=iap, **kw)

        self.q[eng].append((waits, fn, i))
        self._commit(tok, rk, wk)
        self.nops += 1
        self.nwaits += len(waits)
        return tok

    def _semh(self, semid):
        if semid in self.sem:
            return self.sem[semid]
        return self.dsem[int(semid[1:])]

    def emit(self, final_wait_eng="sp"):
        nc = self.nc
        finals = []
        for e in ENGS:
            if self.cnt[e] > 0:
                finals.append((e, self.cnt[e]))
        for i in range(NDS):
            if self.dcnt[i] > 0:
                finals.append(("d%d" % i, self.dcnt[i]))
        eng_objs = {}
        with nc.Block() as block:
            def mk(ename):
                def body(e):
                    for waits, fn, dsi in self.q[ename]:
                        if dsi is not None or len(waits) > MAXEMB:
                            for semid, val in waits:
                                e.wait_ge(self._semh(semid), val)
                            ins = fn(e)
                        else:
                            ins = fn(e)
                            for semid, val in waits:
                                ins._wait_ge(self._semh(semid), val)
                        if dsi is None:
                            ins.then_inc(self.sem[ename], 1)
                        else:
                            ins.then_inc(self.dsem[dsi], 16)
                    if ename == final_wait_eng:
                        for semid, val in finals:
                            e.wait_ge(self._semh(semid), val)
                return body
            block.tensor(mk("pe"))
            block.scalar(mk("act"))
            block.vector(mk("dve"))
            block.gpsimd(mk("pool"))
            block.sync(mk("sp"))

    def mm(self, out, lhsT, rhs, start=True, stop=True, extra_reads=()):
        o, l, r = out.ap, lhsT.ap, rhs.ap
        return self.op("pe", lambda e: e.matmul(o, l, r, start=start, stop=stop),
                       reads=[lhsT, rhs] + list(extra_reads) + ([] if start else [out]), writes=[out])

    def tr(self, out, in_, ident):
        o, i, d = out.ap, in_.ap, ident.ap
        return self.op("pe", lambda e: e.transpose(o, i, d), reads=[in_, ident], writes=[out])

    def act(self, out, in_, func, bias=None, scale=None, eng="act"):
        o, i = out.ap, in_.ap
        kw = {}
        rd = [in_]
        if bias is not None:
            if isinstance(bias, R):
                kw["bias"] = bias.ap
                rd.append(bias)
            else:
                kw["bias"] = float(bias)
        if scale is not None:
            if isinstance(scale, R):
                kw["scale"] = scale.ap
                rd.append(scale)
            else:
                kw["scale"] = float(scale)
        return self.op("act", lambda e: e.activation(out=o, in_=i, func=func, **kw), reads=rd, writes=[out])

    def tt(self, eng, out, in0, in1, op):
        o, a, b = out.ap, in0.ap, in1.ap
        return self.op(eng, lambda e: e.tensor_tensor(out=o, in0=a, in1=b, op=op), reads=[in0, in1], writes=[out])

    def ts(self, eng, out, in0, s1, op0, s2=None, op1=None):
        o, a = out.ap, in0.ap
        rd = [in0]
        v1 = s1
        if isinstance(s1, R):
            rd.append(s1)
            v1 = s1.ap
        v2 = s2
        if isinstance(s2, R):
            rd.append(s2)
            v2 = s2.ap
        if op1 is None:
            return self.op(eng, lambda e: e.tensor_scalar(out=o, in0=a, scalar1=v1, scalar2=None, op0=op0), reads=rd, writes=[out])
        return self.op(eng, lambda e: e.tensor_scalar(out=o, in0=a, scalar1=v1, scalar2=v2, op0=op0, op1=op1), reads=rd, writes=[out])

    def stt(self, out, in0, scalar, in1, op0, op1):
        o, a, b = out.ap, in0.ap, in1.ap
        rd = [in0, in1]
        sv = scalar
        if isinstance(scalar, R):
            rd.append(scalar)
            sv = scalar.ap
        return self.op("dve", lambda e: e.scalar_tensor_tensor(out=o, in0=a, scalar=sv, in1=b, op0=op0, op1=op1), reads=rd, writes=[out])

    def copy(self, eng, out, in_):
        o, i = out.ap, in_.ap
        if eng == "act":
            return self.op("act", lambda e: e.copy(out=o, in_=i), reads=[in_], writes=[out])
        return self.op(eng, lambda e: e.tensor_copy(out=o, in_=i), reads=[in_], writes=[out])

    def memset(self, eng, out, val):
        o = out.ap
        return self.op(eng, lambda e: e.memset(o, val), reads=[], writes=[out])


D = 1024
NT = 2560
NBLK = 5
BLK = 512
DEPTH = 4
FH = 2816
NPAIR = 22
UPW = 2 * 258 + 34 * 66
ALPHA = (2 * DEPTH) ** 0.25
LN_EPS = 1e-5
NW = 5
WSL = 1024


def bc(b):
    return 0 if b == 0 else 1


class K:
    pass


def build(cfg):
    nc = bass.Bass("TRN2", target_bir_lowering=False)
    k = K()
    k.nc = nc
    k.cfg = cfg

    shapes = {
        "xin": [D, NT], "cond": [128, 8, 2], "ident": [128, 128],
        "ada_b": [DEPTH, 128, 48], "lng": [DEPTH, 2, 128, 8], "lnb": [DEPTH, 2, 128, 8],
        "fcw": [DEPTH, 128, 44, 9],
        "lru_w_in": [1, D, 2560], "lru_w_out": [1, 1280, D], "lru_gate_w": [1, 2, 2, 10, 128, 128],
        "lru_sm": [128, 10, 12], "lru_s0": [128, 10, 2],
        "cmask": [4, 128, 128],
        "ssd_w_in": [1, D, 5184], "ssd_w_out": [1, 2048, D], "ssd_cw": [128, 24, 5], "ssd_rows": [128, 160],
        "ssd_nrm": [128, 2048], "ssd_s0": [2, 8, 128, 256],
    }
    for jj in range(2):
        shapes["gdn_w_in%d" % jj] = [1, D, 4128]
        shapes["gdn_w_out%d" % jj] = [1, D, D]
    shapes.update({"gdn_cw": [2, 128, 24, 4], "gdn_rows": [2, 128, 160], "gdn_s0": [2, 2, 8, 128, 128], "glvl": [2, 7, 2, 128, 128]})
    for L in range(DEPTH):
        shapes["ada_w%d" % L] = [1, D, 6 * D]
        shapes["ffn_w_in%d" % L] = [1, D, 2 * FH]
        shapes["ffn_w_out%d" % L] = [1, FH, D]
    oshapes = {"yout": [D, NT], "lru_out": [128, 10, 2, 2], "ssd_out": [2, 2, 2048, 128], "gdn_out": [2, 2, 2, 8, 128, 128]}
    k.shapes = shapes
    k.oshapes = oshapes
    k.decl = {}

    def dram(name):
        if name in k.decl:
            return k.decl[name]
        if name in shapes:
            t = nc.dram_tensor(name, list(shapes[name]), F32, kind="ExternalInput").ap()
        else:
            t = nc.dram_tensor(name, list(oshapes[name]), F32, kind="ExternalOutput").ap()
        k.decl[name] = R(t, [("dram_" + name, 0)])
        return k.decl[name]
    k.dram = dram

    with ExitStack() as st:
        P = Prog(nc, st)
        k.P = P
        k.x = P.buf("x", [128, 8, NT], F32, nsub=40)
        k.h = P.buf("h", [128, 8, NT], BF16, nsub=40)
        k.wr = P.buf("wr", [128, NW, WSL], BF16, nsub=NW)
        k.wnext = 0
        k.ps = P.buf("ps", [128, 8, 512], F32, nsub=8, psum=True)
        k.identf = P.buf("identf", [128, 128], F32)
        k.identb = P.buf("identb", [128, 128], BF16)
        k.onesb = P.buf("onesb", [128, 128], BF16)
        k.csil = P.buf("csil", [128, 8, 2], BF16)
        k.mod = P.buf("mod", [128, 48, 2], F32)
        k.msc = P.buf("msc", [128, 2, 8, 2], F32)
        k.lngb = P.buf("lngb", [128, DEPTH, 2, 8], F32)
        k.lnbb = P.buf("lnbb", [128, DEPTH, 2, 8], F32)
        k.adab = P.buf("adab", [128, DEPTH, 48], F32)

        prologue(k)
        for L in cfg.get("layers", list(range(DEPTH))):
            layer(k, L)
        for j in range(8):
            P.dma("sp", R(k.dram("yout").ap[j * 128:(j + 1) * 128, :], k.dram("yout").keys), k.x.r(k.x.t[:, j, :], range(j * 5, j * 5 + 5)))
        P.emit()
        print("ops", P.nops, "waits", P.nwaits, {e: P.cnt[e] for e in ENGS}, "apeak", P.apeak)
    return nc, set(n for n in k.decl if n in k.shapes)


def xs(k, j, b):
    return k.x.r(k.x.t[:, j, b * BLK:(b + 1) * BLK], j * 5 + b)


def hs(k, j, b):
    return k.h.r(k.h.t[:, j, b * BLK:(b + 1) * BLK], j * 5 + b)


def psb(k, b, n=512):
    return k.ps.r(k.ps.t[:, b, 0:n], b)


def prologue(k):
    P = k.P
    for j in range(8):
        P.dma("sp", k.x.r(k.x.t[:, j, :], range(j * 5, j * 5 + 5)), R(k.dram("xin").ap[j * 128:(j + 1) * 128, :], k.dram("xin").keys))
    P.dma("sp", k.identf.r(), k.dram("ident"))
    P.copy("dve", k.identb.r(), k.identf.r())
    P.memset("dve", k.onesb.r(), 1.0)
    ctmp = P.buf("ctmp", [128, 8, 2], F32)
    P.dma("sp", ctmp.r(), k.dram("cond"))
    P.act(k.csil.r(), ctmp.r(), AF.Silu)
    for L in range(DEPTH):
        P.dma("sp", k.lngb.r(k.lngb.t[:, L]), R(k.dram("lng").ap[L].rearrange("s p j -> p s j"), k.dram("lng").keys))
        P.dma("sp", k.lnbb.r(k.lnbb.t[:, L]), R(k.dram("lnb").ap[L].rearrange("s p j -> p s j"), k.dram("lnb").keys))
        P.dma("sp", k.adab.r(k.adab.t[:, L]), R(k.dram("ada_b").ap[L], k.dram("ada_b").keys))
    last_L = k.cfg.get("layers", list(range(DEPTH)))[-1]
    for L in range(DEPTH):
        for sl in range(2):
            if L == last_L and sl == 1:
                continue
            P.ts("dve", k.lngb.r(k.lngb.t[:, L, sl]), k.lngb.r(k.lngb.t[:, L, sl]), ALPHA, ALU.mult)
            P.ts("dve", k.lnbb.r(k.lnbb.t[:, L, sl]), k.lnbb.r(k.lnbb.t[:, L, sl]), ALPHA, ALU.mult)
    for j in range(8):
        for b in range(NBLK):
            P.ts("dve", xs(k, j, b), xs(k, j, b), ALPHA, ALU.mult)


def wslot(k):
    s = k.wnext
    k.wnext = (k.wnext + 1) % NW
    return s


def load_in_w(k, W, L, c0, ncols=128):
    P = k.P
    s = wslot(k)
    src = W.ap[L][:, c0:c0 + ncols].rearrange("(kk p) c -> p kk c", p=128)
    dst = k.wr.t[:, s, 0:8 * ncols].rearrange("p (kk c) -> p kk c", c=ncols)
    P.dma("pool", k.wr.r(dst, s), R(src, W.keys))
    return s, dst


def load_out_w(k, W, L, r0):
    P = k.P
    s = wslot(k)
    src = W.ap[L][r0:r0 + 128, :]
    dst = k.wr.t[:, s, 0:1024]
    P.dma("pool", k.wr.r(dst, s), R(src, W.keys))
    return s, dst


def ada(k, L):
    P = k.P
    b = P.bank()
    for q in range(48):
        s, w = load_in_w(k, k.dram("ada_w%d" % L), 0, q * 128)
        for kk in range(8):
            P.mm(k.ps.r(k.ps.t[:, b, q * 2:q * 2 + 2], b), k.wr.r(w[:, kk, :], s),
                 k.csil.r(k.csil.t[:, kk, :]), start=(kk == 0), stop=(kk == 7))
    src = k.ps.t[:, b, 0:96].rearrange("p (q c) -> p q c", c=2)
    bias = k.adab.t[:, L, :].unsqueeze(2).to_broadcast([128, 48, 2])
    P.tt("dve", k.mod.r(), k.ps.r(src, b), k.adab.r(bias), ALU.add)
    for sl in range(2):
        q0 = (1 + 3 * sl) * 8
        P.ts("dve", k.msc.r(k.msc.t[:, sl]), k.mod.r(k.mod.t[:, q0:q0 + 8, :]), 1.0, ALU.add, 1.0 / ALPHA, ALU.mult)


def modulate(k, sl):
    P = k.P
    q_sh = (0 + 3 * sl) * 8
    for j in range(8):
        for b in range(NBLK):
            c = bc(b)
            P.act(hs(k, j, b), xs(k, j, b), AF.Identity,
                  bias=k.mod.r(k.mod.t[:, q_sh + j, c:c + 1]), scale=k.msc.r(k.msc.t[:, sl, j, c:c + 1]))


def layernorm(k, L, sl, lb):
    P = k.P
    xb, sq, mean, m2, var, rstd, nmr, t1 = lb["xb"], lb["sq"], lb["mean"], lb["m2"], lb["var"], lb["rstd"], lb["nmr"], lb["t1"]
    for b in range(NBLK):
        pb = b % 2
        for j in range(8):
            P.act(xb.r(xb.t[:, pb, j, :], pb * 8 + j), xs(k, j, b), AF.Identity)
            P.act(sq.r(sq.t[:, pb, j, :], pb * 8 + j), xs(k, j, b), AF.Square)
        b1 = P.bank()
        b2 = P.bank()
        for j in range(8):
            P.mm(psb(k, b1), k.onesb.r(), xb.r(xb.t[:, pb, j, :], pb * 8 + j), start=(j == 0), stop=(j == 7))
        for j in range(8):
            P.mm(psb(k, b2), k.onesb.r(), sq.r(sq.t[:, pb, j, :], pb * 8 + j), start=(j == 0), stop=(j == 7))
        P.act(mean.r(mean.t[:, pb], pb), psb(k, b1), AF.Identity, scale=1.0 / D)
        P.tt("dve", m2.r(m2.t[:, pb], pb), mean.r(mean.t[:, pb], pb), mean.r(mean.t[:, pb], pb), ALU.mult)
        P.stt(var.r(var.t[:, pb], pb), psb(k, b2), 1.0 / D, m2.r(m2.t[:, pb], pb), ALU.mult, ALU.subtract)
        P.act(var.r(var.t[:, pb], pb), var.r(var.t[:, pb], pb), AF.Ln, bias=lb["eps"].r())
        P.act(rstd.r(rstd.t[:, pb], pb), var.r(var.t[:, pb], pb), AF.Exp, scale=-0.5)
        P.stt(nmr.r(nmr.t[:, pb], pb), mean.r(mean.t[:, pb], pb), -1.0, rstd.r(rstd.t[:, pb], pb), ALU.mult, ALU.mult)
        for j in range(8):
            tb = j % 2
            P.tt("dve", t1.r(t1.t[:, tb], tb), xs(k, j, b), rstd.r(rstd.t[:, pb], pb), ALU.mult)
            P.tt("dve", t1.r(t1.t[:, tb], tb), t1.r(t1.t[:, tb], tb), nmr.r(nmr.t[:, pb], pb), ALU.add)
            gb_, bb_ = k.lngb, k.lnbb
            P.act(xs(k, j, b), t1.r(t1.t[:, tb], tb), AF.Identity,
                  bias=bb_.r(bb_.t[:, L, sl, j:j + 1]), scale=gb_.r(gb_.t[:, L, sl, j:j + 1]))


def ln_scope(k, L, sl):
    P = k.P
    P.barrier()
    mk = P.mark()
    if True:
        lb = {
            "xb": P.buf("ln_xb", [128, 2, 8, BLK], BF16, nsub=16),
            "sq": P.buf("ln_sq", [128, 2, 8, BLK], BF16, nsub=16),
            "mean": P.buf("ln_mean", [128, 2, BLK], F32, nsub=2),
            "m2": P.buf("ln_m2", [128, 2, BLK], F32, nsub=2),
            "var": P.buf("ln_var", [128, 2, BLK], F32, nsub=2),
            "rstd": P.buf("ln_rstd", [128, 2, BLK], F32, nsub=2),
            "nmr": P.buf("ln_nmr", [128, 2, BLK], F32, nsub=2),
            "t1": P.buf("ln_t1", [128, 2, BLK], F32, nsub=2),
            "eps": P.buf("ln_eps", [128, 1], F32),
        }
        P.memset("dve", lb["eps"].r(), LN_EPS)
        lb["final"] = (sl == 1 and L == k.cfg.get("layers", list(range(DEPTH)))[-1])
        layernorm(k, L, sl, lb)
        P.barrier()
        P.release(mk)


def ffn(k, L):
    P = k.P
    G = 2
    P.barrier()
    mk = P.mark()
    if True:
        up = P.buf("f_up", [128, 2, 2, UPW], BF16, nsub=4)
        ab = P.buf("f_ab", [128, G, NT], BF16, nsub=G * 5)
        dg = P.buf("f_dg", [128, 2, 2, 9, 128], BF16, nsub=4)
        sg = P.buf("f_sg", [128, 2, BLK], F32, nsub=2)
        fcw = P.buf("f_fcw", [128, 44, 9], F32)
        P.dma("sp", fcw.r(), R(k.dram("fcw").ap[L], k.dram("fcw").keys))
        P.memset("pool", up.r(), 0.0)
        q_g = 5 * 8
        nsg = 0
        for g0 in range(0, NPAIR, G):
            pairs = list(range(g0, min(g0 + G, NPAIR)))
            for jj, j in enumerate(pairs):
                db = j % 2
                sg_, wg = load_in_w(k, k.dram("ffn_w_in%d" % L), 0, j * 128)
                sv_, wv = load_in_w(k, k.dram("ffn_w_in%d" % L), 0, FH + j * 128)
                ws = [wg, wv]
                wss = [sg_, sv_]
                for gv in range(2):
                    tile_idx = j + gv * NPAIR
                    i0 = k.identf.t[:].unsqueeze(1).to_broadcast([128, 9, 128])
                    i1 = fcw.t[:, tile_idx, :].unsqueeze(2).to_broadcast([128, 9, 128])
                    P.tt("pool", dg.r(dg.t[:, db, gv], db * 2 + gv), k.identf.r(i0), fcw.r(i1), ALU.mult)
                for b in range(NBLK):
                    for gv in range(2):
                        bk = P.bank()
                        for kk in range(8):
                            P.mm(psb(k, bk), k.wr.r(ws[gv][:, kk, :], wss[gv]), hs(k, kk, b), start=(kk == 0), stop=(kk == 7))
                        upt = up.t[:, db, gv]
                        if b == 0:
                            dst = upt[:, 0:516].rearrange("p (s w) -> p s w", w=258)[:, :, 1:257]
                            src = k.ps.t[:, bk, :].rearrange("p (s w) -> p s w", w=256)
                        else:
                            r0 = 8 * (b - 1)
                            dst = upt[:, 516:].rearrange("p (r w) -> p r w", w=66)[:, 1 + r0:9 + r0, 1:65]
                            src = k.ps.t[:, bk, :].rearrange("p (r w) -> p r w", w=64)
                        P.copy("act" if gv == 0 else "dve", up.r(dst, db * 2 + gv), k.ps.r(src, bk))
                for b in range(NBLK):
                    bks = []
                    for gv in range(2):
                        bk = P.bank()
                        bks.append(bk)
                        upt = up.t[:, db, gv]
                        if b == 0:
                            taps = [(1, kw) for kw in range(3)]
                            outv = k.ps.t[:, bk, :].rearrange("p (s w) -> p s w", w=256)
                        else:
                            taps = [(kh, kw) for kh in range(3) for kw in range(3)]
                            outv = k.ps.t[:, bk, :].rearrange("p (r w) -> p r w", w=64)
                        for ti, (kh, kw) in enumerate(taps):
                            if b == 0:
                                rhs = upt[:, 0:516].rearrange("p (s w) -> p s w", w=258)[:, :, kw:kw + 256]
                            else:
                                r0 = 8 * (b - 1)
                                rhs = upt[:, 516:].rearrange("p (r w) -> p r w", w=66)[:, r0 + kh:r0 + kh + 8, kw:kw + 64]
                            P.mm(k.ps.r(outv, bk), dg.r(dg.t[:, db, gv, kh * 3 + kw, :], db * 2 + gv), up.r(rhs, db * 2 + gv),
                                 start=(ti == 0), stop=(ti == len(taps) - 1))
                    sb_ = nsg % 2
                    nsg += 1
                    P.act(sg.r(sg.t[:, sb_], sb_), psb(k, bks[0]), AF.Silu)
                    P.tt("dve", ab.r(ab.t[:, jj, b * BLK:(b + 1) * BLK], jj * 5 + b), psb(k, bks[1]), sg.r(sg.t[:, sb_], sb_), ALU.mult)
            wos = [load_out_w(k, k.dram("ffn_w_out%d" % L), 0, (g0 + jj) * 128) for jj in range(len(pairs))]
            for m in range(8):
                for b in range(NBLK):
                    bk = P.bank()
                    for jj in range(len(pairs)):
                        P.mm(psb(k, bk), k.wr.r(wos[jj][1][:, m * 128:(m + 1) * 128], wos[jj][0]), ab.r(ab.t[:, jj, b * BLK:(b + 1) * BLK], jj * 5 + b),
                             start=(jj == 0), stop=(jj == len(pairs) - 1))
                    c = bc(b)
                    P.stt(xs(k, m, b), psb(k, bk), k.mod.r(k.mod.t[:, q_g + m, c:c + 1]), xs(k, m, b), ALU.mult, ALU.add)
        P.barrier()
        P.release(mk)


def layer(k, L):
    cfg = k.cfg
    ada(k, L)
    modulate(k, 0)
    if cfg.get("mixers", True):
        mixer(k, L)
    ln_scope(k, L, 0)
    modulate(k, 1)
    if cfg.get("ffn", True):
        ffn(k, L)
    ln_scope(k, L, 1)


LW = 1280
SEGS = [(0, 256), (256, 256), (512, 2048)]
Q_G1 = 2 * 8


def mixer(k, L):
    kind = L % 3
    if kind == 2:
        lru(k, L // 3)
    elif kind == 1:
        ssd(k, L // 3)
    else:
        gdn(k, L // 3)


def outproj_acc(k, W, Lw, r0, ntile, o_regions):
    P = k.P
    slots = [load_out_w(k, W, Lw, r0 + i * 128) for i in range(ntile)]
    for m in range(8):
        for b in range(NBLK):
            bk = P.bank()
            for jj in range(ntile):
                s, wo = slots[jj]
                P.mm(psb(k, bk), k.wr.r(wo[:, m * 128:(m + 1) * 128], s), o_regions(jj, b), start=(jj == 0), stop=(jj == ntile - 1))
            c = bc(b)
            P.stt(xs(k, m, b), psb(k, bk), k.mod.r(k.mod.t[:, Q_G1 + m, c:c + 1]), xs(k, m, b), ALU.mult, ALU.add)


def conv1d_pad_layout():
    offs = []
    o = 0
    for (t0, n) in SEGS:
        offs.append(o)
        o += n + 3
    return offs, o


CPO, CPW = conv1d_pad_layout()


def conv_dst(xpad_t, b):
    if b == 0:
        return xpad_t[:, 0:518].rearrange("p (s w) -> p s w", w=259)[:, :, 1:257], "p (s w) -> p s w", 256
    o = CPO[2] + 1 + (b - 1) * BLK
    return xpad_t[:, o:o + BLK], None, None


def conv_rhs(xpad_t, b, kk):
    if b == 0:
        return xpad_t[:, 0:518].rearrange("p (s w) -> p s w", w=259)[:, :, kk:kk + 256]
    o = CPO[2] + (b - 1) * BLK + kk
    return xpad_t[:, o:o + BLK]


def psview(k, bk, b):
    if b == 0:
        return k.ps.t[:, bk, :].rearrange("p (s w) -> p s w", w=256)
    return k.ps.t[:, bk, :]


def lru(k, j):
    P = k.P
    G = 2
    P.barrier()
    mk = P.mark()
    xpad = P.buf("l_xpad", [128, CPW + 1], BF16)
    xrb = P.buf("l_xrb", [128, NT], BF16)
    gg = P.buf("l_gg", [128, NT], BF16)
    Ib = P.buf("l_i", [128, NT], BF16)
    A = P.buf("l_a", [128, NT], F32)
    T = P.buf("l_t", [128, NT], F32)
    H = [P.buf("l_h0", [128, NT], BF16), P.buf("l_h1", [128, NT], BF16)]
    ob = P.buf("l_o", [128, G, NT], BF16, nsub=G)
    dgl = P.buf("l_dg", [128, 4, 128], BF16)
    sm = P.buf("l_sm", [128, 10, 12], F32)
    s0 = P.buf("l_s0", [128, 10, 2], F32)
    sp = P.buf("l_sp", [128, 10, 2], F32)
    ep = P.buf("l_ep", [128, 10, 2], F32)
    one = P.buf("l_one", [128, 1], F32)
    sto = P.buf("l_sto", [128, 10, 2, 2], F32)
    P.dma("sp", sm.r(), k.dram("lru_sm"))
    P.dma("sp", s0.r(), k.dram("lru_s0"))
    P.memset("dve", one.r(), 1.0)
    P.memset("pool", xpad.r(), 0.0)
    P.act(ep.r(), sm.r(sm.t[:, :, 9:11]), AF.Exp, scale=-1.0)
    P.ts("dve", sp.r(), ep.r(), -0.2, ALU.mult, 0.25, ALU.add)
    for cst in (1.0 / 3, 0.5, 1.0):
        P.tt("dve", sp.r(), sp.r(), ep.r(), ALU.mult)
        P.ts("dve", sp.r(), sp.r(), -1.0, ALU.mult, cst, ALU.add)
    P.tt("dve", sp.r(), sp.r(), ep.r(), ALU.mult)
    P.ts("dve", sp.r(), sp.r(), -8.0, ALU.mult)

    for g0 in range(0, 10, G):
        tiles = list(range(g0, min(g0 + G, 10)))
        for jj, n in enumerate(tiles):
            s, wgb = load_in_w(k, k.dram("lru_w_in"), j, n * 128)
            sx, wxr = load_in_w(k, k.dram("lru_w_in"), j, LW + n * 128)
            s2 = wslot(k)
            gsrc = k.dram("lru_gate_w").ap[j][:, :, n].rearrange("d g kk m -> kk (d g) m")
            gw = k.wr.t[:, s2, 0:512].rearrange("p (a m) -> p a m", m=128)
            P.dma("pool", k.wr.r(gw, s2), R(gsrc, k.dram("lru_gate_w").keys))
            i0 = k.identf.t[:].unsqueeze(1).to_broadcast([128, 4, 128])
            i1 = sm.t[:, n, 0:4].unsqueeze(2).to_broadcast([128, 4, 128])
            P.tt("pool", dgl.r(), k.identf.r(i0), sm.r(i1), ALU.mult)
            for b in range(NBLK):
                bk = P.bank()
                for kk in range(8):
                    P.mm(psb(k, bk), k.wr.r(wgb[:, kk, :], s), hs(k, kk, b), start=(kk == 0), stop=(kk == 7))
                P.act(gg.r(gg.t[:, b * BLK:(b + 1) * BLK]), psb(k, bk), AF.Gelu_apprx_tanh)
            for b in range(NBLK):
                bk = P.bank()
                for kk in range(8):
                    P.mm(psb(k, bk), k.wr.r(wxr[:, kk, :], sx), hs(k, kk, b), start=(kk == 0), stop=(kk == 7))
                dst, _, _ = conv_dst(xpad.t, b)
                P.copy("dve", xpad.r(dst), k.ps.r(psview(k, bk, b), bk))
            for b in range(NBLK):
                bk = P.bank()
                for kk in range(4):
                    P.mm(k.ps.r(psview(k, bk, b), bk), dgl.r(dgl.t[:, kk, :]), xpad.r(conv_rhs(xpad.t, b, kk)), start=(kk == 0), stop=(kk == 3))
                P.act(xrb.r(xrb.t[:, b * BLK:(b + 1) * BLK]), psb(k, bk), AF.Identity, bias=sm.r(sm.t[:, n, 4:5]))
            for d in range(2):
                for b in range(NBLK):
                    for g in range(2):
                        bk = P.bank()
                        P.mm(psb(k, bk), k.wr.r(gw[:, d * 2 + g, :], s2), xrb.r(xrb.t[:, b * BLK:(b + 1) * BLK]))
                        dstb = A if g == 0 else Ib
                        P.act(dstb.r(dstb.t[:, b * BLK:(b + 1) * BLK]), psb(k, bk), AF.Sigmoid, bias=sm.r(sm.t[:, n, 5 + d * 2 + g:6 + d * 2 + g]))
                P.act(A.r(), A.r(), AF.Exp, scale=sp.r(sp.t[:, n, d:d + 1]))
                P.tt("dve", T.r(), A.r(), A.r(), ALU.mult)
                P.act(T.r(), T.r(), AF.Sqrt, bias=one.r(), scale=-1.0)
                P.tt("dve", T.r(), T.r(), Ib.r(), ALU.mult)
                P.tt("dve", T.r(), T.r(), xrb.r(), ALU.mult)
                for si, (t0, n_t) in enumerate(SEGS):
                    if d == 0:
                        o_, a_, b_ = H[0].t[:, t0:t0 + n_t], A.t[:, t0:t0 + n_t], T.t[:, t0:t0 + n_t]
                    else:
                        lo = t0 - 1 if t0 > 0 else None
                        o_, a_, b_ = H[1].t[:, t0 + n_t - 1:lo:-1], A.t[:, t0 + n_t - 1:lo:-1], T.t[:, t0 + n_t - 1:lo:-1]
                    if si == 2:
                        init = s0.t[:, n, d:d + 1]
                        rd = [A.r(), T.r(), s0.r()]
                    else:
                        init = 0.0
                        rd = [A.r(), T.r()]
                    P.op("dve", lambda e, o_=o_, a_=a_, b_=b_, init=init: e.tensor_tensor_scan(out=o_, data0=a_, data1=b_, initial=init, op0=ALU.mult, op1=ALU.add),
                         reads=rd, writes=[H[d].r()])
                    if si < 2:
                        tl = t0 + n_t - 1 if d == 0 else t0
                        P.copy("act", sto.r(sto.t[:, n, si, d:d + 1]), H[d].r(H[d].t[:, tl:tl + 1]))
            P.tt("dve", H[0].r(), H[0].r(), H[1].r(), ALU.add)
            P.tt("dve", ob.r(ob.t[:, jj, :], jj), H[0].r(), gg.r(), ALU.mult)
        outproj_acc(k, k.dram("lru_w_out"), j, g0 * 128, len(tiles), lambda jj, b: ob.r(ob.t[:, jj, b * BLK:(b + 1) * BLK], jj))
    P.dma("sp", k.dram("lru_out"), sto.r())
    P.barrier()
    P.release(mk)


NCH = 20
SEG_CH = [(0, 2), (2, 2), (4, 16)]


def seg_of_chunk(c):
    return 0 if c < 2 else (1 if c < 4 else 2)


def hch(k, kk, c):
    b = c // 4
    return k.h.r(k.h.t[:, kk, c * 128:(c + 1) * 128], kk * 5 + b)


def ssd(k, j):
    P = k.P
    P.barrier()
    mk = P.mark()
    y_tm = P.buf("s_ytm", [128, NCH, 512], BF16, nsub=NCH)
    xs_tm = P.buf("s_xstm", [128, NCH, 256], BF16, nsub=NCH)
    Bfm = P.buf("s_bfm", [128, NT], BF16)
    Cfm = P.buf("s_cfm", [128, NT], BF16)
    wz = P.buf("s_wz", [128, 8, 256], BF16)
    wdt = P.buf("s_wdt", [128, 8, 64], BF16)
    cw = P.buf("s_cw", [128, 24, 5], F32)
    rows = P.buf("s_rows", [128, 160], F32)
    nrm = P.buf("s_nrm", [128, 512], BF16)
    masks = P.buf("s_masks", [128, 2, 128], F32)
    negb = P.buf("s_negb", [128, 2, 128], BF16)
    onesf = P.buf("s_onesf", [128, 128], F32)
    one1 = P.buf("s_one1", [128, 1], F32)
    eps1 = P.buf("s_eps1", [128, 1], F32)
    sc = {nm: P.buf("s_" + nm, [128, NCH, 8], F32) for nm in ["dt", "da", "nacum", "eac", "cdec", "ce"]}
    sc["tmp"] = sc["ce"]
    fm_tmp = Cfm
    ssq = P.buf("s_ssq", [128, NCH, 2], F32)
    rstd = P.buf("s_rstd", [128, NCH], F32)
    cbT = P.buf("s_cbT", [128, 2, 2, 128], BF16, nsub=2)
    dec = P.buf("s_dec", [128, 2, 4, 128], BF16, nsub=2)
    ST = P.buf("s_ST", [128, 2, 256], F32, nsub=2)
    STb = P.buf("s_STb", [128, 2, 256], BF16, nsub=2)
    t1 = P.buf("s_t1", [128, 2, 256], F32, nsub=2)
    t2 = P.buf("s_t2", [128, 2, 256], F32, nsub=2)
    sz = t2
    sig = P.buf("s_sig", [128, BLK], BF16)
    ncb = P.buf("s_ncb", [128, 24], F32)
    junk = sig
    sto = P.buf("s_sto", [128, 2, 128], F32, nsub=2)

    P.dma("sp", cw.r(), k.dram("ssd_cw"))
    P.dma("sp", rows.r(), k.dram("ssd_rows"))
    P.dma("sp", masks.r(), R(k.dram("cmask").ap[0:2].rearrange("a p m -> p a m"), k.dram("cmask").keys))
    P.dma("pool", negb.r(), R(k.dram("cmask").ap[2:4].rearrange("a p m -> p a m"), k.dram("cmask").keys))
    P.memset("dve", onesf.r(), 1.0)
    P.memset("dve", one1.r(), 1.0)
    P.ts("dve", ncb.r(), cw.r(cw.t[:, :, 4]), -1.0, ALU.mult)
    P.memset("dve", eps1.r(), 1e-5)
    P.act(rows.r(rows.t[:, 64:128]), rows.r(rows.t[:, 64:128]), AF.Exp)
    P.ts("dve", rows.r(rows.t[:, 64:128]), rows.r(rows.t[:, 64:128]), -1.0, ALU.mult)
    src = k.dram("ssd_w_in").ap[j][:, 5120:5184].rearrange("(kk p) c -> p kk c", p=128)
    P.dma("pool", wdt.r(), R(src, k.dram("ssd_w_in").keys))

    loc = {}

    def conv_tile(col0, cidx, dst_writer):
        xpad, dgl = loc["xpad"], loc["dgl"]
        s, w = load_in_w(k, k.dram("ssd_w_in"), j, col0)
        i0 = k.identf.t[:].unsqueeze(1).to_broadcast([128, 4, 128])
        i1 = cw.t[:, cidx, 0:4].unsqueeze(2).to_broadcast([128, 4, 128])
        P.tt("pool", dgl.r(), k.identf.r(i0), cw.r(i1), ALU.mult)
        for b in range(NBLK):
            bk = P.bank()
            for kk in range(8):
                P.mm(psb(k, bk), k.wr.r(w[:, kk, :], s), hs(k, kk, b), start=(kk == 0), stop=(kk == 7))
            dst, _, _ = conv_dst(xpad.t, b)
            P.copy("dve", xpad.r(dst), k.ps.r(psview(k, bk, b), bk))
        for b in range(NBLK):
            bk = P.bank()
            for kk in range(4):
                P.mm(k.ps.r(psview(k, bk, b), bk), dgl.r(dgl.t[:, kk, :]), xpad.r(conv_rhs(xpad.t, b, kk)), start=(kk == 0), stop=(kk == 3))
            P.act(sig.r(), psb(k, bk), AF.Exp, bias=ncb.r(ncb.t[:, cidx:cidx + 1]), scale=-1.0)
            P.act(sig.r(), sig.r(), AF.Ln, bias=one1.r())
            P.act(sig.r(), sig.r(), AF.Exp, scale=-1.0)
            P.stt(dst_writer(b), psb(k, bk), cw.r(cw.t[:, cidx, 4:5]), sig.r(), ALU.add, ALU.mult)

    stop = k.cfg.get("ssd_stop", 99)
    for hg in range(k.cfg.get("ssd_nhg", 8)):
        g, half = hg // 2, hg % 2
        src = k.dram("ssd_w_in").ap[j][:, hg * 256:(hg + 1) * 256].rearrange("(kk p) c -> p kk c", p=128)
        P.dma("pool", wz.r(), R(src, k.dram("ssd_w_in").keys))
        if half == 0:
            P.dma("pool", nrm.r(), R(k.dram("ssd_nrm").ap[:, g * 512:(g + 1) * 512], k.dram("ssd_nrm").keys))
        if stop <= 0.5:
            continue
        bk = P.bank()
        for c in range(NCH):
            for kk in range(8):
                rhs = wdt.t[:, kk, :].rearrange("p (d h) -> p d h", d=2)[:, :, hg * 4:hg * 4 + 4]
                out = k.ps.t[:, bk, c * 8:(c + 1) * 8].rearrange("p (d h) -> p d h", d=2)
                P.mm(k.ps.r(out, bk), hch(k, kk, c), wdt.r(rhs), start=(kk == 0), stop=(kk == 7))
        psv = k.ps.t[:, bk, 0:NCH * 8].rearrange("p (c d h) -> p c d h", d=2, h=4)
        if stop <= 0.7:
            continue

        def rowbc(off):
            return rows.t[:, off:off + 64].rearrange("p (d h) -> p d h", d=2)[:, :, hg * 4:hg * 4 + 4].unsqueeze(1).to_broadcast([128, NCH, 2, 4])

        def v4(bf):
            return bf.t.rearrange("p c (d h) -> p c d h", d=2)
        tmp, dt, da = sc["tmp"], sc["dt"], sc["da"]
        P.tt("dve", tmp.r(v4(tmp)), k.ps.r(psv, bk), rows.r(rowbc(0)), ALU.add)
        P.ts("dve", dt.r(), tmp.r(), -1.0, ALU.mult)
        P.tt("dve", dt.r(), dt.r(), tmp.r(), ALU.max)
        P.act(dt.r(), dt.r(), AF.Exp, scale=-1.0)
        P.act(dt.r(), dt.r(), AF.Ln, bias=one1.r())
        P.ts("dve", tmp.r(), tmp.r(), 0.0, ALU.max)
        P.tt("dve", dt.r(), dt.r(), tmp.r(), ALU.add)
        P.tt("dve", da.r(v4(da)), dt.r(v4(dt)), rows.r(rowbc(64)), ALU.mult)
        if stop <= 0.8:
            continue
        bk2 = P.bank()
        bk3 = P.bank()
        for c in range(NCH):
            for d in range(2):
                P.mm(k.ps.r(k.ps.t[:, bk2, c * 8 + d * 4:c * 8 + d * 4 + 4], bk2), masks.r(masks.t[:, d, :]), da.r(da.t[:, c, d * 4:d * 4 + 4]))
            P.mm(k.ps.r(k.ps.t[:, bk3, c * 8:(c + 1) * 8], bk3), onesf.r(), da.r(da.t[:, c, :]))
        nacum, eac, cdec, ce = sc["nacum"], sc["eac"], sc["cdec"], sc["ce"]
        ps2 = k.ps.t[:, bk2, 0:NCH * 8].rearrange("p (c e) -> p c e", e=8)
        ps3 = k.ps.t[:, bk3, 0:NCH * 8].rearrange("p (c e) -> p c e", e=8)
        if stop <= 0.9:
            continue
        P.ts("dve", nacum.r(), k.ps.r(ps2, bk2), -1.0, ALU.mult)
        P.act(eac.r(), nacum.r(), AF.Exp, scale=-1.0)
        if stop <= 0.95:
            continue
        P.copy("dve", cdec.r(), k.ps.r(ps3, bk3))
        P.tt("dve", ce.r(), cdec.r(), nacum.r(), ALU.add)
        if stop <= 0.96:
            continue
        P.act(cdec.r(), cdec.r(), AF.Exp)
        if stop <= 0.97:
            continue
        P.act(ce.r(), ce.r(), AF.Exp)
        P.tt("dve", ce.r(), ce.r(), dt.r(), ALU.mult)
        if stop <= 1:
            continue
        P.barrier()
        mkp = P.mark()
        loc["xpad"] = P.buf("s_xpad", [128, CPW + 1], BF16)
        loc["dgl"] = P.buf("s_dg", [128, 4, 128], BF16)
        P.memset("pool", loc["xpad"].r(), 0.0)
        for ti in range(2):
            conv_tile(2048 + hg * 256 + ti * 128, hg * 2 + ti, lambda b: fm_tmp.r(fm_tmp.t[:, b * BLK:(b + 1) * BLK]))
            for c0 in range(0, NCH, 8):
                bk = P.bank()
                pb = k.ps.t[:, bk, :].bitcast(BF16)
                n = min(8, NCH - c0)
                for ci in range(n):
                    c = c0 + ci
                    P.tr(k.ps.r(pb[:, ci * 128:(ci + 1) * 128], bk), fm_tmp.r(fm_tmp.t[:, c * 128:(c + 1) * 128]), k.identb.r())
                dst = xs_tm.t[:, c0:c0 + n, ti * 128:(ti + 1) * 128]
                srcv = pb[:, 0:n * 128].rearrange("p (c m) -> p c m", m=128)
                P.copy("act", xs_tm.r(dst, range(c0, c0 + n)), k.ps.r(srcv, bk))
        conv_tile(2048 + 2048 + g * 128, 16 + g, lambda b: Bfm.r(Bfm.t[:, b * BLK:(b + 1) * BLK]))
        conv_tile(2048 + 2560 + g * 128, 20 + g, lambda b: Cfm.r(Cfm.t[:, b * BLK:(b + 1) * BLK]))
        if stop <= 2:
            continue
        P.barrier()
        P.release(mkp)
        mku = P.mark()
        Btm = P.buf("s_btm", [128, 4, 128], BF16, nsub=4)
        Mb = P.buf("s_M", [128, 4, 4, 128], BF16, nsub=4)
        xd = P.buf("s_xd", [128, 4, 256], BF16, nsub=4)
        xdt = P.buf("s_xdt", [128, 4, 256], BF16, nsub=4)
        nsteps = k.cfg.get("ssd_steps", NCH)

        def prelim_gen(step):
            banks = {}
            units = [(d, (step if d == 0 else NCH - 1 - step), (step % 2) * 2 + d) for d in range(2)]
            for (d, c, sl) in units:
                tok = slice(c * 128, (c + 1) * 128)
                bk = d * 2
                banks[("cb", d)] = bk
                P.mm(k.ps.r(k.ps.t[:, bk, 0:128], bk), Bfm.r(Bfm.t[:, tok]), Cfm.r(Cfm.t[:, tok]))
                bkt = d * 2
                banks[("bt", d)] = bkt
                pbb = k.ps.t[:, bkt, :].bitcast(BF16)
                P.tr(k.ps.r(pbb[:, 512:640], bkt), Bfm.r(Bfm.t[:, tok]), k.identb.r())
                yield
            for (d, c, sl) in units:
                bk = banks[("cb", d)]
                bkt = banks[("bt", d)]
                pbb = k.ps.t[:, bkt, :].bitcast(BF16)
                P.tt("dve", cbT.r(cbT.t[:, d, d], d), k.ps.r(k.ps.t[:, bk, 0:128], bk), masks.r(masks.t[:, d, :]), ALU.mult)
                P.copy("act", Btm.r(Btm.t[:, sl], sl), k.ps.r(pbb[:, 512:640], bkt))
                xsv = xs_tm.t[:, c, :].rearrange("p (q e) -> p q e", e=64)
                dtb = dt.t[:, c, d * 4:d * 4 + 4].unsqueeze(2).to_broadcast([128, 4, 64])
                ceb = ce.t[:, c, d * 4:d * 4 + 4].unsqueeze(2).to_broadcast([128, 4, 64])
                P.tt("pool", xd.r(xd.t[:, sl].rearrange("p (q e) -> p q e", e=64), sl), xs_tm.r(xsv, c), dt.r(dtb), ALU.mult)
                P.tt("pool", xdt.r(xdt.t[:, sl].rearrange("p (q e) -> p q e", e=64), sl), xs_tm.r(xsv, c), ce.r(ceb), ALU.mult)
                yield
            for (d, c, sl) in units:
                bk = d * 2 + 1
                banks[("R", d)] = bk
                for q in range(4):
                    col = d * 4 + q
                    lhs = da.t[:, c, col:col + 1].to_broadcast([128, 128])
                    o_ = k.ps.r(k.ps.t[:, bk, q * 128:(q + 1) * 128], bk)
                    P.mm(o_, da.r(lhs), masks.r(masks.t[:, d, :]), start=True, stop=False)
                    P.mm(o_, k.identb.r(), negb.r(negb.t[:, d, :]), start=False, stop=True)
                yield
            for (d, c, sl) in units:
                bk = banks[("R", d)]
                for q in range(4):
                    col = d * 4 + q
                    P.act(dec.r(dec.t[:, d, q, :], d), k.ps.r(k.ps.t[:, bk, q * 128:(q + 1) * 128], bk), AF.Exp, bias=nacum.r(nacum.t[:, c, col:col + 1]))
                    yield
            for (d, c, sl) in units:
                cb_b = cbT.t[:, d, d].unsqueeze(1).to_broadcast([128, 4, 128])
                P.tt("dve", Mb.r(Mb.t[:, sl], sl), dec.r(dec.t[:, d], d), cbT.r(cb_b, d), ALU.mult)
                yield

        cbs = [0, 0]

        def chain_gen(step):
            gens = [unit_chain(step, d_) for d_ in range(2)]
            alive = [True, True]
            while alive[0] or alive[1]:
                for d_ in range(2):
                    if alive[d_]:
                        try:
                            next(gens[d_])
                        except StopIteration:
                            alive[d_] = False
                yield

        def unit_chain(step, d):
            def cbank():
                cbs[d] ^= 1
                return 4 + 2 * d + cbs[d]
            if True:
                c = step if d == 0 else NCH - 1 - step
                sl = (step % 2) * 2 + d
                first_visit = (c <= 9) if d == 0 else (c >= 10)
                if nsteps < NCH:
                    first_visit = True
                seg = seg_of_chunk(c)
                c_first, c_n = SEG_CH[seg]
                seg_start = (c == c_first) if d == 0 else (c == c_first + c_n - 1)
                seg_end = (c == c_first + c_n - 1) if d == 0 else (c == c_first)
                tok = slice(c * 128, (c + 1) * 128)
                if seg_start:
                    if seg == 2:
                        P.dma("sp", ST.r(ST.t[:, d], d), R(k.dram("ssd_s0").ap[d, hg], k.dram("ssd_s0").keys))
                    else:
                        P.memset("dve", ST.r(ST.t[:, d], d), 0.0)
                    P.copy("act", STb.r(STb.t[:, d], d), ST.r(ST.t[:, d], d))
                    yield
                xsv = xs_tm.t[:, c, :].rearrange("p (q e) -> p q e", e=64)
                bkY = cbank()
                for q in range(4):
                    P.mm(k.ps.r(k.ps.t[:, bkY, q * 64:(q + 1) * 64], bkY), Mb.r(Mb.t[:, sl, q, :], sl), xd.r(xd.t[:, sl, q * 64:(q + 1) * 64], sl))
                P.mm(k.ps.r(k.ps.t[:, bkY, 256:512], bkY), Cfm.r(Cfm.t[:, tok]), STb.r(STb.t[:, d], d))
                bkS = cbank()
                P.mm(k.ps.r(k.ps.t[:, bkS, 0:256], bkS), Btm.r(Btm.t[:, sl], sl), xdt.r(xdt.t[:, sl], sl))
                yield
                cdb = cdec.t[:, c, d * 4:d * 4 + 4].unsqueeze(2).to_broadcast([128, 4, 64])
                STv = ST.t[:, d].rearrange("p (q e) -> p q e", e=64)
                P.tt("dve", ST.r(STv, d), ST.r(STv, d), cdec.r(cdb), ALU.mult)
                P.tt("dve", ST.r(ST.t[:, d], d), ST.r(ST.t[:, d], d), k.ps.r(k.ps.t[:, bkS, 0:256], bkS), ALU.add)
                yield
                P.copy("act", STb.r(STb.t[:, d], d), ST.r(ST.t[:, d], d))
                yield
                eab = eac.t[:, c, d * 4:d * 4 + 4].unsqueeze(2).to_broadcast([128, 4, 64])
                t1v = t1.t[:, d].rearrange("p (q e) -> p q e", e=64)
                P.tt("dve", t1.r(t1v, d), k.ps.r(k.ps.t[:, bkY, 256:512].rearrange("p (q e) -> p q e", e=64), bkY), eac.r(eab), ALU.mult)
                P.tt("dve", t1.r(t1.t[:, d], d), t1.r(t1.t[:, d], d), k.ps.r(k.ps.t[:, bkY, 0:256], bkY), ALU.add)
                yield
                ysl = y_tm.r(y_tm.t[:, c, half * 256:(half + 1) * 256], c)
                if first_visit:
                    P.copy("act", ysl, t1.r(t1.t[:, d], d))
                    yield
                else:
                    P.tt("dve", t1.r(t1.t[:, d], d), t1.r(t1.t[:, d], d), ysl, ALU.add)
                    dsb = rows.t[:, 128 + hg * 4:128 + hg * 4 + 4].unsqueeze(2).to_broadcast([128, 4, 64])
                    P.tt("pool", t2.r(t2.t[:, d].rearrange("p (q e) -> p q e", e=64), d), xs_tm.r(xsv, c), rows.r(dsb), ALU.mult)
                    yield
                    P.tt("dve", t1.r(t1.t[:, d], d), t1.r(t1.t[:, d], d), t2.r(t2.t[:, d], d), ALU.add)
                    bkZ = cbank()
                    for kk in range(8):
                        P.mm(k.ps.r(k.ps.t[:, bkZ, 0:256], bkZ), hch(k, kk, c), wz.r(wz.t[:, kk, :]), start=(kk == 0), stop=(kk == 7))
                    yield
                    P.act(t2.r(t2.t[:, d], d), k.ps.r(k.ps.t[:, bkZ, 0:256], bkZ), AF.Exp, scale=-1.0)
                    P.act(t2.r(t2.t[:, d], d), t2.r(t2.t[:, d], d), AF.Ln, bias=one1.r())
                    P.act(t2.r(t2.t[:, d], d), t2.r(t2.t[:, d], d), AF.Exp, scale=-1.0)
                    yield
                    P.tt("dve", t1.r(t1.t[:, d], d), t1.r(t1.t[:, d], d), t2.r(t2.t[:, d], d), ALU.mult)
                    P.tt("dve", t1.r(t1.t[:, d], d), t1.r(t1.t[:, d], d), k.ps.r(k.ps.t[:, bkZ, 0:256], bkZ), ALU.mult)
                    yield
                    P.copy("dve", ysl, t1.r(t1.t[:, d], d))
                    o_, i_, a_ = junk.t[:, 0:256], t1.t[:, d], ssq.t[:, c, half:half + 1]
                    P.op("act", lambda e, o_=o_, i_=i_, a_=a_: e.activation(out=o_, in_=i_, func=AF.Square, accum_out=a_),
                         reads=[t1.r(t1.t[:, d], d)], writes=[junk.r(), ssq.r()])
                    yield
                if seg_end and seg < 2:
                    for pr in range(2):
                        bk = cbank()
                        P.tr(k.ps.r(k.ps.t[:, bk, 0:128], bk), ST.r(ST.t[:, d, pr * 128:(pr + 1) * 128], d), k.identf.r())
                        P.copy("dve", sto.r(sto.t[:, pr], pr), k.ps.r(k.ps.t[:, bk, 0:128], bk))
                        r0 = (hg * 4 + pr * 2) * 64
                        P.dma("sp", R(k.dram("ssd_out").ap[seg, d, r0:r0 + 128, :], k.dram("ssd_out").keys), sto.r(sto.t[:, pr], pr))
                    yield

        def run_all(gen):
            for _ in gen:
                pass

        def merge(pg, cg, ratio):
            pdone = cdone = False
            while not (pdone and cdone):
                if not cdone:
                    try:
                        next(cg)
                    except StopIteration:
                        cdone = True
                for _ in range(ratio):
                    if pdone:
                        break
                    try:
                        next(pg)
                    except StopIteration:
                        pdone = True

        run_all(prelim_gen(0))
        for step in range(nsteps):
            if step + 1 < nsteps:
                merge(prelim_gen(step + 1), chain_gen(step), k.cfg.get("ssd_ratio", 1))
            else:
                run_all(chain_gen(step))
        P.barrier()
        P.release(mku)
        if half == 1 and stop > 3:
            P.tt("dve", rstd.r(), ssq.r(ssq.t[:, :, 0]), ssq.r(ssq.t[:, :, 1]), ALU.add)
            P.act(rstd.r(), rstd.r(), AF.Ln, bias=eps1.r(), scale=1.0 / 512)
            P.act(rstd.r(), rstd.r(), AF.Exp, scale=-0.5)
            slots = [load_out_w(k, k.dram("ssd_w_out"), j, g * 512 + ti * 128) for ti in range(4)]
            ofm_t = xs_tm.t[:, 0:8, :].rearrange("p c e -> p (c e)").rearrange("p (t m) -> p t m", m=512)
            yn_t = xs_tm.t[:, 8:12, :].rearrange("p c e -> p (c e)").rearrange("p (t m) -> p t m", m=512)
            OK_ = list(range(0, 8))
            YK_ = list(range(8, 12))
            for b in range(NBLK):
                for ci in range(4):
                    c = b * 4 + ci
                    yb = ci % 2
                    P.stt(xs_tm.r(yn_t[:, yb], YK_), y_tm.r(y_tm.t[:, c, :], c), rstd.r(rstd.t[:, c:c + 1]), nrm.r(), ALU.mult, ALU.mult)
                    bk = P.bank()
                    pbb = k.ps.t[:, bk, :].bitcast(BF16)
                    for ti in range(4):
                        P.tr(k.ps.r(pbb[:, ti * 128:(ti + 1) * 128], bk), xs_tm.r(yn_t[:, yb, ti * 128:(ti + 1) * 128], YK_), k.identb.r())
                    srcv = pbb[:, 0:512].rearrange("p (t m) -> p t m", m=128)
                    P.copy("act", xs_tm.r(ofm_t[:, :, ci * 128:(ci + 1) * 128], OK_), k.ps.r(srcv, bk))
                for m in range(8):
                    bk = P.bank()
                    for ti in range(4):
                        s_, wo = slots[ti]
                        P.mm(psb(k, bk), k.wr.r(wo[:, m * 128:(m + 1) * 128], s_), xs_tm.r(ofm_t[:, ti, :], OK_), start=(ti == 0), stop=(ti == 3))
                    cc = bc(b)
                    P.stt(xs(k, m, b), psb(k, bk), k.mod.r(k.mod.t[:, Q_G1 + m, cc:cc + 1]), xs(k, m, b), ALU.mult, ALU.add)
    P.barrier()
    P.release(mk)


def gdn(k, j):
    P = k.P
    P.barrier()
    mk = P.mark()
    W_in = k.dram("gdn_w_in%d" % j)
    W_out = k.dram("gdn_w_out%d" % j)
    kq = P.buf("g_kq", [128, NCH, 2, 128], BF16, nsub=NCH)
    v_fm = P.buf("g_vfm", [128, NT], BF16, nsub=NCH)
    v_tm = P.buf("g_vtm", [128, NCH, 128], BF16, nsub=NCH)
    k_tm = P.buf("g_ktm", [128, NCH, 128], BF16, nsub=NCH)
    o_tm = P.buf("g_otm", [128, NCH, 128], BF16, nsub=NCH)
    wz = P.buf("g_wz", [128, 8, 128], BF16)
    wsm = P.buf("g_wsm", [128, 8, 32], BF16)
    cw = P.buf("g_cw", [128, 24, 4], F32)
    rows = P.buf("g_rows", [128, 160], F32)
    masks = P.buf("g_masks", [128, 2, 128], F32)
    negb = P.buf("g_negb", [128, 2, 128], BF16)
    lvl = P.buf("g_lvl", [128, 2, 7, 2, 128], BF16)
    I2 = P.buf("g_I2", [128, 2, 128], BF16)
    onesf = P.buf("g_onesf", [128, 128], F32)
    c_one = P.buf("g_c1", [128, 1], F32)
    c_eps6 = P.buf("g_c2", [128, 1], F32)
    c_eps6q = P.buf("g_c3", [128, 1], F32)
    c_eps5 = P.buf("g_c4", [128, 1], F32)
    sc = {nm: P.buf("g_" + nm, [128, NCH, 2], F32) for nm in ["beta", "g", "ngc", "negegc", "eout", "egl", "tmp"]}
    Rp = P.buf("g_Rp", [128, 2, 128], BF16, nsub=2)
    vn = P.buf("g_vn", [128, 2, 128], BF16, nsub=2)
    S = P.buf("g_S", [128, 2, 128], F32, nsub=2)
    Sb = P.buf("g_Sb", [128, 2, 128], BF16, nsub=2)
    ot2 = P.buf("g_ot", [128, 2, 128], F32, nsub=2)
    sz2 = P.buf("g_sz", [128, 2, 128], F32, nsub=2)
    ogb2 = P.buf("g_og", [128, 2, 128], BF16, nsub=2)
    junk2 = P.buf("g_junk", [128, 2, 128], BF16, nsub=2)
    ssq2 = P.buf("g_ssq", [128, 2, 2], F32, nsub=2)

    P.dma("sp", cw.r(), R(k.dram("gdn_cw").ap[j], k.dram("gdn_cw").keys))
    P.dma("sp", rows.r(), R(k.dram("gdn_rows").ap[j], k.dram("gdn_rows").keys))
    P.dma("sp", masks.r(), R(k.dram("cmask").ap[0:2].rearrange("a p m -> p a m"), k.dram("cmask").keys))
    P.dma("pool", negb.r(), R(k.dram("cmask").ap[2:4].rearrange("a p m -> p a m"), k.dram("cmask").keys))
    for d in range(2):
        P.dma("pool", lvl.r(lvl.t[:, d]), R(k.dram("glvl").ap[d].rearrange("l a p m -> p l a m"), k.dram("glvl").keys))
    for a in range(2):
        P.copy("dve", I2.r(I2.t[:, a, :]), k.identf.r())
    P.memset("dve", onesf.r(), 1.0)
    P.memset("dve", c_one.r(), 1.0)
    P.memset("dve", c_eps6.r(), 1e-6)
    P.memset("dve", c_eps6q.r(), 128e-6)
    P.memset("dve", c_eps5.r(), 1e-5)
    P.act(rows.r(rows.t[:, 16:32]), rows.r(rows.t[:, 16:32]), AF.Exp)
    P.ts("dve", rows.r(rows.t[:, 16:32]), rows.r(rows.t[:, 16:32]), -1.0, ALU.mult)
    src = W_in.ap[0][:, 4096:4128].rearrange("(kk p) c -> p kk c", p=128)
    P.dma("pool", wsm.r(), R(src, W_in.keys))

    loc = {}

    def conv_tile(col0, cidx, post):
        xpad, dgl = loc["xpad"], loc["dgl"]
        s, w = load_in_w(k, W_in, 0, col0)
        i0 = k.identf.t[:].unsqueeze(1).to_broadcast([128, 4, 128])
        i1 = cw.t[:, cidx, 0:4].unsqueeze(2).to_broadcast([128, 4, 128])
        P.tt("pool", dgl.r(), k.identf.r(i0), cw.r(i1), ALU.mult)
        for b in range(NBLK):
            bk = P.bank()
            for kk in range(8):
                P.mm(psb(k, bk), k.wr.r(w[:, kk, :], s), hs(k, kk, b), start=(kk == 0), stop=(kk == 7))
            dst, _, _ = conv_dst(xpad.t, b)
            P.copy("dve", xpad.r(dst), k.ps.r(psview(k, bk, b), bk))
        for b in range(NBLK):
            bk = P.bank()
            for kk in range(4):
                P.mm(k.ps.r(psview(k, bk, b), bk), dgl.r(dgl.t[:, kk, :]), xpad.r(conv_rhs(xpad.t, b, kk)), start=(kk == 0), stop=(kk == 3))
            post(b, bk)

    def norm_post(which, scale, epsb):
        def post(b, bk):
            qf, sqb, rn = loc["qf"], loc["sqb"], loc["rn"]
            P.act(qf.r(), psb(k, bk), AF.Exp, scale=-1.0)
            P.act(qf.r(), qf.r(), AF.Ln, bias=c_one.r())
            P.act(qf.r(), qf.r(), AF.Exp, scale=-1.0)
            P.tt("dve", qf.r(), psb(k, bk), qf.r(), ALU.mult)
            P.act(sqb.r(), qf.r(), AF.Square)
            b2 = P.bank()
            P.mm(psb(k, b2), k.onesb.r(), sqb.r())
            P.act(rn.r(), psb(k, b2), AF.Ln, bias=epsb.r(), scale=scale)
            P.act(rn.r(), rn.r(), AF.Exp, scale=-0.5)
            dst = kq.t[:, 4 * b:4 * b + 4, which, :]
            P.tt("dve", kq.r(dst, range(4 * b, 4 * b + 4)), qf.r(qf.t.rearrange("p (c m) -> p c m", m=128)), rn.r(rn.t.rearrange("p (c m) -> p c m", m=128)), ALU.mult)
        return post

    def v_post(b, bk):
        qf = loc["qf"]
        P.act(qf.r(), psb(k, bk), AF.Exp, scale=-1.0)
        P.act(qf.r(), qf.r(), AF.Ln, bias=c_one.r())
        P.act(qf.r(), qf.r(), AF.Exp, scale=-1.0)
        P.tt("dve", v_fm.r(v_fm.t[:, b * BLK:(b + 1) * BLK], range(4 * b, 4 * b + 4)), psb(k, bk), qf.r(), ALU.mult)

    nheads = k.cfg.get("gdn_heads", 8)
    for hh in range(nheads):
        src = W_in.ap[0][:, 3072 + hh * 128:3072 + (hh + 1) * 128].rearrange("(kk p) c -> p kk c", p=128)
        P.dma("pool", wz.r(), R(src, W_in.keys))
        bk = P.bank()
        for c in range(NCH):
            for kk in range(8):
                rhs = wsm.t[:, kk, :].rearrange("p (t d h) -> p t d h", t=2, d=2)[:, :, :, hh]
                out = k.ps.t[:, bk, c * 4:(c + 1) * 4].rearrange("p (t d) -> p t d", t=2)
                P.mm(k.ps.r(out, bk), hch(k, kk, c), wsm.r(rhs), start=(kk == 0), stop=(kk == 7))
        psv = k.ps.t[:, bk, 0:NCH * 4].rearrange("p (c t d) -> p c t d", t=2, d=2)
        beta, g, ngc, negegc, eout, egl, tmp = sc["beta"], sc["g"], sc["ngc"], sc["negegc"], sc["eout"], sc["egl"], sc["tmp"]
        P.copy("dve", tmp.r(), k.ps.r(psv[:, :, 0, :], bk))
        P.act(beta.r(), tmp.r(), AF.Exp, scale=-1.0)
        P.act(beta.r(), beta.r(), AF.Ln, bias=c_one.r())
        P.act(beta.r(), beta.r(), AF.Exp, scale=-1.0)

        def rowbc(off):
            return rows.t[:, off:off + 16].rearrange("p (d h) -> p d h", d=2)[:, :, hh].unsqueeze(1).to_broadcast([128, NCH, 2])
        P.tt("dve", tmp.r(), k.ps.r(psv[:, :, 1, :], bk), rows.r(rowbc(0)), ALU.add)
        P.ts("dve", g.r(), tmp.r(), -1.0, ALU.mult)
        P.tt("dve", g.r(), g.r(), tmp.r(), ALU.max)
        P.act(g.r(), g.r(), AF.Exp, scale=-1.0)
        P.act(g.r(), g.r(), AF.Ln, bias=c_one.r())
        P.ts("dve", tmp.r(), tmp.r(), 0.0, ALU.max)
        P.tt("dve", g.r(), g.r(), tmp.r(), ALU.add)
        P.tt("dve", g.r(), g.r(), rows.r(rowbc(16)), ALU.mult)
        bk2 = P.bank()
        bk3 = P.bank()
        for c in range(NCH):
            for d in range(2):
                P.mm(k.ps.r(k.ps.t[:, bk2, c * 2 + d:c * 2 + d + 1], bk2), masks.r(masks.t[:, d, :]), g.r(g.t[:, c, d:d + 1]))
            P.mm(k.ps.r(k.ps.t[:, bk3, c * 2:c * 2 + 2], bk3), onesf.r(), g.r(g.t[:, c, :]))
        ps2 = k.ps.t[:, bk2, 0:NCH * 2].rearrange("p (c d) -> p c d", d=2)
        ps3 = k.ps.t[:, bk3, 0:NCH * 2].rearrange("p (c d) -> p c d", d=2)
        P.ts("dve", ngc.r(), k.ps.r(ps2, bk2), -1.0, ALU.mult)
        P.act(negegc.r(), ngc.r(), AF.Exp, scale=-1.0)
        P.ts("dve", negegc.r(), negegc.r(), -1.0, ALU.mult)
        P.copy("dve", egl.r(), k.ps.r(ps3, bk3))
        P.tt("dve", eout.r(), egl.r(), ngc.r(), ALU.add)
        P.act(eout.r(), eout.r(), AF.Exp)
        P.act(egl.r(), egl.r(), AF.Exp)
        P.barrier()
        mkp = P.mark()
        xpad = P.buf("g_xpad", [128, CPW + 1], BF16)
        dgl = P.buf("g_dg", [128, 4, 128], BF16)
        qf = P.buf("g_qf", [128, BLK], F32)
        sqb = P.buf("g_sq", [128, BLK], BF16)
        rn = P.buf("g_rn", [128, BLK], F32)
        loc.update(xpad=xpad, dgl=dgl, qf=qf, sqb=sqb, rn=rn)
        P.memset("pool", xpad.r(), 0.0)
        conv_tile(hh * 128, hh, norm_post(1, 128.0, c_eps6q))
        conv_tile(1024 + hh * 128, 8 + hh, norm_post(0, 1.0, c_eps6))
        conv_tile(2048 + hh * 128, 16 + hh, v_post)
        for (srcfn, dstb) in ((lambda c: v_fm.r(v_fm.t[:, c * 128:(c + 1) * 128], c), v_tm), (lambda c: kq.r(kq.t[:, c, 0, :], c), k_tm)):
            for c0 in range(0, NCH, 8):
                bk = P.bank()
                pb = k.ps.t[:, bk, :].bitcast(BF16)
                n = min(8, NCH - c0)
                for ci in range(n):
                    P.tr(k.ps.r(pb[:, ci * 128:(ci + 1) * 128], bk), srcfn(c0 + ci), k.identb.r())
                srcv = pb[:, 0:n * 128].rearrange("p (c m) -> p c m", m=128)
                P.copy("act", dstb.r(dstb.t[:, c0:c0 + n, :], range(c0, c0 + n)), k.ps.r(srcv, bk))
        P.barrier()
        P.release(mkp)
        mku = P.mark()
        GS = 2
        G = 2 * GS
        NR = 2 * G
        U = {}
        for nm in ["egcr", "dec", "NB", "NA"]:
            U[nm] = P.buf("gu_" + nm, [128, G, 128], BF16, nsub=G)
        U["Y"] = P.buf("gu_Y", [128, G, 2, 128], BF16, nsub=G)
        for nm in ["attn", "qin", "kout"]:
            U[nm] = P.buf("gu_" + nm, [128, NR, 128], BF16, nsub=NR)
        U["T"] = P.buf("gu_T", [128, NR, 2, 128], BF16, nsub=NR)
        RES = ("attn", "qin", "kout", "T")
        nsteps = k.cfg.get("gdn_steps", NCH)
        groups = [list(range(g0, min(g0 + GS, nsteps))) for g0 in range(0, nsteps, GS)]

        def ur(nm, ui, gi, sub=None):
            b_ = U[nm]
            si = (gi % 2) * G + ui if nm in RES else ui
            return b_.r(b_.t[:, si] if sub is None else b_.t[:, si, sub], si)

        def prelim_gen(gi):
            units = [(step, d) for step in groups[gi] for d in range(2)]
            cs = [(st_ if d_ == 0 else NCH - 1 - st_) for (st_, d_) in units]
            banks = {}
            for ui, (st_, d) in enumerate(units):
                c = cs[ui]
                bkR = ui
                banks[("R", ui)] = bkR
                gbc = g.t[:, c, d:d + 1].to_broadcast([128, 128])
                r0 = k.ps.r(k.ps.t[:, bkR, 0:128], bkR)
                r1 = k.ps.r(k.ps.t[:, bkR, 128:256], bkR)
                P.mm(r0, g.r(gbc), masks.r(masks.t[:, d, :]))
                P.mm(r1, g.r(gbc), masks.r(masks.t[:, d, :]), start=True, stop=False)
                P.mm(r1, k.identb.r(), negb.r(negb.t[:, d, :]), start=False, stop=True)
                yield
            for ui, (st_, d) in enumerate(units):
                c = cs[ui]
                bkR = banks[("R", ui)]
                P.act(ur("egcr", ui, gi), k.ps.r(k.ps.t[:, bkR, 0:128], bkR), AF.Exp)
                P.act(ur("dec", ui, gi), k.ps.r(k.ps.t[:, bkR, 128:256], bkR), AF.Exp, bias=ngc.r(ngc.t[:, c, d:d + 1]))
                yield
            for ui, (st_, d) in enumerate(units):
                c = cs[ui]
                bkK = ui
                banks[("K", ui)] = bkK
                P.mm(k.ps.r(k.ps.t[:, bkK, 0:256], bkK), kq.r(kq.t[:, c, 0, :], c), kq.r(kq.t[:, c, :, :].rearrange("p a m -> p (a m)"), c))
                yield
            for ui, (st_, d) in enumerate(units):
                c = cs[ui]
                bkK = banks[("K", ui)]
                P.stt(ur("NB", ui, gi), k.ps.r(k.ps.t[:, bkK, 0:128], bkK), beta.r(beta.t[:, c, d:d + 1]), ur("dec", ui, gi), ALU.mult, ALU.mult)
                P.tt("dve", ur("attn", ui, gi), k.ps.r(k.ps.t[:, bkK, 128:256], bkK), ur("dec", ui, gi), ALU.mult)
                yield
            for ui, (st_, d) in enumerate(units):
                bkT = ui
                banks[("T", ui)] = bkT
                pbT = k.ps.t[:, bkT, :].bitcast(BF16)
                P.tr(k.ps.r(pbT[:, 0:128], bkT), ur("NB", ui, gi), k.identb.r())
                yield
            for ui, (st_, d) in enumerate(units):
                c = cs[ui]
                bkT = banks[("T", ui)]
                pbT = k.ps.t[:, bkT, :].bitcast(BF16)
                P.copy("act", ur("NA", ui, gi), k.ps.r(pbT[:, 0:128], bkT))
                P.tt("pool", ur("qin", ui, gi), kq.r(kq.t[:, c, 1, :], c), ur("egcr", ui, gi), ALU.mult)
                P.act(ur("kout", ui, gi), k_tm.r(k_tm.t[:, c, :], c), AF.Identity, scale=eout.r(eout.t[:, c, d:d + 1]))
                yield
            for ui, (st_, d) in enumerate(units):
                P.tt("pool", ur("Y", ui, gi, 0), ur("NA", ui, gi), lvl.r(lvl.t[:, d, 0, 0]), ALU.mult)
                P.tt("pool", ur("Y", ui, gi, 1), ur("NB", ui, gi), lvl.r(lvl.t[:, d, 0, 1]), ALU.mult)
                P.tt("pool", ur("T", ui, gi), I2.r(), ur("Y", ui, gi), ALU.add)
                yield
            for lv in range(1, 7):
                last = (lv == 6)
                for ui, (st_, d) in enumerate(units):
                    bkY = ui
                    banks[("Y", ui)] = bkY
                    if not last:
                        P.mm(k.ps.r(k.ps.t[:, bkY, 0:128], bkY), ur("NB", ui, gi), ur("T", ui, gi, 0))
                    P.mm(k.ps.r(k.ps.t[:, bkY, 128:256], bkY), ur("NA", ui, gi), ur("T", ui, gi, 1))
                    yield
                for ui, (st_, d) in enumerate(units):
                    bkY = banks[("Y", ui)]
                    if not last:
                        P.tt("dve", ur("Y", ui, gi), k.ps.r(k.ps.t[:, bkY, 0:256].rearrange("p (a m) -> p a m", m=128), bkY), lvl.r(lvl.t[:, d, lv]), ALU.mult)
                    else:
                        P.tt("dve", ur("Y", ui, gi, 1), k.ps.r(k.ps.t[:, bkY, 128:256], bkY), lvl.r(lvl.t[:, d, lv, 1]), ALU.mult)
                    yield
                for ui, (st_, d) in enumerate(units):
                    bkZ = ui
                    banks[("Z", ui)] = bkZ
                    if not last:
                        P.mm(k.ps.r(k.ps.t[:, bkZ, 0:128], bkZ), ur("T", ui, gi, 1), ur("Y", ui, gi, 0))
                    P.mm(k.ps.r(k.ps.t[:, bkZ, 128:256], bkZ), ur("T", ui, gi, 0), ur("Y", ui, gi, 1))
                    yield
                for ui, (st_, d) in enumerate(units):
                    bkZ = banks[("Z", ui)]
                    if not last:
                        P.tt("dve", ur("T", ui, gi), ur("T", ui, gi), k.ps.r(k.ps.t[:, bkZ, 0:256].rearrange("p (a m) -> p a m", m=128), bkZ), ALU.add)
                    else:
                        P.tt("dve", ur("T", ui, gi, 1), ur("T", ui, gi, 1), k.ps.r(k.ps.t[:, bkZ, 128:256], bkZ), ALU.add)
                    yield

        cbs = [0, 0]

        def chain_gen(gi):
            units = [(step, d) for step in groups[gi] for d in range(2)]
            for s_i in range(len(groups[gi])):
                gens = [unit_chain(gi, 2 * s_i + d_, units[2 * s_i + d_][0], d_) for d_ in range(2)]
                alive = [True, True]
                while alive[0] or alive[1]:
                    for d_ in range(2):
                        if alive[d_]:
                            try:
                                next(gens[d_])
                            except StopIteration:
                                alive[d_] = False
                    yield

        def unit_chain(gi, ui, step, d):
            def cbank():
                cbs[d] ^= 1
                return 4 + 2 * d + cbs[d]
            class _V:
                def __init__(self, buf):
                    self.buf = buf
                    self.t = buf.t[:, d]

                def r(self, ap=None):
                    return self.buf.r(self.t if ap is None else ap, d)
            ot, sz, ogb, junk, ssq = _V(ot2), _V(sz2), _V(ogb2), _V(junk2), _V(ssq2)
            if True:
                c = step if d == 0 else NCH - 1 - step
                first_visit = (c <= 9) if d == 0 else (c >= 10)
                if nsteps < NCH:
                    first_visit = True
                seg = seg_of_chunk(c)
                c_first, c_n = SEG_CH[seg]
                seg_start = (c == c_first) if d == 0 else (c == c_first + c_n - 1)
                seg_end = (c == c_first + c_n - 1) if d == 0 else (c == c_first)
                if seg_start:
                    if seg == 2:
                        P.dma("sp", S.r(S.t[:, d], d), R(k.dram("gdn_s0").ap[j, d, hh], k.dram("gdn_s0").keys))
                    else:
                        P.memset("dve", S.r(S.t[:, d], d), 0.0)
                    P.copy("act", Sb.r(Sb.t[:, d], d), S.r(S.t[:, d], d))
                    yield
                kc = kq.r(kq.t[:, c, 0, :], c)
                bkC = cbank()
                P.mm(k.ps.r(k.ps.t[:, bkC, 0:128], bkC), kc, Sb.r(Sb.t[:, d], d))
                yield
                P.stt(Rp.r(Rp.t[:, d], d), k.ps.r(k.ps.t[:, bkC, 0:128], bkC), negegc.r(negegc.t[:, c, d:d + 1]), v_tm.r(v_tm.t[:, c, :], c), ALU.mult, ALU.add)
                yield
                bkV = cbank()
                P.mm(k.ps.r(k.ps.t[:, bkV, 0:128], bkV), ur("T", ui, gi, 1), Rp.r(Rp.t[:, d], d))
                yield
                P.act(vn.r(vn.t[:, d], d), k.ps.r(k.ps.t[:, bkV, 0:128], bkV), AF.Identity, scale=beta.r(beta.t[:, c, d:d + 1]))
                yield
                bkS = cbank()
                pS = k.ps.r(k.ps.t[:, bkS, 0:128], bkS)
                P.mm(pS, ur("kout", ui, gi), vn.r(vn.t[:, d], d))
                bkO = cbank()
                po = k.ps.r(k.ps.t[:, bkO, 0:128], bkO)
                P.mm(po, ur("qin", ui, gi), Sb.r(Sb.t[:, d], d), start=True, stop=False)
                P.mm(po, ur("attn", ui, gi), vn.r(vn.t[:, d], d), start=False, stop=True)
                yield
                P.stt(S.r(S.t[:, d], d), S.r(S.t[:, d], d), egl.r(egl.t[:, c, d:d + 1]), pS, ALU.mult, ALU.add)
                yield
                P.copy("act", Sb.r(Sb.t[:, d], d), S.r(S.t[:, d], d))
                yield
                if first_visit:
                    P.copy("act", o_tm.r(o_tm.t[:, c, :], c), po)
                    yield
                else:
                    P.tt("dve", ot.r(), po, o_tm.r(o_tm.t[:, c, :], c), ALU.add)
                    yield
                    o_, i_, a_ = junk.t, ot.t, ssq.t[:, 0:1]
                    P.op("act", lambda e, o_=o_, i_=i_, a_=a_: e.activation(out=o_, in_=i_, func=AF.Square, accum_out=a_),
                         reads=[ot.r()], writes=[junk.r(), ssq.r()])
                    P.act(ssq.r(ssq.t[:, 1:2]), ssq.r(ssq.t[:, 0:1]), AF.Ln, bias=c_eps5.r(), scale=1.0 / 128)
                    P.act(ssq.r(ssq.t[:, 1:2]), ssq.r(ssq.t[:, 1:2]), AF.Exp, scale=-0.5)
                    yield
                    bkZ = cbank()
                    for kk in range(8):
                        P.mm(k.ps.r(k.ps.t[:, bkZ, 0:128], bkZ), hch(k, kk, c), wz.r(wz.t[:, kk, :]), start=(kk == 0), stop=(kk == 7))
                    yield
                    P.act(sz.r(), k.ps.r(k.ps.t[:, bkZ, 0:128], bkZ), AF.Exp, scale=-1.0)
                    P.act(sz.r(), sz.r(), AF.Ln, bias=c_one.r())
                    P.act(sz.r(), sz.r(), AF.Exp, scale=-1.0)
                    P.stt(ot.r(), ot.r(), ssq.r(ssq.t[:, 1:2]), rows.r(rows.t[:, 32:160]), ALU.mult, ALU.mult)
                    yield
                    P.tt("dve", ot.r(), ot.r(), sz.r(), ALU.mult)
                    P.tt("dve", ogb.r(), ot.r(), k.ps.r(k.ps.t[:, bkZ, 0:128], bkZ), ALU.mult)
                    yield
                    bkT2 = cbank()
                    pbT2 = k.ps.t[:, bkT2, :].bitcast(BF16)
                    P.tr(k.ps.r(pbT2[:, 0:128], bkT2), ogb.r(), k.identb.r())
                    yield
                    P.copy("act", v_fm.r(v_fm.t[:, c * 128:(c + 1) * 128], c), k.ps.r(pbT2[:, 0:128], bkT2))
                    yield
                if seg_end and seg < 2:
                    P.dma("sp", R(k.dram("gdn_out").ap[j, seg, d, hh], k.dram("gdn_out").keys), S.r(S.t[:, d], d))

        def run_all(gen):
            for _ in gen:
                pass

        def merge(pg, cg, ratio):
            pdone = cdone = False
            while not (pdone and cdone):
                if not cdone:
                    try:
                        next(cg)
                    except StopIteration:
                        cdone = True
                for _ in range(ratio):
                    if pdone:
                        break
                    try:
                        next(pg)
                    except StopIteration:
                        pdone = True

        run_all(prelim_gen(0))
        for gi in range(len(groups)):
            if gi + 1 < len(groups):
                merge(prelim_gen(gi + 1), chain_gen(gi), k.cfg.get("gdn_ratio", 4))
            else:
                run_all(chain_gen(gi))
        P.barrier()
        P.release(mku)
        if nsteps == NCH:
            outproj_acc_g(k, W_out, hh * 128, lambda b: v_fm.r(v_fm.t[:, b * BLK:(b + 1) * BLK], range(4 * b, 4 * b + 4)))
    P.barrier()
    P.release(mk)


def outproj_acc_g(k, W, r0, o_region):
    P = k.P
    s, wo = load_out_w(k, W, 0, r0)
    for m in range(8):
        for b in range(NBLK):
            bk = P.bank()
            P.mm(psb(k, bk), k.wr.r(wo[:, m * 128:(m + 1) * 128], s), o_region(b))
            c = bc(b)
            P.stt(xs(k, m, b), psb(k, bk), k.mod.r(k.mod.t[:, Q_G1 + m, c:c + 1]), xs(k, m, b), ALU.mult, ALU.add)


NCORES = 8


def f32(a):
    return np.ascontiguousarray(np.asarray(a, dtype=np.float32))


def prep_inputs(inp):
    g = {k: np.asarray(v) for k, v in inp.items()}
    DEPTH = 4
    shared = {}
    shared["ident"] = np.eye(128, dtype=np.float32)
    for L in range(DEPTH):
        shared["ada_w%d" % L] = f32(g["ada_w"][L:L + 1])
        shared["ffn_w_in%d" % L] = f32(g["ffn_w_in"][L:L + 1])
        shared["ffn_w_out%d" % L] = f32(g["ffn_w_out"][L:L + 1])
    shared["ada_b"] = f32(g["ada_b"].reshape(DEPTH, 48, 128).transpose(0, 2, 1))
    shared["lng"] = f32(g["ln_g"].reshape(DEPTH, 2, 8, 128).transpose(0, 1, 3, 2))
    shared["lnb"] = f32(g["ln_b"].reshape(DEPTH, 2, 8, 128).transpose(0, 1, 3, 2))
    shared["fcw"] = f32(g["ffn_conv"].reshape(DEPTH, 9, 44, 128).transpose(0, 3, 2, 1))
    shared["lru_w_in"] = f32(g["lru_w_in"])
    shared["lru_w_out"] = f32(g["lru_w_out"])
    shared["lru_gate_w"] = f32(g["lru_gate_w"])
    sm = np.zeros((128, 10, 12), np.float32)
    sm[:, :, 0:4] = g["lru_conv"][0].reshape(4, 10, 128).transpose(2, 1, 0)
    sm[:, :, 4] = g["lru_conv_b"][0].reshape(10, 128).T
    sm[:, :, 5:9] = g["lru_gate_b"][0].reshape(4, 10, 128).transpose(2, 1, 0)
    sm[:, :, 9:11] = g["lru_lambda"][0].reshape(2, 10, 128).transpose(2, 1, 0)
    shared["lru_sm"] = sm
    tri_f = np.triu(np.ones((128, 128), np.float32))
    tri_b = np.tril(np.ones((128, 128), np.float32))
    shared["cmask"] = np.stack([tri_f, tri_b, (1 - tri_f) * -30000.0, (1 - tri_b) * -30000.0]).astype(np.float32)
    shared["ssd_w_in"] = f32(g["ssd_w_in"])
    shared["ssd_w_out"] = f32(g["ssd_w_out"])
    cw = np.zeros((128, 24, 5), np.float32)
    cw[:, :, 0:4] = g["ssd_conv"][0].reshape(4, 24, 128).transpose(2, 1, 0)
    cw[:, :, 4] = g["ssd_conv_b"][0].reshape(24, 128).T
    shared["ssd_cw"] = cw
    row = np.concatenate([g["ssd_dt_bias"][0].reshape(64), g["ssd_a_log"][0].reshape(64), g["ssd_d"][0].reshape(32)])
    shared["ssd_rows"] = f32(np.broadcast_to(row[None, :], (128, 160)))
    shared["ssd_nrm"] = f32(np.broadcast_to(g["ssd_norm"][0][None, :], (128, 2048)))
    for jj in range(2):
        shared["gdn_w_in%d" % jj] = f32(g["gdn_w_in"][jj:jj + 1])
        shared["gdn_w_out%d" % jj] = f32(g["gdn_w_out"][jj:jj + 1])
    shared["gdn_cw"] = f32(g["gdn_conv"].reshape(2, 4, 24, 128).transpose(0, 3, 2, 1))
    grow = np.concatenate([g["gdn_dt_bias"].reshape(2, 16), g["gdn_a_log"].reshape(2, 16), g["gdn_norm"].reshape(2, 128)], axis=1)
    shared["gdn_rows"] = f32(np.broadcast_to(grow[:, None, :], (2, 128, 160)))
    li = np.arange(128)[:, None]
    si = np.arange(128)[None, :]
    lv = np.zeros((2, 7, 2, 128, 128), np.float32)
    for jl in range(7):
        Bs = 2 ** jl
        mA = ((li // (2 * Bs)) == (si // (2 * Bs))) & ((li % (2 * Bs)) >= Bs) & ((si % (2 * Bs)) < Bs)
        mA = mA.astype(np.float32)
        lv[0, jl, 0] = -mA
        lv[0, jl, 1] = -mA.T
        lv[1, jl, 0] = -mA.T
        lv[1, jl, 1] = -mA
    shared["glvl"] = lv
    maps = []
    for i in range(NCORES):
        p0, p1, sb = 2 * i, 2 * i + 1, i % 4
        xin = np.concatenate([g["x_prompt"][p0].T, g["x_prompt"][p1].T, g["x_sample"][sb].T], axis=1)
        cond = np.stack([g["c_ctx"].reshape(8, 128).T, g["c"][sb].reshape(8, 128).T], axis=-1)
        m = dict(shared)
        m["xin"] = f32(xin)
        m["cond"] = f32(cond)
        m["ssd_s0"] = f32(g["state_ssd"][sb, 0].reshape(2, 8, 4, 64, 128).transpose(0, 1, 4, 2, 3).reshape(2, 8, 128, 256))
        m["gdn_s0"] = f32(g["state_gdn"][sb])
        m["lru_s0"] = f32(g["state_lru"][sb, 0].reshape(2, 10, 128).transpose(2, 1, 0))
        maps.append(m)
    return maps


def assemble(results, inp):
    BATCH, SEQ, D = 16, 256, 1024
    yp = np.zeros((BATCH, SEQ, D), np.float32)
    ys = np.zeros((4, 2048, D), np.float32)
    for i in range(NCORES):
        y = np.asarray(results[i]["yout"])
        yp[2 * i] = y[:, 0:256].T
        yp[2 * i + 1] = y[:, 256:512].T
        if i < 4:
            ys[i] = y[:, 512:].T
    nl = np.zeros((BATCH, 1, 2, 1280), np.float32)
    for i in range(NCORES):
        if "lru_out" in results[i]:
            o = np.asarray(results[i]["lru_out"])
            for pi in range(2):
                nl[2 * i + pi, 0] = o[:, :, pi, :].transpose(2, 1, 0).reshape(2, 1280)
    nssd = np.zeros((BATCH, 1, 2, 32, 64, 128), np.float32)
    for i in range(NCORES):
        if "ssd_out" in results[i]:
            o = np.asarray(results[i]["ssd_out"])
            for pi in range(2):
                nssd[2 * i + pi, 0] = o[pi].reshape(2, 32, 64, 128)
    ngdn = np.zeros((BATCH, 2, 2, 8, 128, 128), np.float32)
    for i in range(NCORES):
        if "gdn_out" in results[i]:
            o = np.asarray(results[i]["gdn_out"])
            for pi in range(2):
                ngdn[2 * i + pi] = o[:, pi]
    return yp, ys, nl, nssd, ngdn


_NC_CACHE = {}


def kernel(**inputs):
    cfg = {}
    if "nc" not in _NC_CACHE:
        _NC_CACHE["nc"] = build(cfg)
    nc, used = _NC_CACHE["nc"]
    maps = prep_inputs(inputs)
    maps = [{kk: v for kk, v in mm.items() if kk in used} for mm in maps]
    res = run_bass_kernel_spmd(nc, maps, core_ids=list(range(NCORES)))
    yp, ys, nl, nssd, ngdn = assemble(res.results, inputs)
    return (yp, ys, ngdn, nssd, nl)
```

```python
import numpy as np
import concourse.bass as bass
import concourse.mybir as mybir
from concourse.bass_utils import run_bass_kernel_spmd
from contextlib import ExitStack

F32 = mybir.dt.float32
F32R = mybir.dt.float32r
BF16 = mybir.dt.bfloat16
ALU = mybir.AluOpType
AF = mybir.ActivationFunctionType
AX = mybir.AxisListType

ENGS = ["pe", "act", "dve", "pool", "sp"]
NDS = 24
MAXEMB = 1
ARENA_WORDS = 53000


class R:
    __slots__ = ("ap", "keys")

    def __init__(self, ap, keys):
        self.ap = ap
        self.keys = keys


class Buf:
    def __init__(self, prog, name, shape, dtype, nsub=1, psum=False):
        self.name = name
        self.nsub = nsub
        if psum:
            self.t = prog.st.enter_context(prog.nc.psum_tensor(name, shape, dtype))
        else:
            n = 1
            for d in shape[1:]:
                n *= d
            esz = 2 if dtype == BF16 else 4
            words = (n * esz + 3) // 4
            off = prog.aoff
            prog.aoff += words
            assert prog.aoff <= ARENA_WORDS, (name, prog.aoff)
            prog.apeak = max(prog.apeak, prog.aoff)
            ap = prog.arena[:, off:off + words]
            if dtype == BF16:
                ap = ap.bitcast(BF16)
            ap = ap[:, 0:n]
            if len(shape) > 2:
                names = " ".join("d%d" % i for i in range(len(shape) - 1))
                kw = {"d%d" % i: shape[i + 1] for i in range(len(shape) - 1)}
                ap = ap.rearrange("p (%s) -> p %s" % (names, names), **kw)
            self.t = ap

    def r(self, ap=None, sub=None):
        if ap is None:
            ap = self.t[:] if not hasattr(self.t, "rearrange") else self.t
        if sub is None:
            keys = [(self.name, i) for i in range(self.nsub)]
        elif isinstance(sub, (list, tuple, range)):
            keys = [(self.name, i) for i in sub]
        else:
            keys = [(self.name, sub)]
        return R(ap, keys)


class Prog:
    def __init__(self, nc, st):
        self.nc = nc
        self.st = st
        self.q = {e: [] for e in ENGS}
        self.cnt = {e: 0 for e in ENGS}
        self.sem = {e: st.enter_context(nc.semaphore("s_" + e)) for e in ENGS}
        self.dsem = [st.enter_context(nc.semaphore("d%d" % i)) for i in range(NDS)]
        self.dcnt = [0] * NDS
        self.dnext = 0
        self.known = {e: {} for e in ENGS}
        self.last_w = {}
        self.readers = {}
        self.nops = 0
        self.nwaits = 0
        self.bar = {e: None for e in ENGS}
        self._bank = 0
        self.arena_t = st.enter_context(nc.sbuf_tensor("arena", [128, ARENA_WORDS], F32))
        self.arena = self.arena_t[:]
        self.aoff = 0
        self.apeak = 0

    def mark(self):
        return self.aoff

    def release(self, m):
        self.aoff = m

    def bank(self):
        b = self._bank
        self._bank = (self._bank + 1) % 8
        return b

    def barrier(self):
        snap = {e: self.cnt[e] for e in ENGS if self.cnt[e] > 0}
        for i in range(NDS):
            if self.dcnt[i] > 0:
                snap["d%d" % i] = self.dcnt[i]
        for e in ENGS:
            self.bar[e] = dict(snap)

    def buf(self, name, shape, dtype, nsub=1, psum=False):
        return Buf(self, name, shape, dtype, nsub, psum)

    def _collect(self, eng, reads, writes):
        need = {}

        def add(tok):
            if tok is None:
                return
            semid, val, snap = tok
            if eng == "pe" and semid == "pe":
                return
            if need.get(semid, (0, None))[0] < val:
                need[semid] = (val, snap)

        for k in reads:
            add(self.last_w.get(k))
        for k in writes:
            add(self.last_w.get(k))
            rd = self.readers.get(k)
            if rd:
                for tok in rd.values():
                    add(tok)
        kn = self.known[eng]
        if self.bar[eng] is not None:
            for semid, val in self.bar[eng].items():
                if eng == "pe" and semid == "pe":
                    continue
                if semid == eng and val >= self.cnt[eng] + 1:
                    continue
                if need.get(semid, (0, None))[0] < val:
                    need[semid] = (val, None)
            self.bar[eng] = None
        waits = []
        for semid, (val, snap) in need.items():
            if kn.get(semid, 0) >= val:
                continue
            waits.append((semid, val))
        for semid, (val, snap) in need.items():
            if kn.get(semid, 0) < val:
                kn[semid] = val
            if snap:
                for s2, v2 in snap.items():
                    if kn.get(s2, 0) < v2:
                        kn[s2] = v2
        return waits

    def _commit(self, tok, reads, writes):
        for k in writes:
            self.last_w[k] = tok
            self.readers[k] = {}
        for k in reads:
            if k in writes:
                continue
            self.readers.setdefault(k, {})[tok[0]] = tok

    def op(self, eng, fn, reads=(), writes=()):
        rk = [k for r in reads if r is not None for k in r.keys]
        wk = [k for r in writes if r is not None for k in r.keys]
        if eng != "pe":
            for k_ in rk:
                if k_[0] == "ps" and k_ not in wk:
                    wk.append(k_)
        waits = self._collect(eng, rk, wk)
        self.cnt[eng] += 1
        tok = (eng, self.cnt[eng], dict(self.known[eng]))
        self.q[eng].append((waits, fn, None))
        self._commit(tok, rk, wk)
        self.nops += 1
        self.nwaits += len(waits)
        return tok

    def dma(self, eng, out, in_, **kw):
        rk = list(in_.keys)
        wk = list(out.keys)
        i = self.dnext
        self.dnext = (self.dnext + 1) % NDS
        semid = "d%d" % i
        waits = self._collect(eng, rk, wk)
        kn = self.known[eng]
        if kn.get(semid, 0) < self.dcnt[i]:
            waits.append((semid, self.dcnt[i]))
            kn[semid] = self.dcnt[i]
        self.dcnt[i] += 16
        tok = (semid, self.dcnt[i], dict(kn))
        oap, iap = out.ap, in_.ap

        def fn(e):
            return e.dma_start(out=oap, in_=iap, **kw)

        self.q[eng].append((waits, fn, i))
        self._commit(tok, rk, wk)
        self.nops += 1
        self.nwaits += len(waits)
        return tok

    def _semh(self, semid):
        if semid in self.sem:
            return self.sem[semid]
        return self.dsem[int(semid[1:])]

    def emit(self, final_wait_eng="sp"):
        nc = self.nc
        finals = []
        for e in ENGS:
            if self.cnt[e] > 0:
                finals.append((e, self.cnt[e]))
        for i in range(NDS):
            if self.dcnt[i] > 0:
                finals.append(("d%d" % i, self.dcnt[i]))
        eng_objs = {}
        with nc.Block() as block:
            def mk(ename):
                def body(e):
                    for waits, fn, dsi in self.q[ename]:
                        if dsi is not None or len(waits) > MAXEMB:
                            for semid, val in waits:
                                e.wait_ge(self._semh(semid), val)
                            ins = fn(e)
                        else:
                            ins = fn(e)
                            for semid, val in waits:
                                ins._wait_ge(self._semh(semid), val)
                        if dsi is None:
                            ins.then_inc(self.sem[ename], 1)
                        else:
                            ins.then_inc(self.dsem[dsi], 16)
                    if ename == final_wait_eng:
                        for semid, val in finals:
                            e.wait_ge(self._semh(semid), val)
                return body
            block.tensor(mk("pe"))
            block.scalar(mk("act"))
            block.vector(mk("dve"))
            block.gpsimd(mk("pool"))
            block.sync(mk("sp"))

    def mm(self, out, lhsT, rhs, start=True, stop=True, extra_reads=()):
        o, l, r = out.ap, lhsT.ap, rhs.ap
        return self.op("pe", lambda e: e.matmul(o, l, r, start=start, stop=stop),
                       reads=[lhsT, rhs] + list(extra_reads) + ([] if start else [out]), writes=[out])

    def tr(self, out, in_, ident):
        o, i, d = out.ap, in_.ap, ident.ap
        return self.op("pe", lambda e: e.transpose(o, i, d), reads=[in_, ident], writes=[out])

    def act(self, out, in_, func, bias=None, scale=None, eng="act"):
        o, i = out.ap, in_.ap
        kw = {}
        rd = [in_]
        if bias is not None:
            if isinstance(bias, R):
                kw["bias"] = bias.ap
                rd.append(bias)
            else:
                kw["bias"] = float(bias)
        if scale is not None:
            if isinstance(scale, R):
                kw["scale"] = scale.ap
                rd.append(scale)
            else:
                kw["scale"] = float(scale)
        return self.op("act", lambda e: e.activation(out=o, in_=i, func=func, **kw), reads=rd, writes=[out])

    def tt(self, eng, out, in0, in1, op):
        o, a, b = out.ap, in0.ap, in1.ap
        return self.op(eng, lambda e: e.tensor_tensor(out=o, in0=a, in1=b, op=op), reads=[in0, in1], writes=[out])

    def ts(self, eng, out, in0, s1, op0, s2=None, op1=None):
        o, a = out.ap, in0.ap
        rd = [in0]
        v1 = s1
        if isinstance(s1, R):
            rd.append(s1)
            v1 = s1.ap
        v2 = s2
        if isinstance(s2, R):
            rd.append(s2)
            v2 = s2.ap
        if op1 is None:
            return self.op(eng, lambda e: e.tensor_scalar(out=o, in0=a, scalar1=v1, scalar2=None, op0=op0), reads=rd, writes=[out])
        return self.op(eng, lambda e: e.tensor_scalar(out=o, in0=a, scalar1=v1, scalar2=v2, op0=op0, op1=op1), reads=rd, writes=[out])

    def stt(self, out, in0, scalar, in1, op0, op1):
        o, a, b = out.ap, in0.ap, in1.ap
        rd = [in0, in1]
        sv = scalar
        if isinstance(scalar, R):
            rd.append(scalar)
            sv = scalar.ap
        return self.op("dve", lambda e: e.scalar_tensor_tensor(out=o, in0=a, scalar=sv, in1=b, op0=op0, op1=op1), reads=rd, writes=[out])

    def copy(self, eng, out, in_):
        o, i = out.ap, in_.ap
        if eng == "act":
            return self.op("act", lambda e: e.copy(out=o, in_=i), reads=[in_], writes=[out])
        return self.op(eng, lambda e: e.tensor_copy(out=o, in_=i), reads=[in_], writes=[out])

    def memset(self, eng, out, val):
        o = out.ap
        return self.op(eng, lambda e: e.memset(o, val), reads=[], writes=[out])


D = 1024
NT = 2560
NBLK = 5
BLK = 512
DEPTH = 4
FH = 2816
NPAIR = 22
UPW = 2 * 258 + 34 * 66
ALPHA = (2 * DEPTH) ** 0.25
LN_EPS = 1e-5
NW = 5
WSL = 1024


def bc(b):
    return 0 if b == 0 else 1


class K:
    pass


def build(cfg):
    nc = bass.Bass("TRN2", target_bir_lowering=False)
    k = K()
    k.nc = nc
    k.cfg = cfg

    shapes = {
        "xin": [D, NT], "cond": [128, 8, 2], "ident": [128, 128],
        "ada_b": [DEPTH, 128, 48], "lng": [DEPTH, 2, 128, 8], "lnb": [DEPTH, 2, 128, 8],
        "fcw": [DEPTH, 128, 44, 9],
        "lru_w_in": [1, D, 2560], "lru_w_out": [1, 1280, D], "lru_gate_w": [1, 2, 2, 10, 128, 128],
        "lru_sm": [128, 10, 12], "lru_s0": [128, 10, 2],
        "cmask": [4, 128, 128],
        "ssd_w_in": [1, D, 5184], "ssd_w_out": [1, 2048, D], "ssd_cw": [128, 24, 5], "ssd_rows": [128, 160],
        "ssd_nrm": [128, 2048], "ssd_s0": [2, 8, 128, 256],
    }
    for jj in range(2):
        shapes["gdn_w_in%d" % jj] = [1, D, 4128]
        shapes["gdn_w_out%d" % jj] = [1, D, D]
    shapes.update({"gdn_cw": [2, 128, 24, 4], "gdn_rows": [2, 128, 160], "gdn_s0": [2, 2, 8, 128, 128], "glvl": [2, 7, 2, 128, 128]})
    for L in range(DEPTH):
        shapes["ada_w%d" % L] = [1, D, 6 * D]
        shapes["ffn_w_in%d" % L] = [1, D, 2 * FH]
        shapes["ffn_w_out%d" % L] = [1, FH, D]
    oshapes = {"yout": [D, NT], "lru_out": [128, 10, 2, 2], "ssd_out": [2, 2, 2048, 128], "gdn_out": [2, 2, 2, 8, 128, 128]}
    k.shapes = shapes
    k.oshapes = oshapes
    k.decl = {}

    def dram(name):
        if name in k.decl:
            return k.decl[name]
        if name in shapes:
            t = nc.dram_tensor(name, list(shapes[name]), F32, kind="ExternalInput").ap()
        else:
            t = nc.dram_tensor(name, list(oshapes[name]), F32, kind="ExternalOutput").ap()
        k.decl[name] = R(t, [("dram_" + name, 0)])
        return k.decl[name]
    k.dram = dram

    with ExitStack() as st:
        P = Prog(nc, st)
        k.P = P
        k.x = P.buf("x", [128, 8, NT], F32, nsub=40)
        k.h = P.buf("h", [128, 8, NT], BF16, nsub=40)
        k.wr = P.buf("wr", [128, NW, WSL], BF16, nsub=NW)
        k.wnext = 0
        k.ps = P.buf("ps", [128, 8, 512], F32, nsub=8, psum=True)
        k.identf = P.buf("identf", [128, 128], F32)
        k.identb = P.buf("identb", [128, 128], BF16)
        k.onesb = P.buf("onesb", [128, 128], BF16)
        k.csil = P.buf("csil", [128, 8, 2], BF16)
        k.mod = P.buf("mod", [128, 48, 2], F32)
        k.msc = P.buf("msc", [128, 2, 8, 2], F32)
        k.lngb = P.buf("lngb", [128, DEPTH, 2, 8], F32)
        k.lnbb = P.buf("lnbb", [128, DEPTH, 2, 8], F32)
        k.adab = P.buf("adab", [128, DEPTH, 48], F32)

        prologue(k)
        for L in cfg.get("layers", list(range(DEPTH))):
            layer(k, L)
        for j in range(8):
            P.dma("sp", R(k.dram("yout").ap[j * 128:(j + 1) * 128, :], k.dram("yout").keys), k.x.r(k.x.t[:, j, :], range(j * 5, j * 5 + 5)))
        P.emit()
        print("ops", P.nops, "waits", P.nwaits, {e: P.cnt[e] for e in ENGS}, "apeak", P.apeak)
    return nc, set(n for n in k.decl if n in k.shapes)


def xs(k, j, b):
    return k.x.r(k.x.t[:, j, b * BLK:(b + 1) * BLK], j * 5 + b)


def hs(k, j, b):
    return k.h.r(k.h.t[:, j, b * BLK:(b + 1) * BLK], j * 5 + b)


def psb(k, b, n=512):
    return k.ps.r(k.ps.t[:, b, 0:n], b)


def prologue(k):
    P = k.P
    for j in range(8):
        P.dma("sp", k.x.r(k.x.t[:, j, :], range(j * 5, j * 5 + 5)), R(k.dram("xin").ap[j * 128:(j + 1) * 128, :], k.dram("xin").keys))
    P.dma("sp", k.identf.r(), k.dram("ident"))
    P.copy("dve", k.identb.r(), k.identf.r())
    P.memset("dve", k.onesb.r(), 1.0)
    ctmp = P.buf("ctmp", [128, 8, 2], F32)
    P.dma("sp", ctmp.r(), k.dram("cond"))
    P.act(k.csil.r(), ctmp.r(), AF.Silu)
    for L in range(DEPTH):
        P.dma("sp", k.lngb.r(k.lngb.t[:, L]), R(k.dram("lng").ap[L].rearrange("s p j -> p s j"), k.dram("lng").keys))
        P.dma("sp", k.lnbb.r(k.lnbb.t[:, L]), R(k.dram("lnb").ap[L].rearrange("s p j -> p s j"), k.dram("lnb").keys))
        P.dma("sp", k.adab.r(k.adab.t[:, L]), R(k.dram("ada_b").ap[L], k.dram("ada_b").keys))


def wslot(k):
    s = k.wnext
    k.wnext = (k.wnext + 1) % NW
    return s


def load_in_w(k, W, L, c0, ncols=128):
    P = k.P
    s = wslot(k)
    src = W.ap[L][:, c0:c0 + ncols].rearrange("(kk p) c -> p kk c", p=128)
    dst = k.wr.t[:, s, 0:8 * ncols].rearrange("p (kk c) -> p kk c", c=ncols)
    P.dma("pool", k.wr.r(dst, s), R(src, W.keys))
    return s, dst


def load_out_w(k, W, L, r0):
    P = k.P
    s = wslot(k)
    src = W.ap[L][r0:r0 + 128, :]
    dst = k.wr.t[:, s, 0:1024]
    P.dma("pool", k.wr.r(dst, s), R(src, W.keys))
    return s, dst


def ada(k, L):
    P = k.P
    b = P.bank()
    for q in range(48):
        s, w = load_in_w(k, k.dram("ada_w%d" % L), 0, q * 128)
        for kk in range(8):
            P.mm(k.ps.r(k.ps.t[:, b, q * 2:q * 2 + 2], b), k.wr.r(w[:, kk, :], s),
                 k.csil.r(k.csil.t[:, kk, :]), start=(kk == 0), stop=(kk == 7))
    src = k.ps.t[:, b, 0:96].rearrange("p (q c) -> p q c", c=2)
    bias = k.adab.t[:, L, :].unsqueeze(2).to_broadcast([128, 48, 2])
    P.tt("dve", k.mod.r(), k.ps.r(src, b), k.adab.r(bias), ALU.add)
    for sl in range(2):
        q0 = (1 + 3 * sl) * 8
        P.ts("dve", k.msc.r(k.msc.t[:, sl]), k.mod.r(k.mod.t[:, q0:q0 + 8, :]), 1.0, ALU.add)


def modulate(k, sl):
    P = k.P
    q_sh = (0 + 3 * sl) * 8
    for j in range(8):
        for b in range(NBLK):
            c = bc(b)
            P.act(hs(k, j, b), xs(k, j, b), AF.Identity,
                  bias=k.mod.r(k.mod.t[:, q_sh + j, c:c + 1]), scale=k.msc.r(k.msc.t[:, sl, j, c:c + 1]))
    for j in range(8):
        for b in range(NBLK):
            P.ts("dve", xs(k, j, b), xs(k, j, b), ALPHA, ALU.mult)


def layernorm(k, L, sl, lb):
    P = k.P
    xb, sq, mean, m2, var, rstd, nmr, t1 = lb["xb"], lb["sq"], lb["mean"], lb["m2"], lb["var"], lb["rstd"], lb["nmr"], lb["t1"]
    for b in range(NBLK):
        pb = b % 2
        for j in range(8):
            P.act(xb.r(xb.t[:, pb, j, :], pb * 8 + j), xs(k, j, b), AF.Identity)
            P.act(sq.r(sq.t[:, pb, j, :], pb * 8 + j), xs(k, j, b), AF.Square)
        b1 = P.bank()
        b2 = P.bank()
        for j in range(8):
            P.mm(psb(k, b1), k.onesb.r(), xb.r(xb.t[:, pb, j, :], pb * 8 + j), start=(j == 0), stop=(j == 7))
        for j in range(8):
            P.mm(psb(k, b2), k.onesb.r(), sq.r(sq.t[:, pb, j, :], pb * 8 + j), start=(j == 0), stop=(j == 7))
        P.act(mean.r(mean.t[:, pb], pb), psb(k, b1), AF.Identity, scale=1.0 / D)
        P.tt("dve", m2.r(m2.t[:, pb], pb), mean.r(mean.t[:, pb], pb), mean.r(mean.t[:, pb], pb), ALU.mult)
        P.stt(var.r(var.t[:, pb], pb), psb(k, b2), 1.0 / D, m2.r(m2.t[:, pb], pb), ALU.mult, ALU.subtract)
        P.act(var.r(var.t[:, pb], pb), var.r(var.t[:, pb], pb), AF.Ln, bias=lb["eps"].r())
        P.act(rstd.r(rstd.t[:, pb], pb), var.r(var.t[:, pb], pb), AF.Exp, scale=-0.5)
        P.stt(nmr.r(nmr.t[:, pb], pb), mean.r(mean.t[:, pb], pb), -1.0, rstd.r(rstd.t[:, pb], pb), ALU.mult, ALU.mult)
        for j in range(8):
            tb = j % 2
            P.tt("dve", t1.r(t1.t[:, tb], tb), xs(k, j, b), rstd.r(rstd.t[:, pb], pb), ALU.mult)
            P.tt("dve", t1.r(t1.t[:, tb], tb), t1.r(t1.t[:, tb], tb), nmr.r(nmr.t[:, pb], pb), ALU.add)
            P.act(xs(k, j, b), t1.r(t1.t[:, tb], tb), AF.Identity,
                  bias=k.lnbb.r(k.lnbb.t[:, L, sl, j:j + 1]), scale=k.lngb.r(k.lngb.t[:, L, sl, j:j + 1]))


def ln_scope(k, L, sl):
    P = k.P
    mk = P.mark()
    if True:
        lb = {
            "xb": P.buf("ln_xb", [128, 2, 8, BLK], BF16, nsub=16),
            "sq": P.buf("ln_sq", [128, 2, 8, BLK], BF16, nsub=16),
            "mean": P.buf("ln_mean", [128, 2, BLK], F32, nsub=2),
            "m2": P.buf("ln_m2", [128, 2, BLK], F32, nsub=2),
            "var": P.buf("ln_var", [128, 2, BLK], F32, nsub=2),
            "rstd": P.buf("ln_rstd", [128, 2, BLK], F32, nsub=2),
            "nmr": P.buf("ln_nmr", [128, 2, BLK], F32, nsub=2),
            "t1": P.buf("ln_t1", [128, 2, BLK], F32, nsub=2),
            "eps": P.buf("ln_eps", [128, 1], F32),
        }
        P.memset("dve", lb["eps"].r(), LN_EPS)
        layernorm(k, L, sl, lb)
        P.barrier()
        P.release(mk)


def ffn(k, L):
    P = k.P
    G = 2
    mk = P.mark()
    if True:
        up = P.buf("f_up", [128, 2, 2, UPW], BF16, nsub=4)
        ab = P.buf("f_ab", [128, G, NT], BF16, nsub=G * 5)
        dg = P.buf("f_dg", [128, 2, 2, 9, 128], BF16, nsub=4)
        sg = P.buf("f_sg", [128, 2, BLK], F32, nsub=2)
        fcw = P.buf("f_fcw", [128, 44, 9], F32)
        P.dma("sp", fcw.r(), R(k.dram("fcw").ap[L], k.dram("fcw").keys))
        P.memset("pool", up.r(), 0.0)
        q_g = 5 * 8
        nsg = 0
        for g0 in range(0, NPAIR, G):
            pairs = list(range(g0, min(g0 + G, NPAIR)))
            for jj, j in enumerate(pairs):
                db = j % 2
                sg_, wg = load_in_w(k, k.dram("ffn_w_in%d" % L), 0, j * 128)
                sv_, wv = load_in_w(k, k.dram("ffn_w_in%d" % L), 0, FH + j * 128)
                ws = [wg, wv]
                wss = [sg_, sv_]
                for gv in range(2):
                    tile_idx = j + gv * NPAIR
                    i0 = k.identf.t[:].unsqueeze(1).to_broadcast([128, 9, 128])
                    i1 = fcw.t[:, tile_idx, :].unsqueeze(2).to_broadcast([128, 9, 128])
                    P.tt("pool", dg.r(dg.t[:, db, gv], db * 2 + gv), k.identf.r(i0), fcw.r(i1), ALU.mult)
                for b in range(NBLK):
                    for gv in range(2):
                        bk = P.bank()
                        for kk in range(8):
                            P.mm(psb(k, bk), k.wr.r(ws[gv][:, kk, :], wss[gv]), hs(k, kk, b), start=(kk == 0), stop=(kk == 7))
                        upt = up.t[:, db, gv]
                        if b == 0:
                            dst = upt[:, 0:516].rearrange("p (s w) -> p s w", w=258)[:, :, 1:257]
                            src = k.ps.t[:, bk, :].rearrange("p (s w) -> p s w", w=256)
                        else:
                            r0 = 8 * (b - 1)
                            dst = upt[:, 516:].rearrange("p (r w) -> p r w", w=66)[:, 1 + r0:9 + r0, 1:65]
                            src = k.ps.t[:, bk, :].rearrange("p (r w) -> p r w", w=64)
                        P.copy("act" if gv == 0 else "dve", up.r(dst, db * 2 + gv), k.ps.r(src, bk))
                for b in range(NBLK):
                    bks = []
                    for gv in range(2):
                        bk = P.bank()
                        bks.append(bk)
                        upt = up.t[:, db, gv]
                        if b == 0:
                            taps = [(1, kw) for kw in range(3)]
                            outv = k.ps.t[:, bk, :].rearrange("p (s w) -> p s w", w=256)
                        else:
                            taps = [(kh, kw) for kh in range(3) for kw in range(3)]
                            outv = k.ps.t[:, bk, :].rearrange("p (r w) -> p r w", w=64)
                        for ti, (kh, kw) in enumerate(taps):
                            if b == 0:
                                rhs = upt[:, 0:516].rearrange("p (s w) -> p s w", w=258)[:, :, kw:kw + 256]
                            else:
                                r0 = 8 * (b - 1)
                                rhs = upt[:, 516:].rearrange("p (r w) -> p r w", w=66)[:, r0 + kh:r0 + kh + 8, kw:kw + 64]
                            P.mm(k.ps.r(outv, bk), dg.r(dg.t[:, db, gv, kh * 3 + kw, :], db * 2 + gv), up.r(rhs, db * 2 + gv),
                                 start=(ti == 0), stop=(ti == len(taps) - 1))
                    sb_ = nsg % 2
                    nsg += 1
                    P.act(sg.r(sg.t[:, sb_], sb_), psb(k, bks[0]), AF.Silu)
                    P.tt("dve", ab.r(ab.t[:, jj, b * BLK:(b + 1) * BLK], jj * 5 + b), psb(k, bks[1]), sg.r(sg.t[:, sb_], sb_), ALU.mult)
            wos = [load_out_w(k, k.dram("ffn_w_out%d" % L), 0, (g0 + jj) * 128) for jj in range(len(pairs))]
            for m in range(8):
                for b in range(NBLK):
                    bk = P.bank()
                    for jj in range(len(pairs)):
                        P.mm(psb(k, bk), k.wr.r(wos[jj][1][:, m * 128:(m + 1) * 128], wos[jj][0]), ab.r(ab.t[:, jj, b * BLK:(b + 1) * BLK], jj * 5 + b),
                             start=(jj == 0), stop=(jj == len(pairs) - 1))
                    c = bc(b)
                    P.stt(xs(k, m, b), psb(k, bk), k.mod.r(k.mod.t[:, q_g + m, c:c + 1]), xs(k, m, b), ALU.mult, ALU.add)
        P.barrier()
        P.release(mk)


def layer(k, L):
    cfg = k.cfg
    ada(k, L)
    modulate(k, 0)
    if cfg.get("mixers", True):
        mixer(k, L)
    ln_scope(k, L, 0)
    modulate(k, 1)
    if cfg.get("ffn", True):
        ffn(k, L)
    ln_scope(k, L, 1)


LW = 1280
SEGS = [(0, 256), (256, 256), (512, 2048)]
Q_G1 = 2 * 8


def mixer(k, L):
    kind = L % 3
    if kind == 2:
        lru(k, L // 3)
    elif kind == 1:
        ssd(k, L // 3)
    else:
        gdn(k, L // 3)


def outproj_acc(k, W, Lw, r0, ntile, o_regions):
    P = k.P
    slots = [load_out_w(k, W, Lw, r0 + i * 128) for i in range(ntile)]
    for m in range(8):
        for b in range(NBLK):
            bk = P.bank()
            for jj in range(ntile):
                s, wo = slots[jj]
                P.mm(psb(k, bk), k.wr.r(wo[:, m * 128:(m + 1) * 128], s), o_regions(jj, b), start=(jj == 0), stop=(jj == ntile - 1))
            c = bc(b)
            P.stt(xs(k, m, b), psb(k, bk), k.mod.r(k.mod.t[:, Q_G1 + m, c:c + 1]), xs(k, m, b), ALU.mult, ALU.add)


def conv1d_pad_layout():
    offs = []
    o = 0
    for (t0, n) in SEGS:
        offs.append(o)
        o += n + 3
    return offs, o


CPO, CPW = conv1d_pad_layout()


def conv_dst(xpad_t, b):
    if b == 0:
        return xpad_t[:, 0:518].rearrange("p (s w) -> p s w", w=259)[:, :, 1:257], "p (s w) -> p s w", 256
    o = CPO[2] + 1 + (b - 1) * BLK
    return xpad_t[:, o:o + BLK], None, None


def conv_rhs(xpad_t, b, kk):
    if b == 0:
        return xpad_t[:, 0:518].rearrange("p (s w) -> p s w", w=259)[:, :, kk:kk + 256]
    o = CPO[2] + (b - 1) * BLK + kk
    return xpad_t[:, o:o + BLK]


def psview(k, bk, b):
    if b == 0:
        return k.ps.t[:, bk, :].rearrange("p (s w) -> p s w", w=256)
    return k.ps.t[:, bk, :]


def lru(k, j):
    P = k.P
    G = 2
    mk = P.mark()
    xpad = P.buf("l_xpad", [128, CPW + 1], BF16)
    xrb = P.buf("l_xrb", [128, NT], BF16)
    gg = P.buf("l_gg", [128, NT], BF16)
    Ib = P.buf("l_i", [128, NT], BF16)
    A = P.buf("l_a", [128, NT], F32)
    T = P.buf("l_t", [128, NT], F32)
    H = [P.buf("l_h0", [128, NT], BF16), P.buf("l_h1", [128, NT], BF16)]
    ob = P.buf("l_o", [128, G, NT], BF16, nsub=G)
    dgl = P.buf("l_dg", [128, 4, 128], BF16)
    sm = P.buf("l_sm", [128, 10, 12], F32)
    s0 = P.buf("l_s0", [128, 10, 2], F32)
    sp = P.buf("l_sp", [128, 10, 2], F32)
    ep = P.buf("l_ep", [128, 10, 2], F32)
    one = P.buf("l_one", [128, 1], F32)
    sto = P.buf("l_sto", [128, 10, 2, 2], F32)
    P.dma("sp", sm.r(), k.dram("lru_sm"))
    P.dma("sp", s0.r(), k.dram("lru_s0"))
    P.memset("dve", one.r(), 1.0)
    P.memset("pool", xpad.r(), 0.0)
    P.act(ep.r(), sm.r(sm.t[:, :, 9:11]), AF.Exp, scale=-1.0)
    P.ts("dve", sp.r(), ep.r(), -0.2, ALU.mult, 0.25, ALU.add)
    for cst in (1.0 / 3, 0.5, 1.0):
        P.tt("dve", sp.r(), sp.r(), ep.r(), ALU.mult)
        P.ts("dve", sp.r(), sp.r(), -1.0, ALU.mult, cst, ALU.add)
    P.tt("dve", sp.r(), sp.r(), ep.r(), ALU.mult)
    P.ts("dve", sp.r(), sp.r(), -8.0, ALU.mult)

    for g0 in range(0, 10, G):
        tiles = list(range(g0, min(g0 + G, 10)))
        for jj, n in enumerate(tiles):
            s, wgb = load_in_w(k, k.dram("lru_w_in"), j, n * 128)
            sx, wxr = load_in_w(k, k.dram("lru_w_in"), j, LW + n * 128)
            s2 = wslot(k)
            gsrc = k.dram("lru_gate_w").ap[j][:, :, n].rearrange("d g kk m -> kk (d g) m")
            gw = k.wr.t[:, s2, 0:512].rearrange("p (a m) -> p a m", m=128)
            P.dma("pool", k.wr.r(gw, s2), R(gsrc, k.dram("lru_gate_w").keys))
            i0 = k.identf.t[:].unsqueeze(1).to_broadcast([128, 4, 128])
            i1 = sm.t[:, n, 0:4].unsqueeze(2).to_broadcast([128, 4, 128])
            P.tt("pool", dgl.r(), k.identf.r(i0), sm.r(i1), ALU.mult)
            for b in range(NBLK):
                bk = P.bank()
                for kk in range(8):
                    P.mm(psb(k, bk), k.wr.r(wgb[:, kk, :], s), hs(k, kk, b), start=(kk == 0), stop=(kk == 7))
                P.act(gg.r(gg.t[:, b * BLK:(b + 1) * BLK]), psb(k, bk), AF.Gelu_apprx_tanh)
            for b in range(NBLK):
                bk = P.bank()
                for kk in range(8):
                    P.mm(psb(k, bk), k.wr.r(wxr[:, kk, :], sx), hs(k, kk, b), start=(kk == 0), stop=(kk == 7))
                dst, _, _ = conv_dst(xpad.t, b)
                P.copy("dve", xpad.r(dst), k.ps.r(psview(k, bk, b), bk))
            for b in range(NBLK):
                bk = P.bank()
                for kk in range(4):
                    P.mm(k.ps.r(psview(k, bk, b), bk), dgl.r(dgl.t[:, kk, :]), xpad.r(conv_rhs(xpad.t, b, kk)), start=(kk == 0), stop=(kk == 3))
                P.act(xrb.r(xrb.t[:, b * BLK:(b + 1) * BLK]), psb(k, bk), AF.Identity, bias=sm.r(sm.t[:, n, 4:5]))
            for d in range(2):
                for b in range(NBLK):
                    for g in range(2):
                        bk = P.bank()
                        P.mm(psb(k, bk), k.wr.r(gw[:, d * 2 + g, :], s2), xrb.r(xrb.t[:, b * BLK:(b + 1) * BLK]))
                        dstb = A if g == 0 else Ib
                        P.act(dstb.r(dstb.t[:, b * BLK:(b + 1) * BLK]), psb(k, bk), AF.Sigmoid, bias=sm.r(sm.t[:, n, 5 + d * 2 + g:6 + d * 2 + g]))
                P.act(A.r(), A.r(), AF.Exp, scale=sp.r(sp.t[:, n, d:d + 1]))
                P.tt("dve", T.r(), A.r(), A.r(), ALU.mult)
                P.act(T.r(), T.r(), AF.Sqrt, bias=one.r(), scale=-1.0)
                P.tt("dve", T.r(), T.r(), Ib.r(), ALU.mult)
                P.tt("dve", T.r(), T.r(), xrb.r(), ALU.mult)
                for si, (t0, n_t) in enumerate(SEGS):
                    if d == 0:
                        o_, a_, b_ = H[0].t[:, t0:t0 + n_t], A.t[:, t0:t0 + n_t], T.t[:, t0:t0 + n_t]
                    else:
                        lo = t0 - 1 if t0 > 0 else None
                        o_, a_, b_ = H[1].t[:, t0 + n_t - 1:lo:-1], A.t[:, t0 + n_t - 1:lo:-1], T.t[:, t0 + n_t - 1:lo:-1]
                    if si == 2:
                        init = s0.t[:, n, d:d + 1]
                        rd = [A.r(), T.r(), s0.r()]
                    else:
                        init = 0.0
                        rd = [A.r(), T.r()]
                    P.op("dve", lambda e, o_=o_, a_=a_, b_=b_, init=init: e.tensor_tensor_scan(out=o_, data0=a_, data1=b_, initial=init, op0=ALU.mult, op1=ALU.add),
                         reads=rd, writes=[H[d].r()])
                    if si < 2:
                        tl = t0 + n_t - 1 if d == 0 else t0
                        P.copy("act", sto.r(sto.t[:, n, si, d:d + 1]), H[d].r(H[d].t[:, tl:tl + 1]))
            P.tt("dve", H[0].r(), H[0].r(), H[1].r(), ALU.add)
            P.tt("dve", ob.r(ob.t[:, jj, :], jj), H[0].r(), gg.r(), ALU.mult)
        outproj_acc(k, k.dram("lru_w_out"), j, g0 * 128, len(tiles), lambda jj, b: ob.r(ob.t[:, jj, b * BLK:(b + 1) * BLK], jj))
    P.dma("sp", k.dram("lru_out"), sto.r())
    P.barrier()
    P.release(mk)


NCH = 20
SEG_CH = [(0, 2), (2, 2), (4, 16)]


def seg_of_chunk(c):
    return 0 if c < 2 else (1 if c < 4 else 2)


def hch(k, kk, c):
    b = c // 4
    return k.h.r(k.h.t[:, kk, c * 128:(c + 1) * 128], kk * 5 + b)


def ssd(k, j):
    P = k.P
    mk = P.mark()
    y_tm = P.buf("s_ytm", [128, NCH, 512], BF16, nsub=NCH)
    xs_tm = P.buf("s_xstm", [128, NCH, 256], BF16, nsub=NCH)
    Bfm = P.buf("s_bfm", [128, NT], BF16)
    Cfm = P.buf("s_cfm", [128, NT], BF16)
    wz = P.buf("s_wz", [128, 8, 256], BF16)
    wdt = P.buf("s_wdt", [128, 8, 64], BF16)
    cw = P.buf("s_cw", [128, 24, 5], F32)
    rows = P.buf("s_rows", [128, 160], F32)
    nrm = P.buf("s_nrm", [128, 512], BF16)
    masks = P.buf("s_masks", [128, 2, 128], F32)
    negb = P.buf("s_negb", [128, 2, 128], BF16)
    onesf = P.buf("s_onesf", [128, 128], F32)
    one1 = P.buf("s_one1", [128, 1], F32)
    eps1 = P.buf("s_eps1", [128, 1], F32)
    sc = {nm: P.buf("s_" + nm, [128, NCH, 8], F32) for nm in ["dt", "da", "nacum", "eac", "cdec", "ce"]}
    sc["tmp"] = sc["ce"]
    fm_tmp = Cfm
    ssq = P.buf("s_ssq", [128, NCH, 2], F32)
    rstd = P.buf("s_rstd", [128, NCH], F32)
    cbT = P.buf("s_cbT", [128, 2, 2, 128], BF16, nsub=2)
    dec = P.buf("s_dec", [128, 2, 4, 128], BF16, nsub=2)
    ST = P.buf("s_ST", [128, 2, 256], F32, nsub=2)
    STb = P.buf("s_STb", [128, 2, 256], BF16, nsub=2)
    t1 = P.buf("s_t1", [128, 1, 256], F32, nsub=1)
    t2 = P.buf("s_t2", [128, 1, 256], F32, nsub=1)
    sz = t2
    sig = P.buf("s_sig", [128, BLK], BF16)
    ncb = P.buf("s_ncb", [128, 24], F32)
    junk = sig
    sto = P.buf("s_sto", [128, 2, 128], F32, nsub=2)

    P.dma("sp", cw.r(), k.dram("ssd_cw"))
    P.dma("sp", rows.r(), k.dram("ssd_rows"))
    P.dma("sp", masks.r(), R(k.dram("cmask").ap[0:2].rearrange("a p m -> p a m"), k.dram("cmask").keys))
    P.dma("pool", negb.r(), R(k.dram("cmask").ap[2:4].rearrange("a p m -> p a m"), k.dram("cmask").keys))
    P.memset("dve", onesf.r(), 1.0)
    P.memset("dve", one1.r(), 1.0)
    P.ts("dve", ncb.r(), cw.r(cw.t[:, :, 4]), -1.0, ALU.mult)
    P.memset("dve", eps1.r(), 1e-5)
    P.act(rows.r(rows.t[:, 64:128]), rows.r(rows.t[:, 64:128]), AF.Exp)
    P.ts("dve", rows.r(rows.t[:, 64:128]), rows.r(rows.t[:, 64:128]), -1.0, ALU.mult)
    src = k.dram("ssd_w_in").ap[j][:, 5120:5184].rearrange("(kk p) c -> p kk c", p=128)
    P.dma("pool", wdt.r(), R(src, k.dram("ssd_w_in").keys))

    loc = {}

    def conv_tile(col0, cidx, dst_writer):
        xpad, dgl = loc["xpad"], loc["dgl"]
        s, w = load_in_w(k, k.dram("ssd_w_in"), j, col0)
        i0 = k.identf.t[:].unsqueeze(1).to_broadcast([128, 4, 128])
        i1 = cw.t[:, cidx, 0:4].unsqueeze(2).to_broadcast([128, 4, 128])
        P.tt("pool", dgl.r(), k.identf.r(i0), cw.r(i1), ALU.mult)
        for b in range(NBLK):
            bk = P.bank()
            for kk in range(8):
                P.mm(psb(k, bk), k.wr.r(w[:, kk, :], s), hs(k, kk, b), start=(kk == 0), stop=(kk == 7))
            dst, _, _ = conv_dst(xpad.t, b)
            P.copy("dve", xpad.r(dst), k.ps.r(psview(k, bk, b), bk))
        for b in range(NBLK):
            bk = P.bank()
            for kk in range(4):
                P.mm(k.ps.r(psview(k, bk, b), bk), dgl.r(dgl.t[:, kk, :]), xpad.r(conv_rhs(xpad.t, b, kk)), start=(kk == 0), stop=(kk == 3))
            P.act(sig.r(), psb(k, bk), AF.Exp, bias=ncb.r(ncb.t[:, cidx:cidx + 1]), scale=-1.0)
            P.act(sig.r(), sig.r(), AF.Ln, bias=one1.r())
            P.act(sig.r(), sig.r(), AF.Exp, scale=-1.0)
            P.stt(dst_writer(b), psb(k, bk), cw.r(cw.t[:, cidx, 4:5]), sig.r(), ALU.add, ALU.mult)

    stop = k.cfg.get("ssd_stop", 99)
    for hg in range(k.cfg.get("ssd_nhg", 8)):
        g, half = hg // 2, hg % 2
        src = k.dram("ssd_w_in").ap[j][:, hg * 256:(hg + 1) * 256].rearrange("(kk p) c -> p kk c", p=128)
        P.dma("pool", wz.r(), R(src, k.dram("ssd_w_in").keys))
        if half == 0:
            P.dma("pool", nrm.r(), R(k.dram("ssd_nrm").ap[:, g * 512:(g + 1) * 512], k.dram("ssd_nrm").keys))
        if stop <= 0.5:
            continue
        bk = P.bank()
        for c in range(NCH):
            for kk in range(8):
                rhs = wdt.t[:, kk, :].rearrange("p (d h) -> p d h", d=2)[:, :, hg * 4:hg * 4 + 4]
                out = k.ps.t[:, bk, c * 8:(c + 1) * 8].rearrange("p (d h) -> p d h", d=2)
                P.mm(k.ps.r(out, bk), hch(k, kk, c), wdt.r(rhs), start=(kk == 0), stop=(kk == 7))
        psv = k.ps.t[:, bk, 0:NCH * 8].rearrange("p (c d h) -> p c d h", d=2, h=4)
        if stop <= 0.7:
            continue

        def rowbc(off):
            return rows.t[:, off:off + 64].rearrange("p (d h) -> p d h", d=2)[:, :, hg * 4:hg * 4 + 4].unsqueeze(1).to_broadcast([128, NCH, 2, 4])

        def v4(bf):
            return bf.t.rearrange("p c (d h) -> p c d h", d=2)
        tmp, dt, da = sc["tmp"], sc["dt"], sc["da"]
        P.tt("dve", tmp.r(v4(tmp)), k.ps.r(psv, bk), rows.r(rowbc(0)), ALU.add)
        P.ts("dve", dt.r(), tmp.r(), -1.0, ALU.mult)
        P.tt("dve", dt.r(), dt.r(), tmp.r(), ALU.max)
        P.act(dt.r(), dt.r(), AF.Exp, scale=-1.0)
        P.act(dt.r(), dt.r(), AF.Ln, bias=one1.r())
        P.ts("dve", tmp.r(), tmp.r(), 0.0, ALU.max)
        P.tt("dve", dt.r(), dt.r(), tmp.r(), ALU.add)
        P.tt("dve", da.r(v4(da)), dt.r(v4(dt)), rows.r(rowbc(64)), ALU.mult)
        if stop <= 0.8:
            continue
        bk2 = P.bank()
        bk3 = P.bank()
        for c in range(NCH):
            for d in range(2):
                P.mm(k.ps.r(k.ps.t[:, bk2, c * 8 + d * 4:c * 8 + d * 4 + 4], bk2), masks.r(masks.t[:, d, :]), da.r(da.t[:, c, d * 4:d * 4 + 4]))
            P.mm(k.ps.r(k.ps.t[:, bk3, c * 8:(c + 1) * 8], bk3), onesf.r(), da.r(da.t[:, c, :]))
        nacum, eac, cdec, ce = sc["nacum"], sc["eac"], sc["cdec"], sc["ce"]
        ps2 = k.ps.t[:, bk2, 0:NCH * 8].rearrange("p (c e) -> p c e", e=8)
        ps3 = k.ps.t[:, bk3, 0:NCH * 8].rearrange("p (c e) -> p c e", e=8)
        if stop <= 0.9:
            continue
        P.ts("dve", nacum.r(), k.ps.r(ps2, bk2), -1.0, ALU.mult)
        P.act(eac.r(), nacum.r(), AF.Exp, scale=-1.0)
        if stop <= 0.95:
            continue
        P.copy("dve", cdec.r(), k.ps.r(ps3, bk3))
        P.tt("dve", ce.r(), cdec.r(), nacum.r(), ALU.add)
        if stop <= 0.96:
            continue
        P.act(cdec.r(), cdec.r(), AF.Exp)
        if stop <= 0.97:
            continue
        P.act(ce.r(), ce.r(), AF.Exp)
        P.tt("dve", ce.r(), ce.r(), dt.r(), ALU.mult)
        if stop <= 1:
            continue
        mkp = P.mark()
        loc["xpad"] = P.buf("s_xpad", [128, CPW + 1], BF16)
        loc["dgl"] = P.buf("s_dg", [128, 4, 128], BF16)
        P.memset("pool", loc["xpad"].r(), 0.0)
        for ti in range(2):
            conv_tile(2048 + hg * 256 + ti * 128, hg * 2 + ti, lambda b: fm_tmp.r(fm_tmp.t[:, b * BLK:(b + 1) * BLK]))
            for c0 in range(0, NCH, 8):
                bk = P.bank()
                pb = k.ps.t[:, bk, :].bitcast(BF16)
                n = min(8, NCH - c0)
                for ci in range(n):
                    c = c0 + ci
                    P.tr(k.ps.r(pb[:, ci * 128:(ci + 1) * 128], bk), fm_tmp.r(fm_tmp.t[:, c * 128:(c + 1) * 128]), k.identb.r())
                dst = xs_tm.t[:, c0:c0 + n, ti * 128:(ti + 1) * 128]
                srcv = pb[:, 0:n * 128].rearrange("p (c m) -> p c m", m=128)
                P.copy("act", xs_tm.r(dst, range(c0, c0 + n)), k.ps.r(srcv, bk))
        conv_tile(2048 + 2048 + g * 128, 16 + g, lambda b: Bfm.r(Bfm.t[:, b * BLK:(b + 1) * BLK]))
        conv_tile(2048 + 2560 + g * 128, 20 + g, lambda b: Cfm.r(Cfm.t[:, b * BLK:(b + 1) * BLK]))
        if stop <= 2:
            continue
        P.barrier()
        P.release(mkp)
        mku = P.mark()
        Btm = P.buf("s_btm", [128, 4, 128], BF16, nsub=4)
        Mb = P.buf("s_M", [128, 4, 4, 128], BF16, nsub=4)
        xd = P.buf("s_xd", [128, 4, 256], BF16, nsub=4)
        xdt = P.buf("s_xdt", [128, 4, 256], BF16, nsub=4)
        nsteps = k.cfg.get("ssd_steps", NCH)

        def prelim_gen(step):
            banks = {}
            units = [(d, (step if d == 0 else NCH - 1 - step), (step % 2) * 2 + d) for d in range(2)]
            for (d, c, sl) in units:
                tok = slice(c * 128, (c + 1) * 128)
                bk = d * 3
                banks[("cb", d)] = bk
                P.mm(k.ps.r(k.ps.t[:, bk, 0:128], bk), Bfm.r(Bfm.t[:, tok]), Cfm.r(Cfm.t[:, tok]))
                bkt = d * 3 + 1
                banks[("bt", d)] = bkt
                pbb = k.ps.t[:, bkt, :].bitcast(BF16)
                P.tr(k.ps.r(pbb[:, 0:128], bkt), Bfm.r(Bfm.t[:, tok]), k.identb.r())
                yield
            for (d, c, sl) in units:
                bk = banks[("cb", d)]
                bkt = banks[("bt", d)]
                pbb = k.ps.t[:, bkt, :].bitcast(BF16)
                P.tt("dve", cbT.r(cbT.t[:, d, d], d), k.ps.r(k.ps.t[:, bk, 0:128], bk), masks.r(masks.t[:, d, :]), ALU.mult)
                P.copy("act", Btm.r(Btm.t[:, sl], sl), k.ps.r(pbb[:, 0:128], bkt))
                xsv = xs_tm.t[:, c, :].rearrange("p (q e) -> p q e", e=64)
                dtb = dt.t[:, c, d * 4:d * 4 + 4].unsqueeze(2).to_broadcast([128, 4, 64])
                ceb = ce.t[:, c, d * 4:d * 4 + 4].unsqueeze(2).to_broadcast([128, 4, 64])
                P.tt("pool", xd.r(xd.t[:, sl].rearrange("p (q e) -> p q e", e=64), sl), xs_tm.r(xsv, c), dt.r(dtb), ALU.mult)
                P.tt("pool", xdt.r(xdt.t[:, sl].rearrange("p (q e) -> p q e", e=64), sl), xs_tm.r(xsv, c), ce.r(ceb), ALU.mult)
                yield
            for (d, c, sl) in units:
                bk = d * 3 + 2
                banks[("R", d)] = bk
                for q in range(4):
                    col = d * 4 + q
                    lhs = da.t[:, c, col:col + 1].to_broadcast([128, 128])
                    o_ = k.ps.r(k.ps.t[:, bk, q * 128:(q + 1) * 128], bk)
                    P.mm(o_, da.r(lhs), masks.r(masks.t[:, d, :]), start=True, stop=False)
                    P.mm(o_, k.identb.r(), negb.r(negb.t[:, d, :]), start=False, stop=True)
                yield
            for (d, c, sl) in units:
                bk = banks[("R", d)]
                for q in range(4):
                    col = d * 4 + q
                    P.act(dec.r(dec.t[:, d, q, :], d), k.ps.r(k.ps.t[:, bk, q * 128:(q + 1) * 128], bk), AF.Exp, bias=nacum.r(nacum.t[:, c, col:col + 1]))
                    yield
            for (d, c, sl) in units:
                cb_b = cbT.t[:, d, d].unsqueeze(1).to_broadcast([128, 4, 128])
                P.tt("dve", Mb.r(Mb.t[:, sl], sl), dec.r(dec.t[:, d], d), cbT.r(cb_b, d), ALU.mult)
                yield

        cbs = [0]

        def cbank():
            cbs[0] ^= 1
            return 6 + cbs[0]

        def chain_gen(step):
            for d in range(2):
                c = step if d == 0 else NCH - 1 - step
                sl = (step % 2) * 2 + d
                first_visit = (c <= 9) if d == 0 else (c >= 10)
                if nsteps < NCH:
                    first_visit = True
                seg = seg_of_chunk(c)
                c_first, c_n = SEG_CH[seg]
                seg_start = (c == c_first) if d == 0 else (c == c_first + c_n - 1)
                seg_end = (c == c_first + c_n - 1) if d == 0 else (c == c_first)
                tok = slice(c * 128, (c + 1) * 128)
                if seg_start:
                    if seg == 2:
                        P.dma("sp", ST.r(ST.t[:, d], d), R(k.dram("ssd_s0").ap[d, hg], k.dram("ssd_s0").keys))
                    else:
                        P.memset("dve", ST.r(ST.t[:, d], d), 0.0)
                    P.copy("act", STb.r(STb.t[:, d], d), ST.r(ST.t[:, d], d))
                    yield
                xsv = xs_tm.t[:, c, :].rearrange("p (q e) -> p q e", e=64)
                bkY = cbank()
                for q in range(4):
                    P.mm(k.ps.r(k.ps.t[:, bkY, q * 64:(q + 1) * 64], bkY), Mb.r(Mb.t[:, sl, q, :], sl), xd.r(xd.t[:, sl, q * 64:(q + 1) * 64], sl))
                P.mm(k.ps.r(k.ps.t[:, bkY, 256:512], bkY), Cfm.r(Cfm.t[:, tok]), STb.r(STb.t[:, d], d))
                bkS = cbank()
                P.mm(k.ps.r(k.ps.t[:, bkS, 0:256], bkS), Btm.r(Btm.t[:, sl], sl), xdt.r(xdt.t[:, sl], sl))
                yield
                cdb = cdec.t[:, c, d * 4:d * 4 + 4].unsqueeze(2).to_broadcast([128, 4, 64])
                STv = ST.t[:, d].rearrange("p (q e) -> p q e", e=64)
                P.tt("dve", ST.r(STv, d), ST.r(STv, d), cdec.r(cdb), ALU.mult)
                P.tt("dve", ST.r(ST.t[:, d], d), ST.r(ST.t[:, d], d), k.ps.r(k.ps.t[:, bkS, 0:256], bkS), ALU.add)
                yield
                P.copy("act", STb.r(STb.t[:, d], d), ST.r(ST.t[:, d], d))
                yield
                eab = eac.t[:, c, d * 4:d * 4 + 4].unsqueeze(2).to_broadcast([128, 4, 64])
                t1v = t1.t[:, 0].rearrange("p (q e) -> p q e", e=64)
                P.tt("dve", t1.r(t1v, 0), k.ps.r(k.ps.t[:, bkY, 256:512].rearrange("p (q e) -> p q e", e=64), bkY), eac.r(eab), ALU.mult)
                P.tt("dve", t1.r(t1.t[:, 0], 0), t1.r(t1.t[:, 0], 0), k.ps.r(k.ps.t[:, bkY, 0:256], bkY), ALU.add)
                yield
                ysl = y_tm.r(y_tm.t[:, c, half * 256:(half + 1) * 256], c)
                if first_visit:
                    P.copy("act", ysl, t1.r(t1.t[:, 0], 0))
                    yield
                else:
                    P.tt("dve", t1.r(t1.t[:, 0], 0), t1.r(t1.t[:, 0], 0), ysl, ALU.add)
                    dsb = rows.t[:, 128 + hg * 4:128 + hg * 4 + 4].unsqueeze(2).to_broadcast([128, 4, 64])
                    P.tt("pool", t2.r(t2.t[:, 0].rearrange("p (q e) -> p q e", e=64), 0), xs_tm.r(xsv, c), rows.r(dsb), ALU.mult)
                    yield
                    P.tt("dve", t1.r(t1.t[:, 0], 0), t1.r(t1.t[:, 0], 0), t2.r(t2.t[:, 0], 0), ALU.add)
                    bkZ = cbank()
                    for kk in range(8):
                        P.mm(k.ps.r(k.ps.t[:, bkZ, 0:256], bkZ), hch(k, kk, c), wz.r(wz.t[:, kk, :]), start=(kk == 0), stop=(kk == 7))
                    yield
                    P.act(t2.r(t2.t[:, 0], 0), k.ps.r(k.ps.t[:, bkZ, 0:256], bkZ), AF.Exp, scale=-1.0)
                    P.act(t2.r(t2.t[:, 0], 0), t2.r(t2.t[:, 0], 0), AF.Ln, bias=one1.r())
                    P.act(t2.r(t2.t[:, 0], 0), t2.r(t2.t[:, 0], 0), AF.Exp, scale=-1.0)
                    yield
                    P.tt("dve", t1.r(t1.t[:, 0], 0), t1.r(t1.t[:, 0], 0), t2.r(t2.t[:, 0], 0), ALU.mult)
                    P.tt("dve", t1.r(t1.t[:, 0], 0), t1.r(t1.t[:, 0], 0), k.ps.r(k.ps.t[:, bkZ, 0:256], bkZ), ALU.mult)
                    yield
                    P.copy("dve", ysl, t1.r(t1.t[:, 0], 0))
                    o_, i_, a_ = junk.t[:, 0:256], t1.t[:, 0], ssq.t[:, c, half:half + 1]
                    P.op("act", lambda e, o_=o_, i_=i_, a_=a_: e.activation(out=o_, in_=i_, func=AF.Square, accum_out=a_),
                         reads=[t1.r(t1.t[:, 0], 0)], writes=[junk.r(), ssq.r()])
                    yield
                if seg_end and seg < 2:
                    for pr in range(2):
                        bk = cbank()
                        P.tr(k.ps.r(k.ps.t[:, bk, 0:128], bk), ST.r(ST.t[:, d, pr * 128:(pr + 1) * 128], d), k.identf.r())
                        P.copy("dve", sto.r(sto.t[:, pr], pr), k.ps.r(k.ps.t[:, bk, 0:128], bk))
                        r0 = (hg * 4 + pr * 2) * 64
                        P.dma("sp", R(k.dram("ssd_out").ap[seg, d, r0:r0 + 128, :], k.dram("ssd_out").keys), sto.r(sto.t[:, pr], pr))
                    yield

        def run_all(gen):
            for _ in gen:
                pass

        def merge(pg, cg, ratio):
            pdone = cdone = False
            while not (pdone and cdone):
                if not cdone:
                    try:
                        next(cg)
                    except StopIteration:
                        cdone = True
                for _ in range(ratio):
                    if pdone:
                        break
                    try:
                        next(pg)
                    except StopIteration:
                        pdone = True

        run_all(prelim_gen(0))
        for step in range(nsteps):
            if step + 1 < nsteps:
                merge(prelim_gen(step + 1), chain_gen(step), k.cfg.get("ssd_ratio", 1))
            else:
                run_all(chain_gen(step))
        P.barrier()
        P.release(mku)
        if half == 1 and stop > 3:
            P.tt("dve", rstd.r(), ssq.r(ssq.t[:, :, 0]), ssq.r(ssq.t[:, :, 1]), ALU.add)
            P.act(rstd.r(), rstd.r(), AF.Ln, bias=eps1.r(), scale=1.0 / 512)
            P.act(rstd.r(), rstd.r(), AF.Exp, scale=-0.5)
            slots = [load_out_w(k, k.dram("ssd_w_out"), j, g * 512 + ti * 128) for ti in range(4)]
            ofm_t = xs_tm.t[:, 0:8, :].rearrange("p c e -> p (c e)").rearrange("p (t m) -> p t m", m=512)
            yn_t = xs_tm.t[:, 8:12, :].rearrange("p c e -> p (c e)").rearrange("p (t m) -> p t m", m=512)
            OK_ = list(range(0, 8))
            YK_ = list(range(8, 12))
            for b in range(NBLK):
                for ci in range(4):
                    c = b * 4 + ci
                    yb = ci % 2
                    P.stt(xs_tm.r(yn_t[:, yb], YK_), y_tm.r(y_tm.t[:, c, :], c), rstd.r(rstd.t[:, c:c + 1]), nrm.r(), ALU.mult, ALU.mult)
                    bk = P.bank()
                    pbb = k.ps.t[:, bk, :].bitcast(BF16)
                    for ti in range(4):
                        P.tr(k.ps.r(pbb[:, ti * 128:(ti + 1) * 128], bk), xs_tm.r(yn_t[:, yb, ti * 128:(ti + 1) * 128], YK_), k.identb.r())
                    srcv = pbb[:, 0:512].rearrange("p (t m) -> p t m", m=128)
                    P.copy("act", xs_tm.r(ofm_t[:, :, ci * 128:(ci + 1) * 128], OK_), k.ps.r(srcv, bk))
                for m in range(8):
                    bk = P.bank()
                    for ti in range(4):
                        s_, wo = slots[ti]
                        P.mm(psb(k, bk), k.wr.r(wo[:, m * 128:(m + 1) * 128], s_), xs_tm.r(ofm_t[:, ti, :], OK_), start=(ti == 0), stop=(ti == 3))
                    cc = bc(b)
                    P.stt(xs(k, m, b), psb(k, bk), k.mod.r(k.mod.t[:, Q_G1 + m, cc:cc + 1]), xs(k, m, b), ALU.mult, ALU.add)
    P.barrier()
    P.release(mk)


def gdn(k, j):
    P = k.P
    mk = P.mark()
    W_in = k.dram("gdn_w_in%d" % j)
    W_out = k.dram("gdn_w_out%d" % j)
    kq = P.buf("g_kq", [128, NCH, 2, 128], BF16, nsub=NCH)
    v_fm = P.buf("g_vfm", [128, NT], BF16, nsub=NCH)
    v_tm = P.buf("g_vtm", [128, NCH, 128], BF16, nsub=NCH)
    k_tm = P.buf("g_ktm", [128, NCH, 128], BF16, nsub=NCH)
    o_tm = P.buf("g_otm", [128, NCH, 128], BF16, nsub=NCH)
    wz = P.buf("g_wz", [128, 8, 128], BF16)
    wsm = P.buf("g_wsm", [128, 8, 32], BF16)
    cw = P.buf("g_cw", [128, 24, 4], F32)
    rows = P.buf("g_rows", [128, 160], F32)
    masks = P.buf("g_masks", [128, 2, 128], F32)
    negb = P.buf("g_negb", [128, 2, 128], BF16)
    lvl = P.buf("g_lvl", [128, 2, 7, 2, 128], BF16)
    I2 = P.buf("g_I2", [128, 2, 128], BF16)
    onesf = P.buf("g_onesf", [128, 128], F32)
    c_one = P.buf("g_c1", [128, 1], F32)
    c_eps6 = P.buf("g_c2", [128, 1], F32)
    c_eps6q = P.buf("g_c3", [128, 1], F32)
    c_eps5 = P.buf("g_c4", [128, 1], F32)
    sc = {nm: P.buf("g_" + nm, [128, NCH, 2], F32) for nm in ["beta", "g", "ngc", "negegc", "eout", "egl", "tmp"]}
    Rp = P.buf("g_Rp", [128, 2, 128], BF16, nsub=2)
    vn = P.buf("g_vn", [128, 2, 128], BF16, nsub=2)
    S = P.buf("g_S", [128, 2, 128], F32, nsub=2)
    Sb = P.buf("g_Sb", [128, 2, 128], BF16, nsub=2)
    ot2 = P.buf("g_ot", [128, 2, 128], F32, nsub=2)
    sz2 = P.buf("g_sz", [128, 2, 128], F32, nsub=2)
    ogb2 = P.buf("g_og", [128, 2, 128], BF16, nsub=2)
    junk2 = P.buf("g_junk", [128, 2, 128], BF16, nsub=2)
    ssq2 = P.buf("g_ssq", [128, 2, 2], F32, nsub=2)

    P.dma("sp", cw.r(), R(k.dram("gdn_cw").ap[j], k.dram("gdn_cw").keys))
    P.dma("sp", rows.r(), R(k.dram("gdn_rows").ap[j], k.dram("gdn_rows").keys))
    P.dma("sp", masks.r(), R(k.dram("cmask").ap[0:2].rearrange("a p m -> p a m"), k.dram("cmask").keys))
    P.dma("pool", negb.r(), R(k.dram("cmask").ap[2:4].rearrange("a p m -> p a m"), k.dram("cmask").keys))
    for d in range(2):
        P.dma("pool", lvl.r(lvl.t[:, d]), R(k.dram("glvl").ap[d].rearrange("l a p m -> p l a m"), k.dram("glvl").keys))
    for a in range(2):
        P.copy("dve", I2.r(I2.t[:, a, :]), k.identf.r())
    P.memset("dve", onesf.r(), 1.0)
    P.memset("dve", c_one.r(), 1.0)
    P.memset("dve", c_eps6.r(), 1e-6)
    P.memset("dve", c_eps6q.r(), 128e-6)
    P.memset("dve", c_eps5.r(), 1e-5)
    P.act(rows.r(rows.t[:, 16:32]), rows.r(rows.t[:, 16:32]), AF.Exp)
    P.ts("dve", rows.r(rows.t[:, 16:32]), rows.r(rows.t[:, 16:32]), -1.0, ALU.mult)
    src = W_in.ap[0][:, 4096:4128].rearrange("(kk p) c -> p kk c", p=128)
    P.dma("pool", wsm.r(), R(src, W_in.keys))

    loc = {}

    def conv_tile(col0, cidx, post):
        xpad, dgl = loc["xpad"], loc["dgl"]
        s, w = load_in_w(k, W_in, 0, col0)
        i0 = k.identf.t[:].unsqueeze(1).to_broadcast([128, 4, 128])
        i1 = cw.t[:, cidx, 0:4].unsqueeze(2).to_broadcast([128, 4, 128])
        P.tt("pool", dgl.r(), k.identf.r(i0), cw.r(i1), ALU.mult)
        for b in range(NBLK):
            bk = P.bank()
            for kk in range(8):
                P.mm(psb(k, bk), k.wr.r(w[:, kk, :], s), hs(k, kk, b), start=(kk == 0), stop=(kk == 7))
            dst, _, _ = conv_dst(xpad.t, b)
            P.copy("dve", xpad.r(dst), k.ps.r(psview(k, bk, b), bk))
        for b in range(NBLK):
            bk = P.bank()
            for kk in range(4):
                P.mm(k.ps.r(psview(k, bk, b), bk), dgl.r(dgl.t[:, kk, :]), xpad.r(conv_rhs(xpad.t, b, kk)), start=(kk == 0), stop=(kk == 3))
            post(b, bk)

    def norm_post(which, scale, epsb):
        def post(b, bk):
            qf, sqb, rn = loc["qf"], loc["sqb"], loc["rn"]
            P.act(qf.r(), psb(k, bk), AF.Exp, scale=-1.0)
            P.act(qf.r(), qf.r(), AF.Ln, bias=c_one.r())
            P.act(qf.r(), qf.r(), AF.Exp, scale=-1.0)
            P.tt("dve", qf.r(), psb(k, bk), qf.r(), ALU.mult)
            P.act(sqb.r(), qf.r(), AF.Square)
            b2 = P.bank()
            P.mm(psb(k, b2), k.onesb.r(), sqb.r())
            P.act(rn.r(), psb(k, b2), AF.Ln, bias=epsb.r(), scale=scale)
            P.act(rn.r(), rn.r(), AF.Exp, scale=-0.5)
            dst = kq.t[:, 4 * b:4 * b + 4, which, :]
            P.tt("dve", kq.r(dst, range(4 * b, 4 * b + 4)), qf.r(qf.t.rearrange("p (c m) -> p c m", m=128)), rn.r(rn.t.rearrange("p (c m) -> p c m", m=128)), ALU.mult)
        return post

    def v_post(b, bk):
        qf = loc["qf"]
        P.act(qf.r(), psb(k, bk), AF.Exp, scale=-1.0)
        P.act(qf.r(), qf.r(), AF.Ln, bias=c_one.r())
        P.act(qf.r(), qf.r(), AF.Exp, scale=-1.0)
        P.tt("dve", v_fm.r(v_fm.t[:, b * BLK:(b + 1) * BLK], range(4 * b, 4 * b + 4)), psb(k, bk), qf.r(), ALU.mult)

    nheads = k.cfg.get("gdn_heads", 8)
    for hh in range(nheads):
        src = W_in.ap[0][:, 3072 + hh * 128:3072 + (hh + 1) * 128].rearrange("(kk p) c -> p kk c", p=128)
        P.dma("pool", wz.r(), R(src, W_in.keys))
        bk = P.bank()
        for c in range(NCH):
            for kk in range(8):
                rhs = wsm.t[:, kk, :].rearrange("p (t d h) -> p t d h", t=2, d=2)[:, :, :, hh]
                out = k.ps.t[:, bk, c * 4:(c + 1) * 4].rearrange("p (t d) -> p t d", t=2)
                P.mm(k.ps.r(out, bk), hch(k, kk, c), wsm.r(rhs), start=(kk == 0), stop=(kk == 7))
        psv = k.ps.t[:, bk, 0:NCH * 4].rearrange("p (c t d) -> p c t d", t=2, d=2)
        beta, g, ngc, negegc, eout, egl, tmp = sc["beta"], sc["g"], sc["ngc"], sc["negegc"], sc["eout"], sc["egl"], sc["tmp"]
        P.copy("dve", tmp.r(), k.ps.r(psv[:, :, 0, :], bk))
        P.act(beta.r(), tmp.r(), AF.Exp, scale=-1.0)
        P.act(beta.r(), beta.r(), AF.Ln, bias=c_one.r())
        P.act(beta.r(), beta.r(), AF.Exp, scale=-1.0)

        def rowbc(off):
            return rows.t[:, off:off + 16].rearrange("p (d h) -> p d h", d=2)[:, :, hh].unsqueeze(1).to_broadcast([128, NCH, 2])
        P.tt("dve", tmp.r(), k.ps.r(psv[:, :, 1, :], bk), rows.r(rowbc(0)), ALU.add)
        P.ts("dve", g.r(), tmp.r(), -1.0, ALU.mult)
        P.tt("dve", g.r(), g.r(), tmp.r(), ALU.max)
        P.act(g.r(), g.r(), AF.Exp, scale=-1.0)
        P.act(g.r(), g.r(), AF.Ln, bias=c_one.r())
        P.ts("dve", tmp.r(), tmp.r(), 0.0, ALU.max)
        P.tt("dve", g.r(), g.r(), tmp.r(), ALU.add)
        P.tt("dve", g.r(), g.r(), rows.r(rowbc(16)), ALU.mult)
        bk2 = P.bank()
        bk3 = P.bank()
        for c in range(NCH):
            for d in range(2):
                P.mm(k.ps.r(k.ps.t[:, bk2, c * 2 + d:c * 2 + d + 1], bk2), masks.r(masks.t[:, d, :]), g.r(g.t[:, c, d:d + 1]))
            P.mm(k.ps.r(k.ps.t[:, bk3, c * 2:c * 2 + 2], bk3), onesf.r(), g.r(g.t[:, c, :]))
        ps2 = k.ps.t[:, bk2, 0:NCH * 2].rearrange("p (c d) -> p c d", d=2)
        ps3 = k.ps.t[:, bk3, 0:NCH * 2].rearrange("p (c d) -> p c d", d=2)
        P.ts("dve", ngc.r(), k.ps.r(ps2, bk2), -1.0, ALU.mult)
        P.act(negegc.r(), ngc.r(), AF.Exp, scale=-1.0)
        P.ts("dve", negegc.r(), negegc.r(), -1.0, ALU.mult)
        P.copy("dve", egl.r(), k.ps.r(ps3, bk3))
        P.tt("dve", eout.r(), egl.r(), ngc.r(), ALU.add)
        P.act(eout.r(), eout.r(), AF.Exp)
        P.act(egl.r(), egl.r(), AF.Exp)
        mkp = P.mark()
        xpad = P.buf("g_xpad", [128, CPW + 1], BF16)
        dgl = P.buf("g_dg", [128, 4, 128], BF16)
        qf = P.buf("g_qf", [128, BLK], F32)
        sqb = P.buf("g_sq", [128, BLK], BF16)
        rn = P.buf("g_rn", [128, BLK], F32)
        loc.update(xpad=xpad, dgl=dgl, qf=qf, sqb=sqb, rn=rn)
        P.memset("pool", xpad.r(), 0.0)
        conv_tile(hh * 128, hh, norm_post(1, 128.0, c_eps6q))
        conv_tile(1024 + hh * 128, 8 + hh, norm_post(0, 1.0, c_eps6))
        conv_tile(2048 + hh * 128, 16 + hh, v_post)
        for (srcfn, dstb) in ((lambda c: v_fm.r(v_fm.t[:, c * 128:(c + 1) * 128], c), v_tm), (lambda c: kq.r(kq.t[:, c, 0, :], c), k_tm)):
            for c0 in range(0, NCH, 8):
                bk = P.bank()
                pb = k.ps.t[:, bk, :].bitcast(BF16)
                n = min(8, NCH - c0)
                for ci in range(n):
                    P.tr(k.ps.r(pb[:, ci * 128:(ci + 1) * 128], bk), srcfn(c0 + ci), k.identb.r())
                srcv = pb[:, 0:n * 128].rearrange("p (c m) -> p c m", m=128)
                P.copy("act", dstb.r(dstb.t[:, c0:c0 + n, :], range(c0, c0 + n)), k.ps.r(srcv, bk))
        P.barrier()
        P.release(mkp)
        mku = P.mark()
        GS = 2
        G = 2 * GS
        NR = 2 * G
        U = {}
        for nm in ["egcr", "dec", "NB", "NA"]:
            U[nm] = P.buf("gu_" + nm, [128, G, 128], BF16, nsub=G)
        U["Y"] = P.buf("gu_Y", [128, G, 2, 128], BF16, nsub=G)
        for nm in ["attn", "qin", "kout"]:
            U[nm] = P.buf("gu_" + nm, [128, NR, 128], BF16, nsub=NR)
        U["T"] = P.buf("gu_T", [128, NR, 2, 128], BF16, nsub=NR)
        RES = ("attn", "qin", "kout", "T")
        nsteps = k.cfg.get("gdn_steps", NCH)
        groups = [list(range(g0, min(g0 + GS, nsteps))) for g0 in range(0, nsteps, GS)]

        def ur(nm, ui, gi, sub=None):
            b_ = U[nm]
            si = (gi % 2) * G + ui if nm in RES else ui
            return b_.r(b_.t[:, si] if sub is None else b_.t[:, si, sub], si)

        def prelim_gen(gi):
            units = [(step, d) for step in groups[gi] for d in range(2)]
            cs = [(st_ if d_ == 0 else NCH - 1 - st_) for (st_, d_) in units]
            banks = {}
            for ui, (st_, d) in enumerate(units):
                c = cs[ui]
                bkR = ui
                banks[("R", ui)] = bkR
                gbc = g.t[:, c, d:d + 1].to_broadcast([128, 128])
                r0 = k.ps.r(k.ps.t[:, bkR, 0:128], bkR)
                r1 = k.ps.r(k.ps.t[:, bkR, 128:256], bkR)
                P.mm(r0, g.r(gbc), masks.r(masks.t[:, d, :]))
                P.mm(r1, g.r(gbc), masks.r(masks.t[:, d, :]), start=True, stop=False)
                P.mm(r1, k.identb.r(), negb.r(negb.t[:, d, :]), start=False, stop=True)
                yield
            for ui, (st_, d) in enumerate(units):
                c = cs[ui]
                bkR = banks[("R", ui)]
                P.act(ur("egcr", ui, gi), k.ps.r(k.ps.t[:, bkR, 0:128], bkR), AF.Exp)
                P.act(ur("dec", ui, gi), k.ps.r(k.ps.t[:, bkR, 128:256], bkR), AF.Exp, bias=ngc.r(ngc.t[:, c, d:d + 1]))
                yield
            for ui, (st_, d) in enumerate(units):
                c = cs[ui]
                bkK = ui
                banks[("K", ui)] = bkK
                P.mm(k.ps.r(k.ps.t[:, bkK, 0:256], bkK), kq.r(kq.t[:, c, 0, :], c), kq.r(kq.t[:, c, :, :].rearrange("p a m -> p (a m)"), c))
                yield
            for ui, (st_, d) in enumerate(units):
                c = cs[ui]
                bkK = banks[("K", ui)]
                P.stt(ur("NB", ui, gi), k.ps.r(k.ps.t[:, bkK, 0:128], bkK), beta.r(beta.t[:, c, d:d + 1]), ur("dec", ui, gi), ALU.mult, ALU.mult)
                P.tt("dve", ur("attn", ui, gi), k.ps.r(k.ps.t[:, bkK, 128:256], bkK), ur("dec", ui, gi), ALU.mult)
                yield
            for ui, (st_, d) in enumerate(units):
                bkT = ui
                banks[("T", ui)] = bkT
                pbT = k.ps.t[:, bkT, :].bitcast(BF16)
                P.tr(k.ps.r(pbT[:, 0:128], bkT), ur("NB", ui, gi), k.identb.r())
                yield
            for ui, (st_, d) in enumerate(units):
                c = cs[ui]
                bkT = banks[("T", ui)]
                pbT = k.ps.t[:, bkT, :].bitcast(BF16)
                P.copy("act", ur("NA", ui, gi), k.ps.r(pbT[:, 0:128], bkT))
                P.tt("pool", ur("qin", ui, gi), kq.r(kq.t[:, c, 1, :], c), ur("egcr", ui, gi), ALU.mult)
                P.act(ur("kout", ui, gi), k_tm.r(k_tm.t[:, c, :], c), AF.Identity, scale=eout.r(eout.t[:, c, d:d + 1]))
                yield
            for ui, (st_, d) in enumerate(units):
                P.tt("pool", ur("Y", ui, gi, 0), ur("NA", ui, gi), lvl.r(lvl.t[:, d, 0, 0]), ALU.mult)
                P.tt("pool", ur("Y", ui, gi, 1), ur("NB", ui, gi), lvl.r(lvl.t[:, d, 0, 1]), ALU.mult)
                P.tt("pool", ur("T", ui, gi), I2.r(), ur("Y", ui, gi), ALU.add)
                yield
            for lv in range(1, 7):
                last = (lv == 6)
                for ui, (st_, d) in enumerate(units):
                    bkY = ui
                    banks[("Y", ui)] = bkY
                    if not last:
                        P.mm(k.ps.r(k.ps.t[:, bkY, 0:128], bkY), ur("NB", ui, gi), ur("T", ui, gi, 0))
                    P.mm(k.ps.r(k.ps.t[:, bkY, 128:256], bkY), ur("NA", ui, gi), ur("T", ui, gi, 1))
                    yield
                for ui, (st_, d) in enumerate(units):
                    bkY = banks[("Y", ui)]
                    if not last:
                        P.tt("dve", ur("Y", ui, gi), k.ps.r(k.ps.t[:, bkY, 0:256].rearrange("p (a m) -> p a m", m=128), bkY), lvl.r(lvl.t[:, d, lv]), ALU.mult)
                    else:
                        P.tt("dve", ur("Y", ui, gi, 1), k.ps.r(k.ps.t[:, bkY, 128:256], bkY), lvl.r(lvl.t[:, d, lv, 1]), ALU.mult)
                    yield
                for ui, (st_, d) in enumerate(units):
                    bkZ = ui
                    banks[("Z", ui)] = bkZ
                    if not last:
                        P.mm(k.ps.r(k.ps.t[:, bkZ, 0:128], bkZ), ur("T", ui, gi, 1), ur("Y", ui, gi, 0))
                    P.mm(k.ps.r(k.ps.t[:, bkZ, 128:256], bkZ), ur("T", ui, gi, 0), ur("Y", ui, gi, 1))
                    yield
                for ui, (st_, d) in enumerate(units):
                    bkZ = banks[("Z", ui)]
                    if not last:
                        P.tt("dve", ur("T", ui, gi), ur("T", ui, gi), k.ps.r(k.ps.t[:, bkZ, 0:256].rearrange("p (a m) -> p a m", m=128), bkZ), ALU.add)
                    else:
                        P.tt("dve", ur("T", ui, gi, 1), ur("T", ui, gi, 1), k.ps.r(k.ps.t[:, bkZ, 128:256], bkZ), ALU.add)
                    yield

        cbs = [0, 0]

        def chain_gen(gi):
            units = [(step, d) for step in groups[gi] for d in range(2)]
            for s_i in range(len(groups[gi])):
                gens = [unit_chain(gi, 2 * s_i + d_, units[2 * s_i + d_][0], d_) for d_ in range(2)]
                alive = [True, True]
                while alive[0] or alive[1]:
                    for d_ in range(2):
                        if alive[d_]:
                            try:
                                next(gens[d_])
                            except StopIteration:
                                alive[d_] = False
                    yield

        def unit_chain(gi, ui, step, d):
            def cbank():
                cbs[d] ^= 1
                return 4 + 2 * d + cbs[d]
            class _V:
                def __init__(self, buf):
                    self.buf = buf
                    self.t = buf.t[:, d]

                def r(self, ap=None):
                    return self.buf.r(self.t if ap is None else ap, d)
            ot, sz, ogb, junk, ssq = _V(ot2), _V(sz2), _V(ogb2), _V(junk2), _V(ssq2)
            if True:
                c = step if d == 0 else NCH - 1 - step
                first_visit = (c <= 9) if d == 0 else (c >= 10)
                if nsteps < NCH:
                    first_visit = True
                seg = seg_of_chunk(c)
                c_first, c_n = SEG_CH[seg]
                seg_start = (c == c_first) if d == 0 else (c == c_first + c_n - 1)
                seg_end = (c == c_first + c_n - 1) if d == 0 else (c == c_first)
                if seg_start:
                    if seg == 2:
                        P.dma("sp", S.r(S.t[:, d], d), R(k.dram("gdn_s0").ap[j, d, hh], k.dram("gdn_s0").keys))
                    else:
                        P.memset("dve", S.r(S.t[:, d], d), 0.0)
                    P.copy("act", Sb.r(Sb.t[:, d], d), S.r(S.t[:, d], d))
                    yield
                kc = kq.r(kq.t[:, c, 0, :], c)
                bkC = cbank()
                P.mm(k.ps.r(k.ps.t[:, bkC, 0:128], bkC), kc, Sb.r(Sb.t[:, d], d))
                yield
                P.stt(Rp.r(Rp.t[:, d], d), k.ps.r(k.ps.t[:, bkC, 0:128], bkC), negegc.r(negegc.t[:, c, d:d + 1]), v_tm.r(v_tm.t[:, c, :], c), ALU.mult, ALU.add)
                yield
                bkV = cbank()
                P.mm(k.ps.r(k.ps.t[:, bkV, 0:128], bkV), ur("T", ui, gi, 1), Rp.r(Rp.t[:, d], d))
                yield
                P.act(vn.r(vn.t[:, d], d), k.ps.r(k.ps.t[:, bkV, 0:128], bkV), AF.Identity, scale=beta.r(beta.t[:, c, d:d + 1]))
                yield
                bkS = cbank()
                pS = k.ps.r(k.ps.t[:, bkS, 0:128], bkS)
                P.mm(pS, ur("kout", ui, gi), vn.r(vn.t[:, d], d))
                bkO = cbank()
                po = k.ps.r(k.ps.t[:, bkO, 0:128], bkO)
                P.mm(po, ur("qin", ui, gi), Sb.r(Sb.t[:, d], d), start=True, stop=False)
                P.mm(po, ur("attn", ui, gi), vn.r(vn.t[:, d], d), start=False, stop=True)
                yield
                P.stt(S.r(S.t[:, d], d), S.r(S.t[:, d], d), egl.r(egl.t[:, c, d:d + 1]), pS, ALU.mult, ALU.add)
                yield
                P.copy("act", Sb.r(Sb.t[:, d], d), S.r(S.t[:, d], d))
                yield
                if first_visit:
                    P.copy("act", o_tm.r(o_tm.t[:, c, :], c), po)
                    yield
                else:
                    P.tt("dve", ot.r(), po, o_tm.r(o_tm.t[:, c, :], c), ALU.add)
                    yield
                    o_, i_, a_ = junk.t, ot.t, ssq.t[:, 0:1]
                    P.op("act", lambda e, o_=o_, i_=i_, a_=a_: e.activation(out=o_, in_=i_, func=AF.Square, accum_out=a_),
                         reads=[ot.r()], writes=[junk.r(), ssq.r()])
                    P.act(ssq.r(ssq.t[:, 1:2]), ssq.r(ssq.t[:, 0:1]), AF.Ln, bias=c_eps5.r(), scale=1.0 / 128)
                    P.act(ssq.r(ssq.t[:, 1:2]), ssq.r(ssq.t[:, 1:2]), AF.Exp, scale=-0.5)
                    yield
                    bkZ = cbank()
                    for kk in range(8):
                        P.mm(k.ps.r(k.ps.t[:, bkZ, 0:128], bkZ), hch(k, kk, c), wz.r(wz.t[:, kk, :]), start=(kk == 0), stop=(kk == 7))
                    yield
                    P.act(sz.r(), k.ps.r(k.ps.t[:, bkZ, 0:128], bkZ), AF.Exp, scale=-1.0)
                    P.act(sz.r(), sz.r(), AF.Ln, bias=c_one.r())
                    P.act(sz.r(), sz.r(), AF.Exp, scale=-1.0)
                    P.stt(ot.r(), ot.r(), ssq.r(ssq.t[:, 1:2]), rows.r(rows.t[:, 32:160]), ALU.mult, ALU.mult)
                    yield
                    P.tt("dve", ot.r(), ot.r(), sz.r(), ALU.mult)
                    P.tt("dve", ogb.r(), ot.r(), k.ps.r(k.ps.t[:, bkZ, 0:128], bkZ), ALU.mult)
                    yield
                    bkT2 = cbank()
                    pbT2 = k.ps.t[:, bkT2, :].bitcast(BF16)
                    P.tr(k.ps.r(pbT2[:, 0:128], bkT2), ogb.r(), k.identb.r())
                    yield
                    P.copy("act", v_fm.r(v_fm.t[:, c * 128:(c + 1) * 128], c), k.ps.r(pbT2[:, 0:128], bkT2))
                    yield
                if seg_end and seg < 2:
                    P.dma("sp", R(k.dram("gdn_out").ap[j, seg, d, hh], k.dram("gdn_out").keys), S.r(S.t[:, d], d))

        def run_all(gen):
            for _ in gen:
                pass

        def merge(pg, cg, ratio):
            pdone = cdone = False
            while not (pdone and cdone):
                if not cdone:
                    try:
                        next(cg)
                    except StopIteration:
                        cdone = True
                for _ in range(ratio):
                    if pdone:
                        break
                    try:
                        next(pg)
                    except StopIteration:
                        pdone = True

        run_all(prelim_gen(0))
        for gi in range(len(groups)):
            if gi + 1 < len(groups):
                merge(prelim_gen(gi + 1), chain_gen(gi), k.cfg.get("gdn_ratio", 4))
            else:
                run_all(chain_gen(gi))
        P.barrier()
        P.release(mku)
        if nsteps == NCH:
            outproj_acc_g(k, W_out, hh * 128, lambda b: v_fm.r(v_fm.t[:, b * BLK:(b + 1) * BLK], range(4 * b, 4 * b + 4)))
    P.barrier()
    P.release(mk)


def outproj_acc_g(k, W, r0, o_region):
    P = k.P
    s, wo = load_out_w(k, W, 0, r0)
    for m in range(8):
        for b in range(NBLK):
            bk = P.bank()
            P.mm(psb(k, bk), k.wr.r(wo[:, m * 128:(m + 1) * 128], s), o_region(b))
            c = bc(b)
            P.stt(xs(k, m, b), psb(k, bk), k.mod.r(k.mod.t[:, Q_G1 + m, c:c + 1]), xs(k, m, b), ALU.mult, ALU.add)


NCORES = 8


def f32(a):
    return np.ascontiguousarray(np.asarray(a, dtype=np.float32))


def prep_inputs(inp):
    g = {k: np.asarray(v) for k, v in inp.items()}
    DEPTH = 4
    shared = {}
    shared["ident"] = np.eye(128, dtype=np.float32)
    for L in range(DEPTH):
        shared["ada_w%d" % L] = f32(g["ada_w"][L:L + 1])
        shared["ffn_w_in%d" % L] = f32(g["ffn_w_in"][L:L + 1])
        shared["ffn_w_out%d" % L] = f32(g["ffn_w_out"][L:L + 1])
    shared["ada_b"] = f32(g["ada_b"].reshape(DEPTH, 48, 128).transpose(0, 2, 1))
    shared["lng"] = f32(g["ln_g"].reshape(DEPTH, 2, 8, 128).transpose(0, 1, 3, 2))
    shared["lnb"] = f32(g["ln_b"].reshape(DEPTH, 2, 8, 128).transpose(0, 1, 3, 2))
    shared["fcw"] = f32(g["ffn_conv"].reshape(DEPTH, 9, 44, 128).transpose(0, 3, 2, 1))
    shared["lru_w_in"] = f32(g["lru_w_in"])
    shared["lru_w_out"] = f32(g["lru_w_out"])
    shared["lru_gate_w"] = f32(g["lru_gate_w"])
    sm = np.zeros((128, 10, 12), np.float32)
    sm[:, :, 0:4] = g["lru_conv"][0].reshape(4, 10, 128).transpose(2, 1, 0)
    sm[:, :, 4] = g["lru_conv_b"][0].reshape(10, 128).T
    sm[:, :, 5:9] = g["lru_gate_b"][0].reshape(4, 10, 128).transpose(2, 1, 0)
    sm[:, :, 9:11] = g["lru_lambda"][0].reshape(2, 10, 128).transpose(2, 1, 0)
    shared["lru_sm"] = sm
    tri_f = np.triu(np.ones((128, 128), np.float32))
    tri_b = np.tril(np.ones((128, 128), np.float32))
    shared["cmask"] = np.stack([tri_f, tri_b, (1 - tri_f) * -30000.0, (1 - tri_b) * -30000.0]).astype(np.float32)
    shared["ssd_w_in"] = f32(g["ssd_w_in"])
    shared["ssd_w_out"] = f32(g["ssd_w_out"])
    cw = np.zeros((128, 24, 5), np.float32)
    cw[:, :, 0:4] = g["ssd_conv"][0].reshape(4, 24, 128).transpose(2, 1, 0)
    cw[:, :, 4] = g["ssd_conv_b"][0].reshape(24, 128).T
    shared["ssd_cw"] = cw
    row = np.concatenate([g["ssd_dt_bias"][0].reshape(64), g["ssd_a_log"][0].reshape(64), g["ssd_d"][0].reshape(32)])
    shared["ssd_rows"] = f32(np.broadcast_to(row[None, :], (128, 160)))
    shared["ssd_nrm"] = f32(np.broadcast_to(g["ssd_norm"][0][None, :], (128, 2048)))
    for jj in range(2):
        shared["gdn_w_in%d" % jj] = f32(g["gdn_w_in"][jj:jj + 1])
        shared["gdn_w_out%d" % jj] = f32(g["gdn_w_out"][jj:jj + 1])
    shared["gdn_cw"] = f32(g["gdn_conv"].reshape(2, 4, 24, 128).transpose(0, 3, 2, 1))
    grow = np.concatenate([g["gdn_dt_bias"].reshape(2, 16), g["gdn_a_log"].reshape(2, 16), g["gdn_norm"].reshape(2, 128)], axis=1)
    shared["gdn_rows"] = f32(np.broadcast_to(grow[:, None, :], (2, 128, 160)))
    li = np.arange(128)[:, None]
    si = np.arange(128)[None, :]
    lv = np.zeros((2, 7, 2, 128, 128), np.float32)
    for jl in range(7):
        Bs = 2 ** jl
        mA = ((li // (2 * Bs)) == (si // (2 * Bs))) & ((li % (2 * Bs)) >= Bs) & ((si % (2 * Bs)) < Bs)
        mA = mA.astype(np.float32)
        lv[0, jl, 0] = -mA
        lv[0, jl, 1] = -mA.T
        lv[1, jl, 0] = -mA.T
        lv[1, jl, 1] = -mA
    shared["glvl"] = lv
    maps = []
    for i in range(NCORES):
        p0, p1, sb = 2 * i, 2 * i + 1, i % 4
        xin = np.concatenate([g["x_prompt"][p0].T, g["x_prompt"][p1].T, g["x_sample"][sb].T], axis=1)
        cond = np.stack([g["c_ctx"].reshape(8, 128).T, g["c"][sb].reshape(8, 128).T], axis=-1)
        m = dict(shared)
        m["xin"] = f32(xin)
        m["cond"] = f32(cond)
        m["ssd_s0"] = f32(g["state_ssd"][sb, 0].reshape(2, 8, 4, 64, 128).transpose(0, 1, 4, 2, 3).reshape(2, 8, 128, 256))
        m["gdn_s0"] = f32(g["state_gdn"][sb])
        m["lru_s0"] = f32(g["state_lru"][sb, 0].reshape(2, 10, 128).transpose(2, 1, 0))
        maps.append(m)
    return maps


def assemble(results, inp):
    BATCH, SEQ, D = 16, 256, 1024
    yp = np.zeros((BATCH, SEQ, D), np.float32)
    ys = np.zeros((4, 2048, D), np.float32)
    for i in range(NCORES):
        y = np.asarray(results[i]["yout"])
        yp[2 * i] = y[:, 0:256].T
        yp[2 * i + 1] = y[:, 256:512].T
        if i < 4:
            ys[i] = y[:, 512:].T
    nl = np.zeros((BATCH, 1, 2, 1280), np.float32)
    for i in range(NCORES):
        if "lru_out" in results[i]:
            o = np.asarray(results[i]["lru_out"])
            for pi in range(2):
                nl[2 * i + pi, 0] = o[:, :, pi, :].transpose(2, 1, 0).reshape(2, 1280)
    nssd = np.zeros((BATCH, 1, 2, 32, 64, 128), np.float32)
    for i in range(NCORES):
        if "ssd_out" in results[i]:
            o = np.asarray(results[i]["ssd_out"])
            for pi in range(2):
                nssd[2 * i + pi, 0] = o[pi].reshape(2, 32, 64, 128)
    ngdn = np.zeros((BATCH, 2, 2, 8, 128, 128), np.float32)
    for i in range(NCORES):
        if "gdn_out" in results[i]:
            o = np.asarray(results[i]["gdn_out"])
            for pi in range(2):
                ngdn[2 * i + pi] = o[:, pi]
    return yp, ys, nl, nssd, ngdn


_NC_CACHE = {}


def kernel(**inputs):
    cfg = {}
    if "nc" not in _NC_CACHE:
        _NC_CACHE["nc"] = build(cfg)
    nc, used = _NC_CACHE["nc"]
    maps = prep_inputs(inputs)
    maps = [{kk: v for kk, v in mm.items() if kk in used} for mm in maps]
    res = run_bass_kernel_spmd(nc, maps, core_ids=list(range(NCORES)))
    yp, ys, nl, nssd, ngdn = assemble(res.results, inputs)
    return (yp, ys, ngdn, nssd, nl)
```
